# Optimizing a Trainium2 kernel written in Bass

```python
import jax, jax.numpy as jnp
from jax import lax
import numpy as np

D_MODEL = 1024
BATCH = 4
SEQ = 8192
DEPTH = 1

RET_HEADS = 8
RET_HEAD_DIM = 64
RET_WIDTH = RET_HEADS * RET_HEAD_DIM
RET_CHUNK = 128
MLA_HEADS = 8
MLA_NOPE_DIM = 64
MLA_ROPE_DIM = 32
MLA_V_DIM = 64
MLA_Q_RANK = 256
MLA_KV_RANK = 128
MLA_WIDTH = MLA_HEADS * MLA_V_DIM
MIX_WIDTH = RET_WIDTH + MLA_WIDTH
IN_WIDTH = 4 * RET_WIDTH + MLA_Q_RANK + MLA_KV_RANK + MLA_ROPE_DIM
D_FF = 2816
CONV_WIDTH = 3
Q_BLOCK = 128
ROPE_BASE = 10000.0
EPS = 1e-6

kernel_name = "hybrid_retention_mla_convffn"


def rms_norm(x, w):
    xf = x.astype(jnp.float32)
    y = xf * lax.rsqrt(jnp.mean(xf * xf, axis=-1, keepdims=True) + EPS)
    return (y * w.astype(jnp.float32)).astype(x.dtype)


def rope(x, positions):
    d = x.shape[-1]
    inv_freq = ROPE_BASE ** (-jnp.arange(0, d, 2, dtype=jnp.float32) / d)
    ang = positions.astype(jnp.float32)[..., None] * inv_freq
    if x.ndim == 4:
        ang = ang[:, :, None, :]
    cos, sin = jnp.cos(ang), jnp.sin(ang)
    xf = x.astype(jnp.float32)
    x1, x2 = xf[..., : d // 2], xf[..., d // 2:]
    return jnp.concatenate([x1 * cos - x2 * sin, x1 * sin + x2 * cos], axis=-1).astype(x.dtype)


def retention(q, k, v):
    B, S, H, dk = q.shape
    dv = v.shape[-1]
    C = RET_CHUNK
    N = S // C
    log_gamma = jnp.log1p(-jnp.power(2.0, -5.0 - jnp.arange(H, dtype=jnp.float32)))
    qc = q.astype(jnp.float32).reshape(B, N, C, H, dk)
    kc = k.astype(jnp.float32).reshape(B, N, C, H, dk)
    vc = v.astype(jnp.float32).reshape(B, N, C, H, dv)
    idx = jnp.arange(C, dtype=jnp.float32)
    diff = idx[:, None] - idx[None, :]
    decay_mask = jnp.where(diff >= 0, jnp.exp(log_gamma[:, None, None] * jnp.maximum(diff, 0.0)), 0.0)
    scores = jnp.einsum('bnihd,bnjhd->bnhij', qc, kc) * decay_mask
    o_inner = jnp.einsum('bnhij,bnjhe->bnihe', scores, vc)
    zeta = jnp.exp(log_gamma[:, None] * (C - 1.0 - idx))
    chunk_states = jnp.einsum('bnjhd,hj,bnjhe->nbhde', kc, zeta, vc)
    chunk_decay = jnp.exp(log_gamma * C)[None, :, None, None]

    def step(R, s_n):
        return chunk_decay * R + s_n, R

    _, r_prev = lax.scan(step, jnp.zeros((B, H, dk, dv), jnp.float32), chunk_states)
    r_prev = jnp.moveaxis(r_prev, 0, 1)
    xi = jnp.exp(log_gamma[:, None] * (idx + 1.0))
    o_cross = jnp.einsum('bnihd,bnhde,hi->bnihe', qc, r_prev, xi)
    return (o_inner + o_cross).reshape(B, S, H, dv)


def retention_group(q, k, v, g, positions, gn_w):
    B, S, _ = q.shape
    q = rope(q.reshape(B, S, RET_HEADS, RET_HEAD_DIM), positions)
    k = rope(k.reshape(B, S, RET_HEADS, RET_HEAD_DIM), positions) * (RET_HEAD_DIM ** -0.5)
    v = v.reshape(B, S, RET_HEADS, RET_HEAD_DIM)
    o = retention(q, k, v)
    mu = jnp.mean(o, axis=-1, keepdims=True)
    var = jnp.mean(jnp.square(o - mu), axis=-1, keepdims=True)
    o = ((o - mu) * lax.rsqrt(var + EPS)).reshape(B, S, RET_WIDTH) * gn_w.astype(jnp.float32)
    return (jax.nn.silu(g.astype(jnp.float32)) * o).astype(g.dtype)


def mla_group(c_q, c_kv, k_pe, positions, q_norm_w, w_uq, kv_norm_w, w_ukv):
    B, S, _ = c_q.shape
    H = MLA_HEADS
    q = jnp.einsum('bsr,rf->bsf', rms_norm(c_q, q_norm_w), w_uq).reshape(B, S, H, MLA_NOPE_DIM + MLA_ROPE_DIM)
    q_nope = q[..., :MLA_NOPE_DIM]
    q_pe = rope(q[..., MLA_NOPE_DIM:], positions)
    kv = jnp.einsum('bsr,rf->bsf', rms_norm(c_kv, kv_norm_w), w_ukv).reshape(B, S, H, MLA_NOPE_DIM + MLA_V_DIM)
    k_nope = kv[..., :MLA_NOPE_DIM]
    v = kv[..., MLA_NOPE_DIM:]
    k_pe = rope(k_pe, positions)
    scale = (MLA_NOPE_DIM + MLA_ROPE_DIM) ** -0.5
    N = S // Q_BLOCK
    qn_b = jnp.moveaxis(q_nope.reshape(B, N, Q_BLOCK, H, MLA_NOPE_DIM), 1, 0)
    qp_b = jnp.moveaxis(q_pe.reshape(B, N, Q_BLOCK, H, MLA_ROPE_DIM), 1, 0)
    key_pos = jnp.arange(S)
    neg = jnp.finfo(jnp.float32).min

    def block(args):
        qn, qp, blk = args
        s = (jnp.einsum('bqhd,bkhd->bhqk', qn, k_nope)
             + jnp.einsum('bqhr,bkr->bhqk', qp, k_pe)).astype(jnp.float32) * scale
        q_pos = blk * Q_BLOCK + jnp.arange(Q_BLOCK)
        s = jnp.where(key_pos[None, :] <= q_pos[:, None], s, neg)
        p = jax.nn.softmax(s, axis=-1).astype(v.dtype)
        return jnp.einsum('bhqk,bkhd->bqhd', p, v)

    o = lax.map(block, (qn_b, qp_b, jnp.arange(N)))
    return jnp.moveaxis(o, 0, 1).reshape(B, S, MLA_WIDTH)


def conv_ffn(h, w_up, conv_w, conv_b, w_down):
    S = h.shape[1]
    u = jnp.einsum('bsd,df->bsf', h, w_up)
    up = jnp.pad(u, ((0, 0), (CONV_WIDTH - 1, 0), (0, 0)))
    u = conv_b + sum(conv_w[j] * up[:, j:j + S] for j in range(CONV_WIDTH))
    gate, val = u[..., :D_FF], u[..., D_FF:]
    return jnp.einsum('bsf,fd->bsd', jax.nn.silu(gate) * val, w_down)


def setup_inputs(seed: int = 0) -> dict:
    key = jax.random.key(seed)
    ks = jax.random.split(key, 20)
    f32 = jnp.float32

    def nrm(k, shape, fan_in):
        return jax.random.normal(k, shape, f32) * (fan_in ** -0.5)

    def gain(k, shape):
        return 1.0 + 0.02 * jax.random.normal(k, shape, f32)

    x = jax.random.normal(ks[0], (BATCH, SEQ, D_MODEL), f32)
    offset = jax.random.randint(ks[1], (BATCH, 1), 0, 4096, dtype=jnp.int32)
    positions = (offset + jnp.arange(SEQ, dtype=jnp.int32)[None, :]).astype(jnp.int32)
    return {
        "x": x,
        "positions": positions,
        "attn_norm_w": gain(ks[2], (DEPTH, D_MODEL)),
        "w_in": nrm(ks[3], (DEPTH, D_MODEL, IN_WIDTH), D_MODEL),
        "ret_gn_w": gain(ks[4], (DEPTH, RET_WIDTH)),
        "mla_q_norm_w": gain(ks[5], (DEPTH, MLA_Q_RANK)),
        "w_uq": nrm(ks[6], (DEPTH, MLA_Q_RANK, MLA_HEADS * (MLA_NOPE_DIM + MLA_ROPE_DIM)), MLA_Q_RANK),
        "mla_kv_norm_w": gain(ks[7], (DEPTH, MLA_KV_RANK)),
        "w_ukv": nrm(ks[8], (DEPTH, MLA_KV_RANK, MLA_HEADS * (MLA_NOPE_DIM + MLA_V_DIM)), MLA_KV_RANK),
        "w_out": nrm(ks[9], (DEPTH, MIX_WIDTH, D_MODEL), MIX_WIDTH),
        "ffn_norm_w": gain(ks[10], (DEPTH, D_MODEL)),
        "w_up": nrm(ks[11], (DEPTH, D_MODEL, 2 * D_FF), D_MODEL),
        "conv_w": nrm(ks[12], (DEPTH, CONV_WIDTH, 2 * D_FF), CONV_WIDTH),
        "conv_b": 0.01 * jax.random.normal(ks[13], (DEPTH, 2 * D_FF), f32),
        "w_down": nrm(ks[14], (DEPTH, D_FF, D_MODEL), D_FF),
        "final_norm_w": gain(ks[15], (D_MODEL,)),
    }


def reference(x, positions, attn_norm_w, w_in, ret_gn_w, mla_q_norm_w, w_uq, mla_kv_norm_w, w_ukv,
              w_out, ffn_norm_w, w_up, conv_w, conv_b, w_down, final_norm_w):
    splits = np.cumsum([RET_WIDTH, RET_WIDTH, RET_WIDTH, RET_WIDTH, MLA_Q_RANK, MLA_KV_RANK]).tolist()
    for l in range(DEPTH):
        h = rms_norm(x, attn_norm_w[l])
        proj = jnp.einsum('bsd,df->bsf', h, w_in[l])
        r_q, r_k, r_v, r_g, c_q, c_kv, k_pe = jnp.split(proj, splits, axis=-1)
        y_ret = retention_group(r_q, r_k, r_v, r_g, positions, ret_gn_w[l])
        y_mla = mla_group(c_q, c_kv, k_pe, positions, mla_q_norm_w[l], w_uq[l],
                          mla_kv_norm_w[l], w_ukv[l])
        mixed = jnp.concatenate([y_ret, y_mla.astype(y_ret.dtype)], axis=-1)
        x = x + jnp.einsum('bsm,md->bsd', mixed, w_out[l])
        x = x + conv_ffn(rms_norm(x, ffn_norm_w[l]), w_up[l], conv_w[l], conv_b[l], w_down[l])
    return rms_norm(x, final_norm_w)
```

```python
import math
from contextlib import ExitStack
import numpy as np
import concourse.bass as bass
import concourse.mybir as mybir
from concourse.bass_utils import run_bass_kernel_spmd

F32 = mybir.dt.float32
BF16 = mybir.dt.bfloat16
I32 = mybir.dt.int32
ALU = mybir.AluOpType
AF = mybir.ActivationFunctionType
AX = mybir.AxisListType

NT = 64
HALO = 31
NOWN = 33
EPS = 1e-6
TWO_PI = 2.0 * math.pi


class Res:
    __slots__ = ("name", "w", "r", "excl")

    def __init__(self, name, excl=False):
        self.name = name
        self.w = None
        self.r = {}
        self.excl = excl


class Sched:
    ENG = ("pe", "act", "dve", "pool", "sp")

    def __init__(self, nc, stack):
        self.nc = nc
        self.stack = stack
        self.sem = {e: stack.enter_context(nc.semaphore("s_" + e)) for e in self.ENG}
        self.cnt = {e: 0 for e in self.ENG}
        self.seen = {e: {} for e in self.ENG}
        self.streams = {e: [] for e in self.ENG}
        self.dsem = {}
        self.dcnt = {}

    def _wait(self, eng, toks):
        best = {}
        for t in toks:
            if t is None:
                continue
            k, s, v = t
            if k not in best or best[k][2] < v:
                best[k] = t
        for k, (kk, s, v) in best.items():
            if self.seen[eng].get(k, 0) >= v:
                continue
            self.seen[eng][k] = v
            self.streams[eng].append(("wait", s, v))

    def _deps(self, reads, writes, me=None):
        toks = []
        for r in reads:
            toks.append(r.w)
        skip = me if me == "pe" else None
        for w in writes:
            if w.w is not None and w.w[0] != skip:
                toks.append(w.w)
            toks.extend(t for k, t in w.r.items() if k != skip)
        return toks

    def op(self, eng, fn, reads=(), writes=(), inc=True):
        ex = [r for r in reads if r.excl and r not in writes]
        if ex:
            writes = list(writes) + ex
        self._wait(eng, self._deps(reads, writes, eng))
        if inc:
            self.cnt[eng] += 1
            tok = (eng, self.sem[eng], self.cnt[eng])
        else:
            tok = (eng, self.sem[eng], self.cnt[eng] + 1)
        self.streams[eng].append(("op", fn, inc))
        for r in reads:
            r.r[eng] = tok
        for w in writes:
            w.w = tok
            w.r = {}
        return tok

    def dma(self, eng, slot, fns, reads=(), writes=()):
        if slot not in self.dsem:
            self.dsem[slot] = self.stack.enter_context(self.nc.semaphore("d_" + slot))
            self.dcnt[slot] = 0
        self._wait(eng, self._deps(reads, writes))
        for fn in fns:
            self.dcnt[slot] += 16
            self.streams[eng].append(("dma", fn, self.dsem[slot]))
        tok = ("d_" + slot, self.dsem[slot], self.dcnt[slot])
        for r in reads:
            r.r["d_" + slot] = tok
        for w in writes:
            w.w = tok
            w.r = {}
        return tok

    def barrier(self):
        toks = [(e, self.sem[e], self.cnt[e]) for e in self.ENG if self.cnt[e] > 0]
        toks += [("d_" + s, self.dsem[s], self.dcnt[s]) for s in self.dsem if self.dcnt[s] > 0]
        for e in self.ENG:
            self._wait(e, toks)

    def emit(self):
        nc = self.nc
        with nc.Block() as block:
            def run(eng, e):
                sem = self.sem[eng]
                for item in self.streams[eng]:
                    if item[0] == "wait":
                        e.wait_ge(item[1], item[2])
                    elif item[0] == "op":
                        ins = item[1](e)
                        if item[2]:
                            ins.then_inc(sem, 1)
                    else:
                        item[1](e).then_inc(item[2], 16)

            @block.tensor
            def _(e):
                run("pe", e)

            @block.scalar
            def _(e):
                run("act", e)

            @block.vector
            def _(e):
                run("dve", e)

            @block.gpsimd
            def _(e):
                run("pool", e)

            @block.sync
            def _(e):
                run("sp", e)


def bc_mid(a, k):
    return bass.AP(a.tensor, a.offset, [list(a.ap[0]), [0, k]] + [list(x) for x in a.ap[1:]])


def bc_last(a, m):
    return bass.AP(a.tensor, a.offset, [list(x) for x in a.ap] + [[0, m]])


class Arena:
    def __init__(self, t, total):
        self.t = t
        self.total = total
        self.off = 0

    def f32(self, n):
        assert self.off + n <= self.total, ("arena overflow", self.off, n, self.total)
        a = self.t[:, self.off:self.off + n]
        self.off += n
        return a

    def bf(self, n):
        w = (n + 1) // 2
        return self.f32(w).bitcast(BF16)[:, 0:n]


import os
KSTOP = os.environ.get('KSTOP', '')
KSUB = int(os.environ.get('KSUB', 0))


def build_program():
    nc = bass.Bass("TRN2", target_bir_lowering=False)
    din = {}

    def inp(name, shape, dt=F32):
        din[name] = nc.dram_tensor(name, list(shape), dt, kind="ExternalInput").ap()
        return din[name]

    xc = inp("xc", [NT * 128, 1024])
    posc = inp("posc", [128, NT], I32)
    valid = inp("valid", [128, NT])
    c_dt = inp("c_dt", [128, 1024])
    c_xi = inp("c_xi", [128, 512])
    c_zeta = inp("c_zeta", [128, 512])
    c_dec = inp("c_dec", [128, 4])
    c_invf = inp("c_invf", [128, 192])
    c_off = inp("c_off", [128, 192])
    c_mask = inp("c_mask", [128, 128])
    b_anw = inp("b_anw", [128, 1024])
    b_fnw = inp("b_fnw", [128, 1024])
    b_onw = inp("b_onw", [128, 1024])
    b_qnw = inp("b_qnw", [128, 256])
    b_kvnw = inp("b_kvnw", [128, 128])
    b_gnw = inp("b_gnw", [128, 512])
    c_cw = inp("c_cw", [128, 44 * 3])
    c_cb = inp("c_cb", [128, 44])
    w_in_l = inp("w_in_l", [128, 8 * 2464])
    w_uq_l = inp("w_uq_l", [128, 2 * 768])
    wk_l = inp("wk_l", [128, 512])
    wv_l = inp("wv_l", [128, 512])
    w_out_l = inp("w_out_l", [128, 8 * 1024])
    w_up_l = inp("w_up_l", [128, 44 * 1024])
    w_down_l = inp("w_down_l", [128, 22 * 1024])
    yout = nc.dram_tensor("yout", [4096, 1024], F32, kind="ExternalOutput").ap()

    s_wup = nc.dram_tensor("s_wup", [128, 44 * 1024], BF16).ap()
    s_wdown = nc.dram_tensor("s_wdown", [128, 22 * 1024], BF16).ap()
    s_wout = nc.dram_tensor("s_wout", [128, 8 * 1024], BF16).ap()
    s_kt = nc.dram_tensor("s_kt", [8, 96, NT * 128], BF16).ap()
    s_v = nc.dram_tensor("s_v", [8, 128, NT * 65], BF16).ap()
    s_qt = nc.dram_tensor("s_qt", [8, 96, NOWN * 128], BF16).ap()

    with ExitStack() as st:
        S = Sched(nc, st)
        TOT = 53000
        arena_t = st.enter_context(nc.sbuf_tensor("arena", [128, TOT], F32))
        ps = st.enter_context(nc.psum_tensor("ps", [128, 4096], F32))
        A = Arena(arena_t, TOT)

        def bank(i, n=512):
            return ps[:, i * 512:i * 512 + n]

        PB = [Res("psb%d" % i, excl=True) for i in range(8)]

        ident = A.bf(128)
        maskb = A.bf(128)
        mixT_r = A.bf(4 * NOWN * 128).rearrange("p (c t) -> p c t", c=4)
        R_ident, R_mask, R_mixr = Res("ident"), Res("mask"), [Res("mixr%d" % i) for i in range(NOWN)]
        R_mixm = [Res("mixm%d" % h) for h in range(8)]
        mark_persist = A.off

        S.op("pool", lambda e: e.memset(ident, 0.0), writes=[R_ident])
        S.op("pool", lambda e: e.affine_select(out=ident, in_=ident, pattern=[[-1, 128]], compare_op=ALU.not_equal,
                                               fill=1.0, base=0, channel_multiplier=1), reads=[R_ident], writes=[R_ident])

        w_in = A.bf(8 * 2464)
        w_in3 = w_in.rearrange("p (c f) -> p c f", c=8)
        w_uq = A.bf(2 * 768)
        w_uq3 = w_uq.rearrange("p (c f) -> p c f", c=2)
        wk = A.bf(512)
        wv = A.bf(512)
        R_win, R_wuq, R_wk, R_wv = Res("w_in"), Res("w_uq"), Res("wk"), Res("wv")
        stage = [A.f32(1024) for _ in range(2)]
        stageb = [A.bf(1024) for _ in range(2)]
        R_stage = [Res("stage0"), Res("stage1")]
        R_stageb = [Res("stageb0"), Res("stageb1")]
        R_scr = {k: Res(k) for k in ["s_wup", "s_wdown", "s_wout"]}
        pieces = []

        def add_pieces(src, ncols, dst_sb=None, dst_res=None, dst_dram=None, dram_res=None):
            c0 = 0
            while c0 < ncols:
                n = min(1024, ncols - c0)
                pieces.append((src, c0, n, dst_sb, dst_res, dst_dram, dram_res))
                c0 += n

        def piece_load_cast(k):
            src, c0, n, dst_sb, dst_res, dst_dram, dram_res = pieces[k]
            i = k % 2
            S.dma("act", "stg%d" % i, [lambda e: e.dma_start(out=stage[i][:, 0:n], in_=src[:, c0:c0 + n])], writes=[R_stage[i]])
            if dst_sb is not None:
                S.op("pool", lambda e: e.tensor_copy(out=dst_sb[:, c0:c0 + n], in_=stage[i][:, 0:n]), reads=[R_stage[i]], writes=[dst_res])
            else:
                S.op("pool", lambda e: e.tensor_copy(out=stageb[i][:, 0:n], in_=stage[i][:, 0:n]), reads=[R_stage[i]], writes=[R_stageb[i]])

        def piece_store(k):
            src, c0, n, dst_sb, dst_res, dst_dram, dram_res = pieces[k]
            i = k % 2
            if dst_dram is not None:
                S.dma("act", "stb%d" % i, [lambda e: e.dma_start(out=dst_dram[:, c0:c0 + n], in_=stageb[i][:, 0:n])],
                      reads=[R_stageb[i]], writes=[dram_res])

        add_pieces(w_in_l, 8 * 2464, dst_sb=w_in, dst_res=R_win)
        add_pieces(w_uq_l, 2 * 768, dst_sb=w_uq, dst_res=R_wuq)
        add_pieces(wk_l, 512, dst_sb=wk, dst_res=R_wk)
        add_pieces(wv_l, 512, dst_sb=wv, dst_res=R_wv)
        add_pieces(c_mask, 128, dst_sb=maskb, dst_res=R_mask)
        n_first = len(pieces)
        add_pieces(w_out_l, 8 * 1024, dst_dram=s_wout, dram_res=R_scr["s_wout"])
        add_pieces(w_down_l, 22 * 1024, dst_dram=s_wdown, dram_res=R_scr["s_wdown"])
        add_pieces(w_up_l, 44 * 1024, dst_dram=s_wup, dram_res=R_scr["s_wup"])
        for k in range(n_first):
            piece_load_cast(k)
        pk = [n_first, n_first]

        def cast_step(nload):
            while pk[1] < pk[0]:
                piece_store(pk[1])
                pk[1] += 1
            for _ in range(nload):
                if pk[0] < len(pieces):
                    piece_load_cast(pk[0])
                    pk[0] += 1

        def load_const(src, n, name, dt=F32):
            a = A.f32(n)
            if dt is not F32:
                a = a.bitcast(dt)
            r = Res(name)
            S.dma("sp", "c_" + name, [lambda e: e.dma_start(out=a, in_=src)], writes=[r])
            return a, r

        dt_t, R_dt = load_const(c_dt, 1024, "dt")
        xi_t, R_xi = load_const(c_xi, 512, "xi")
        zeta_t, R_zeta = load_const(c_zeta, 512, "zeta")
        dec_t, R_dec = load_const(c_dec, 4, "dec")
        invf_t, R_invf = load_const(c_invf, 192, "invf")
        off_t, R_off = load_const(c_off, 192, "off")
        anw_t, R_anw = load_const(b_anw, 1024, "anw")
        qnw_t, R_qnw = load_const(b_qnw, 256, "qnw")
        kvnw_t, R_kvnw = load_const(b_kvnw, 128, "kvnw")
        gnw_t, R_gnw = load_const(b_gnw, 512, "gnw")
        posi, R_pos = load_const(posc, NT, "posi", I32)
        valid_t, R_valid = load_const(valid, NT, "valid")
        posf = A.f32(NT)
        S.op("dve", lambda e: e.tensor_copy(out=posf, in_=posi), reads=[R_pos], writes=[R_pos])

        TB = 16
        tab = A.f32(TB * 192)
        tab3 = tab.rearrange("p (n f) -> p n f", n=TB)
        R_tab = Res("tab")
        ttmp = A.f32(192)
        tti = A.f32(192).bitcast(I32)
        R_tt = Res("ttmp")

        def make_tables(n0):
            for n in range(n0, n0 + TB):
                S.op("dve", lambda e, n=n: e.scalar_tensor_tensor(out=ttmp, in0=invf_t, scalar=posf[:, n:n + 1], in1=off_t,
                                                                op0=ALU.mult, op1=ALU.add),
                     reads=[R_invf, R_off, R_pos], writes=[R_tt])
                S.op("dve", lambda e: e.tensor_copy(out=tti, in_=ttmp), reads=[R_tt], writes=[R_tt])
                S.op("dve", lambda e, n=n: e.tensor_tensor(out=tab3[:, n % TB, :], in0=ttmp, in1=tti, op=ALU.subtract),
                     reads=[R_tt], writes=[R_tab])
            S.op("act", lambda e: e.activation(out=tab, in_=tab, func=AF.Sin, scale=TWO_PI * (1.0 - 1e-6)),
                 reads=[R_tab], writes=[R_tab])

        xbuf = [A.f32(1024) for _ in range(2)]
        R_x = [Res("x0"), Res("x1")]
        junk = A.f32(1024)
        R_junk = Res("junk")
        st_small = A.f32(64)
        R_ss = Res("ss")
        hb = A.bf(1024)
        R_hb = Res("hb")
        hT = A.bf(1024)
        hT3 = hT.rearrange("p (c t) -> p c t", c=8)
        R_hT = Res("hT")
        tmpA = A.f32(1024)
        tmpB = A.f32(1024)
        R_tA, R_tB = Res("tmpA"), Res("tmpB")
        qkr = A.bf(1024)
        R_qkr = Res("qkr")
        kz = A.bf(512)
        R_kz = Res("kz")
        vb = A.bf(512)
        R_vb = Res("vb")
        sg = A.f32(512)
        R_sg = Res("sg")
        qT = A.bf(1024)
        qxT = A.bf(512)
        kT = A.bf(512)
        R_qT, R_qxT, R_kT = Res("qT"), Res("qxT"), Res("kT")
        sd = A.bf(1024)
        R_sd = Res("sd")
        R32 = A.f32(512)
        Rb = A.bf(512)
        R_R32, R_Rb = Res("R32"), Res("Rb")
        gn1 = A.f32(512)
        gn2 = A.f32(512)
        R_gn1, R_gn2 = Res("gn1"), Res("gn2")
        yb = A.bf(512)
        R_yb = Res("yb")
        cqn = A.bf(256)
        ckvn = A.bf(128)
        kr = A.bf(32)
        R_cqn, R_ckvn, R_kr = Res("cqn"), Res("ckvn"), Res("kr")
        cqnT = A.bf(256)
        ckvnT = A.bf(128)
        R_cqnT, R_ckvnT = Res("cqnT"), Res("ckvnT")
        mt1 = A.f32(64)
        mt2 = A.f32(64)
        R_mt = Res("mt")
        qb = A.bf(768)
        R_qb = Res("qb")
        kn_g = [A.bf(4 * 512) for _ in range(2)]
        kpe_g = [A.bf(512) for _ in range(2)]
        v_g = [A.bf(4 * 8 * 65) for _ in range(2)]
        q_g = [A.bf(8 * 512) for _ in range(2)]
        R_kng = [Res("kng0"), Res("kng1")]
        R_kpg = [Res("kpg0"), Res("kpg1")]
        R_vg = [Res("vg0"), Res("vg1")]
        R_qg = [Res("qg0"), Res("qg1")]
        R_skt, R_sv, R_sqt = Res("s_kt"), Res("s_v"), Res("s_qt")

        S.op("dve", lambda e: e.memset(qT, 0.0), writes=[R_qT])
        S.op("dve", lambda e: e.memset(R32, 0.0), writes=[R_R32])
        S.op("dve", lambda e: e.memset(Rb, 0.0), writes=[R_Rb])

        x_tiles = xc.rearrange("(n p) d -> n p d", p=128)

        def load_x(n):
            i = n % 2
            S.dma("sp", "x%d" % i, [lambda e, n=n, i=i: e.dma_start(out=xbuf[i], in_=x_tiles[n])], writes=[R_x[i]])

        if KSTOP == 'A0':
            npz = int(os.environ.get('KNP', 0))
            while pk[1] < min(len(pieces), n_first + npz):
                cast_step(2)
            S.barrier(); S.emit(); return nc
        load_x(0)

        def _tileA(n):
            cast_step(2)
            full = n >= HALO
            g, gi = divmod(n, 4)
            gb = g % 2
            if n + 1 < NT:
                load_x(n + 1)
            xb_, Rx = xbuf[n % 2], R_x[n % 2]
            ss = st_small[:, 0:1]
            rstd = st_small[:, 1:2]
            S.op("act", lambda e, xb_=xb_: e.activation(out=junk, in_=xb_, func=AF.Square, accum_out=ss),
                 reads=[Rx], writes=[R_junk, R_ss])
            S.op("dve", lambda e: e.tensor_scalar(out=ss, in0=ss, scalar1=1.0 / 1024, scalar2=EPS, op0=ALU.mult, op1=ALU.add),
                 reads=[R_ss], writes=[R_ss])
            S.op("act", lambda e: e.activation(out=ss, in_=ss, func=AF.Sqrt), reads=[R_ss], writes=[R_ss])
            S.op("dve", lambda e: e.reciprocal(out=rstd, in_=ss), reads=[R_ss], writes=[R_ss])
            S.op("dve", lambda e, xb_=xb_: e.scalar_tensor_tensor(out=hb, in0=xb_, scalar=rstd, in1=anw_t, op0=ALU.mult, op1=ALU.mult),
                 reads=[Rx, R_ss, R_anw], writes=[R_hb])
            tb = bank(0).bitcast(BF16)
            for c in range(8):
                S.op("pe", lambda e, c=c: e.transpose(out=tb[:, c * 128:(c + 1) * 128], in_=hb[:, c * 128:(c + 1) * 128], identity=ident),
                     reads=[R_hb, R_ident], writes=[PB[0]], inc=(c == 7))
            S.op("act", lambda e: e.activation(out=hT, in_=tb, func=AF.Copy), reads=[PB[0]], writes=[R_hT])
            if KSUB == 1 and n >= HALO:
                return


            def proj(bk, col0, ncol, n_=None):
                for c in range(8):
                    S.op("pe", lambda e, c=c: e.matmul(bank(bk, ncol), lhsT=hT3[:, c, :], rhs=w_in3[:, c, col0:col0 + ncol],
                                                       start=(c == 0), stop=(c == 7)),
                         reads=[R_hT, R_win], writes=[PB[bk]], inc=(c == 7))

            if full:
                proj(1, 0, 512)
            proj(2, 512, 512)
            proj(3, 1024, 512)
            if full:
                proj(4, 1536, 512)
                proj(5, 2048, 416)
            else:
                proj(5, 2304, 160)
            if KSUB == 2 and n >= HALO:
                return

            latoff = 0 if full else -256
            if n % TB == 0:
                make_tables(n)
            tabn = tab3[:, n % TB, :]
            cs_r, ss_r = tabn[:, 0:64], tabn[:, 64:128]
            cs_m, ss_m = tabn[:, 128:160], tabn[:, 160:192]

            def rope(src_ap, nh, hd, cs, sn, dstA, dstB, dst, reads, wres):
                half = hd // 2
                x3 = src_ap.rearrange("p (h d) -> p h d", h=nh)
                sw = bass.AP(src_ap.tensor, src_ap.offset + half,
                             [list(src_ap.ap[0]), [hd, nh], [-half, 2], [1, half]])
                a3 = dstA.rearrange("p (h d) -> p h d", h=nh)
                b4 = dstB.rearrange("p (h a d) -> p h a d", h=nh, a=2)
                S.op("dve", lambda e: e.tensor_tensor(out=a3, in0=x3, in1=bc_mid(cs, nh), op=ALU.mult),
                     reads=reads + [R_tab], writes=[R_tA])
                S.op("dve", lambda e: e.tensor_tensor(out=b4, in0=sw, in1=bc_mid(sn.rearrange("p (a d) -> p a d", a=2), nh), op=ALU.mult),
                     reads=reads + [R_tab], writes=[R_tB])
                S.op("dve", lambda e: e.tensor_tensor(out=dst, in0=dstA, in1=dstB, op=ALU.add),
                     reads=[R_tA, R_tB], writes=[wres])

            if full:
                rope(ps[:, 512:1536], 16, 64, cs_r, ss_r, tmpA, tmpB, qkr, [PB[1], PB[2]], R_qkr)
            else:
                rope(ps[:, 1024:1536], 8, 64, cs_r, ss_r, tmpA[:, 0:512], tmpB[:, 0:512], qkr[:, 512:1024], [PB[2]], R_qkr)
            S.op("dve", lambda e: e.tensor_tensor(out=kz, in0=qkr[:, 512:1024], in1=zeta_t, op=ALU.mult),
                 reads=[R_qkr, R_zeta], writes=[R_kz])
            S.op("act", lambda e: e.activation(out=vb, in_=bank(3), func=AF.Copy), reads=[PB[3]], writes=[R_vb])
            if KSUB == 3 and n >= HALO:
                return


            if full:
                S.op("act", lambda e: e.activation(out=sg, in_=bank(4), func=AF.Silu), reads=[PB[4]], writes=[R_sg])
                for c in range(8):
                    S.op("pe", lambda e, c=c: e.transpose(out=tb[:, c * 128:(c + 1) * 128], in_=qkr[:, c * 128:(c + 1) * 128], identity=ident),
                         reads=[R_qkr, R_ident], writes=[PB[0]], inc=(c == 7))
                S.op("act", lambda e: e.activation(out=qT[0:64, 0:512], in_=tb[0:64, 0:512], func=AF.Copy), reads=[PB[0]], writes=[R_qT])
                S.op("act", lambda e: e.activation(out=qT[64:128, 512:1024], in_=tb[64:128, 0:512], func=AF.Copy), reads=[PB[0]], writes=[R_qT])
                S.op("dve", lambda e: e.tensor_tensor(out=qxT, in0=tb[:, 0:512], in1=xi_t, op=ALU.mult), reads=[PB[0], R_xi], writes=[R_qxT])
                S.op("act", lambda e: e.activation(out=kT, in_=tb[:, 512:1024], func=AF.Copy), reads=[PB[0]], writes=[R_kT])
                for h in range(8):
                    p_, a_ = divmod(h, 2)
                    rows = slice(a_ * 64, a_ * 64 + 64)
                    S.op("pe", lambda e, h=h, p_=p_, rows=rows: e.matmul(ps[:, 3072 + h * 128:3072 + (h + 1) * 128],
                                                                         lhsT=kT[:, p_ * 128:(p_ + 1) * 128],
                                                                         rhs=qT[:, (h % 2) * 512 + p_ * 128:(h % 2) * 512 + (p_ + 1) * 128],
                                                                         start=True, stop=True),
                         reads=[R_kT, R_qT], writes=[PB[6], PB[7]], inc=(h == 7))
                S.op("dve", lambda e: e.tensor_tensor(out=sd, in0=ps[:, 3072:4096], in1=dt_t, op=ALU.mult),
                     reads=[PB[6], PB[7], R_dt], writes=[R_sd])
                for h in range(8):
                    p_, a_ = divmod(h, 2)
                    rows = slice(a_ * 64, a_ * 64 + 64)
                    S.op("pe", lambda e, h=h: e.matmul(ps[:, 512 + h * 64:512 + (h + 1) * 64], lhsT=sd[:, h * 128:(h + 1) * 128],
                                                       rhs=vb[:, h * 64:(h + 1) * 64], start=True, stop=False),
                         reads=[R_sd, R_vb, R_qkr], writes=[PB[1]], inc=False)
                    S.op("pe", lambda e, h=h, p_=p_, rows=rows: e.matmul(ps[:, 512 + h * 64:512 + (h + 1) * 64],
                                                                         lhsT=qxT[:, p_ * 128:(p_ + 1) * 128],
                                                                         rhs=Rb[:, h * 64:(h + 1) * 64],
                                                                         start=False, stop=True),
                         reads=[R_qxT, R_Rb], writes=[PB[1]], inc=(h == 7))
            for p_ in range(4):
                S.op("pe", lambda e, p_=p_: e.matmul(ps[:, 3072 + p_ * 128:3072 + (p_ + 1) * 128], lhsT=kz[:, p_ * 128:(p_ + 1) * 128],
                                                     rhs=vb[:, p_ * 128:(p_ + 1) * 128], start=True, stop=True),
                     reads=[R_kz, R_vb, R_sd], writes=[PB[6]], inc=(p_ == 3))
            for h in range(8):
                p_, a_ = divmod(h, 2)
                rows = slice(a_ * 64, a_ * 64 + 64)
                S.op("dve", lambda e, h=h, p_=p_, a_=a_, rows=rows: e.scalar_tensor_tensor(
                    out=R32[rows, h * 64:(h + 1) * 64], in0=R32[rows, h * 64:(h + 1) * 64], scalar=dec_t[rows, p_:p_ + 1],
                    in1=ps[rows, 3072 + p_ * 128 + a_ * 64:3072 + p_ * 128 + a_ * 64 + 64], op0=ALU.mult, op1=ALU.add),
                    reads=[PB[6], R_dec, R_R32], writes=[R_R32])
            S.op("act", lambda e: e.activation(out=Rb, in_=R32, func=AF.Copy), reads=[R_R32], writes=[R_Rb])
            if KSUB == 4 and n >= HALO:
                return


            if full:
                o3 = bank(1).rearrange("p (h d) -> p h d", h=8)
                s1, s2, mean, msq, var = (mt1[:, 0:8], mt1[:, 8:16], mt1[:, 16:24], mt1[:, 24:32], mt1[:, 32:40])
                S.op("dve", lambda e: e.tensor_reduce(out=s1, in_=o3, axis=AX.X, op=ALU.add), reads=[PB[1]], writes=[R_mt])
                S.op("act", lambda e: e.activation(out=gn1, in_=bank(1), func=AF.Square), reads=[PB[1]], writes=[R_gn1])
                S.op("dve", lambda e: e.tensor_reduce(out=s2, in_=gn1.rearrange("p (h d) -> p h d", h=8), axis=AX.X, op=ALU.add),
                     reads=[R_gn1], writes=[R_mt])
                S.op("dve", lambda e: e.tensor_scalar(out=mean, in0=s1, scalar1=1.0 / 64, scalar2=None, op0=ALU.mult), reads=[R_mt], writes=[R_mt])
                S.op("dve", lambda e: e.tensor_tensor(out=msq, in0=mean, in1=mean, op=ALU.mult), reads=[R_mt], writes=[R_mt])
                S.op("dve", lambda e: e.scalar_tensor_tensor(out=var, in0=s2, scalar=1.0 / 64, in1=msq, op0=ALU.mult, op1=ALU.subtract),
                     reads=[R_mt], writes=[R_mt])
                S.op("dve", lambda e: e.tensor_scalar(out=var, in0=var, scalar1=EPS, scalar2=None, op0=ALU.add), reads=[R_mt], writes=[R_mt])
                S.op("act", lambda e: e.activation(out=var, in_=var, func=AF.Sqrt), reads=[R_mt], writes=[R_mt])
                S.op("dve", lambda e: e.reciprocal(out=var, in_=var), reads=[R_mt], writes=[R_mt])
                g13 = gn1.rearrange("p (h d) -> p h d", h=8)
                S.op("dve", lambda e: e.tensor_tensor(out=g13, in0=o3, in1=bc_last(mean, 64), op=ALU.subtract),
                     reads=[PB[1], R_mt], writes=[R_gn1])
                S.op("dve", lambda e: e.tensor_tensor(out=g13, in0=g13, in1=bc_last(var, 64), op=ALU.mult), reads=[R_gn1, R_mt], writes=[R_gn1])
                S.op("dve", lambda e: e.tensor_tensor(out=gn2, in0=sg, in1=gnw_t, op=ALU.mult), reads=[R_sg, R_gnw], writes=[R_gn2])
                S.op("dve", lambda e: e.tensor_tensor(out=yb, in0=gn1, in1=gn2, op=ALU.mult), reads=[R_gn1, R_gn2], writes=[R_yb])
                for c in range(4):
                    S.op("pe", lambda e, c=c: e.transpose(out=tb[:, c * 128:(c + 1) * 128], in_=yb[:, c * 128:(c + 1) * 128], identity=ident),
                         reads=[R_yb, R_ident], writes=[PB[0]], inc=(c == 3))
                m = n - HALO
                S.op("act", lambda e, m=m: e.activation(out=mixT_r[:, :, m * 128:(m + 1) * 128], in_=tb[:, 0:512].rearrange("p (c t) -> p c t", c=4), func=AF.Copy),
                     reads=[PB[0]], writes=[R_mixr[m]])

            lat = bank(5)
            ckv_ap = lat[:, 256 + latoff:384 + latoff]
            kpe_ap = lat[:, 384 + latoff:416 + latoff]
            ssq = st_small[:, 4:6]
            rq = st_small[:, 6:8]
            if full:
                S.op("act", lambda e: e.activation(out=junk[:, 0:256], in_=lat[:, 0:256], func=AF.Square, accum_out=ssq[:, 0:1]),
                     reads=[PB[5]], writes=[R_junk, R_ss])
            S.op("act", lambda e: e.activation(out=junk[:, 256:384], in_=ckv_ap, func=AF.Square, accum_out=ssq[:, 1:2]),
                 reads=[PB[5]], writes=[R_junk, R_ss])
            if full:
                S.op("dve", lambda e: e.tensor_scalar(out=ssq[:, 0:1], in0=ssq[:, 0:1], scalar1=1.0 / 256, scalar2=EPS, op0=ALU.mult, op1=ALU.add),
                     reads=[R_ss], writes=[R_ss])
            S.op("dve", lambda e: e.tensor_scalar(out=ssq[:, 1:2], in0=ssq[:, 1:2], scalar1=1.0 / 128, scalar2=EPS, op0=ALU.mult, op1=ALU.add),
                 reads=[R_ss], writes=[R_ss])
            lo = 0 if full else 1
            S.op("act", lambda e, lo=lo: e.activation(out=ssq[:, lo:2], in_=ssq[:, lo:2], func=AF.Sqrt), reads=[R_ss], writes=[R_ss])
            S.op("dve", lambda e, lo=lo: e.reciprocal(out=rq[:, lo:2], in_=ssq[:, lo:2]), reads=[R_ss], writes=[R_ss])
            if full:
                S.op("dve", lambda e: e.scalar_tensor_tensor(out=cqn, in0=lat[:, 0:256], scalar=rq[:, 0:1], in1=qnw_t, op0=ALU.mult, op1=ALU.mult),
                     reads=[PB[5], R_ss, R_qnw], writes=[R_cqn])
            S.op("dve", lambda e: e.scalar_tensor_tensor(out=ckvn, in0=ckv_ap, scalar=rq[:, 1:2], in1=kvnw_t, op0=ALU.mult, op1=ALU.mult),
                 reads=[PB[5], R_ss, R_kvnw], writes=[R_ckvn])
            rope(kpe_ap, 1, 32, cs_m, ss_m, tmpA[:, 0:32], tmpB[:, 0:32], kr, [PB[5]], R_kr)
            if KSUB == 5 and n >= HALO:
                return

            S.op("pe", lambda e: e.transpose(out=tb[:, 0:128], in_=ckvn, identity=ident), reads=[R_ckvn, R_ident], writes=[PB[0]], inc=False)
            S.op("pe", lambda e: e.transpose(out=tb[0:32, 128:256], in_=kr, identity=ident), reads=[R_kr, R_ident], writes=[PB[0]], inc=not full)
            if full:
                for c in range(2):
                    S.op("pe", lambda e, c=c: e.transpose(out=tb[:, 256 + c * 128:256 + (c + 1) * 128], in_=cqn[:, c * 128:(c + 1) * 128], identity=ident),
                         reads=[R_cqn, R_ident], writes=[PB[0]], inc=(c == 1))
            S.op("act", lambda e: e.activation(out=ckvnT, in_=tb[:, 0:128], func=AF.Copy), reads=[PB[0]], writes=[R_ckvnT])
            S.op("act", lambda e, gb=gb, gi=gi: e.activation(out=kpe_g[gb][0:32, gi * 128:(gi + 1) * 128], in_=tb[0:32, 128:256], func=AF.Copy),
                 reads=[PB[0]], writes=[R_kpg[gb]])
            if KSUB == 6 and n >= HALO:
                return

            if full:
                S.op("act", lambda e: e.activation(out=cqnT, in_=tb[:, 256:512], func=AF.Copy), reads=[PB[0]], writes=[R_cqnT])
            for p_ in range(4):
                S.op("pe", lambda e, p_=p_: e.matmul(ps[:, 3584 + p_ * 128:3584 + (p_ + 1) * 128], lhsT=wk[:, p_ * 128:(p_ + 1) * 128], rhs=ckvnT,
                                                     start=True, stop=True), reads=[R_wk, R_ckvnT, R_sd], writes=[PB[7]], inc=(p_ == 3))
            kng3 = kn_g[gb].rearrange("p (a k) -> p a k", a=4)
            S.op("act", lambda e, gi=gi, kng3=kng3: e.activation(out=kng3[:, :, gi * 128:(gi + 1) * 128], in_=bank(7).rearrange("p (a k) -> p a k", a=4), func=AF.Copy),
                 reads=[PB[7]], writes=[R_kng[gb]])
            S.op("pe", lambda e: e.matmul(bank(4), lhsT=ckvnT, rhs=wv, start=True, stop=True), reads=[R_wv, R_ckvnT, R_sg], writes=[PB[4]])
            vg4 = v_g[gb].rearrange("p (h t e) -> p h t e", h=8, t=4)
            S.op("dve", lambda e, gi=gi, vg4=vg4: e.tensor_copy(out=vg4[:, :, gi, 0:64], in_=bank(4).rearrange("p (h e) -> p h e", h=8)),
                 reads=[PB[4]], writes=[R_vg[gb]])
            S.op("dve", lambda e, gi=gi, vg4=vg4, n=n: e.tensor_copy(out=vg4[:, :, gi, 64:65], in_=bc_mid(valid_t[:, n:n + 1], 8)),
                 reads=[R_valid], writes=[R_vg[gb]])
            if KSUB == 7 and n >= HALO:
                return

            if full:
                for hf in range(2):
                    for c in range(2):
                        S.op("pe", lambda e, hf=hf, c=c: e.matmul(ps[:, (2 + hf) * 512:(2 + hf) * 512 + 384], lhsT=cqnT[:, c * 128:(c + 1) * 128],
                                                                  rhs=w_uq3[:, c, hf * 384:(hf + 1) * 384], start=(c == 0), stop=(c == 1)),
                             reads=[R_cqnT, R_wuq, R_vb, R_kz, R_qkr], writes=[PB[2 + hf]], inc=(c == 1))
                qb3 = qb.rearrange("p (h d) -> p h d", h=8)
                for hf in range(2):
                    src = ps[:, (2 + hf) * 512:(2 + hf) * 512 + 384]
                    s3 = src.rearrange("p (h d) -> p h d", h=4)
                    S.op("act", lambda e, hf=hf, s3=s3: e.activation(out=qb3[:, hf * 4:(hf + 1) * 4, 0:64], in_=s3[:, :, 0:64], func=AF.Copy),
                         reads=[PB[2 + hf]], writes=[R_qb])
                    x3 = s3[:, :, 64:96]
                    sw = bass.AP(src.tensor, src.offset + 64 + 16, [list(src.ap[0]), [96, 4], [-16, 2], [1, 16]])
                    a3 = tmpA[:, 0:128].rearrange("p (h d) -> p h d", h=4)
                    b4 = tmpB[:, 0:128].rearrange("p (h a d) -> p h a d", h=4, a=2)
                    S.op("dve", lambda e, x3=x3, a3=a3: e.tensor_tensor(out=a3, in0=x3, in1=bc_mid(cs_m, 4), op=ALU.mult),
                         reads=[PB[2 + hf], R_tab], writes=[R_tA])
                    S.op("dve", lambda e, sw=sw, b4=b4: e.tensor_tensor(out=b4, in0=sw, in1=bc_mid(ss_m.rearrange("p (a d) -> p a d", a=2), 4), op=ALU.mult),
                         reads=[PB[2 + hf], R_tab], writes=[R_tB])
                    S.op("dve", lambda e, hf=hf, a3=a3: e.tensor_tensor(out=qb3[:, hf * 4:(hf + 1) * 4, 64:96], in0=a3,
                                                                        in1=tmpB[:, 0:128].rearrange("p (h d) -> p h d", h=4), op=ALU.add),
                         reads=[R_tA, R_tB], writes=[R_qb])
                for h in range(8):
                    S.op("pe", lambda e, h=h: e.transpose(out=tb[0:96, h * 128:(h + 1) * 128], in_=qb[:, h * 96:(h + 1) * 96], identity=ident),
                         reads=[R_qb, R_ident], writes=[PB[0]], inc=(h == 7))
                mg, mi = divmod(n - HALO, 4)
                qg3 = q_g[mg % 2].rearrange("p (h t) -> p h t", h=8)
                S.op("act", lambda e, mi=mi, qg3=qg3: e.activation(out=qg3[0:96, :, mi * 128:(mi + 1) * 128],
                                                                   in_=tb[0:96, :].rearrange("p (h t) -> p h t", h=8), func=AF.Copy),
                     reads=[PB[0]], writes=[R_qg[mg % 2]])
                if mi == 3 or n == NT - 1:
                    ntl = mi + 1
                    S.dma("sp", "sq%d" % (mg % 2),
                          [lambda e, mg=mg, ntl=ntl, qg3=qg3: e.dma_start(
                              out=s_qt[:, :, mg * 512:mg * 512 + ntl * 128].rearrange("h r t -> r h t"),
                              in_=qg3[0:96, :, 0:ntl * 128])],
                          reads=[R_qg[mg % 2]], writes=[R_sqt])
            if gi == 3:
                fns = []
                for a_ in range(2):
                    fns.append(lambda e, a_=a_, g=g, kng3=kng3: e.dma_start(
                        out=s_kt[:, 0:64, g * 512:(g + 1) * 512].rearrange("(p a) r k -> a r p k", a=2)[a_],
                        in_=kng3[a_ * 64:(a_ + 1) * 64, :, :]))
                fns.append(lambda e, g=g, gb=gb: e.dma_start(
                    out=s_kt[:, 64:96, g * 512:(g + 1) * 512].rearrange("h r k -> r h k"),
                    in_=bc_mid(kpe_g[gb][0:32, :], 8)))
                S.dma("sp", "sk%d" % gb, fns, reads=[R_kng[gb], R_kpg[gb]], writes=[R_skt])
                S.dma("sp", "sv%d" % gb,
                      [lambda e, g=g, vg4=vg4: e.dma_start(
                          out=s_v[:, :, g * 4 * 65:(g + 1) * 4 * 65].rearrange("h p (t e) -> p h t e", t=4),
                          in_=vg4)],
                      reads=[R_vg[gb]], writes=[R_sv])

        for n in range(int(os.environ.get('KNT', NT))):
            _tileA(n)
        while pk[1] < len(pieces):
            cast_step(2)

        S.barrier()
        if KSTOP == 'A':
            S.emit(); return nc
        A.off = mark_persist
        mixT_m = A.bf(4 * NOWN * 128).rearrange("p (c t) -> p c t", c=4)
        mark_persist = A.off
        ytmp = [A.bf(512) for _ in range(2)]
        R_ytmp = [Res("ytmp0"), Res("ytmp1")]
        QT = [A.bf(NOWN * 128) for _ in range(2)]
        KT = [A.bf(NT * 128) for _ in range(2)]
        VV = [A.bf(NT * 65) for _ in range(2)]
        R_Q, R_K, R_V = [Res("Q0"), Res("Q1")], [Res("K0"), Res("K1")], [Res("V0"), Res("V1")]
        PT = [A.bf(1024) for _ in range(3)]
        R_PT = [Res("PT%d" % i) for i in range(3)]
        rrow = A.f32(512)
        R_rrow = Res("rrow")
        ones_t = A.f32(64)
        R_ones = Res("ones")
        bcs = A.f32(512)
        R_bcs = Res("bcs")
        S.op("dve", lambda e: e.memset(ones_t, 1.0), writes=[R_ones])
        scale = (64 + 32) ** -0.5

        def load_head(h):
            i = h % 2
            S.dma("sp", "lq%d" % i, [lambda e: e.dma_start(out=QT[i][0:96, :], in_=s_qt[h])], reads=[R_sqt], writes=[R_Q[i]])
            S.dma("sp", "lk%d" % i, [lambda e: e.dma_start(out=KT[i][0:96, :], in_=s_kt[h])], reads=[R_skt], writes=[R_K[i]])
            S.dma("sp", "lv%d" % i, [lambda e: e.dma_start(out=VV[i], in_=s_v[h])], reads=[R_sv], writes=[R_V[i]])

        load_head(0)
        sgi = [0]
        pti = [0]
        oi = [0]
        def _headB(h):
            if h + 1 < 8:
                load_head(h + 1)
            i = h % 2
            V3 = VV[i].rearrange("p (t e) -> p t e", t=NT)
            qblocks = [(0, 128, [(kt, 0) for kt in range(HALO)] + [(HALO, 0)], HALO)]
            for j in range(8):
                q0 = 128 + 512 * j
                kts = [(kt, 0) for kt in range(32 + 4 * j)] + [(32 + 4 * j + m, 128 * m) for m in range(4)]
                qblocks.append((q0, 512, kts, 32 + 4 * j))
            def _qblockB(q0, qw, kts, diag0):
                ob = 4 + (oi[0] % 2)
                oi[0] += 1
                Ops = ps[0:65, ob * 512:ob * 512 + qw]
                npairs = (len(kts) + 1) // 2
                def _pairB(gidx):
                    pair = kts[2 * gidx:2 * gidx + 2]
                    sb_ = (sgi[0] % 2) * 2
                    sgi[0] += 1
                    pt = PT[pti[0] % 3]
                    Rpt = R_PT[pti[0] % 3]
                    pti[0] += 1
                    for u, (kt, c0) in enumerate(pair):
                        w_ = qw - c0
                        dst = ps[:, (sb_ + u) * 512 + c0:(sb_ + u) * 512 + qw]
                        isdiag = kt >= diag0
                        S.op("pe", lambda e, kt=kt, c0=c0, dst=dst, isdiag=isdiag: e.matmul(
                            dst, lhsT=KT[i][0:96, kt * 128:(kt + 1) * 128], rhs=QT[i][0:96, q0 + c0:q0 + qw], start=True, stop=not isdiag),
                            reads=[R_K[i], R_Q[i]], writes=[PB[sb_ + u]], inc=not isdiag)
                        if isdiag:
                            S.op("pe", lambda e, dst=dst: e.matmul(dst[:, 0:128], lhsT=ident, rhs=maskb, start=False, stop=True),
                                 reads=[R_ident, R_mask], writes=[PB[sb_ + u]])
                    if len(pair) == 2 and pair[0][1] == 0 and pair[1][1] == 0 and qw == 512:
                        S.op("act", lambda e, sb_=sb_, pt=pt: e.activation(out=pt, in_=ps[:, sb_ * 512:sb_ * 512 + 1024], func=AF.Exp, scale=scale),
                             reads=[PB[sb_], PB[sb_ + 1]], writes=[Rpt])
                    else:
                        for u, (kt, c0) in enumerate(pair):
                            S.op("act", lambda e, sb_=sb_, u=u, c0=c0, pt=pt: e.activation(
                                out=pt[:, u * 512 + c0:u * 512 + qw], in_=ps[:, (sb_ + u) * 512 + c0:(sb_ + u) * 512 + qw], func=AF.Exp, scale=scale),
                                reads=[PB[sb_ + u]], writes=[Rpt])
                    for u, (kt, c0) in enumerate(pair):
                        first = (gidx == 0 and u == 0)
                        last = (gidx == npairs - 1 and u == len(pair) - 1)
                        S.op("pe", lambda e, kt=kt, c0=c0, u=u, pt=pt, first=first, last=last: e.matmul(
                            Ops[:, c0:qw], lhsT=V3[:, kt, :], rhs=pt[:, u * 512 + c0:u * 512 + qw], start=first, stop=last),
                            reads=[R_V[i], Rpt], writes=[PB[ob]], inc=(last or u == len(pair) - 1))
                for gidx in range(npairs):
                    _pairB(gidx)
                S.op("dve", lambda e, ob=ob: e.tensor_scalar(out=rrow[64:65, 0:qw], in0=ps[64:65, ob * 512:ob * 512 + qw], scalar1=1e-30, scalar2=None, op0=ALU.max),
                     reads=[PB[ob]], writes=[R_rrow])
                S.op("dve", lambda e: e.reciprocal(out=rrow[64:65, 0:qw], in_=rrow[64:65, 0:qw]), reads=[R_rrow], writes=[R_rrow])
                S.op("pe", lambda e: e.matmul(ps[0:64, 6 * 512:6 * 512 + qw], lhsT=ones_t[64:65, 0:64], rhs=rrow[64:65, 0:qw], start=True, stop=True),
                     reads=[R_ones, R_rrow], writes=[PB[6]])
                S.op("act", lambda e: e.activation(out=bcs[0:64, 0:qw], in_=ps[0:64, 6 * 512:6 * 512 + qw], func=AF.Copy), reads=[PB[6]], writes=[R_bcs])
                yi_ = oi[0] % 2
                S.op("dve", lambda e, ob=ob, yi_=yi_: e.tensor_tensor(out=ytmp[yi_][0:64, 0:qw], in0=ps[0:64, ob * 512:ob * 512 + qw], in1=bcs[0:64, 0:qw], op=ALU.mult),
                     reads=[PB[ob], R_bcs], writes=[R_ytmp[yi_]])
                S.dma("sp", "ym%d" % yi_, [lambda e, h=h, yi_=yi_, q0=q0, qw=qw: e.dma_start(
                    out=mixT_m[(h % 2) * 64:(h % 2) * 64 + 64, h // 2, q0:q0 + qw], in_=ytmp[yi_][0:64, 0:qw])],
                    reads=[R_ytmp[yi_]], writes=[R_mixm[h]])

            for qb_ in qblocks:
                _qblockB(*qb_)

        for h in range(8):
            _headB(h)

        S.barrier()
        if KSTOP == 'AB':
            S.emit(); return nc
        A.off = mark_persist
        wout = A.bf(8 * 1024)
        wout3 = wout.rearrange("p (c f) -> p c f", c=8)
        wdown = A.bf(22 * 1024)
        wdown3 = wdown.rearrange("p (c f) -> p c f", c=22)
        R_wout, R_wdown = Res("wout"), Res("wdown")
        S.dma("sp", "wc1", [lambda e: e.dma_start(out=wout, in_=s_wout)], reads=[R_scr["s_wout"]], writes=[R_wout])
        S.dma("sp", "wc2", [lambda e: e.dma_start(out=wdown, in_=s_wdown)], reads=[R_scr["s_wdown"]], writes=[R_wdown])
        fnw_t, R_fnw = load_const(b_fnw, 1024, "fnw")
        onw_t, R_onw = load_const(b_onw, 1024, "onw")
        cw_t, R_cw = load_const(c_cw, 132, "cw")
        cw3 = cw_t.rearrange("p (c j) -> p c j", c=44)
        cb_t, R_cb = load_const(c_cb, 44, "cb")
        wupb = [A.bf(1024) for _ in range(2)]
        R_wupb = [Res("wup%d" % i) for i in range(2)]
        xb2 = [A.f32(1024) for _ in range(2)]
        R_xb2 = [Res("xb2_0"), Res("xb2_1")]
        x1 = A.f32(4 * 1024)
        x13 = x1.rearrange("p (t d) -> p t d", t=4)
        R_x1 = [Res("x1_%d" % i) for i in range(4)]
        h2b = A.bf(1024)
        R_h2b = Res("h2b")
        h2T = A.bf(8 * 512)
        h2T3 = h2T.rearrange("p (c t) -> p c t", c=8)
        R_h2T = Res("h2T")
        gT = A.bf(22 * 512)
        gT3 = gT.rearrange("p (c t) -> p c t", c=22)
        R_gT = Res("gT")
        ubuf = [A.f32(514) for _ in range(2)]
        R_ub = [Res("ub0"), Res("ub1")]
        acc = [A.f32(512) for _ in range(2)]
        R_acc = [Res("acc0"), Res("acc1")]
        carry = A.f32(44 * 2)
        carry3 = carry.rearrange("p (c j) -> p c j", c=44)
        R_carry = Res("carry")
        st2 = A.f32(16)
        R_st2 = Res("st2")
        ybuf = xb2
        R_yb2 = R_xb2
        S.op("dve", lambda e: e.memset(carry, 0.0), writes=[R_carry])
        wupi = [0]
        xli = [0]
        ybi = [0]
        y_tiles = yout.rearrange("(n p) d -> n p d", p=128)

        blocks = [(0, 1)] + [(1 + 4 * j, 4) for j in range(8)]
        def _blockC(bi, m0, ntl):
            W = ntl * 128
            tb = bank(7).bitcast(BF16)
            for t in range(ntl):
                m = m0 + t
                xi_ = xli[0] % 2
                xli[0] += 1
                S.dma("sp", "xc%d" % xi_, [lambda e, m=m, xi_=xi_: e.dma_start(out=xb2[xi_], in_=x_tiles[HALO + m])], writes=[R_xb2[xi_]])
                for hf in range(2):
                    for c in range(8):
                        src_ = mixT_r[:, c, m * 128:(m + 1) * 128] if c < 4 else mixT_m[:, c - 4, m * 128:(m + 1) * 128]
                        S.op("pe", lambda e, c=c, hf=hf, src_=src_: e.matmul(bank(hf), lhsT=src_, rhs=wout3[:, c, hf * 512:(hf + 1) * 512],
                                                                            start=(c == 0), stop=(c == 7)),
                             reads=[R_mixr[m], R_wout] + R_mixm, writes=[PB[hf]], inc=(c == 7))
                S.op("dve", lambda e, t=t, xi_=xi_: e.tensor_tensor(out=x13[:, t, :], in0=ps[:, 0:1024], in1=xb2[xi_], op=ALU.add),
                     reads=[PB[0], PB[1], R_xb2[xi_]], writes=[R_x1[t]])
                ss = st2[:, 0:1]
                rstd = st2[:, 1:2]
                S.op("act", lambda e, t=t: e.activation(out=h2b, in_=x13[:, t, :], func=AF.Square, accum_out=ss), reads=[R_x1[t]], writes=[R_h2b, R_st2])
                S.op("dve", lambda e: e.tensor_scalar(out=ss, in0=ss, scalar1=1.0 / 1024, scalar2=EPS, op0=ALU.mult, op1=ALU.add), reads=[R_st2], writes=[R_st2])
                S.op("act", lambda e: e.activation(out=ss, in_=ss, func=AF.Sqrt), reads=[R_st2], writes=[R_st2])
                S.op("dve", lambda e: e.reciprocal(out=rstd, in_=ss), reads=[R_st2], writes=[R_st2])
                S.op("dve", lambda e, t=t: e.scalar_tensor_tensor(out=h2b, in0=x13[:, t, :], scalar=rstd, in1=fnw_t, op0=ALU.mult, op1=ALU.mult),
                     reads=[R_x1[t], R_st2, R_fnw], writes=[R_h2b])
                for c in range(8):
                    S.op("pe", lambda e, c=c: e.transpose(out=tb[:, c * 128:(c + 1) * 128], in_=h2b[:, c * 128:(c + 1) * 128], identity=ident),
                         reads=[R_h2b, R_ident], writes=[PB[7]], inc=(c == 7))
                S.op("act", lambda e, t=t: e.activation(out=h2T3[:, :, t * 128:(t + 1) * 128], in_=tb.rearrange("p (c t) -> p c t", c=8), func=AF.Copy),
                     reads=[PB[7]], writes=[R_h2T])
            for fc in range(22):
                for half in range(2):
                    cidx = fc + 22 * half
                    wi = wupi[0] % 2
                    wupi[0] += 1
                    S.dma("sp", "wu%d" % wi, [lambda e, cidx=cidx, wi=wi: e.dma_start(out=wupb[wi], in_=s_wup[:, cidx * 1024:(cidx + 1) * 1024])],
                          reads=[R_scr["s_wup"]], writes=[R_wupb[wi]])
                    bk = 2 + half + 2 * (fc % 2)
                    w3 = wupb[wi].rearrange("p (c f) -> p c f", c=8)
                    for c in range(8):
                        S.op("pe", lambda e, c=c, bk=bk, w3=w3: e.matmul(bank(bk, W), lhsT=w3[:, c, :], rhs=h2T3[:, c, 0:W], start=(c == 0), stop=(c == 7)),
                             reads=[R_wupb[wi], R_h2T], writes=[PB[bk]], inc=(c == 7))
                    ub, Rub = ubuf[half], R_ub[half]
                    ac, Rac = acc[half], R_acc[half]
                    S.op("act", lambda e, ub=ub, cidx=cidx: e.activation(out=ub[:, 0:2], in_=carry3[:, cidx, :], func=AF.Copy), reads=[R_carry], writes=[Rub])
                    S.op("act", lambda e, ub=ub, bk=bk: e.activation(out=ub[:, 2:2 + W], in_=bank(bk, W), func=AF.Copy), reads=[PB[bk]], writes=[Rub])
                    S.op("act", lambda e, ub=ub, cidx=cidx: e.activation(out=carry3[:, cidx, :], in_=ub[:, W:W + 2], func=AF.Copy), reads=[Rub], writes=[R_carry])
                    if bi == 0:
                        continue
                    S.op("act", lambda e, ac=ac, bk=bk, cidx=cidx: e.activation(out=ac[:, 0:W], in_=bank(bk, W), func=AF.Identity,
                                                                               scale=cw3[:, cidx, 2:3], bias=cb_t[:, cidx:cidx + 1]),
                         reads=[PB[bk], R_cw, R_cb], writes=[Rac])
                    S.op("dve", lambda e, ac=ac, ub=ub, cidx=cidx: e.scalar_tensor_tensor(out=ac[:, 0:W], in0=ub[:, 1:1 + W], scalar=cw3[:, cidx, 1:2], in1=ac[:, 0:W],
                                                                                         op0=ALU.mult, op1=ALU.add), reads=[Rub, Rac, R_cw], writes=[Rac])
                    S.op("dve", lambda e, ac=ac, ub=ub, cidx=cidx: e.scalar_tensor_tensor(out=ac[:, 0:W], in0=ub[:, 0:W], scalar=cw3[:, cidx, 0:1], in1=ac[:, 0:W],
                                                                                         op0=ALU.mult, op1=ALU.add), reads=[Rub, Rac, R_cw], writes=[Rac])
                if bi == 0:
                    continue
                S.op("act", lambda e: e.activation(out=acc[0][:, 0:W], in_=acc[0][:, 0:W], func=AF.Silu), reads=[R_acc[0]], writes=[R_acc[0]])
                S.op("dve", lambda e, fc=fc: e.tensor_tensor(out=gT3[:, fc, 0:W], in0=acc[0][:, 0:W], in1=acc[1][:, 0:W], op=ALU.mult),
                     reads=[R_acc[0], R_acc[1]], writes=[R_gT])
            if bi == 0:
                return
            for t in range(ntl):
                m = m0 + t
                for hf in range(2):
                    for fc in range(22):
                        S.op("pe", lambda e, fc=fc, hf=hf, t=t: e.matmul(bank(hf), lhsT=gT3[:, fc, t * 128:(t + 1) * 128],
                                                                        rhs=wdown3[:, fc, hf * 512:(hf + 1) * 512], start=(fc == 0), stop=(fc == 21)),
                             reads=[R_gT, R_wdown], writes=[PB[hf]], inc=(fc == 21))
                yi = ybi[0] % 2
                ybi[0] += 1
                yb_, Ryb = ybuf[yi], R_yb2[yi]
                S.op("dve", lambda e, t=t: e.tensor_tensor(out=x13[:, t, :], in0=ps[:, 0:1024], in1=x13[:, t, :], op=ALU.add),
                     reads=[PB[0], PB[1], R_x1[t]], writes=[R_x1[t]])
                ss2 = st2[:, 2:3]
                rstd2 = st2[:, 3:4]
                S.op("act", lambda e, t=t, yb_=yb_: e.activation(out=h2b, in_=x13[:, t, :], func=AF.Square, accum_out=ss2), reads=[R_x1[t]], writes=[R_h2b, R_st2])
                S.op("dve", lambda e: e.tensor_scalar(out=ss2, in0=ss2, scalar1=1.0 / 1024, scalar2=EPS, op0=ALU.mult, op1=ALU.add), reads=[R_st2], writes=[R_st2])
                S.op("act", lambda e: e.activation(out=ss2, in_=ss2, func=AF.Sqrt), reads=[R_st2], writes=[R_st2])
                S.op("dve", lambda e: e.reciprocal(out=rstd2, in_=ss2), reads=[R_st2], writes=[R_st2])
                S.op("dve", lambda e, t=t, yb_=yb_: e.scalar_tensor_tensor(out=yb_, in0=x13[:, t, :], scalar=rstd2, in1=onw_t, op0=ALU.mult, op1=ALU.mult),
                     reads=[R_x1[t], R_st2, R_onw], writes=[Ryb])
                S.dma("sp", "yo%d" % yi, [lambda e, m=m, yb_=yb_: e.dma_start(out=y_tiles[m - 1], in_=yb_)], reads=[Ryb])

        for bi, (m0, ntl) in enumerate(blocks):
            _blockC(bi, m0, ntl)

        S.barrier()
        S.emit()
    return nc


def _consts():
    H, C = 8, 128
    lg = np.log1p(-np.power(2.0, -5.0 - np.arange(H, dtype=np.float64)))
    idx = np.arange(C, dtype=np.float64)
    diff = idx[None, :] - idx[:, None]
    dt = np.where(diff[:, None, :] >= 0, np.exp(lg[None, :, None] * np.maximum(diff[:, None, :], 0.0)), 0.0) / 8.0
    c_dt = dt.reshape(128, 1024).astype(np.float32)
    xi = np.exp(lg[:, None] * (idx[None, :] + 1.0))
    c_xi = np.zeros((128, 4, 128))
    for p in range(4):
        for a in range(2):
            c_xi[a * 64:(a + 1) * 64, p, :] = xi[2 * p + a][None, :]
    c_xi = c_xi.reshape(128, 512).astype(np.float32)
    zeta = np.exp(lg[:, None] * (C - 1.0 - idx[None, :])) / 8.0
    c_zeta = np.repeat(zeta.T[:, :, None], 64, axis=2).reshape(128, 512).astype(np.float32)
    dec = np.exp(lg * C)
    c_dec = np.zeros((128, 4))
    for p in range(4):
        c_dec[0:64, p] = dec[2 * p]
        c_dec[64:128, p] = dec[2 * p + 1]
    c_dec = c_dec.astype(np.float32)
    fr = (10000.0 ** (-np.arange(0, 64, 2, dtype=np.float32) / np.float32(64))).astype(np.float32)
    fm = (10000.0 ** (-np.arange(0, 32, 2, dtype=np.float32) / np.float32(32))).astype(np.float32)
    invf = np.concatenate([fr, fr, fr, fr, fm, fm, fm, fm]).astype(np.float64) / (2 * np.pi)
    off = np.concatenate([np.full(64, 0.25), np.full(32, 0.5), np.zeros(32), np.full(32, 0.25), np.full(16, 0.5), np.zeros(16)])
    c_invf = np.broadcast_to(invf[None, :], (128, 192)).astype(np.float32).copy()
    c_off = np.broadcast_to(off[None, :], (128, 192)).astype(np.float32).copy()
    k = np.arange(128)
    c_mask = np.where(k[None, :] < k[:, None], -30000.0, 0.0).astype(np.float32)
    return dict(c_dt=c_dt, c_xi=c_xi, c_zeta=c_zeta, c_dec=c_dec, c_invf=c_invf, c_off=c_off, c_mask=c_mask)


def _bc(v, n=128):
    return np.ascontiguousarray(np.broadcast_to(np.asarray(v, np.float32)[None, :], (n, v.shape[0])))


_PROG = None


def kernel(x, positions, attn_norm_w, w_in, ret_gn_w, mla_q_norm_w, w_uq, mla_kv_norm_w, w_ukv,
           w_out, ffn_norm_w, w_up, conv_w, conv_b, w_down, final_norm_w):
    global _PROG
    x = np.asarray(x, np.float32)
    positions = np.asarray(positions, np.int32)
    shared = _consts()
    shared["b_anw"] = _bc(np.asarray(attn_norm_w)[0])
    shared["b_fnw"] = _bc(np.asarray(ffn_norm_w)[0])
    shared["b_onw"] = _bc(np.asarray(final_norm_w))
    shared["b_qnw"] = _bc(np.asarray(mla_q_norm_w)[0])
    shared["b_kvnw"] = _bc(np.asarray(mla_kv_norm_w)[0])
    shared["b_gnw"] = _bc(np.asarray(ret_gn_w)[0])
    cw = np.asarray(conv_w, np.float32)[0]
    shared["c_cw"] = np.ascontiguousarray(cw.reshape(3, 44, 128).transpose(2, 1, 0)).reshape(128, 132)
    shared["c_cb"] = np.ascontiguousarray(np.asarray(conv_b, np.float32)[0].reshape(44, 128).T)
    shared["w_in_l"] = np.ascontiguousarray(np.asarray(w_in, np.float32)[0].reshape(8, 128, 2464).transpose(1, 0, 2)).reshape(128, -1)
    shared["w_uq_l"] = np.ascontiguousarray(np.asarray(w_uq, np.float32)[0].reshape(2, 128, 768).transpose(1, 0, 2)).reshape(128, -1)
    wukv = np.asarray(w_ukv, np.float32)[0].reshape(128, 8, 128)
    shared["wk_l"] = np.ascontiguousarray(wukv[:, :, 0:64]).reshape(128, 512)
    shared["wv_l"] = np.ascontiguousarray(wukv[:, :, 64:128]).reshape(128, 512)
    wo = np.asarray(w_out, np.float32)[0]
    shared["w_out_l"] = np.ascontiguousarray(wo.reshape(8, 128, 1024).transpose(1, 0, 2)).reshape(128, -1)
    wu = np.asarray(w_up, np.float32)[0]
    shared["w_up_l"] = np.ascontiguousarray(wu.reshape(8, 128, 44, 128).transpose(1, 2, 0, 3)).reshape(128, -1)
    wd = np.asarray(w_down, np.float32)[0]
    shared["w_down_l"] = np.ascontiguousarray(wd.reshape(22, 128, 1024).transpose(1, 0, 2)).reshape(128, -1)

    in_maps = []
    for c in range(8):
        b, z = divmod(c, 2)
        m = dict(shared)
        if z == 1:
            xcore = x[b]
            pc = positions[b]
            vd = np.ones(8192, np.float32)
        else:
            xcore = np.concatenate([np.zeros((4096, 1024), np.float32), x[b, :4096]], axis=0)
            pc = np.concatenate([np.zeros(4096, np.int32), positions[b, :4096]])
            vd = np.concatenate([np.zeros(4096, np.float32), np.ones(4096, np.float32)])
        m["xc"] = np.ascontiguousarray(xcore)
        m["posc"] = np.ascontiguousarray(pc.reshape(NT, 128).T)
        m["valid"] = np.ascontiguousarray(vd.reshape(NT, 128).T)
        in_maps.append(m)
    if _PROG is None:
        _PROG = build_program()
    res = run_bass_kernel_spmd(_PROG, in_maps, core_ids=list(range(8)))
    out = np.empty((4, 8192, 1024), np.float32)
    for c in range(8):
        b, z = divmod(c, 2)
        out[b, z * 4096:(z + 1) * 4096] = res.results[c]["yout"]
    return out
```

```python
import math
from contextlib import ExitStack
import numpy as np
import concourse.bass as bass
import concourse.mybir as mybir
from concourse.bass_utils import run_bass_kernel_spmd

F32 = mybir.dt.float32
BF16 = mybir.dt.bfloat16
I32 = mybir.dt.int32
ALU = mybir.AluOpType
AF = mybir.ActivationFunctionType
AX = mybir.AxisListType

NT = 64
HALO = 31
NOWN = 33
EPS = 1e-6
TWO_PI = 2.0 * math.pi


class Res:
    __slots__ = ("name", "w", "r", "excl")

    def __init__(self, name, excl=False):
        self.name = name
        self.w = None
        self.r = {}
        self.excl = excl


class Sched:
    ENG = ("pe", "act", "dve", "pool", "sp")

    def __init__(self, nc, stack):
        self.nc = nc
        self.stack = stack
        self.sem = {e: stack.enter_context(nc.semaphore("s_" + e)) for e in self.ENG}
        self.cnt = {e: 0 for e in self.ENG}
        self.seen = {e: {} for e in self.ENG}
        self.streams = {e: [] for e in self.ENG}
        self.dsem = {}
        self.dcnt = {}

    def _wait(self, eng, toks):
        best = {}
        for t in toks:
            if t is None:
                continue
            k, s, v = t
            if k not in best or best[k][2] < v:
                best[k] = t
        for k, (kk, s, v) in best.items():
            if self.seen[eng].get(k, 0) >= v:
                continue
            self.seen[eng][k] = v
            self.streams[eng].append(("wait", s, v))

    def _deps(self, reads, writes, me=None):
        toks = []
        for r in reads:
            toks.append(r.w)
        skip = me if me == "pe" else None
        for w in writes:
            if w.w is not None and w.w[0] != skip:
                toks.append(w.w)
            toks.extend(t for k, t in w.r.items() if k != skip)
        return toks

    def op(self, eng, fn, reads=(), writes=(), inc=True):
        ex = [r for r in reads if r.excl and r not in writes]
        if ex:
            writes = list(writes) + ex
        self._wait(eng, self._deps(reads, writes, eng))
        if inc:
            self.cnt[eng] += 1
            tok = (eng, self.sem[eng], self.cnt[eng])
        else:
            tok = (eng, self.sem[eng], self.cnt[eng] + 1)
        self.streams[eng].append(("op", fn, inc))
        for r in reads:
            r.r[eng] = tok
        for w in writes:
            w.w = tok
            w.r = {}
        return tok

    def dma(self, eng, slot, fns, reads=(), writes=()):
        if slot not in self.dsem:
            self.dsem[slot] = self.stack.enter_context(self.nc.semaphore("d_" + slot))
            self.dcnt[slot] = 0
        self._wait(eng, self._deps(reads, writes))
        for fn in fns:
            self.dcnt[slot] += 16
            self.streams[eng].append(("dma", fn, self.dsem[slot]))
        tok = ("d_" + slot, self.dsem[slot], self.dcnt[slot])
        for r in reads:
            r.r["d_" + slot] = tok
        for w in writes:
            w.w = tok
            w.r = {}
        return tok

    def barrier(self):
        toks = [(e, self.sem[e], self.cnt[e]) for e in self.ENG if self.cnt[e] > 0]
        toks += [("d_" + s, self.dsem[s], self.dcnt[s]) for s in self.dsem if self.dcnt[s] > 0]
        for e in self.ENG:
            self._wait(e, toks)

    def emit(self):
        nc = self.nc
        with nc.Block() as block:
            def run(eng, e):
                sem = self.sem[eng]
                for item in self.streams[eng]:
                    if item[0] == "wait":
                        e.wait_ge(item[1], item[2])
                    elif item[0] == "op":
                        ins = item[1](e)
                        if item[2]:
                            ins.then_inc(sem, 1)
                    else:
                        item[1](e).then_inc(item[2], 16)

            @block.tensor
            def _(e):
                run("pe", e)

            @block.scalar
            def _(e):
                run("act", e)

            @block.vector
            def _(e):
                run("dve", e)

            @block.gpsimd
            def _(e):
                run("pool", e)

            @block.sync
            def _(e):
                run("sp", e)


def bc_mid(a, k):
    return bass.AP(a.tensor, a.offset, [list(a.ap[0]), [0, k]] + [list(x) for x in a.ap[1:]])


def bc_last(a, m):
    return bass.AP(a.tensor, a.offset, [list(x) for x in a.ap] + [[0, m]])


class Arena:
    def __init__(self, t, total):
        self.t = t
        self.total = total
        self.off = 0

    def f32(self, n):
        assert self.off + n <= self.total, ("arena overflow", self.off, n, self.total)
        a = self.t[:, self.off:self.off + n]
        self.off += n
        return a

    def bf(self, n):
        w = (n + 1) // 2
        return self.f32(w).bitcast(BF16)[:, 0:n]


import os
KSTOP = os.environ.get('KSTOP', '')
KSUB = int(os.environ.get('KSUB', 0))


def build_program():
    nc = bass.Bass("TRN2", target_bir_lowering=False)
    din = {}

    def inp(name, shape, dt=F32):
        din[name] = nc.dram_tensor(name, list(shape), dt, kind="ExternalInput").ap()
        return din[name]

    xc = inp("xc", [NT * 128, 1024])
    posc = inp("posc", [128, NT], I32)
    valid = inp("valid", [128, NT])
    c_dt = inp("c_dt", [128, 1024])
    c_xi = inp("c_xi", [128, 512])
    c_zeta = inp("c_zeta", [128, 512])
    c_dec = inp("c_dec", [128, 4])
    c_invf = inp("c_invf", [128, 192])
    c_off = inp("c_off", [128, 192])
    c_mask = inp("c_mask", [128, 128])
    b_anw = inp("b_anw", [128, 1024])
    b_fnw = inp("b_fnw", [128, 1024])
    b_onw = inp("b_onw", [128, 1024])
    b_qnw = inp("b_qnw", [128, 256])
    b_kvnw = inp("b_kvnw", [128, 128])
    b_gnw = inp("b_gnw", [128, 512])
    c_cw = inp("c_cw", [128, 44 * 3])
    c_cb = inp("c_cb", [128, 44])
    w_in_l = inp("w_in_l", [128, 8 * 2464])
    w_uq_l = inp("w_uq_l", [128, 2 * 768])
    wk_l = inp("wk_l", [128, 512])
    wv_l = inp("wv_l", [128, 512])
    w_out_l = inp("w_out_l", [128, 8 * 1024])
    w_up_l = inp("w_up_l", [128, 44 * 1024])
    w_down_l = inp("w_down_l", [128, 22 * 1024])
    yout = nc.dram_tensor("yout", [4096, 1024], F32, kind="ExternalOutput").ap()

    s_wup = nc.dram_tensor("s_wup", [128, 44 * 1024], BF16).ap()
    s_wdown = nc.dram_tensor("s_wdown", [128, 22 * 1024], BF16).ap()
    s_wout = nc.dram_tensor("s_wout", [128, 8 * 1024], BF16).ap()
    s_kt = nc.dram_tensor("s_kt", [8, 96, NT * 128], BF16).ap()
    s_v = nc.dram_tensor("s_v", [8, 128, NT * 65], BF16).ap()
    s_qt = nc.dram_tensor("s_qt", [8, 96, NOWN * 128], BF16).ap()

    with ExitStack() as st:
        S = Sched(nc, st)
        TOT = 53000
        arena_t = st.enter_context(nc.sbuf_tensor("arena", [128, TOT], F32))
        ps = st.enter_context(nc.psum_tensor("ps", [128, 4096], F32))
        A = Arena(arena_t, TOT)

        def bank(i, n=512):
            return ps[:, i * 512:i * 512 + n]

        PB = [Res("psb%d" % i, excl=True) for i in range(8)]

        ident = A.bf(128)
        maskb = A.bf(128)
        mixT_r = A.bf(4 * NOWN * 128).rearrange("p (c t) -> p c t", c=4)
        R_ident, R_mask, R_mixr = Res("ident"), Res("mask"), [Res("mixr%d" % i) for i in range(NOWN)]
        R_mixm = [Res("mixm%d" % h) for h in range(8)]
        mark_persist = A.off

        S.op("pool", lambda e: e.memset(ident, 0.0), writes=[R_ident])
        S.op("pool", lambda e: e.affine_select(out=ident, in_=ident, pattern=[[-1, 128]], compare_op=ALU.not_equal,
                                               fill=1.0, base=0, channel_multiplier=1), reads=[R_ident], writes=[R_ident])

        w_in = A.bf(8 * 2464)
        w_in3 = w_in.rearrange("p (c f) -> p c f", c=8)
        w_uq = A.bf(2 * 768)
        w_uq3 = w_uq.rearrange("p (c f) -> p c f", c=2)
        wk = A.bf(512)
        wv = A.bf(512)
        R_win, R_wuq, R_wk, R_wv = Res("w_in"), Res("w_uq"), Res("wk"), Res("wv")
        stage = [A.f32(1024) for _ in range(2)]
        stageb = [A.bf(1024) for _ in range(2)]
        R_stage = [Res("stage0"), Res("stage1")]
        R_stageb = [Res("stageb0"), Res("stageb1")]
        R_scr = {k: Res(k) for k in ["s_wup", "s_wdown", "s_wout"]}
        pieces = []

        def add_pieces(src, ncols, dst_sb=None, dst_res=None, dst_dram=None, dram_res=None):
            c0 = 0
            while c0 < ncols:
                n = min(1024, ncols - c0)
                pieces.append((src, c0, n, dst_sb, dst_res, dst_dram, dram_res))
                c0 += n

        def piece_load_cast(k):
            src, c0, n, dst_sb, dst_res, dst_dram, dram_res = pieces[k]
            i = k % 2
            S.dma("act", "stg%d" % i, [lambda e: e.dma_start(out=stage[i][:, 0:n], in_=src[:, c0:c0 + n])], writes=[R_stage[i]])
            if dst_sb is not None:
                S.op("pool", lambda e: e.tensor_copy(out=dst_sb[:, c0:c0 + n], in_=stage[i][:, 0:n]), reads=[R_stage[i]], writes=[dst_res])
            else:
                S.op("pool", lambda e: e.tensor_copy(out=stageb[i][:, 0:n], in_=stage[i][:, 0:n]), reads=[R_stage[i]], writes=[R_stageb[i]])

        def piece_store(k):
            src, c0, n, dst_sb, dst_res, dst_dram, dram_res = pieces[k]
            i = k % 2
            if dst_dram is not None:
                S.dma("act", "stb%d" % i, [lambda e: e.dma_start(out=dst_dram[:, c0:c0 + n], in_=stageb[i][:, 0:n])],
                      reads=[R_stageb[i]], writes=[dram_res])

        add_pieces(w_in_l, 8 * 2464, dst_sb=w_in, dst_res=R_win)
        add_pieces(w_uq_l, 2 * 768, dst_sb=w_uq, dst_res=R_wuq)
        add_pieces(wk_l, 512, dst_sb=wk, dst_res=R_wk)
        add_pieces(wv_l, 512, dst_sb=wv, dst_res=R_wv)
        add_pieces(c_mask, 128, dst_sb=maskb, dst_res=R_mask)
        n_first = len(pieces)
        add_pieces(w_out_l, 8 * 1024, dst_dram=s_wout, dram_res=R_scr["s_wout"])
        add_pieces(w_down_l, 22 * 1024, dst_dram=s_wdown, dram_res=R_scr["s_wdown"])
        add_pieces(w_up_l, 44 * 1024, dst_dram=s_wup, dram_res=R_scr["s_wup"])
        for k in range(n_first):
            piece_load_cast(k)
        pk = [n_first, n_first]

        def cast_step(nload):
            while pk[1] < pk[0]:
                piece_store(pk[1])
                pk[1] += 1
            for _ in range(nload):
                if pk[0] < len(pieces):
                    piece_load_cast(pk[0])
                    pk[0] += 1

        def load_const(src, n, name, dt=F32):
            a = A.f32(n)
            if dt is not F32:
                a = a.bitcast(dt)
            r = Res(name)
            S.dma("sp", "c_" + name, [lambda e: e.dma_start(out=a, in_=src)], writes=[r])
            return a, r

        dt_t, R_dt = load_const(c_dt, 1024, "dt")
        xi_t, R_xi = load_const(c_xi, 512, "xi")
        zeta_t, R_zeta = load_const(c_zeta, 512, "zeta")
        dec_t, R_dec = load_const(c_dec, 4, "dec")
        invf_t, R_invf = load_const(c_invf, 192, "invf")
        off_t, R_off = load_const(c_off, 192, "off")
        anw_t, R_anw = load_const(b_anw, 1024, "anw")
        qnw_t, R_qnw = load_const(b_qnw, 256, "qnw")
        kvnw_t, R_kvnw = load_const(b_kvnw, 128, "kvnw")
        gnw_t, R_gnw = load_const(b_gnw, 512, "gnw")
        posi, R_pos = load_const(posc, NT, "posi", I32)
        valid_t, R_valid = load_const(valid, NT, "valid")
        posf = A.f32(NT)
        S.op("dve", lambda e: e.tensor_copy(out=posf, in_=posi), reads=[R_pos], writes=[R_pos])

        TB = 16
        tab = A.f32(TB * 192)
        tab3 = tab.rearrange("p (n f) -> p n f", n=TB)
        R_tab = Res("tab")
        ttmp = A.f32(192)
        tti = A.f32(192).bitcast(I32)
        R_tt = Res("ttmp")

        def make_tables(n0):
            for n in range(n0, n0 + TB):
                S.op("dve", lambda e, n=n: e.scalar_tensor_tensor(out=ttmp, in0=invf_t, scalar=posf[:, n:n + 1], in1=off_t,
                                                                op0=ALU.mult, op1=ALU.add),
                     reads=[R_invf, R_off, R_pos], writes=[R_tt])
                S.op("dve", lambda e: e.tensor_copy(out=tti, in_=ttmp), reads=[R_tt], writes=[R_tt])
                S.op("dve", lambda e, n=n: e.tensor_tensor(out=tab3[:, n % TB, :], in0=ttmp, in1=tti, op=ALU.subtract),
                     reads=[R_tt], writes=[R_tab])
            S.op("act", lambda e: e.activation(out=tab, in_=tab, func=AF.Sin, scale=TWO_PI * (1.0 - 1e-6)),
                 reads=[R_tab], writes=[R_tab])

        xbuf = [A.f32(1024) for _ in range(2)]
        R_x = [Res("x0"), Res("x1")]
        junk = A.f32(1024)
        R_junk = Res("junk")
        st_small = A.f32(64)
        R_ss = Res("ss")
        hb = A.bf(1024)
        R_hb = Res("hb")
        hT = A.bf(1024)
        hT3 = hT.rearrange("p (c t) -> p c t", c=8)
        R_hT = Res("hT")
        tmpA = A.f32(1024)
        tmpB = A.f32(1024)
        R_tA, R_tB = Res("tmpA"), Res("tmpB")
        qkr = A.bf(1024)
        R_qkr = Res("qkr")
        kz = A.bf(512)
        R_kz = Res("kz")
        vb = A.bf(512)
        R_vb = Res("vb")
        sg = A.f32(512)
        R_sg = Res("sg")
        qT = A.bf(1024)
        qxT = A.bf(512)
        kT = A.bf(512)
        R_qT, R_qxT, R_kT = Res("qT"), Res("qxT"), Res("kT")
        sd = A.bf(1024)
        R_sd = Res("sd")
        R32 = A.f32(512)
        Rb = A.bf(512)
        R_R32, R_Rb = Res("R32"), Res("Rb")
        gn1 = A.f32(512)
        gn2 = A.f32(512)
        R_gn1, R_gn2 = Res("gn1"), Res("gn2")
        yb = A.bf(512)
        R_yb = Res("yb")
        cqn = A.bf(256)
        ckvn = A.bf(128)
        kr = A.bf(32)
        R_cqn, R_ckvn, R_kr = Res("cqn"), Res("ckvn"), Res("kr")
        cqnT = A.bf(256)
        ckvnT = A.bf(128)
        R_cqnT, R_ckvnT = Res("cqnT"), Res("ckvnT")
        mt1 = A.f32(64)
        mt2 = A.f32(64)
        R_mt = Res("mt")
        qb = A.bf(768)
        R_qb = Res("qb")
        kn_g = [A.bf(4 * 512) for _ in range(2)]
        kpe_g = [A.bf(512) for _ in range(2)]
        v_g = [A.bf(4 * 8 * 65) for _ in range(2)]
        q_g = [A.bf(8 * 512) for _ in range(2)]
        R_kng = [Res("kng0"), Res("kng1")]
        R_kpg = [Res("kpg0"), Res("kpg1")]
        R_vg = [Res("vg0"), Res("vg1")]
        R_qg = [Res("qg0"), Res("qg1")]
        R_skt, R_sv, R_sqt = Res("s_kt"), Res("s_v"), Res("s_qt")

        S.op("dve", lambda e: e.memset(qT, 0.0), writes=[R_qT])
        S.op("dve", lambda e: e.memset(R32, 0.0), writes=[R_R32])
        S.op("dve", lambda e: e.memset(Rb, 0.0), writes=[R_Rb])

        x_tiles = xc.rearrange("(n p) d -> n p d", p=128)

        def load_x(n):
            i = n % 2
            S.dma("sp", "x%d" % i, [lambda e, n=n, i=i: e.dma_start(out=xbuf[i], in_=x_tiles[n])], writes=[R_x[i]])

        if KSTOP == 'A0':
            npz = int(os.environ.get('KNP', 0))
            while pk[1] < min(len(pieces), n_first + npz):
                cast_step(2)
            S.barrier(); S.emit(); return nc
        load_x(0)

        def _tileA(n):
            cast_step(2)
            full = n >= HALO
            g, gi = divmod(n, 4)
            gb = g % 2
            if n + 1 < NT:
                load_x(n + 1)
            xb_, Rx = xbuf[n % 2], R_x[n % 2]
            ss = st_small[:, 0:1]
            rstd = st_small[:, 1:2]
            S.op("act", lambda e, xb_=xb_: e.activation(out=junk, in_=xb_, func=AF.Square, accum_out=ss),
                 reads=[Rx], writes=[R_junk, R_ss])
            S.op("dve", lambda e: e.tensor_scalar(out=ss, in0=ss, scalar1=1.0 / 1024, scalar2=EPS, op0=ALU.mult, op1=ALU.add),
                 reads=[R_ss], writes=[R_ss])
            S.op("act", lambda e: e.activation(out=ss, in_=ss, func=AF.Sqrt), reads=[R_ss], writes=[R_ss])
            S.op("dve", lambda e: e.reciprocal(out=rstd, in_=ss), reads=[R_ss], writes=[R_ss])
            S.op("dve", lambda e, xb_=xb_: e.scalar_tensor_tensor(out=hb, in0=xb_, scalar=rstd, in1=anw_t, op0=ALU.mult, op1=ALU.mult),
                 reads=[Rx, R_ss, R_anw], writes=[R_hb])
            tb = bank(0).bitcast(BF16)
            for c in range(8):
                S.op("pe", lambda e, c=c: e.transpose(out=tb[:, c * 128:(c + 1) * 128], in_=hb[:, c * 128:(c + 1) * 128], identity=ident),
                     reads=[R_hb, R_ident], writes=[PB[0]], inc=(c == 7))
            S.op("act", lambda e: e.activation(out=hT, in_=tb, func=AF.Copy), reads=[PB[0]], writes=[R_hT])
            if KSUB == 1 and n >= HALO:
                return


            def proj(bk, col0, ncol, n_=None):
                for c in range(8):
                    S.op("pe", lambda e, c=c: e.matmul(bank(bk, ncol), lhsT=hT3[:, c, :], rhs=w_in3[:, c, col0:col0 + ncol],
                                                       start=(c == 0), stop=(c == 7)),
                         reads=[R_hT, R_win], writes=[PB[bk]], inc=(c == 7))

            if full:
                proj(1, 0, 512)
            proj(2, 512, 512)
            proj(3, 1024, 512)
            if full:
                proj(4, 1536, 512)
                proj(5, 2048, 416)
            else:
                proj(5, 2304, 160)
            if KSUB == 2 and n >= HALO:
                return

            latoff = 0 if full else -256
            if n % TB == 0:
                make_tables(n)
            tabn = tab3[:, n % TB, :]
            cs_r, ss_r = tabn[:, 0:64], tabn[:, 64:128]
            cs_m, ss_m = tabn[:, 128:160], tabn[:, 160:192]

            def rope(src_ap, nh, hd, cs, sn, dstA, dstB, dst, reads, wres):
                half = hd // 2
                x3 = src_ap.rearrange("p (h d) -> p h d", h=nh)
                sw = bass.AP(src_ap.tensor, src_ap.offset + half,
                             [list(src_ap.ap[0]), [hd, nh], [-half, 2], [1, half]])
                a3 = dstA.rearrange("p (h d) -> p h d", h=nh)
                b4 = dstB.rearrange("p (h a d) -> p h a d", h=nh, a=2)
                S.op("dve", lambda e: e.tensor_tensor(out=a3, in0=x3, in1=bc_mid(cs, nh), op=ALU.mult),
                     reads=reads + [R_tab], writes=[R_tA])
                S.op("dve", lambda e: e.tensor_tensor(out=b4, in0=sw, in1=bc_mid(sn.rearrange("p (a d) -> p a d", a=2), nh), op=ALU.mult),
                     reads=reads + [R_tab], writes=[R_tB])
                S.op("dve", lambda e: e.tensor_tensor(out=dst, in0=dstA, in1=dstB, op=ALU.add),
                     reads=[R_tA, R_tB], writes=[wres])

            if full:
                rope(ps[:, 512:1536], 16, 64, cs_r, ss_r, tmpA, tmpB, qkr, [PB[1], PB[2]], R_qkr)
            else:
                rope(ps[:, 1024:1536], 8, 64, cs_r, ss_r, tmpA[:, 0:512], tmpB[:, 0:512], qkr[:, 512:1024], [PB[2]], R_qkr)
            S.op("dve", lambda e: e.tensor_tensor(out=kz, in0=qkr[:, 512:1024], in1=zeta_t, op=ALU.mult),
                 reads=[R_qkr, R_zeta], writes=[R_kz])
            S.op("act", lambda e: e.activation(out=vb, in_=bank(3), func=AF.Copy), reads=[PB[3]], writes=[R_vb])
            if KSUB == 3 and n >= HALO:
                return


            if full:
                S.op("act", lambda e: e.activation(out=sg, in_=bank(4), func=AF.Silu), reads=[PB[4]], writes=[R_sg])
                for c in range(8):
                    S.op("pe", lambda e, c=c: e.transpose(out=tb[:, c * 128:(c + 1) * 128], in_=qkr[:, c * 128:(c + 1) * 128], identity=ident),
                         reads=[R_qkr, R_ident], writes=[PB[0]], inc=(c == 7))
                S.op("act", lambda e: e.activation(out=qT[0:64, 0:512], in_=tb[0:64, 0:512], func=AF.Copy), reads=[PB[0]], writes=[R_qT])
                S.op("act", lambda e: e.activation(out=qT[64:128, 512:1024], in_=tb[64:128, 0:512], func=AF.Copy), reads=[PB[0]], writes=[R_qT])
                S.op("dve", lambda e: e.tensor_tensor(out=qxT, in0=tb[:, 0:512], in1=xi_t, op=ALU.mult), reads=[PB[0], R_xi], writes=[R_qxT])
                S.op("act", lambda e: e.activation(out=kT, in_=tb[:, 512:1024], func=AF.Copy), reads=[PB[0]], writes=[R_kT])
                for h in range(8):
                    p_, a_ = divmod(h, 2)
                    rows = slice(a_ * 64, a_ * 64 + 64)
                    S.op("pe", lambda e, h=h, p_=p_, rows=rows: e.matmul(ps[:, 3072 + h * 128:3072 + (h + 1) * 128],
                                                                         lhsT=kT[:, p_ * 128:(p_ + 1) * 128],
                                                                         rhs=qT[:, (h % 2) * 512 + p_ * 128:(h % 2) * 512 + (p_ + 1) * 128],
                                                                         start=True, stop=True),
                         reads=[R_kT, R_qT], writes=[PB[6], PB[7]], inc=(h == 7))
                S.op("dve", lambda e: e.tensor_tensor(out=sd, in0=ps[:, 3072:4096], in1=dt_t, op=ALU.mult),
                     reads=[PB[6], PB[7], R_dt], writes=[R_sd])
                for h in range(8):
                    p_, a_ = divmod(h, 2)
                    rows = slice(a_ * 64, a_ * 64 + 64)
                    S.op("pe", lambda e, h=h: e.matmul(ps[:, 512 + h * 64:512 + (h + 1) * 64], lhsT=sd[:, h * 128:(h + 1) * 128],
                                                       rhs=vb[:, h * 64:(h + 1) * 64], start=True, stop=False),
                         reads=[R_sd, R_vb, R_qkr], writes=[PB[1]], inc=False)
                    S.op("pe", lambda e, h=h, p_=p_, rows=rows: e.matmul(ps[:, 512 + h * 64:512 + (h + 1) * 64],
                                                                         lhsT=qxT[:, p_ * 128:(p_ + 1) * 128],
                                                                         rhs=Rb[:, h * 64:(h + 1) * 64],
                                                                         start=False, stop=True),
                         reads=[R_qxT, R_Rb], writes=[PB[1]], inc=(h == 7))
            for p_ in range(4):
                S.op("pe", lambda e, p_=p_: e.matmul(ps[:, 3072 + p_ * 128:3072 + (p_ + 1) * 128], lhsT=kz[:, p_ * 128:(p_ + 1) * 128],
                                                     rhs=vb[:, p_ * 128:(p_ + 1) * 128], start=True, stop=True),
                     reads=[R_kz, R_vb, R_sd], writes=[PB[6]], inc=(p_ == 3))
            for h in range(8):
                p_, a_ = divmod(h, 2)
                rows = slice(a_ * 64, a_ * 64 + 64)
                S.op("dve", lambda e, h=h, p_=p_, a_=a_, rows=rows: e.scalar_tensor_tensor(
                    out=R32[rows, h * 64:(h + 1) * 64], in0=R32[rows, h * 64:(h + 1) * 64], scalar=dec_t[rows, p_:p_ + 1],
                    in1=ps[rows, 3072 + p_ * 128 + a_ * 64:3072 + p_ * 128 + a_ * 64 + 64], op0=ALU.mult, op1=ALU.add),
                    reads=[PB[6], R_dec, R_R32], writes=[R_R32])
            S.op("act", lambda e: e.activation(out=Rb, in_=R32, func=AF.Copy), reads=[R_R32], writes=[R_Rb])
            if KSUB == 4 and n >= HALO:
                return


            if full:
                o3 = bank(1).rearrange("p (h d) -> p h d", h=8)
                s1, s2, mean, msq, var = (mt1[:, 0:8], mt1[:, 8:16], mt1[:, 16:24], mt1[:, 24:32], mt1[:, 32:40])
                S.op("dve", lambda e: e.tensor_reduce(out=s1, in_=o3, axis=AX.X, op=ALU.add), reads=[PB[1]], writes=[R_mt])
                S.op("act", lambda e: e.activation(out=gn1, in_=bank(1), func=AF.Square), reads=[PB[1]], writes=[R_gn1])
                S.op("dve", lambda e: e.tensor_reduce(out=s2, in_=gn1.rearrange("p (h d) -> p h d", h=8), axis=AX.X, op=ALU.add),
                     reads=[R_gn1], writes=[R_mt])
                S.op("dve", lambda e: e.tensor_scalar(out=mean, in0=s1, scalar1=1.0 / 64, scalar2=None, op0=ALU.mult), reads=[R_mt], writes=[R_mt])
                S.op("dve", lambda e: e.tensor_tensor(out=msq, in0=mean, in1=mean, op=ALU.mult), reads=[R_mt], writes=[R_mt])
                S.op("dve", lambda e: e.scalar_tensor_tensor(out=var, in0=s2, scalar=1.0 / 64, in1=msq, op0=ALU.mult, op1=ALU.subtract),
                     reads=[R_mt], writes=[R_mt])
                S.op("dve", lambda e: e.tensor_scalar(out=var, in0=var, scalar1=EPS, scalar2=None, op0=ALU.add), reads=[R_mt], writes=[R_mt])
                S.op("act", lambda e: e.activation(out=var, in_=var, func=AF.Sqrt), reads=[R_mt], writes=[R_mt])
                S.op("dve", lambda e: e.reciprocal(out=var, in_=var), reads=[R_mt], writes=[R_mt])
                g13 = gn1.rearrange("p (h d) -> p h d", h=8)
                S.op("dve", lambda e: e.tensor_tensor(out=g13, in0=o3, in1=bc_last(mean, 64), op=ALU.subtract),
                     reads=[PB[1], R_mt], writes=[R_gn1])
                S.op("dve", lambda e: e.tensor_tensor(out=g13, in0=g13, in1=bc_last(var, 64), op=ALU.mult), reads=[R_gn1, R_mt], writes=[R_gn1])
                S.op("dve", lambda e: e.tensor_tensor(out=gn2, in0=sg, in1=gnw_t, op=ALU.mult), reads=[R_sg, R_gnw], writes=[R_gn2])
                S.op("dve", lambda e: e.tensor_tensor(out=yb, in0=gn1, in1=gn2, op=ALU.mult), reads=[R_gn1, R_gn2], writes=[R_yb])
                for c in range(4):
                    S.op("pe", lambda e, c=c: e.transpose(out=tb[:, c * 128:(c + 1) * 128], in_=yb[:, c * 128:(c + 1) * 128], identity=ident),
                         reads=[R_yb, R_ident], writes=[PB[0]], inc=(c == 3))
                m = n - HALO
                S.op("act", lambda e, m=m: e.activation(out=mixT_r[:, :, m * 128:(m + 1) * 128], in_=tb[:, 0:512].rearrange("p (c t) -> p c t", c=4), func=AF.Copy),
                     reads=[PB[0]], writes=[R_mixr[m]])

            lat = bank(5)
            ckv_ap = lat[:, 256 + latoff:384 + latoff]
            kpe_ap = lat[:, 384 + latoff:416 + latoff]
            ssq = st_small[:, 4:6]
            rq = st_small[:, 6:8]
            if full:
                S.op("act", lambda e: e.activation(out=junk[:, 0:256], in_=lat[:, 0:256], func=AF.Square, accum_out=ssq[:, 0:1]),
                     reads=[PB[5]], writes=[R_junk, R_ss])
            S.op("act", lambda e: e.activation(out=junk[:, 256:384], in_=ckv_ap, func=AF.Square, accum_out=ssq[:, 1:2]),
                 reads=[PB[5]], writes=[R_junk, R_ss])
            if full:
                S.op("dve", lambda e: e.tensor_scalar(out=ssq[:, 0:1], in0=ssq[:, 0:1], scalar1=1.0 / 256, scalar2=EPS, op0=ALU.mult, op1=ALU.add),
                     reads=[R_ss], writes=[R_ss])
            S.op("dve", lambda e: e.tensor_scalar(out=ssq[:, 1:2], in0=ssq[:, 1:2], scalar1=1.0 / 128, scalar2=EPS, op0=ALU.mult, op1=ALU.add),
                 reads=[R_ss], writes=[R_ss])
            lo = 0 if full else 1
            S.op("act", lambda e, lo=lo: e.activation(out=ssq[:, lo:2], in_=ssq[:, lo:2], func=AF.Sqrt), reads=[R_ss], writes=[R_ss])
            S.op("dve", lambda e, lo=lo: e.reciprocal(out=rq[:, lo:2], in_=ssq[:, lo:2]), reads=[R_ss], writes=[R_ss])
            if full:
                S.op("dve", lambda e: e.scalar_tensor_tensor(out=cqn, in0=lat[:, 0:256], scalar=rq[:, 0:1], in1=qnw_t, op0=ALU.mult, op1=ALU.mult),
                     reads=[PB[5], R_ss, R_qnw], writes=[R_cqn])
            S.op("dve", lambda e: e.scalar_tensor_tensor(out=ckvn, in0=ckv_ap, scalar=rq[:, 1:2], in1=kvnw_t, op0=ALU.mult, op1=ALU.mult),
                 reads=[PB[5], R_ss, R_kvnw], writes=[R_ckvn])
            rope(kpe_ap, 1, 32, cs_m, ss_m, tmpA[:, 0:32], tmpB[:, 0:32], kr, [PB[5]], R_kr)
            if KSUB == 5 and n >= HALO:
                return

            S.op("pe", lambda e: e.transpose(out=tb[:, 0:128], in_=ckvn, identity=ident), reads=[R_ckvn, R_ident], writes=[PB[0]], inc=False)
            S.op("pe", lambda e: e.transpose(out=tb[0:32, 128:256], in_=kr, identity=ident), reads=[R_kr, R_ident], writes=[PB[0]], inc=not full)
            if full:
                for c in range(2):
                    S.op("pe", lambda e, c=c: e.transpose(out=tb[:, 256 + c * 128:256 + (c + 1) * 128], in_=cqn[:, c * 128:(c + 1) * 128], identity=ident),
                         reads=[R_cqn, R_ident], writes=[PB[0]], inc=(c == 1))
            S.op("act", lambda e: e.activation(out=ckvnT, in_=tb[:, 0:128], func=AF.Copy), reads=[PB[0]], writes=[R_ckvnT])
            S.op("act", lambda e, gb=gb, gi=gi: e.activation(out=kpe_g[gb][0:32, gi * 128:(gi + 1) * 128], in_=tb[0:32, 128:256], func=AF.Copy),
                 reads=[PB[0]], writes=[R_kpg[gb]])
            if KSUB == 6 and n >= HALO:
                return

            if full:
                S.op("act", lambda e: e.activation(out=cqnT, in_=tb[:, 256:512], func=AF.Copy), reads=[PB[0]], writes=[R_cqnT])
            for p_ in range(4):
                S.op("pe", lambda e, p_=p_: e.matmul(ps[:, 3584 + p_ * 128:3584 + (p_ + 1) * 128], lhsT=wk[:, p_ * 128:(p_ + 1) * 128], rhs=ckvnT,
                                                     start=True, stop=True), reads=[R_wk, R_ckvnT, R_sd], writes=[PB[7]], inc=(p_ == 3))
            kng3 = kn_g[gb].rearrange("p (a k) -> p a k", a=4)
            S.op("act", lambda e, gi=gi, kng3=kng3: e.activation(out=kng3[:, :, gi * 128:(gi + 1) * 128], in_=bank(7).rearrange("p (a k) -> p a k", a=4), func=AF.Copy),
                 reads=[PB[7]], writes=[R_kng[gb]])
            S.op("pe", lambda e: e.matmul(bank(4), lhsT=ckvnT, rhs=wv, start=True, stop=True), reads=[R_wv, R_ckvnT, R_sg], writes=[PB[4]])
            vg4 = v_g[gb].rearrange("p (h t e) -> p h t e", h=8, t=4)
            S.op("dve", lambda e, gi=gi, vg4=vg4: e.tensor_copy(out=vg4[:, :, gi, 0:64], in_=bank(4).rearrange("p (h e) -> p h e", h=8)),
                 reads=[PB[4]], writes=[R_vg[gb]])
            S.op("dve", lambda e, gi=gi, vg4=vg4, n=n: e.tensor_copy(out=vg4[:, :, gi, 64:65], in_=bc_mid(valid_t[:, n:n + 1], 8)),
                 reads=[R_valid], writes=[R_vg[gb]])
            if KSUB == 7 and n >= HALO:
                return

            if full:
                for hf in range(2):
                    for c in range(2):
                        S.op("pe", lambda e, hf=hf, c=c: e.matmul(ps[:, (2 + hf) * 512:(2 + hf) * 512 + 384], lhsT=cqnT[:, c * 128:(c + 1) * 128],
                                                                  rhs=w_uq3[:, c, hf * 384:(hf + 1) * 384], start=(c == 0), stop=(c == 1)),
                             reads=[R_cqnT, R_wuq, R_vb, R_kz, R_qkr], writes=[PB[2 + hf]], inc=(c == 1))
                qb3 = qb.rearrange("p (h d) -> p h d", h=8)
                for hf in range(2):
                    src = ps[:, (2 + hf) * 512:(2 + hf) * 512 + 384]
                    s3 = src.rearrange("p (h d) -> p h d", h=4)
                    S.op("act", lambda e, hf=hf, s3=s3: e.activation(out=qb3[:, hf * 4:(hf + 1) * 4, 0:64], in_=s3[:, :, 0:64], func=AF.Copy),
                         reads=[PB[2 + hf]], writes=[R_qb])
                    x3 = s3[:, :, 64:96]
                    sw = bass.AP(src.tensor, src.offset + 64 + 16, [list(src.ap[0]), [96, 4], [-16, 2], [1, 16]])
                    a3 = tmpA[:, 0:128].rearrange("p (h d) -> p h d", h=4)
                    b4 = tmpB[:, 0:128].rearrange("p (h a d) -> p h a d", h=4, a=2)
                    S.op("dve", lambda e, x3=x3, a3=a3: e.tensor_tensor(out=a3, in0=x3, in1=bc_mid(cs_m, 4), op=ALU.mult),
                         reads=[PB[2 + hf], R_tab], writes=[R_tA])
                    S.op("dve", lambda e, sw=sw, b4=b4: e.tensor_tensor(out=b4, in0=sw, in1=bc_mid(ss_m.rearrange("p (a d) -> p a d", a=2), 4), op=ALU.mult),
                         reads=[PB[2 + hf], R_tab], writes=[R_tB])
                    S.op("dve", lambda e, hf=hf, a3=a3: e.tensor_tensor(out=qb3[:, hf * 4:(hf + 1) * 4, 64:96], in0=a3,
                                                                        in1=tmpB[:, 0:128].rearrange("p (h d) -> p h d", h=4), op=ALU.add),
                         reads=[R_tA, R_tB], writes=[R_qb])
                for h in range(8):
                    S.op("pe", lambda e, h=h: e.transpose(out=tb[0:96, h * 128:(h + 1) * 128], in_=qb[:, h * 96:(h + 1) * 96], identity=ident),
                         reads=[R_qb, R_ident], writes=[PB[0]], inc=(h == 7))
                mg, mi = divmod(n - HALO, 4)
                qg3 = q_g[mg % 2].rearrange("p (h t) -> p h t", h=8)
                S.op("act", lambda e, mi=mi, qg3=qg3: e.activation(out=qg3[0:96, :, mi * 128:(mi + 1) * 128],
                                                                   in_=tb[0:96, :].rearrange("p (h t) -> p h t", h=8), func=AF.Copy),
                     reads=[PB[0]], writes=[R_qg[mg % 2]])
                if mi == 3 or n == NT - 1:
                    ntl = mi + 1
                    S.dma("sp", "sq%d" % (mg % 2),
                          [lambda e, mg=mg, ntl=ntl, qg3=qg3: e.dma_start(
                              out=s_qt[:, :, mg * 512:mg * 512 + ntl * 128].rearrange("h r t -> r h t"),
                              in_=qg3[0:96, :, 0:ntl * 128])],
                          reads=[R_qg[mg % 2]], writes=[R_sqt])
            if gi == 3:
                fns = []
                for a_ in range(2):
                    fns.append(lambda e, a_=a_, g=g, kng3=kng3: e.dma_start(
                        out=s_kt[:, 0:64, g * 512:(g + 1) * 512].rearrange("(p a) r k -> a r p k", a=2)[a_],
                        in_=kng3[a_ * 64:(a_ + 1) * 64, :, :]))
                fns.append(lambda e, g=g, gb=gb: e.dma_start(
                    out=s_kt[:, 64:96, g * 512:(g + 1) * 512].rearrange("h r k -> r h k"),
                    in_=bc_mid(kpe_g[gb][0:32, :], 8)))
                S.dma("sp", "sk%d" % gb, fns, reads=[R_kng[gb], R_kpg[gb]], writes=[R_skt])
                S.dma("sp", "sv%d" % gb,
                      [lambda e, g=g, vg4=vg4: e.dma_start(
                          out=s_v[:, :, g * 4 * 65:(g + 1) * 4 * 65].rearrange("h p (t e) -> p h t e", t=4),
                          in_=vg4)],
                      reads=[R_vg[gb]], writes=[R_sv])

        for n in range(int(os.environ.get('KNT', NT))):
            _tileA(n)
        while pk[1] < len(pieces):
            cast_step(2)

        S.barrier()
        if KSTOP == 'A':
            S.emit(); return nc
        A.off = mark_persist
        mixT_m = A.bf(4 * NOWN * 128).rearrange("p (c t) -> p c t", c=4)
        mark_persist = A.off
        ytmp = [A.bf(512) for _ in range(2)]
        R_ytmp = [Res("ytmp0"), Res("ytmp1")]
        QT = [A.bf(NOWN * 128) for _ in range(2)]
        KT = [A.bf(NT * 128) for _ in range(2)]
        VV = [A.bf(NT * 65) for _ in range(2)]
        R_Q, R_K, R_V = [Res("Q0"), Res("Q1")], [Res("K0"), Res("K1")], [Res("V0"), Res("V1")]
        PT = [A.bf(1024) for _ in range(3)]
        R_PT = [Res("PT%d" % i) for i in range(3)]
        rrow = A.f32(512)
        R_rrow = Res("rrow")
        ones_t = A.f32(64)
        R_ones = Res("ones")
        bcs = A.f32(512)
        R_bcs = Res("bcs")
        S.op("dve", lambda e: e.memset(ones_t, 1.0), writes=[R_ones])
        scale = (64 + 32) ** -0.5

        def load_head(h):
            i = h % 2
            S.dma("sp", "lq%d" % i, [lambda e: e.dma_start(out=QT[i][0:96, :], in_=s_qt[h])], reads=[R_sqt], writes=[R_Q[i]])
            S.dma("sp", "lk%d" % i, [lambda e: e.dma_start(out=KT[i][0:96, :], in_=s_kt[h])], reads=[R_skt], writes=[R_K[i]])
            S.dma("sp", "lv%d" % i, [lambda e: e.dma_start(out=VV[i], in_=s_v[h])], reads=[R_sv], writes=[R_V[i]])

        load_head(0)

        groups = []
        blk_id = 0
        for h in range(8):
            qblocks = [(0, 128, [(kt, 0) for kt in range(HALO)] + [(HALO, 0)], HALO)]
            for j in range(8):
                kts = [(kt, 0) for kt in range(32 + 4 * j)] + [(32 + 4 * j + m, 128 * m) for m in range(4)]
                qblocks.append((128 + 512 * j, 512, kts, 32 + 4 * j))
            for (q0, qw, kts, diag0) in qblocks:
                npairs = (len(kts) + 1) // 2
                for gidx in range(npairs):
                    groups.append(dict(h=h, i=h % 2, q0=q0, qw=qw, pair=kts[2 * gidx:2 * gidx + 2], diag0=diag0,
                                       ob=4 + blk_id % 2, first=(gidx == 0), last=(gidx == npairs - 1),
                                       sb=(len(groups) % 2) * 2, pt=len(groups) % 3, yi=blk_id % 2,
                                       newhead=(gidx == 0 and q0 == 0)))
                blk_id += 1

        def emit_qk(g):
            i, q0, qw, sb_ = g["i"], g["q0"], g["qw"], g["sb"]
            if g["newhead"] and g["h"] + 1 < 8:
                load_head(g["h"] + 1)
            for u, (kt, c0) in enumerate(g["pair"]):
                dst = ps[:, (sb_ + u) * 512 + c0:(sb_ + u) * 512 + qw]
                isdiag = kt >= g["diag0"]
                S.op("pe", lambda e, kt=kt, c0=c0, dst=dst, isdiag=isdiag: e.matmul(
                    dst, lhsT=KT[i][0:96, kt * 128:(kt + 1) * 128], rhs=QT[i][0:96, q0 + c0:q0 + qw], start=True, stop=not isdiag),
                    reads=[R_K[i], R_Q[i]], writes=[PB[sb_ + u]], inc=not isdiag)
                if isdiag:
                    S.op("pe", lambda e, dst=dst: e.matmul(dst[:, 0:128], lhsT=ident, rhs=maskb, start=False, stop=True),
                         reads=[R_ident, R_mask], writes=[PB[sb_ + u]])

        def emit_exp(g):
            qw, sb_, pair = g["qw"], g["sb"], g["pair"]
            pt, Rpt = PT[g["pt"]], R_PT[g["pt"]]
            if len(pair) == 2 and pair[0][1] == 0 and pair[1][1] == 0 and qw == 512:
                S.op("act", lambda e: e.activation(out=pt, in_=ps[:, sb_ * 512:sb_ * 512 + 1024], func=AF.Exp, scale=scale),
                     reads=[PB[sb_], PB[sb_ + 1]], writes=[Rpt])
            else:
                for u, (kt, c0) in enumerate(pair):
                    S.op("act", lambda e, u=u, c0=c0: e.activation(
                        out=pt[:, u * 512 + c0:u * 512 + qw], in_=ps[:, (sb_ + u) * 512 + c0:(sb_ + u) * 512 + qw], func=AF.Exp, scale=scale),
                        reads=[PB[sb_ + u]], writes=[Rpt])

        def emit_pv(g):
            i, qw, ob, pair = g["i"], g["qw"], g["ob"], g["pair"]
            pt, Rpt = PT[g["pt"]], R_PT[g["pt"]]
            V3 = VV[i].rearrange("p (t e) -> p t e", t=NT)
            for u, (kt, c0) in enumerate(pair):
                first = g["first"] and u == 0
                last = g["last"] and u == len(pair) - 1
                S.op("pe", lambda e, kt=kt, c0=c0, u=u, first=first, last=last: e.matmul(
                    ps[0:65, ob * 512 + c0:ob * 512 + qw], lhsT=V3[:, kt, :], rhs=pt[:, u * 512 + c0:u * 512 + qw], start=first, stop=last),
                    reads=[R_V[i], Rpt], writes=[PB[ob]], inc=(u == len(pair) - 1))

        def emit_norm(g):
            h, q0, qw, ob, yi_ = g["h"], g["q0"], g["qw"], g["ob"], g["yi"]
            S.op("dve", lambda e: e.tensor_scalar(out=rrow[64:65, 0:qw], in0=ps[64:65, ob * 512:ob * 512 + qw], scalar1=1e-30, scalar2=None, op0=ALU.max),
                 reads=[PB[ob]], writes=[R_rrow])
            S.op("dve", lambda e: e.reciprocal(out=rrow[64:65, 0:qw], in_=rrow[64:65, 0:qw]), reads=[R_rrow], writes=[R_rrow])
            S.op("pe", lambda e: e.matmul(ps[0:64, 6 * 512:6 * 512 + qw], lhsT=ones_t[64:65, 0:64], rhs=rrow[64:65, 0:qw], start=True, stop=True),
                 reads=[R_ones, R_rrow], writes=[PB[6]])
            S.op("dve", lambda e: e.tensor_copy(out=bcs[0:64, 0:qw], in_=ps[0:64, 6 * 512:6 * 512 + qw]), reads=[PB[6]], writes=[R_bcs])
            S.op("dve", lambda e: e.tensor_tensor(out=ytmp[yi_][0:64, 0:qw], in0=ps[0:64, ob * 512:ob * 512 + qw], in1=bcs[0:64, 0:qw], op=ALU.mult),
                 reads=[PB[ob], R_bcs], writes=[R_ytmp[yi_]])
            S.dma("sp", "ym%d" % yi_, [lambda e: e.dma_start(
                out=mixT_m[(h % 2) * 64:(h % 2) * 64 + 64, h // 2, q0:q0 + qw], in_=ytmp[yi_][0:64, 0:qw])],
                reads=[R_ytmp[yi_]], writes=[R_mixm[h]])

        pend_norm = None
        emit_qk(groups[0])
        for gi_, g in enumerate(groups):
            emit_exp(g)
            if gi_ + 1 < len(groups):
                emit_qk(groups[gi_ + 1])
            emit_pv(g)
            if pend_norm is not None:
                emit_norm(pend_norm)
                pend_norm = None
            if g["last"]:
                pend_norm = g
        if pend_norm is not None:
            emit_norm(pend_norm)

        S.barrier()
        if KSTOP == 'AB':
            S.emit(); return nc
        A.off = mark_persist
        wout = A.bf(8 * 1024)
        wout3 = wout.rearrange("p (c f) -> p c f", c=8)
        wdown = A.bf(22 * 1024)
        wdown3 = wdown.rearrange("p (c f) -> p c f", c=22)
        R_wout, R_wdown = Res("wout"), Res("wdown")
        S.dma("sp", "wc1", [lambda e: e.dma_start(out=wout, in_=s_wout)], reads=[R_scr["s_wout"]], writes=[R_wout])
        S.dma("sp", "wc2", [lambda e: e.dma_start(out=wdown, in_=s_wdown)], reads=[R_scr["s_wdown"]], writes=[R_wdown])
        fnw_t, R_fnw = load_const(b_fnw, 1024, "fnw")
        onw_t, R_onw = load_const(b_onw, 1024, "onw")
        cw_t, R_cw = load_const(c_cw, 132, "cw")
        cw3 = cw_t.rearrange("p (c j) -> p c j", c=44)
        cb_t, R_cb = load_const(c_cb, 44, "cb")
        wupb = [A.bf(1024) for _ in range(2)]
        R_wupb = [Res("wup%d" % i) for i in range(2)]
        xb2 = [A.f32(1024) for _ in range(2)]
        R_xb2 = [Res("xb2_0"), Res("xb2_1")]
        x1 = A.f32(4 * 1024)
        x13 = x1.rearrange("p (t d) -> p t d", t=4)
        R_x1 = [Res("x1_%d" % i) for i in range(4)]
        h2b = A.bf(1024)
        R_h2b = Res("h2b")
        h2T = A.bf(8 * 512)
        h2T3 = h2T.rearrange("p (c t) -> p c t", c=8)
        R_h2T = Res("h2T")
        gT = A.bf(22 * 512)
        gT3 = gT.rearrange("p (c t) -> p c t", c=22)
        R_gT = Res("gT")
        ubuf = [A.f32(514) for _ in range(2)]
        R_ub = [Res("ub0"), Res("ub1")]
        acc = [A.f32(512) for _ in range(2)]
        R_acc = [Res("acc0"), Res("acc1")]
        carry = A.f32(44 * 2)
        carry3 = carry.rearrange("p (c j) -> p c j", c=44)
        R_carry = Res("carry")
        st2 = A.f32(16)
        R_st2 = Res("st2")
        ybuf = xb2
        R_yb2 = R_xb2
        S.op("dve", lambda e: e.memset(carry, 0.0), writes=[R_carry])
        wupi = [0]
        xli = [0]
        ybi = [0]
        y_tiles = yout.rearrange("(n p) d -> n p d", p=128)

        blocks = [(0, 1)] + [(1 + 4 * j, 4) for j in range(8)]
        def _blockC(bi, m0, ntl):
            W = ntl * 128
            tb = bank(7).bitcast(BF16)
            for t in range(ntl):
                m = m0 + t
                xi_ = xli[0] % 2
                xli[0] += 1
                S.dma("sp", "xc%d" % xi_, [lambda e, m=m, xi_=xi_: e.dma_start(out=xb2[xi_], in_=x_tiles[HALO + m])], writes=[R_xb2[xi_]])
                for hf in range(2):
                    for c in range(8):
                        src_ = mixT_r[:, c, m * 128:(m + 1) * 128] if c < 4 else mixT_m[:, c - 4, m * 128:(m + 1) * 128]
                        S.op("pe", lambda e, c=c, hf=hf, src_=src_: e.matmul(bank(hf), lhsT=src_, rhs=wout3[:, c, hf * 512:(hf + 1) * 512],
                                                                            start=(c == 0), stop=(c == 7)),
                             reads=[R_mixr[m], R_wout] + R_mixm, writes=[PB[hf]], inc=(c == 7))
                S.op("dve", lambda e, t=t, xi_=xi_: e.tensor_tensor(out=x13[:, t, :], in0=ps[:, 0:1024], in1=xb2[xi_], op=ALU.add),
                     reads=[PB[0], PB[1], R_xb2[xi_]], writes=[R_x1[t]])
                ss = st2[:, 0:1]
                rstd = st2[:, 1:2]
                S.op("act", lambda e, t=t: e.activation(out=h2b, in_=x13[:, t, :], func=AF.Square, accum_out=ss), reads=[R_x1[t]], writes=[R_h2b, R_st2])
                S.op("dve", lambda e: e.tensor_scalar(out=ss, in0=ss, scalar1=1.0 / 1024, scalar2=EPS, op0=ALU.mult, op1=ALU.add), reads=[R_st2], writes=[R_st2])
                S.op("act", lambda e: e.activation(out=ss, in_=ss, func=AF.Sqrt), reads=[R_st2], writes=[R_st2])
                S.op("dve", lambda e: e.reciprocal(out=rstd, in_=ss), reads=[R_st2], writes=[R_st2])
                S.op("dve", lambda e, t=t: e.scalar_tensor_tensor(out=h2b, in0=x13[:, t, :], scalar=rstd, in1=fnw_t, op0=ALU.mult, op1=ALU.mult),
                     reads=[R_x1[t], R_st2, R_fnw], writes=[R_h2b])
                for c in range(8):
                    S.op("pe", lambda e, c=c: e.transpose(out=tb[:, c * 128:(c + 1) * 128], in_=h2b[:, c * 128:(c + 1) * 128], identity=ident),
                         reads=[R_h2b, R_ident], writes=[PB[7]], inc=(c == 7))
                S.op("act", lambda e, t=t: e.activation(out=h2T3[:, :, t * 128:(t + 1) * 128], in_=tb.rearrange("p (c t) -> p c t", c=8), func=AF.Copy),
                     reads=[PB[7]], writes=[R_h2T])
            for fc in range(22):
                for half in range(2):
                    cidx = fc + 22 * half
                    wi = wupi[0] % 2
                    wupi[0] += 1
                    S.dma("sp", "wu%d" % wi, [lambda e, cidx=cidx, wi=wi: e.dma_start(out=wupb[wi], in_=s_wup[:, cidx * 1024:(cidx + 1) * 1024])],
                          reads=[R_scr["s_wup"]], writes=[R_wupb[wi]])
                    bk = 2 + half + 2 * (fc % 2)
                    w3 = wupb[wi].rearrange("p (c f) -> p c f", c=8)
                    for c in range(8):
                        S.op("pe", lambda e, c=c, bk=bk, w3=w3: e.matmul(bank(bk, W), lhsT=w3[:, c, :], rhs=h2T3[:, c, 0:W], start=(c == 0), stop=(c == 7)),
                             reads=[R_wupb[wi], R_h2T], writes=[PB[bk]], inc=(c == 7))
                    ub, Rub = ubuf[half], R_ub[half]
                    ac, Rac = acc[half], R_acc[half]
                    S.op("act", lambda e, ub=ub, cidx=cidx: e.activation(out=ub[:, 0:2], in_=carry3[:, cidx, :], func=AF.Copy), reads=[R_carry], writes=[Rub])
                    S.op("act", lambda e, ub=ub, bk=bk: e.activation(out=ub[:, 2:2 + W], in_=bank(bk, W), func=AF.Copy), reads=[PB[bk]], writes=[Rub])
                    S.op("act", lambda e, ub=ub, cidx=cidx: e.activation(out=carry3[:, cidx, :], in_=ub[:, W:W + 2], func=AF.Copy), reads=[Rub], writes=[R_carry])
                    if bi == 0:
                        continue
                    S.op("act", lambda e, ac=ac, bk=bk, cidx=cidx: e.activation(out=ac[:, 0:W], in_=bank(bk, W), func=AF.Identity,
                                                                               scale=cw3[:, cidx, 2:3], bias=cb_t[:, cidx:cidx + 1]),
                         reads=[PB[bk], R_cw, R_cb], writes=[Rac])
                    S.op("dve", lambda e, ac=ac, ub=ub, cidx=cidx: e.scalar_tensor_tensor(out=ac[:, 0:W], in0=ub[:, 1:1 + W], scalar=cw3[:, cidx, 1:2], in1=ac[:, 0:W],
                                                                                         op0=ALU.mult, op1=ALU.add), reads=[Rub, Rac, R_cw], writes=[Rac])
                    S.op("dve", lambda e, ac=ac, ub=ub, cidx=cidx: e.scalar_tensor_tensor(out=ac[:, 0:W], in0=ub[:, 0:W], scalar=cw3[:, cidx, 0:1], in1=ac[:, 0:W],
                                                                                         op0=ALU.mult, op1=ALU.add), reads=[Rub, Rac, R_cw], writes=[Rac])
                if bi == 0:
                    continue
                S.op("act", lambda e: e.activation(out=acc[0][:, 0:W], in_=acc[0][:, 0:W], func=AF.Silu), reads=[R_acc[0]], writes=[R_acc[0]])
                S.op("dve", lambda e, fc=fc: e.tensor_tensor(out=gT3[:, fc, 0:W], in0=acc[0][:, 0:W], in1=acc[1][:, 0:W], op=ALU.mult),
                     reads=[R_acc[0], R_acc[1]], writes=[R_gT])
            if bi == 0:
                return
            for t in range(ntl):
                m = m0 + t
                for hf in range(2):
                    for fc in range(22):
                        S.op("pe", lambda e, fc=fc, hf=hf, t=t: e.matmul(bank(hf), lhsT=gT3[:, fc, t * 128:(t + 1) * 128],
                                                                        rhs=wdown3[:, fc, hf * 512:(hf + 1) * 512], start=(fc == 0), stop=(fc == 21)),
                             reads=[R_gT, R_wdown], writes=[PB[hf]], inc=(fc == 21))
                yi = ybi[0] % 2
                ybi[0] += 1
                yb_, Ryb = ybuf[yi], R_yb2[yi]
                S.op("dve", lambda e, t=t: e.tensor_tensor(out=x13[:, t, :], in0=ps[:, 0:1024], in1=x13[:, t, :], op=ALU.add),
                     reads=[PB[0], PB[1], R_x1[t]], writes=[R_x1[t]])
                ss2 = st2[:, 2:3]
                rstd2 = st2[:, 3:4]
                S.op("act", lambda e, t=t, yb_=yb_: e.activation(out=h2b, in_=x13[:, t, :], func=AF.Square, accum_out=ss2), reads=[R_x1[t]], writes=[R_h2b, R_st2])
                S.op("dve", lambda e: e.tensor_scalar(out=ss2, in0=ss2, scalar1=1.0 / 1024, scalar2=EPS, op0=ALU.mult, op1=ALU.add), reads=[R_st2], writes=[R_st2])
                S.op("act", lambda e: e.activation(out=ss2, in_=ss2, func=AF.Sqrt), reads=[R_st2], writes=[R_st2])
                S.op("dve", lambda e: e.reciprocal(out=rstd2, in_=ss2), reads=[R_st2], writes=[R_st2])
                S.op("dve", lambda e, t=t, yb_=yb_: e.scalar_tensor_tensor(out=yb_, in0=x13[:, t, :], scalar=rstd2, in1=onw_t, op0=ALU.mult, op1=ALU.mult),
                     reads=[R_x1[t], R_st2, R_onw], writes=[Ryb])
                S.dma("sp", "yo%d" % yi, [lambda e, m=m, yb_=yb_: e.dma_start(out=y_tiles[m - 1], in_=yb_)], reads=[Ryb])

        for bi, (m0, ntl) in enumerate(blocks):
            _blockC(bi, m0, ntl)

        S.barrier()
        S.emit()
    return nc


def _consts():
    H, C = 8, 128
    lg = np.log1p(-np.power(2.0, -5.0 - np.arange(H, dtype=np.float64)))
    idx = np.arange(C, dtype=np.float64)
    diff = idx[None, :] - idx[:, None]
    dt = np.where(diff[:, None, :] >= 0, np.exp(lg[None, :, None] * np.maximum(diff[:, None, :], 0.0)), 0.0) / 8.0
    c_dt = dt.reshape(128, 1024).astype(np.float32)
    xi = np.exp(lg[:, None] * (idx[None, :] + 1.0))
    c_xi = np.zeros((128, 4, 128))
    for p in range(4):
        for a in range(2):
            c_xi[a * 64:(a + 1) * 64, p, :] = xi[2 * p + a][None, :]
    c_xi = c_xi.reshape(128, 512).astype(np.float32)
    zeta = np.exp(lg[:, None] * (C - 1.0 - idx[None, :])) / 8.0
    c_zeta = np.repeat(zeta.T[:, :, None], 64, axis=2).reshape(128, 512).astype(np.float32)
    dec = np.exp(lg * C)
    c_dec = np.zeros((128, 4))
    for p in range(4):
        c_dec[0:64, p] = dec[2 * p]
        c_dec[64:128, p] = dec[2 * p + 1]
    c_dec = c_dec.astype(np.float32)
    fr = (10000.0 ** (-np.arange(0, 64, 2, dtype=np.float32) / np.float32(64))).astype(np.float32)
    fm = (10000.0 ** (-np.arange(0, 32, 2, dtype=np.float32) / np.float32(32))).astype(np.float32)
    invf = np.concatenate([fr, fr, fr, fr, fm, fm, fm, fm]).astype(np.float64) / (2 * np.pi)
    off = np.concatenate([np.full(64, 0.25), np.full(32, 0.5), np.zeros(32), np.full(32, 0.25), np.full(16, 0.5), np.zeros(16)])
    c_invf = np.broadcast_to(invf[None, :], (128, 192)).astype(np.float32).copy()
    c_off = np.broadcast_to(off[None, :], (128, 192)).astype(np.float32).copy()
    k = np.arange(128)
    c_mask = np.where(k[None, :] < k[:, None], -30000.0, 0.0).astype(np.float32)
    return dict(c_dt=c_dt, c_xi=c_xi, c_zeta=c_zeta, c_dec=c_dec, c_invf=c_invf, c_off=c_off, c_mask=c_mask)


def _bc(v, n=128):
    return np.ascontiguousarray(np.broadcast_to(np.asarray(v, np.float32)[None, :], (n, v.shape[0])))


_PROG = None


def kernel(x, positions, attn_norm_w, w_in, ret_gn_w, mla_q_norm_w, w_uq, mla_kv_norm_w, w_ukv,
           w_out, ffn_norm_w, w_up, conv_w, conv_b, w_down, final_norm_w):
    global _PROG
    x = np.asarray(x, np.float32)
    positions = np.asarray(positions, np.int32)
    shared = _consts()
    shared["b_anw"] = _bc(np.asarray(attn_norm_w)[0])
    shared["b_fnw"] = _bc(np.asarray(ffn_norm_w)[0])
    shared["b_onw"] = _bc(np.asarray(final_norm_w))
    shared["b_qnw"] = _bc(np.asarray(mla_q_norm_w)[0])
    shared["b_kvnw"] = _bc(np.asarray(mla_kv_norm_w)[0])
    shared["b_gnw"] = _bc(np.asarray(ret_gn_w)[0])
    cw = np.asarray(conv_w, np.float32)[0]
    shared["c_cw"] = np.ascontiguousarray(cw.reshape(3, 44, 128).transpose(2, 1, 0)).reshape(128, 132)
    shared["c_cb"] = np.ascontiguousarray(np.asarray(conv_b, np.float32)[0].reshape(44, 128).T)
    shared["w_in_l"] = np.ascontiguousarray(np.asarray(w_in, np.float32)[0].reshape(8, 128, 2464).transpose(1, 0, 2)).reshape(128, -1)
    shared["w_uq_l"] = np.ascontiguousarray(np.asarray(w_uq, np.float32)[0].reshape(2, 128, 768).transpose(1, 0, 2)).reshape(128, -1)
    wukv = np.asarray(w_ukv, np.float32)[0].reshape(128, 8, 128)
    shared["wk_l"] = np.ascontiguousarray(wukv[:, :, 0:64]).reshape(128, 512)
    shared["wv_l"] = np.ascontiguousarray(wukv[:, :, 64:128]).reshape(128, 512)
    wo = np.asarray(w_out, np.float32)[0]
    shared["w_out_l"] = np.ascontiguousarray(wo.reshape(8, 128, 1024).transpose(1, 0, 2)).reshape(128, -1)
    wu = np.asarray(w_up, np.float32)[0]
    shared["w_up_l"] = np.ascontiguousarray(wu.reshape(8, 128, 44, 128).transpose(1, 2, 0, 3)).reshape(128, -1)
    wd = np.asarray(w_down, np.float32)[0]
    shared["w_down_l"] = np.ascontiguousarray(wd.reshape(22, 128, 1024).transpose(1, 0, 2)).reshape(128, -1)

    in_maps = []
    for c in range(8):
        b, z = divmod(c, 2)
        m = dict(shared)
        if z == 1:
            xcore = x[b]
            pc = positions[b]
            vd = np.ones(8192, np.float32)
        else:
            xcore = np.concatenate([np.zeros((4096, 1024), np.float32), x[b, :4096]], axis=0)
            pc = np.concatenate([np.zeros(4096, np.int32), positions[b, :4096]])
            vd = np.concatenate([np.zeros(4096, np.float32), np.ones(4096, np.float32)])
        m["xc"] = np.ascontiguousarray(xcore)
        m["posc"] = np.ascontiguousarray(pc.reshape(NT, 128).T)
        m["valid"] = np.ascontiguousarray(vd.reshape(NT, 128).T)
        in_maps.append(m)
    if _PROG is None:
        _PROG = build_program()
    res = run_bass_kernel_spmd(_PROG, in_maps, core_ids=list(range(8)))
    out = np.empty((4, 8192, 1024), np.float32)
    for c in range(8):
        b, z = divmod(c, 2)
        out[b, z * 4096:(z + 1) * 4096] = res.results[c]["yout"]
    return out
```

```python
import math
from contextlib import ExitStack
import numpy as np
import concourse.bass as bass
import concourse.mybir as mybir
from concourse.bass_utils import run_bass_kernel_spmd

F32 = mybir.dt.float32
BF16 = mybir.dt.bfloat16
I32 = mybir.dt.int32
ALU = mybir.AluOpType
AF = mybir.ActivationFunctionType
AX = mybir.AxisListType

NT = 64
HALO = 31
NOWN = 33
EPS = 1e-6
TWO_PI = 2.0 * math.pi


class Res:
    __slots__ = ("name", "w", "r", "excl")

    def __init__(self, name, excl=False):
        self.name = name
        self.w = None
        self.r = set()
        self.excl = excl


class _Rec:
    def __init__(self):
        self.call = None

    def __getattr__(self, name):
        def f(*a, **k):
            self.call = (name, a, k)
            return self
        return f


def _fsize(ap):
    n = 1
    for d in list(ap.shape)[1:]:
        n *= int(d)
    return n


def _est(eng, fn):
    try:
        r = _Rec()
        fn(r)
        name, a, k = r.call
        if eng == "pe":
            rhs = k.get("rhs", k.get("identity"))
            n = _fsize(rhs) if name == "matmul" else 128
            return max(n, 64) / 2.4 + 25.0
        out = k.get("out", a[0] if a else None)
        f = _fsize(out)
        if eng == "act":
            return (f + 224) / 1.2
        if eng == "dve":
            if name == "reciprocal":
                return 165 + (6.2 * f if int(out.shape[0]) < 32 else f)
            return (f + 150) / 0.96
        return 200 + 3.4 * f
    except Exception:
        return 500.0


class _Op:
    __slots__ = ("q", "kind", "fns", "deps", "cost", "lat", "slot", "seq", "tag")


class Sched:
    ENG = ("pe", "act", "dve", "pool", "sp")

    def __init__(self, nc, stack):
        self.nc = nc
        self.stack = stack
        self.sem = {e: stack.enter_context(nc.semaphore("s_" + e)) for e in self.ENG}
        self.cnt = {e: 0 for e in self.ENG}
        self.seen = {e: {} for e in self.ENG}
        self.streams = {e: [] for e in self.ENG}
        self.dsem = {}
        self.dcnt = {}
        self.ops = []
        self.base = 0
        self.open_pe = None
        self.reorder = True

    def _record_deps(self, idx, reads, writes):
        deps = self.ops[idx].deps
        for r in reads:
            if r.w is not None and r.w >= self.base and r.w != idx:
                deps.add(r.w)
        for w in writes:
            if w.w is not None and w.w >= self.base and w.w != idx:
                deps.add(w.w)
            for t in w.r:
                if t >= self.base and t != idx:
                    deps.add(t)
        for r in reads:
            if r not in writes:
                r.r.add(idx)
        for w in writes:
            w.w = idx
            w.r = set()

    def op(self, eng, fn, reads=(), writes=(), inc=True, tag=None):
        ex = [r for r in reads if r.excl and r not in writes]
        if ex:
            writes = list(writes) + ex
        if eng == "pe" and self.open_pe is not None:
            idx = self.open_pe
            o = self.ops[idx]
            o.fns.append(fn)
            o.cost += _est(eng, fn)
        else:
            o = _Op()
            o.q, o.kind, o.fns, o.deps, o.cost, o.lat, o.slot, o.seq, o.tag = eng, "op", [fn], set(), _est(eng, fn), 0.0, None, None, tag
            idx = len(self.ops)
            self.ops.append(o)
        self._record_deps(idx, reads, writes)
        if eng == "pe":
            self.open_pe = None if inc else idx
        else:
            assert inc
        return idx

    def dma(self, eng, slot, fns, reads=(), writes=(), nbytes=262144):
        assert self.open_pe is None
        if slot not in self.dsem:
            self.dsem[slot] = self.stack.enter_context(self.nc.semaphore("d_" + slot))
            self.dcnt[slot] = 0
        o = _Op()
        o.q, o.kind, o.fns, o.deps, o.cost, o.slot, o.seq, o.tag = eng, "dma", list(fns), set(), 350.0 * len(fns), slot, None, None
        o.lat = 2000.0 + nbytes / 150.0
        idx = len(self.ops)
        self.ops.append(o)
        self._record_deps(idx, reads, writes)
        return idx

    def _wait(self, eng, toks):
        best = {}
        for t in toks:
            k, s, v = t
            if k not in best or best[k][2] < v:
                best[k] = t
        for k, (kk, s, v) in best.items():
            if self.seen[eng].get(k, 0) >= v:
                continue
            self.seen[eng][k] = v
            self.streams[eng].append(("wait", s, v))

    def _token(self, d):
        o = self.ops[d]
        if o.kind == "dma":
            return ("d_" + o.slot, self.dsem[o.slot], o.seq)
        return (o.q, self.sem[o.q], o.seq)

    def flush(self):
        assert self.open_pe is None
        ops, base = self.ops, self.base
        n = len(ops)
        if n == base:
            return
        if self.reorder:
            order = self._list_schedule(base, n)
        else:
            order = list(range(base, n))
        for i in order:
            o = ops[i]
            toks = []
            for d in o.deps:
                od = ops[d]
                if od.kind == "op" and od.q == "pe" and o.q == "pe" and o.kind == "op":
                    continue
                toks.append(self._token(d))
            self._wait(o.q, toks)
            if o.kind == "dma":
                for fn in o.fns:
                    self.dcnt[o.slot] += 16
                    self.streams[o.q].append(("dma", fn, self.dsem[o.slot]))
                o.seq = self.dcnt[o.slot]
            else:
                self.cnt[o.q] += 1
                o.seq = self.cnt[o.q]
                for j, fn in enumerate(o.fns):
                    self.streams[o.q].append(("op", fn, j == len(o.fns) - 1))
        self.base = n

    def _list_schedule(self, base, n):
        ops = self.ops
        indeg = {}
        succ = {}
        for i in range(base, n):
            dd = [d for d in ops[i].deps if d >= base]
            indeg[i] = len(dd)
            for d in dd:
                succ.setdefault(d, []).append(i)
        etime = {e: 0.0 for e in self.ENG}
        lasttag = {e: None for e in self.ENG}
        finish = {}
        rtime = {}
        ready = {e: [] for e in self.ENG}
        for i in range(base, n):
            if indeg[i] == 0:
                rtime[i] = 0.0
                ready[ops[i].q].append(i)
        order = []
        LOOK = 1 << 30
        while len(order) < n - base:
            bestk, besti = None, None
            for e in self.ENG:
                lst = ready[e]
                if not lst:
                    continue
                t = etime[e]
                cand, ck = None, None
                for i in lst:
                    st = rtime[i] if rtime[i] > t else t
                    k = (st, i)
                    if ck is None or k < ck:
                        cand, ck = i, k
                if bestk is None or ck < bestk:
                    bestk, besti = ck, cand
            i = besti
            o = ops[i]
            st = bestk[0]
            c = o.cost
            if o.tag is not None and o.q == "act":
                if lasttag["act"] is not None and lasttag["act"] != o.tag:
                    c += 1300.0
                lasttag["act"] = o.tag
            etime[o.q] = st + c
            finish[i] = st + c + (o.lat if o.kind == "dma" else 60.0)
            ready[o.q].remove(i)
            order.append(i)
            for j in succ.get(i, ()):
                indeg[j] -= 1
                rt = rtime.get(j, 0.0)
                if finish[i] > rt:
                    rtime[j] = finish[i]
                elif j not in rtime:
                    rtime[j] = rt
                if indeg[j] == 0:
                    ready[ops[j].q].append(j)
        self.est_span = max(etime.values())
        return order

    def barrier(self):
        self.flush()
        toks = [(e, self.sem[e], self.cnt[e]) for e in self.ENG if self.cnt[e] > 0]
        toks += [("d_" + s, self.dsem[s], self.dcnt[s]) for s in self.dsem if self.dcnt[s] > 0]
        for e in self.ENG:
            self._wait(e, toks)

    def emit(self):
        self.flush()
        nc = self.nc
        with nc.Block() as block:
            def run(eng, e):
                sem = self.sem[eng]
                for item in self.streams[eng]:
                    if item[0] == "wait":
                        e.wait_ge(item[1], item[2])
                    elif item[0] == "op":
                        ins = item[1](e)
                        if item[2]:
                            ins.then_inc(sem, 1)
                    else:
                        item[1](e).then_inc(item[2], 16)

            @block.tensor
            def _(e):
                run("pe", e)

            @block.scalar
            def _(e):
                run("act", e)

            @block.vector
            def _(e):
                run("dve", e)

            @block.gpsimd
            def _(e):
                run("pool", e)

            @block.sync
            def _(e):
                run("sp", e)


def bc_mid(a, k):
    return bass.AP(a.tensor, a.offset, [list(a.ap[0]), [0, k]] + [list(x) for x in a.ap[1:]])


def bc_last(a, m):
    return bass.AP(a.tensor, a.offset, [list(x) for x in a.ap] + [[0, m]])


class Arena:
    def __init__(self, t, total):
        self.t = t
        self.total = total
        self.off = 0

    def f32(self, n):
        assert self.off + n <= self.total, ("arena overflow", self.off, n, self.total)
        a = self.t[:, self.off:self.off + n]
        self.off += n
        return a

    def bf(self, n):
        w = (n + 1) // 2
        return self.f32(w).bitcast(BF16)[:, 0:n]


import os
KSTOP = os.environ.get('KSTOP', '')
KSUB = int(os.environ.get('KSUB', 0))


def build_program():
    nc = bass.Bass("TRN2", target_bir_lowering=False)
    din = {}

    def inp(name, shape, dt=F32):
        din[name] = nc.dram_tensor(name, list(shape), dt, kind="ExternalInput").ap()
        return din[name]

    xc = inp("xc", [NT * 128, 1024])
    posc = inp("posc", [128, NT], I32)
    valid = inp("valid", [128, NT])
    c_dt = inp("c_dt", [128, 1024])
    c_xi = inp("c_xi", [128, 512])
    c_zeta = inp("c_zeta", [128, 512])
    c_dec = inp("c_dec", [128, 4])
    c_invf = inp("c_invf", [128, 192])
    c_off = inp("c_off", [128, 192])
    c_mask = inp("c_mask", [128, 128])
    b_anw = inp("b_anw", [128, 1024])
    b_fnw = inp("b_fnw", [128, 1024])
    b_onw = inp("b_onw", [128, 1024])
    b_qnw = inp("b_qnw", [128, 256])
    b_kvnw = inp("b_kvnw", [128, 128])
    b_gnw = inp("b_gnw", [128, 512])
    c_cw = inp("c_cw", [128, 44 * 3])
    c_cb = inp("c_cb", [128, 44])
    w_in_l = inp("w_in_l", [128, 8 * 2464])
    w_uq_l = inp("w_uq_l", [128, 2 * 768])
    wk_l = inp("wk_l", [128, 512])
    wv_l = inp("wv_l", [128, 512])
    w_out_l = inp("w_out_l", [128, 8 * 1024])
    w_up_l = inp("w_up_l", [128, 44 * 1024])
    w_down_l = inp("w_down_l", [128, 22 * 1024])
    yout = nc.dram_tensor("yout", [4096, 1024], F32, kind="ExternalOutput").ap()

    s_wup = nc.dram_tensor("s_wup", [128, 44 * 1024], BF16).ap()
    s_wdown = nc.dram_tensor("s_wdown", [128, 22 * 1024], BF16).ap()
    s_wout = nc.dram_tensor("s_wout", [128, 8 * 1024], BF16).ap()
    s_kt = nc.dram_tensor("s_kt", [8, 96, NT * 128], BF16).ap()
    s_v = nc.dram_tensor("s_v", [8, 128, NT * 65], BF16).ap()
    s_qt = nc.dram_tensor("s_qt", [8, 96, NOWN * 128], BF16).ap()

    with ExitStack() as st:
        S = Sched(nc, st)
        TOT = 53000
        arena_t = st.enter_context(nc.sbuf_tensor("arena", [128, TOT], F32))
        ps = st.enter_context(nc.psum_tensor("ps", [128, 4096], F32))
        A = Arena(arena_t, TOT)

        def bank(i, n=512):
            return ps[:, i * 512:i * 512 + n]

        PB = [Res("psb%d" % i, excl=True) for i in range(8)]

        ident = A.bf(128)
        maskb = A.bf(128)
        mixT_r = A.bf(4 * NOWN * 128).rearrange("p (c t) -> p c t", c=4)
        R_ident, R_mask, R_mixr = Res("ident"), Res("mask"), [Res("mixr%d" % i) for i in range(NOWN)]
        R_mixm = [Res("mixm%d" % h) for h in range(8)]
        mark_persist = A.off

        S.op("pool", lambda e: e.memset(ident, 0.0), writes=[R_ident])
        S.op("pool", lambda e: e.affine_select(out=ident, in_=ident, pattern=[[-1, 128]], compare_op=ALU.not_equal,
                                               fill=1.0, base=0, channel_multiplier=1), reads=[R_ident], writes=[R_ident])

        w_in = A.bf(8 * 2464)
        w_in3 = w_in.rearrange("p (c f) -> p c f", c=8)
        w_uq = A.bf(2 * 768)
        w_uq3 = w_uq.rearrange("p (c f) -> p c f", c=2)
        wk = A.bf(512)
        wv = A.bf(512)
        R_win, R_wuq, R_wk, R_wv = Res("w_in"), Res("w_uq"), Res("wk"), Res("wv")
        stage = [A.f32(1024) for _ in range(2)]
        stageb = [A.bf(1024) for _ in range(2)]
        R_stage = [Res("stage0"), Res("stage1")]
        R_stageb = [Res("stageb0"), Res("stageb1")]
        R_scr = {k: Res(k) for k in ["s_wup", "s_wdown", "s_wout"]}
        pieces = []

        def add_pieces(src, ncols, dst_sb=None, dst_res=None, dst_dram=None, dram_res=None):
            c0 = 0
            while c0 < ncols:
                n = min(1024, ncols - c0)
                pieces.append((src, c0, n, dst_sb, dst_res, dst_dram, dram_res))
                c0 += n

        def piece_load_cast(k):
            src, c0, n, dst_sb, dst_res, dst_dram, dram_res = pieces[k]
            i = k % 2
            S.dma("act", "stg%d" % i, [lambda e: e.dma_start(out=stage[i][:, 0:n], in_=src[:, c0:c0 + n])], writes=[R_stage[i]])
            if dst_sb is not None:
                S.op("pool", lambda e: e.tensor_copy(out=dst_sb[:, c0:c0 + n], in_=stage[i][:, 0:n]), reads=[R_stage[i]], writes=[dst_res])
            else:
                S.op("pool", lambda e: e.tensor_copy(out=stageb[i][:, 0:n], in_=stage[i][:, 0:n]), reads=[R_stage[i]], writes=[R_stageb[i]])

        def piece_store(k):
            src, c0, n, dst_sb, dst_res, dst_dram, dram_res = pieces[k]
            i = k % 2
            if dst_dram is not None:
                S.dma("act", "stb%d" % i, [lambda e: e.dma_start(out=dst_dram[:, c0:c0 + n], in_=stageb[i][:, 0:n])],
                      reads=[R_stageb[i]], writes=[dram_res])

        add_pieces(w_in_l, 8 * 2464, dst_sb=w_in, dst_res=R_win)
        add_pieces(w_uq_l, 2 * 768, dst_sb=w_uq, dst_res=R_wuq)
        add_pieces(wk_l, 512, dst_sb=wk, dst_res=R_wk)
        add_pieces(wv_l, 512, dst_sb=wv, dst_res=R_wv)
        add_pieces(c_mask, 128, dst_sb=maskb, dst_res=R_mask)
        n_first = len(pieces)
        add_pieces(w_out_l, 8 * 1024, dst_dram=s_wout, dram_res=R_scr["s_wout"])
        add_pieces(w_down_l, 22 * 1024, dst_dram=s_wdown, dram_res=R_scr["s_wdown"])
        add_pieces(w_up_l, 44 * 1024, dst_dram=s_wup, dram_res=R_scr["s_wup"])
        for k in range(n_first):
            piece_load_cast(k)
        pk = [n_first, n_first]

        def cast_step(nload):
            while pk[1] < pk[0]:
                piece_store(pk[1])
                pk[1] += 1
            for _ in range(nload):
                if pk[0] < len(pieces):
                    piece_load_cast(pk[0])
                    pk[0] += 1

        def load_const(src, n, name, dt=F32):
            a = A.f32(n)
            if dt is not F32:
                a = a.bitcast(dt)
            r = Res(name)
            S.dma("sp", "c_" + name, [lambda e: e.dma_start(out=a, in_=src)], writes=[r])
            return a, r

        dt_t, R_dt = load_const(c_dt, 1024, "dt")
        xi_t, R_xi = load_const(c_xi, 512, "xi")
        zeta_t, R_zeta = load_const(c_zeta, 512, "zeta")
        dec_t, R_dec = load_const(c_dec, 4, "dec")
        invf_t, R_invf = load_const(c_invf, 192, "invf")
        off_t, R_off = load_const(c_off, 192, "off")
        anw_t, R_anw = load_const(b_anw, 1024, "anw")
        qnw_t, R_qnw = load_const(b_qnw, 256, "qnw")
        kvnw_t, R_kvnw = load_const(b_kvnw, 128, "kvnw")
        gnw_t, R_gnw = load_const(b_gnw, 512, "gnw")
        posi, R_pos = load_const(posc, NT, "posi", I32)
        valid_t, R_valid = load_const(valid, NT, "valid")
        posf = A.f32(NT)
        S.op("dve", lambda e: e.tensor_copy(out=posf, in_=posi), reads=[R_pos], writes=[R_pos])

        TB = 16
        tab = A.f32(TB * 192)
        tab3 = tab.rearrange("p (n f) -> p n f", n=TB)
        R_tab = Res("tab")
        ttmp = A.f32(192)
        tti = A.f32(192).bitcast(I32)
        R_tt = Res("ttmp")

        def make_tables(n0):
            for n in range(n0, n0 + TB):
                S.op("dve", lambda e, n=n: e.scalar_tensor_tensor(out=ttmp, in0=invf_t, scalar=posf[:, n:n + 1], in1=off_t,
                                                                op0=ALU.mult, op1=ALU.add),
                     reads=[R_invf, R_off, R_pos], writes=[R_tt])
                S.op("dve", lambda e: e.tensor_copy(out=tti, in_=ttmp), reads=[R_tt], writes=[R_tt])
                S.op("dve", lambda e, n=n: e.tensor_tensor(out=tab3[:, n % TB, :], in0=ttmp, in1=tti, op=ALU.subtract),
                     reads=[R_tt], writes=[R_tab])
            S.op("act", lambda e: e.activation(out=tab, in_=tab, func=AF.Sin, scale=TWO_PI * (1.0 - 1e-6)),
                 reads=[R_tab], writes=[R_tab])

        xbuf = [A.f32(1024) for _ in range(2)]
        R_x = [Res("x0"), Res("x1")]
        junk = A.f32(1024)
        R_junk = Res("junk")
        st_small = A.f32(64)
        R_ss = Res("ss")
        hb = A.bf(1024)
        R_hb = Res("hb")
        hT = A.bf(1024)
        hT3 = hT.rearrange("p (c t) -> p c t", c=8)
        R_hT = Res("hT")
        tmpA = A.f32(1024)
        tmpB = A.f32(1024)
        R_tA, R_tB = Res("tmpA"), Res("tmpB")
        qkr = A.bf(1024)
        R_qkr = Res("qkr")
        kz = A.bf(512)
        R_kz = Res("kz")
        vb = A.bf(512)
        R_vb = Res("vb")
        sg = A.f32(512)
        R_sg = Res("sg")
        qT = A.bf(1024)
        qxT = A.bf(512)
        kT = A.bf(512)
        R_qT, R_qxT, R_kT = Res("qT"), Res("qxT"), Res("kT")
        sd = A.bf(1024)
        R_sd = Res("sd")
        R32 = A.f32(512)
        Rb = A.bf(512)
        R_R32, R_Rb = Res("R32"), Res("Rb")
        gn1 = A.f32(512)
        gn2 = A.f32(512)
        R_gn1, R_gn2 = Res("gn1"), Res("gn2")
        yb = A.bf(512)
        R_yb = Res("yb")
        cqn = A.bf(256)
        ckvn = A.bf(128)
        kr = A.bf(32)
        R_cqn, R_ckvn, R_kr = Res("cqn"), Res("ckvn"), Res("kr")
        cqnT = A.bf(256)
        ckvnT = A.bf(128)
        R_cqnT, R_ckvnT = Res("cqnT"), Res("ckvnT")
        mt1 = A.f32(64)
        mt2 = A.f32(64)
        R_mt = Res("mt")
        qb = A.bf(768)
        R_qb = Res("qb")
        kn_g = [A.bf(4 * 512) for _ in range(2)]
        kpe_g = [A.bf(512) for _ in range(2)]
        v_g = [A.bf(4 * 8 * 65) for _ in range(2)]
        q_g = [A.bf(8 * 512) for _ in range(2)]
        R_kng = [Res("kng0"), Res("kng1")]
        R_kpg = [Res("kpg0"), Res("kpg1")]
        R_vg = [Res("vg0"), Res("vg1")]
        R_qg = [Res("qg0"), Res("qg1")]
        R_skt, R_sv, R_sqt = Res("s_kt"), Res("s_v"), Res("s_qt")

        S.op("dve", lambda e: e.memset(qT, 0.0), writes=[R_qT])
        S.op("dve", lambda e: e.memset(R32, 0.0), writes=[R_R32])
        S.op("dve", lambda e: e.memset(Rb, 0.0), writes=[R_Rb])

        x_tiles = xc.rearrange("(n p) d -> n p d", p=128)

        def load_x(n):
            i = n % 2
            S.dma("sp", "x%d" % i, [lambda e, n=n, i=i: e.dma_start(out=xbuf[i], in_=x_tiles[n])], writes=[R_x[i]])

        if KSTOP == 'A0':
            npz = int(os.environ.get('KNP', 0))
            while pk[1] < min(len(pieces), n_first + npz):
                cast_step(2)
            S.barrier(); S.emit(); return nc
        load_x(0)

        def _tileA(n):
            cast_step(2)
            full = n >= HALO
            g, gi = divmod(n, 4)
            gb = g % 2
            if n + 1 < NT:
                load_x(n + 1)
            xb_, Rx = xbuf[n % 2], R_x[n % 2]
            ss = st_small[:, 0:1]
            rstd = st_small[:, 1:2]
            S.op("act", lambda e, xb_=xb_: e.activation(out=junk, in_=xb_, func=AF.Square, accum_out=ss),
                 reads=[Rx], writes=[R_junk, R_ss])
            S.op("dve", lambda e: e.tensor_scalar(out=ss, in0=ss, scalar1=1.0 / 1024, scalar2=EPS, op0=ALU.mult, op1=ALU.add),
                 reads=[R_ss], writes=[R_ss])
            S.op("act", lambda e: e.activation(out=ss, in_=ss, func=AF.Sqrt), reads=[R_ss], writes=[R_ss])
            S.op("dve", lambda e: e.reciprocal(out=rstd, in_=ss), reads=[R_ss], writes=[R_ss])
            S.op("dve", lambda e, xb_=xb_: e.scalar_tensor_tensor(out=hb, in0=xb_, scalar=rstd, in1=anw_t, op0=ALU.mult, op1=ALU.mult),
                 reads=[Rx, R_ss, R_anw], writes=[R_hb])
            tb = bank(0).bitcast(BF16)
            for c in range(8):
                S.op("pe", lambda e, c=c: e.transpose(out=tb[:, c * 128:(c + 1) * 128], in_=hb[:, c * 128:(c + 1) * 128], identity=ident),
                     reads=[R_hb, R_ident], writes=[PB[0]], inc=(c == 7))
            S.op("act", lambda e: e.activation(out=hT, in_=tb, func=AF.Copy), reads=[PB[0]], writes=[R_hT])
            if KSUB == 1 and n >= HALO:
                return


            def proj(bk, col0, ncol, n_=None):
                for c in range(8):
                    S.op("pe", lambda e, c=c: e.matmul(bank(bk, ncol), lhsT=hT3[:, c, :], rhs=w_in3[:, c, col0:col0 + ncol],
                                                       start=(c == 0), stop=(c == 7)),
                         reads=[R_hT, R_win], writes=[PB[bk]], inc=(c == 7))

            if full:
                proj(1, 0, 512)
            proj(2, 512, 512)
            proj(3, 1024, 512)
            if full:
                proj(4, 1536, 512)
                proj(5, 2048, 416)
            else:
                proj(5, 2304, 160)
            if KSUB == 2 and n >= HALO:
                return

            latoff = 0 if full else -256
            if n % TB == 0:
                make_tables(n)
            tabn = tab3[:, n % TB, :]
            cs_r, ss_r = tabn[:, 0:64], tabn[:, 64:128]
            cs_m, ss_m = tabn[:, 128:160], tabn[:, 160:192]

            def rope(src_ap, nh, hd, cs, sn, dstA, dstB, dst, reads, wres):
                half = hd // 2
                x3 = src_ap.rearrange("p (h d) -> p h d", h=nh)
                sw = bass.AP(src_ap.tensor, src_ap.offset + half,
                             [list(src_ap.ap[0]), [hd, nh], [-half, 2], [1, half]])
                a3 = dstA.rearrange("p (h d) -> p h d", h=nh)
                b4 = dstB.rearrange("p (h a d) -> p h a d", h=nh, a=2)
                S.op("dve", lambda e: e.tensor_tensor(out=a3, in0=x3, in1=bc_mid(cs, nh), op=ALU.mult),
                     reads=reads + [R_tab], writes=[R_tA])
                S.op("dve", lambda e: e.tensor_tensor(out=b4, in0=sw, in1=bc_mid(sn.rearrange("p (a d) -> p a d", a=2), nh), op=ALU.mult),
                     reads=reads + [R_tab], writes=[R_tB])
                S.op("dve", lambda e: e.tensor_tensor(out=dst, in0=dstA, in1=dstB, op=ALU.add),
                     reads=[R_tA, R_tB], writes=[wres])

            if full:
                rope(ps[:, 512:1536], 16, 64, cs_r, ss_r, tmpA, tmpB, qkr, [PB[1], PB[2]], R_qkr)
            else:
                rope(ps[:, 1024:1536], 8, 64, cs_r, ss_r, tmpA[:, 0:512], tmpB[:, 0:512], qkr[:, 512:1024], [PB[2]], R_qkr)
            S.op("dve", lambda e: e.tensor_tensor(out=kz, in0=qkr[:, 512:1024], in1=zeta_t, op=ALU.mult),
                 reads=[R_qkr, R_zeta], writes=[R_kz])
            S.op("act", lambda e: e.activation(out=vb, in_=bank(3), func=AF.Copy), reads=[PB[3]], writes=[R_vb])
            if KSUB == 3 and n >= HALO:
                return


            if full:
                S.op("act", lambda e: e.activation(out=sg, in_=bank(4), func=AF.Silu), reads=[PB[4]], writes=[R_sg])
                for c in range(8):
                    S.op("pe", lambda e, c=c: e.transpose(out=tb[:, c * 128:(c + 1) * 128], in_=qkr[:, c * 128:(c + 1) * 128], identity=ident),
                         reads=[R_qkr, R_ident], writes=[PB[0]], inc=(c == 7))
                S.op("act", lambda e: e.activation(out=qT[0:64, 0:512], in_=tb[0:64, 0:512], func=AF.Copy), reads=[PB[0]], writes=[R_qT])
                S.op("act", lambda e: e.activation(out=qT[64:128, 512:1024], in_=tb[64:128, 0:512], func=AF.Copy), reads=[PB[0]], writes=[R_qT])
                S.op("dve", lambda e: e.tensor_tensor(out=qxT, in0=tb[:, 0:512], in1=xi_t, op=ALU.mult), reads=[PB[0], R_xi], writes=[R_qxT])
                S.op("act", lambda e: e.activation(out=kT, in_=tb[:, 512:1024], func=AF.Copy), reads=[PB[0]], writes=[R_kT])
                for h in range(8):
                    p_, a_ = divmod(h, 2)
                    rows = slice(a_ * 64, a_ * 64 + 64)
                    S.op("pe", lambda e, h=h, p_=p_, rows=rows: e.matmul(ps[:, 3072 + h * 128:3072 + (h + 1) * 128],
                                                                         lhsT=kT[:, p_ * 128:(p_ + 1) * 128],
                                                                         rhs=qT[:, (h % 2) * 512 + p_ * 128:(h % 2) * 512 + (p_ + 1) * 128],
                                                                         start=True, stop=True),
                         reads=[R_kT, R_qT], writes=[PB[6], PB[7]], inc=(h == 7))
                S.op("dve", lambda e: e.tensor_tensor(out=sd, in0=ps[:, 3072:4096], in1=dt_t, op=ALU.mult),
                     reads=[PB[6], PB[7], R_dt], writes=[R_sd])
                for h in range(8):
                    p_, a_ = divmod(h, 2)
                    rows = slice(a_ * 64, a_ * 64 + 64)
                    S.op("pe", lambda e, h=h: e.matmul(ps[:, 512 + h * 64:512 + (h + 1) * 64], lhsT=sd[:, h * 128:(h + 1) * 128],
                                                       rhs=vb[:, h * 64:(h + 1) * 64], start=True, stop=False),
                         reads=[R_sd, R_vb, R_qkr], writes=[PB[1]], inc=False)
                    S.op("pe", lambda e, h=h, p_=p_, rows=rows: e.matmul(ps[:, 512 + h * 64:512 + (h + 1) * 64],
                                                                         lhsT=qxT[:, p_ * 128:(p_ + 1) * 128],
                                                                         rhs=Rb[:, h * 64:(h + 1) * 64],
                                                                         start=False, stop=True),
                         reads=[R_qxT, R_Rb], writes=[PB[1]], inc=(h == 7))
            for p_ in range(4):
                S.op("pe", lambda e, p_=p_: e.matmul(ps[:, 3072 + p_ * 128:3072 + (p_ + 1) * 128], lhsT=kz[:, p_ * 128:(p_ + 1) * 128],
                                                     rhs=vb[:, p_ * 128:(p_ + 1) * 128], start=True, stop=True),
                     reads=[R_kz, R_vb, R_sd], writes=[PB[6]], inc=(p_ == 3))
            for h in range(8):
                p_, a_ = divmod(h, 2)
                rows = slice(a_ * 64, a_ * 64 + 64)
                S.op("dve", lambda e, h=h, p_=p_, a_=a_, rows=rows: e.scalar_tensor_tensor(
                    out=R32[rows, h * 64:(h + 1) * 64], in0=R32[rows, h * 64:(h + 1) * 64], scalar=dec_t[rows, p_:p_ + 1],
                    in1=ps[rows, 3072 + p_ * 128 + a_ * 64:3072 + p_ * 128 + a_ * 64 + 64], op0=ALU.mult, op1=ALU.add),
                    reads=[PB[6], R_dec, R_R32], writes=[R_R32])
            S.op("act", lambda e: e.activation(out=Rb, in_=R32, func=AF.Copy), reads=[R_R32], writes=[R_Rb])
            if KSUB == 4 and n >= HALO:
                return


            if full:
                o3 = bank(1).rearrange("p (h d) -> p h d", h=8)
                s1, s2, mean, msq, var = (mt1[:, 0:8], mt1[:, 8:16], mt1[:, 16:24], mt1[:, 24:32], mt1[:, 32:40])
                S.op("dve", lambda e: e.tensor_reduce(out=s1, in_=o3, axis=AX.X, op=ALU.add), reads=[PB[1]], writes=[R_mt])
                S.op("act", lambda e: e.activation(out=gn1, in_=bank(1), func=AF.Square), reads=[PB[1]], writes=[R_gn1])
                S.op("dve", lambda e: e.tensor_reduce(out=s2, in_=gn1.rearrange("p (h d) -> p h d", h=8), axis=AX.X, op=ALU.add),
                     reads=[R_gn1], writes=[R_mt])
                S.op("dve", lambda e: e.tensor_scalar(out=mean, in0=s1, scalar1=1.0 / 64, scalar2=None, op0=ALU.mult), reads=[R_mt], writes=[R_mt])
                S.op("dve", lambda e: e.tensor_tensor(out=msq, in0=mean, in1=mean, op=ALU.mult), reads=[R_mt], writes=[R_mt])
                S.op("dve", lambda e: e.scalar_tensor_tensor(out=var, in0=s2, scalar=1.0 / 64, in1=msq, op0=ALU.mult, op1=ALU.subtract),
                     reads=[R_mt], writes=[R_mt])
                S.op("dve", lambda e: e.tensor_scalar(out=var, in0=var, scalar1=EPS, scalar2=None, op0=ALU.add), reads=[R_mt], writes=[R_mt])
                S.op("act", lambda e: e.activation(out=var, in_=var, func=AF.Sqrt), reads=[R_mt], writes=[R_mt])
                S.op("dve", lambda e: e.reciprocal(out=var, in_=var), reads=[R_mt], writes=[R_mt])
                g13 = gn1.rearrange("p (h d) -> p h d", h=8)
                S.op("dve", lambda e: e.tensor_tensor(out=g13, in0=o3, in1=bc_last(mean, 64), op=ALU.subtract),
                     reads=[PB[1], R_mt], writes=[R_gn1])
                S.op("dve", lambda e: e.tensor_tensor(out=g13, in0=g13, in1=bc_last(var, 64), op=ALU.mult), reads=[R_gn1, R_mt], writes=[R_gn1])
                S.op("dve", lambda e: e.tensor_tensor(out=gn2, in0=sg, in1=gnw_t, op=ALU.mult), reads=[R_sg, R_gnw], writes=[R_gn2])
                S.op("dve", lambda e: e.tensor_tensor(out=yb, in0=gn1, in1=gn2, op=ALU.mult), reads=[R_gn1, R_gn2], writes=[R_yb])
                for c in range(4):
                    S.op("pe", lambda e, c=c: e.transpose(out=tb[:, c * 128:(c + 1) * 128], in_=yb[:, c * 128:(c + 1) * 128], identity=ident),
                         reads=[R_yb, R_ident], writes=[PB[0]], inc=(c == 3))
                m = n - HALO
                S.op("act", lambda e, m=m: e.activation(out=mixT_r[:, :, m * 128:(m + 1) * 128], in_=tb[:, 0:512].rearrange("p (c t) -> p c t", c=4), func=AF.Copy),
                     reads=[PB[0]], writes=[R_mixr[m]])

            lat = bank(5)
            ckv_ap = lat[:, 256 + latoff:384 + latoff]
            kpe_ap = lat[:, 384 + latoff:416 + latoff]
            ssq = st_small[:, 4:6]
            rq = st_small[:, 6:8]
            if full:
                S.op("act", lambda e: e.activation(out=junk[:, 0:256], in_=lat[:, 0:256], func=AF.Square, accum_out=ssq[:, 0:1]),
                     reads=[PB[5]], writes=[R_junk, R_ss])
            S.op("act", lambda e: e.activation(out=junk[:, 256:384], in_=ckv_ap, func=AF.Square, accum_out=ssq[:, 1:2]),
                 reads=[PB[5]], writes=[R_junk, R_ss])
            if full:
                S.op("dve", lambda e: e.tensor_scalar(out=ssq[:, 0:1], in0=ssq[:, 0:1], scalar1=1.0 / 256, scalar2=EPS, op0=ALU.mult, op1=ALU.add),
                     reads=[R_ss], writes=[R_ss])
            S.op("dve", lambda e: e.tensor_scalar(out=ssq[:, 1:2], in0=ssq[:, 1:2], scalar1=1.0 / 128, scalar2=EPS, op0=ALU.mult, op1=ALU.add),
                 reads=[R_ss], writes=[R_ss])
            lo = 0 if full else 1
            S.op("act", lambda e, lo=lo: e.activation(out=ssq[:, lo:2], in_=ssq[:, lo:2], func=AF.Sqrt), reads=[R_ss], writes=[R_ss])
            S.op("dve", lambda e, lo=lo: e.reciprocal(out=rq[:, lo:2], in_=ssq[:, lo:2]), reads=[R_ss], writes=[R_ss])
            if full:
                S.op("dve", lambda e: e.scalar_tensor_tensor(out=cqn, in0=lat[:, 0:256], scalar=rq[:, 0:1], in1=qnw_t, op0=ALU.mult, op1=ALU.mult),
                     reads=[PB[5], R_ss, R_qnw], writes=[R_cqn])
            S.op("dve", lambda e: e.scalar_tensor_tensor(out=ckvn, in0=ckv_ap, scalar=rq[:, 1:2], in1=kvnw_t, op0=ALU.mult, op1=ALU.mult),
                 reads=[PB[5], R_ss, R_kvnw], writes=[R_ckvn])
            rope(kpe_ap, 1, 32, cs_m, ss_m, tmpA[:, 0:32], tmpB[:, 0:32], kr, [PB[5]], R_kr)
            if KSUB == 5 and n >= HALO:
                return

            S.op("pe", lambda e: e.transpose(out=tb[:, 0:128], in_=ckvn, identity=ident), reads=[R_ckvn, R_ident], writes=[PB[0]], inc=False)
            S.op("pe", lambda e: e.transpose(out=tb[0:32, 128:256], in_=kr, identity=ident), reads=[R_kr, R_ident], writes=[PB[0]], inc=not full)
            if full:
                for c in range(2):
                    S.op("pe", lambda e, c=c: e.transpose(out=tb[:, 256 + c * 128:256 + (c + 1) * 128], in_=cqn[:, c * 128:(c + 1) * 128], identity=ident),
                         reads=[R_cqn, R_ident], writes=[PB[0]], inc=(c == 1))
            S.op("act", lambda e: e.activation(out=ckvnT, in_=tb[:, 0:128], func=AF.Copy), reads=[PB[0]], writes=[R_ckvnT])
            S.op("act", lambda e, gb=gb, gi=gi: e.activation(out=kpe_g[gb][0:32, gi * 128:(gi + 1) * 128], in_=tb[0:32, 128:256], func=AF.Copy),
                 reads=[PB[0]], writes=[R_kpg[gb]])
            if KSUB == 6 and n >= HALO:
                return

            if full:
                S.op("act", lambda e: e.activation(out=cqnT, in_=tb[:, 256:512], func=AF.Copy), reads=[PB[0]], writes=[R_cqnT])
            for p_ in range(4):
                S.op("pe", lambda e, p_=p_: e.matmul(ps[:, 3584 + p_ * 128:3584 + (p_ + 1) * 128], lhsT=wk[:, p_ * 128:(p_ + 1) * 128], rhs=ckvnT,
                                                     start=True, stop=True), reads=[R_wk, R_ckvnT, R_sd], writes=[PB[7]], inc=(p_ == 3))
            kng3 = kn_g[gb].rearrange("p (a k) -> p a k", a=4)
            S.op("act", lambda e, gi=gi, kng3=kng3: e.activation(out=kng3[:, :, gi * 128:(gi + 1) * 128], in_=bank(7).rearrange("p (a k) -> p a k", a=4), func=AF.Copy),
                 reads=[PB[7]], writes=[R_kng[gb]])
            S.op("pe", lambda e: e.matmul(bank(4), lhsT=ckvnT, rhs=wv, start=True, stop=True), reads=[R_wv, R_ckvnT, R_sg], writes=[PB[4]])
            vg4 = v_g[gb].rearrange("p (h t e) -> p h t e", h=8, t=4)
            S.op("dve", lambda e, gi=gi, vg4=vg4: e.tensor_copy(out=vg4[:, :, gi, 0:64], in_=bank(4).rearrange("p (h e) -> p h e", h=8)),
                 reads=[PB[4]], writes=[R_vg[gb]])
            S.op("dve", lambda e, gi=gi, vg4=vg4, n=n: e.tensor_copy(out=vg4[:, :, gi, 64:65], in_=bc_mid(valid_t[:, n:n + 1], 8)),
                 reads=[R_valid], writes=[R_vg[gb]])
            if KSUB == 7 and n >= HALO:
                return

            if full:
                for hf in range(2):
                    for c in range(2):
                        S.op("pe", lambda e, hf=hf, c=c: e.matmul(ps[:, (2 + hf) * 512:(2 + hf) * 512 + 384], lhsT=cqnT[:, c * 128:(c + 1) * 128],
                                                                  rhs=w_uq3[:, c, hf * 384:(hf + 1) * 384], start=(c == 0), stop=(c == 1)),
                             reads=[R_cqnT, R_wuq, R_vb, R_kz, R_qkr], writes=[PB[2 + hf]], inc=(c == 1))
                qb3 = qb.rearrange("p (h d) -> p h d", h=8)
                for hf in range(2):
                    src = ps[:, (2 + hf) * 512:(2 + hf) * 512 + 384]
                    s3 = src.rearrange("p (h d) -> p h d", h=4)
                    S.op("act", lambda e, hf=hf, s3=s3: e.activation(out=qb3[:, hf * 4:(hf + 1) * 4, 0:64], in_=s3[:, :, 0:64], func=AF.Copy),
                         reads=[PB[2 + hf]], writes=[R_qb])
                    x3 = s3[:, :, 64:96]
                    sw = bass.AP(src.tensor, src.offset + 64 + 16, [list(src.ap[0]), [96, 4], [-16, 2], [1, 16]])
                    a3 = tmpA[:, 0:128].rearrange("p (h d) -> p h d", h=4)
                    b4 = tmpB[:, 0:128].rearrange("p (h a d) -> p h a d", h=4, a=2)
                    S.op("dve", lambda e, x3=x3, a3=a3: e.tensor_tensor(out=a3, in0=x3, in1=bc_mid(cs_m, 4), op=ALU.mult),
                         reads=[PB[2 + hf], R_tab], writes=[R_tA])
                    S.op("dve", lambda e, sw=sw, b4=b4: e.tensor_tensor(out=b4, in0=sw, in1=bc_mid(ss_m.rearrange("p (a d) -> p a d", a=2), 4), op=ALU.mult),
                         reads=[PB[2 + hf], R_tab], writes=[R_tB])
                    S.op("dve", lambda e, hf=hf, a3=a3: e.tensor_tensor(out=qb3[:, hf * 4:(hf + 1) * 4, 64:96], in0=a3,
                                                                        in1=tmpB[:, 0:128].rearrange("p (h d) -> p h d", h=4), op=ALU.add),
                         reads=[R_tA, R_tB], writes=[R_qb])
                for h in range(8):
                    S.op("pe", lambda e, h=h: e.transpose(out=tb[0:96, h * 128:(h + 1) * 128], in_=qb[:, h * 96:(h + 1) * 96], identity=ident),
                         reads=[R_qb, R_ident], writes=[PB[0]], inc=(h == 7))
                mg, mi = divmod(n - HALO, 4)
                qg3 = q_g[mg % 2].rearrange("p (h t) -> p h t", h=8)
                S.op("act", lambda e, mi=mi, qg3=qg3: e.activation(out=qg3[0:96, :, mi * 128:(mi + 1) * 128],
                                                                   in_=tb[0:96, :].rearrange("p (h t) -> p h t", h=8), func=AF.Copy),
                     reads=[PB[0]], writes=[R_qg[mg % 2]])
                if mi == 3 or n == NT - 1:
                    ntl = mi + 1
                    S.dma("sp", "sq%d" % (mg % 2),
                          [lambda e, mg=mg, ntl=ntl, qg3=qg3: e.dma_start(
                              out=s_qt[:, :, mg * 512:mg * 512 + ntl * 128].rearrange("h r t -> r h t"),
                              in_=qg3[0:96, :, 0:ntl * 128])],
                          reads=[R_qg[mg % 2]], writes=[R_sqt])
            if gi == 3:
                fns = []
                for a_ in range(2):
                    fns.append(lambda e, a_=a_, g=g, kng3=kng3: e.dma_start(
                        out=s_kt[:, 0:64, g * 512:(g + 1) * 512].rearrange("(p a) r k -> a r p k", a=2)[a_],
                        in_=kng3[a_ * 64:(a_ + 1) * 64, :, :]))
                fns.append(lambda e, g=g, gb=gb: e.dma_start(
                    out=s_kt[:, 64:96, g * 512:(g + 1) * 512].rearrange("h r k -> r h k"),
                    in_=bc_mid(kpe_g[gb][0:32, :], 8)))
                S.dma("sp", "sk%d" % gb, fns, reads=[R_kng[gb], R_kpg[gb]], writes=[R_skt])
                S.dma("sp", "sv%d" % gb,
                      [lambda e, g=g, vg4=vg4: e.dma_start(
                          out=s_v[:, :, g * 4 * 65:(g + 1) * 4 * 65].rearrange("h p (t e) -> p h t e", t=4),
                          in_=vg4)],
                      reads=[R_vg[gb]], writes=[R_sv])

        for n in range(int(os.environ.get('KNT', NT))):
            _tileA(n)
        while pk[1] < len(pieces):
            cast_step(2)

        S.barrier()
        if KSTOP == 'A':
            S.emit(); return nc
        A.off = mark_persist
        mixT_m = A.bf(4 * NOWN * 128).rearrange("p (c t) -> p c t", c=4)
        mark_persist = A.off
        ytmp = [A.bf(512) for _ in range(2)]
        R_ytmp = [Res("ytmp0"), Res("ytmp1")]
        QT = [A.bf(NOWN * 128) for _ in range(2)]
        KT = [A.bf(NT * 128) for _ in range(2)]
        VV = [A.bf(NT * 65) for _ in range(2)]
        R_Q, R_K, R_V = [Res("Q0"), Res("Q1")], [Res("K0"), Res("K1")], [Res("V0"), Res("V1")]
        PT = [A.bf(1024) for _ in range(3)]
        R_PT = [Res("PT%d" % i) for i in range(3)]
        rrow = A.f32(512)
        R_rrow = Res("rrow")
        ones_t = A.f32(64)
        R_ones = Res("ones")
        bcs = A.f32(512)
        R_bcs = Res("bcs")
        S.op("dve", lambda e: e.memset(ones_t, 1.0), writes=[R_ones])
        scale = (64 + 32) ** -0.5

        def load_head(h):
            i = h % 2
            S.dma("sp", "lq%d" % i, [lambda e: e.dma_start(out=QT[i][0:96, :], in_=s_qt[h])], reads=[R_sqt], writes=[R_Q[i]])
            S.dma("sp", "lk%d" % i, [lambda e: e.dma_start(out=KT[i][0:96, :], in_=s_kt[h])], reads=[R_skt], writes=[R_K[i]])
            S.dma("sp", "lv%d" % i, [lambda e: e.dma_start(out=VV[i], in_=s_v[h])], reads=[R_sv], writes=[R_V[i]])

        load_head(0)

        groups = []
        blk_id = 0
        for h in range(8):
            qblocks = [(0, 128, [(kt, 0) for kt in range(HALO)] + [(HALO, 0)], HALO)]
            for j in range(8):
                kts = [(kt, 0) for kt in range(32 + 4 * j)] + [(32 + 4 * j + m, 128 * m) for m in range(4)]
                qblocks.append((128 + 512 * j, 512, kts, 32 + 4 * j))
            for (q0, qw, kts, diag0) in qblocks:
                npairs = (len(kts) + 1) // 2
                for gidx in range(npairs):
                    groups.append(dict(h=h, i=h % 2, q0=q0, qw=qw, pair=kts[2 * gidx:2 * gidx + 2], diag0=diag0,
                                       ob=4 + blk_id % 2, first=(gidx == 0), last=(gidx == npairs - 1),
                                       sb=(len(groups) % 2) * 2, pt=len(groups) % 3, yi=blk_id % 2,
                                       newhead=(gidx == 0 and q0 == 0)))
                blk_id += 1

        def emit_qk(g):
            i, q0, qw, sb_ = g["i"], g["q0"], g["qw"], g["sb"]
            if g["newhead"] and g["h"] + 1 < 8:
                load_head(g["h"] + 1)
            for u, (kt, c0) in enumerate(g["pair"]):
                dst = ps[:, (sb_ + u) * 512 + c0:(sb_ + u) * 512 + qw]
                isdiag = kt >= g["diag0"]
                S.op("pe", lambda e, kt=kt, c0=c0, dst=dst, isdiag=isdiag: e.matmul(
                    dst, lhsT=KT[i][0:96, kt * 128:(kt + 1) * 128], rhs=QT[i][0:96, q0 + c0:q0 + qw], start=True, stop=not isdiag),
                    reads=[R_K[i], R_Q[i]], writes=[PB[sb_ + u]], inc=not isdiag)
                if isdiag:
                    S.op("pe", lambda e, dst=dst: e.matmul(dst[:, 0:128], lhsT=ident, rhs=maskb, start=False, stop=True),
                         reads=[R_ident, R_mask], writes=[PB[sb_ + u]])

        def emit_exp(g):
            qw, sb_, pair = g["qw"], g["sb"], g["pair"]
            pt, Rpt = PT[g["pt"]], R_PT[g["pt"]]
            if len(pair) == 2 and pair[0][1] == 0 and pair[1][1] == 0 and qw == 512:
                S.op("act", lambda e: e.activation(out=pt, in_=ps[:, sb_ * 512:sb_ * 512 + 1024], func=AF.Exp, scale=scale),
                     reads=[PB[sb_], PB[sb_ + 1]], writes=[Rpt])
            else:
                for u, (kt, c0) in enumerate(pair):
                    S.op("act", lambda e, u=u, c0=c0: e.activation(
                        out=pt[:, u * 512 + c0:u * 512 + qw], in_=ps[:, (sb_ + u) * 512 + c0:(sb_ + u) * 512 + qw], func=AF.Exp, scale=scale),
                        reads=[PB[sb_ + u]], writes=[Rpt])

        def emit_pv(g):
            i, qw, ob, pair = g["i"], g["qw"], g["ob"], g["pair"]
            pt, Rpt = PT[g["pt"]], R_PT[g["pt"]]
            V3 = VV[i].rearrange("p (t e) -> p t e", t=NT)
            for u, (kt, c0) in enumerate(pair):
                first = g["first"] and u == 0
                last = g["last"] and u == len(pair) - 1
                S.op("pe", lambda e, kt=kt, c0=c0, u=u, first=first, last=last: e.matmul(
                    ps[0:65, ob * 512 + c0:ob * 512 + qw], lhsT=V3[:, kt, :], rhs=pt[:, u * 512 + c0:u * 512 + qw], start=first, stop=last),
                    reads=[R_V[i], Rpt], writes=[PB[ob]], inc=(u == len(pair) - 1))

        def emit_norm(g):
            h, q0, qw, ob, yi_ = g["h"], g["q0"], g["qw"], g["ob"], g["yi"]
            S.op("dve", lambda e: e.tensor_scalar(out=rrow[64:65, 0:qw], in0=ps[64:65, ob * 512:ob * 512 + qw], scalar1=1e-30, scalar2=None, op0=ALU.max),
                 reads=[PB[ob]], writes=[R_rrow])
            S.op("dve", lambda e: e.reciprocal(out=rrow[64:65, 0:qw], in_=rrow[64:65, 0:qw]), reads=[R_rrow], writes=[R_rrow])
            S.op("pe", lambda e: e.matmul(ps[0:64, 6 * 512:6 * 512 + qw], lhsT=ones_t[64:65, 0:64], rhs=rrow[64:65, 0:qw], start=True, stop=True),
                 reads=[R_ones, R_rrow], writes=[PB[6]])
            S.op("dve", lambda e: e.tensor_copy(out=bcs[0:64, 0:qw], in_=ps[0:64, 6 * 512:6 * 512 + qw]), reads=[PB[6]], writes=[R_bcs])
            S.op("dve", lambda e: e.tensor_tensor(out=ytmp[yi_][0:64, 0:qw], in0=ps[0:64, ob * 512:ob * 512 + qw], in1=bcs[0:64, 0:qw], op=ALU.mult),
                 reads=[PB[ob], R_bcs], writes=[R_ytmp[yi_]])
            S.dma("sp", "ym%d" % yi_, [lambda e: e.dma_start(
                out=mixT_m[(h % 2) * 64:(h % 2) * 64 + 64, h // 2, q0:q0 + qw], in_=ytmp[yi_][0:64, 0:qw])],
                reads=[R_ytmp[yi_]], writes=[R_mixm[h]])

        pend_norm = None
        emit_qk(groups[0])
        for gi_, g in enumerate(groups):
            emit_exp(g)
            if gi_ + 1 < len(groups):
                emit_qk(groups[gi_ + 1])
            emit_pv(g)
            if pend_norm is not None:
                emit_norm(pend_norm)
                pend_norm = None
            if g["last"]:
                pend_norm = g
        if pend_norm is not None:
            emit_norm(pend_norm)

        S.barrier()
        if KSTOP == 'AB':
            S.emit(); return nc
        A.off = mark_persist
        wout = A.bf(8 * 1024)
        wout3 = wout.rearrange("p (c f) -> p c f", c=8)
        wdown = A.bf(22 * 1024)
        wdown3 = wdown.rearrange("p (c f) -> p c f", c=22)
        R_wout, R_wdown = Res("wout"), Res("wdown")
        S.dma("sp", "wc1", [lambda e: e.dma_start(out=wout, in_=s_wout)], reads=[R_scr["s_wout"]], writes=[R_wout])
        S.dma("sp", "wc2", [lambda e: e.dma_start(out=wdown, in_=s_wdown)], reads=[R_scr["s_wdown"]], writes=[R_wdown])
        fnw_t, R_fnw = load_const(b_fnw, 1024, "fnw")
        onw_t, R_onw = load_const(b_onw, 1024, "onw")
        cw_t, R_cw = load_const(c_cw, 132, "cw")
        cw3 = cw_t.rearrange("p (c j) -> p c j", c=44)
        cb_t, R_cb = load_const(c_cb, 44, "cb")
        wupb = [A.bf(1024) for _ in range(2)]
        R_wupb = [Res("wup%d" % i) for i in range(2)]
        xb2 = [A.f32(1024) for _ in range(2)]
        R_xb2 = [Res("xb2_0"), Res("xb2_1")]
        x1 = A.f32(4 * 1024)
        x13 = x1.rearrange("p (t d) -> p t d", t=4)
        R_x1 = [Res("x1_%d" % i) for i in range(4)]
        h2b = A.bf(1024)
        R_h2b = Res("h2b")
        h2T = A.bf(8 * 512)
        h2T3 = h2T.rearrange("p (c t) -> p c t", c=8)
        R_h2T = Res("h2T")
        gT = A.bf(22 * 512)
        gT3 = gT.rearrange("p (c t) -> p c t", c=22)
        R_gT = Res("gT")
        ubuf = [A.f32(514) for _ in range(2)]
        R_ub = [Res("ub0"), Res("ub1")]
        acc = [A.f32(512) for _ in range(2)]
        R_acc = [Res("acc0"), Res("acc1")]
        carry = A.f32(44 * 2)
        carry3 = carry.rearrange("p (c j) -> p c j", c=44)
        R_carry = Res("carry")
        st2 = A.f32(16)
        R_st2 = Res("st2")
        ybuf = xb2
        R_yb2 = R_xb2
        S.op("dve", lambda e: e.memset(carry, 0.0), writes=[R_carry])
        wupi = [0]
        xli = [0]
        ybi = [0]
        y_tiles = yout.rearrange("(n p) d -> n p d", p=128)

        blocks = [(0, 1)] + [(1 + 4 * j, 4) for j in range(8)]
        def _blockC(bi, m0, ntl):
            W = ntl * 128
            tb = bank(7).bitcast(BF16)
            for t in range(ntl):
                m = m0 + t
                xi_ = xli[0] % 2
                xli[0] += 1
                S.dma("sp", "xc%d" % xi_, [lambda e, m=m, xi_=xi_: e.dma_start(out=xb2[xi_], in_=x_tiles[HALO + m])], writes=[R_xb2[xi_]])
                for hf in range(2):
                    for c in range(8):
                        src_ = mixT_r[:, c, m * 128:(m + 1) * 128] if c < 4 else mixT_m[:, c - 4, m * 128:(m + 1) * 128]
                        S.op("pe", lambda e, c=c, hf=hf, src_=src_: e.matmul(bank(hf), lhsT=src_, rhs=wout3[:, c, hf * 512:(hf + 1) * 512],
                                                                            start=(c == 0), stop=(c == 7)),
                             reads=[R_mixr[m], R_wout] + R_mixm, writes=[PB[hf]], inc=(c == 7))
                S.op("dve", lambda e, t=t, xi_=xi_: e.tensor_tensor(out=x13[:, t, :], in0=ps[:, 0:1024], in1=xb2[xi_], op=ALU.add),
                     reads=[PB[0], PB[1], R_xb2[xi_]], writes=[R_x1[t]])
                ss = st2[:, 0:1]
                rstd = st2[:, 1:2]
                S.op("act", lambda e, t=t: e.activation(out=h2b, in_=x13[:, t, :], func=AF.Square, accum_out=ss), reads=[R_x1[t]], writes=[R_h2b, R_st2])
                S.op("dve", lambda e: e.tensor_scalar(out=ss, in0=ss, scalar1=1.0 / 1024, scalar2=EPS, op0=ALU.mult, op1=ALU.add), reads=[R_st2], writes=[R_st2])
                S.op("act", lambda e: e.activation(out=ss, in_=ss, func=AF.Sqrt), reads=[R_st2], writes=[R_st2])
                S.op("dve", lambda e: e.reciprocal(out=rstd, in_=ss), reads=[R_st2], writes=[R_st2])
                S.op("dve", lambda e, t=t: e.scalar_tensor_tensor(out=h2b, in0=x13[:, t, :], scalar=rstd, in1=fnw_t, op0=ALU.mult, op1=ALU.mult),
                     reads=[R_x1[t], R_st2, R_fnw], writes=[R_h2b])
                for c in range(8):
                    S.op("pe", lambda e, c=c: e.transpose(out=tb[:, c * 128:(c + 1) * 128], in_=h2b[:, c * 128:(c + 1) * 128], identity=ident),
                         reads=[R_h2b, R_ident], writes=[PB[7]], inc=(c == 7))
                S.op("act", lambda e, t=t: e.activation(out=h2T3[:, :, t * 128:(t + 1) * 128], in_=tb.rearrange("p (c t) -> p c t", c=8), func=AF.Copy),
                     reads=[PB[7]], writes=[R_h2T])
            for fc in range(22):
                for half in range(2):
                    cidx = fc + 22 * half
                    wi = wupi[0] % 2
                    wupi[0] += 1
                    S.dma("sp", "wu%d" % wi, [lambda e, cidx=cidx, wi=wi: e.dma_start(out=wupb[wi], in_=s_wup[:, cidx * 1024:(cidx + 1) * 1024])],
                          reads=[R_scr["s_wup"]], writes=[R_wupb[wi]])
                    bk = 2 + half + 2 * (fc % 2)
                    w3 = wupb[wi].rearrange("p (c f) -> p c f", c=8)
                    for c in range(8):
                        S.op("pe", lambda e, c=c, bk=bk, w3=w3: e.matmul(bank(bk, W), lhsT=w3[:, c, :], rhs=h2T3[:, c, 0:W], start=(c == 0), stop=(c == 7)),
                             reads=[R_wupb[wi], R_h2T], writes=[PB[bk]], inc=(c == 7))
                    ub, Rub = ubuf[half], R_ub[half]
                    ac, Rac = acc[half], R_acc[half]
                    S.op("act", lambda e, ub=ub, cidx=cidx: e.activation(out=ub[:, 0:2], in_=carry3[:, cidx, :], func=AF.Copy), reads=[R_carry], writes=[Rub])
                    S.op("act", lambda e, ub=ub, bk=bk: e.activation(out=ub[:, 2:2 + W], in_=bank(bk, W), func=AF.Copy), reads=[PB[bk]], writes=[Rub])
                    S.op("act", lambda e, ub=ub, cidx=cidx: e.activation(out=carry3[:, cidx, :], in_=ub[:, W:W + 2], func=AF.Copy), reads=[Rub], writes=[R_carry])
                    if bi == 0:
                        continue
                    S.op("act", lambda e, ac=ac, bk=bk, cidx=cidx: e.activation(out=ac[:, 0:W], in_=bank(bk, W), func=AF.Identity,
                                                                               scale=cw3[:, cidx, 2:3], bias=cb_t[:, cidx:cidx + 1]),
                         reads=[PB[bk], R_cw, R_cb], writes=[Rac])
                    S.op("dve", lambda e, ac=ac, ub=ub, cidx=cidx: e.scalar_tensor_tensor(out=ac[:, 0:W], in0=ub[:, 1:1 + W], scalar=cw3[:, cidx, 1:2], in1=ac[:, 0:W],
                                                                                         op0=ALU.mult, op1=ALU.add), reads=[Rub, Rac, R_cw], writes=[Rac])
                    S.op("dve", lambda e, ac=ac, ub=ub, cidx=cidx: e.scalar_tensor_tensor(out=ac[:, 0:W], in0=ub[:, 0:W], scalar=cw3[:, cidx, 0:1], in1=ac[:, 0:W],
                                                                                         op0=ALU.mult, op1=ALU.add), reads=[Rub, Rac, R_cw], writes=[Rac])
                if bi == 0:
                    continue
                S.op("act", lambda e: e.activation(out=acc[0][:, 0:W], in_=acc[0][:, 0:W], func=AF.Silu), reads=[R_acc[0]], writes=[R_acc[0]])
                S.op("dve", lambda e, fc=fc: e.tensor_tensor(out=gT3[:, fc, 0:W], in0=acc[0][:, 0:W], in1=acc[1][:, 0:W], op=ALU.mult),
                     reads=[R_acc[0], R_acc[1]], writes=[R_gT])
            if bi == 0:
                return
            for t in range(ntl):
                m = m0 + t
                for hf in range(2):
                    for fc in range(22):
                        S.op("pe", lambda e, fc=fc, hf=hf, t=t: e.matmul(bank(hf), lhsT=gT3[:, fc, t * 128:(t + 1) * 128],
                                                                        rhs=wdown3[:, fc, hf * 512:(hf + 1) * 512], start=(fc == 0), stop=(fc == 21)),
                             reads=[R_gT, R_wdown], writes=[PB[hf]], inc=(fc == 21))
                yi = ybi[0] % 2
                ybi[0] += 1
                yb_, Ryb = ybuf[yi], R_yb2[yi]
                S.op("dve", lambda e, t=t: e.tensor_tensor(out=x13[:, t, :], in0=ps[:, 0:1024], in1=x13[:, t, :], op=ALU.add),
                     reads=[PB[0], PB[1], R_x1[t]], writes=[R_x1[t]])
                ss2 = st2[:, 2:3]
                rstd2 = st2[:, 3:4]
                S.op("act", lambda e, t=t, yb_=yb_: e.activation(out=h2b, in_=x13[:, t, :], func=AF.Square, accum_out=ss2), reads=[R_x1[t]], writes=[R_h2b, R_st2])
                S.op("dve", lambda e: e.tensor_scalar(out=ss2, in0=ss2, scalar1=1.0 / 1024, scalar2=EPS, op0=ALU.mult, op1=ALU.add), reads=[R_st2], writes=[R_st2])
                S.op("act", lambda e: e.activation(out=ss2, in_=ss2, func=AF.Sqrt), reads=[R_st2], writes=[R_st2])
                S.op("dve", lambda e: e.reciprocal(out=rstd2, in_=ss2), reads=[R_st2], writes=[R_st2])
                S.op("dve", lambda e, t=t, yb_=yb_: e.scalar_tensor_tensor(out=yb_, in0=x13[:, t, :], scalar=rstd2, in1=onw_t, op0=ALU.mult, op1=ALU.mult),
                     reads=[R_x1[t], R_st2, R_onw], writes=[Ryb])
                S.dma("sp", "yo%d" % yi, [lambda e, m=m, yb_=yb_: e.dma_start(out=y_tiles[m - 1], in_=yb_)], reads=[Ryb])

        for bi, (m0, ntl) in enumerate(blocks):
            _blockC(bi, m0, ntl)

        S.barrier()
        S.emit()
    return nc


def _consts():
    H, C = 8, 128
    lg = np.log1p(-np.power(2.0, -5.0 - np.arange(H, dtype=np.float64)))
    idx = np.arange(C, dtype=np.float64)
    diff = idx[None, :] - idx[:, None]
    dt = np.where(diff[:, None, :] >= 0, np.exp(lg[None, :, None] * np.maximum(diff[:, None, :], 0.0)), 0.0) / 8.0
    c_dt = dt.reshape(128, 1024).astype(np.float32)
    xi = np.exp(lg[:, None] * (idx[None, :] + 1.0))
    c_xi = np.zeros((128, 4, 128))
    for p in range(4):
        for a in range(2):
            c_xi[a * 64:(a + 1) * 64, p, :] = xi[2 * p + a][None, :]
    c_xi = c_xi.reshape(128, 512).astype(np.float32)
    zeta = np.exp(lg[:, None] * (C - 1.0 - idx[None, :])) / 8.0
    c_zeta = np.repeat(zeta.T[:, :, None], 64, axis=2).reshape(128, 512).astype(np.float32)
    dec = np.exp(lg * C)
    c_dec = np.zeros((128, 4))
    for p in range(4):
        c_dec[0:64, p] = dec[2 * p]
        c_dec[64:128, p] = dec[2 * p + 1]
    c_dec = c_dec.astype(np.float32)
    fr = (10000.0 ** (-np.arange(0, 64, 2, dtype=np.float32) / np.float32(64))).astype(np.float32)
    fm = (10000.0 ** (-np.arange(0, 32, 2, dtype=np.float32) / np.float32(32))).astype(np.float32)
    invf = np.concatenate([fr, fr, fr, fr, fm, fm, fm, fm]).astype(np.float64) / (2 * np.pi)
    off = np.concatenate([np.full(64, 0.25), np.full(32, 0.5), np.zeros(32), np.full(32, 0.25), np.full(16, 0.5), np.zeros(16)])
    c_invf = np.broadcast_to(invf[None, :], (128, 192)).astype(np.float32).copy()
    c_off = np.broadcast_to(off[None, :], (128, 192)).astype(np.float32).copy()
    k = np.arange(128)
    c_mask = np.where(k[None, :] < k[:, None], -30000.0, 0.0).astype(np.float32)
    return dict(c_dt=c_dt, c_xi=c_xi, c_zeta=c_zeta, c_dec=c_dec, c_invf=c_invf, c_off=c_off, c_mask=c_mask)


def _bc(v, n=128):
    return np.ascontiguousarray(np.broadcast_to(np.asarray(v, np.float32)[None, :], (n, v.shape[0])))


_PROG = None


def kernel(x, positions, attn_norm_w, w_in, ret_gn_w, mla_q_norm_w, w_uq, mla_kv_norm_w, w_ukv,
           w_out, ffn_norm_w, w_up, conv_w, conv_b, w_down, final_norm_w):
    global _PROG
    x = np.asarray(x, np.float32)
    positions = np.asarray(positions, np.int32)
    shared = _consts()
    shared["b_anw"] = _bc(np.asarray(attn_norm_w)[0])
    shared["b_fnw"] = _bc(np.asarray(ffn_norm_w)[0])
    shared["b_onw"] = _bc(np.asarray(final_norm_w))
    shared["b_qnw"] = _bc(np.asarray(mla_q_norm_w)[0])
    shared["b_kvnw"] = _bc(np.asarray(mla_kv_norm_w)[0])
    shared["b_gnw"] = _bc(np.asarray(ret_gn_w)[0])
    cw = np.asarray(conv_w, np.float32)[0]
    shared["c_cw"] = np.ascontiguousarray(cw.reshape(3, 44, 128).transpose(2, 1, 0)).reshape(128, 132)
    shared["c_cb"] = np.ascontiguousarray(np.asarray(conv_b, np.float32)[0].reshape(44, 128).T)
    shared["w_in_l"] = np.ascontiguousarray(np.asarray(w_in, np.float32)[0].reshape(8, 128, 2464).transpose(1, 0, 2)).reshape(128, -1)
    shared["w_uq_l"] = np.ascontiguousarray(np.asarray(w_uq, np.float32)[0].reshape(2, 128, 768).transpose(1, 0, 2)).reshape(128, -1)
    wukv = np.asarray(w_ukv, np.float32)[0].reshape(128, 8, 128)
    shared["wk_l"] = np.ascontiguousarray(wukv[:, :, 0:64]).reshape(128, 512)
    shared["wv_l"] = np.ascontiguousarray(wukv[:, :, 64:128]).reshape(128, 512)
    wo = np.asarray(w_out, np.float32)[0]
    shared["w_out_l"] = np.ascontiguousarray(wo.reshape(8, 128, 1024).transpose(1, 0, 2)).reshape(128, -1)
    wu = np.asarray(w_up, np.float32)[0]
    shared["w_up_l"] = np.ascontiguousarray(wu.reshape(8, 128, 44, 128).transpose(1, 2, 0, 3)).reshape(128, -1)
    wd = np.asarray(w_down, np.float32)[0]
    shared["w_down_l"] = np.ascontiguousarray(wd.reshape(22, 128, 1024).transpose(1, 0, 2)).reshape(128, -1)

    in_maps = []
    for c in range(8):
        b, z = divmod(c, 2)
        m = dict(shared)
        if z == 1:
            xcore = x[b]
            pc = positions[b]
            vd = np.ones(8192, np.float32)
        else:
            xcore = np.concatenate([np.zeros((4096, 1024), np.float32), x[b, :4096]], axis=0)
            pc = np.concatenate([np.zeros(4096, np.int32), positions[b, :4096]])
            vd = np.concatenate([np.zeros(4096, np.float32), np.ones(4096, np.float32)])
        m["xc"] = np.ascontiguousarray(xcore)
        m["posc"] = np.ascontiguousarray(pc.reshape(NT, 128).T)
        m["valid"] = np.ascontiguousarray(vd.reshape(NT, 128).T)
        in_maps.append(m)
    if _PROG is None:
        _PROG = build_program()
    res = run_bass_kernel_spmd(_PROG, in_maps, core_ids=list(range(8)))
    out = np.empty((4, 8192, 1024), np.float32)
    for c in range(8):
        b, z = divmod(c, 2)
        out[b, z * 4096:(z + 1) * 4096] = res.results[c]["yout"]
    return out
```

```python
import math
from contextlib import ExitStack
import numpy as np
import concourse.bass as bass
import concourse.mybir as mybir
from concourse.bass_utils import run_bass_kernel_spmd

F32 = mybir.dt.float32
BF16 = mybir.dt.bfloat16
I32 = mybir.dt.int32
ALU = mybir.AluOpType
AF = mybir.ActivationFunctionType
AX = mybir.AxisListType

NT = 64
HALO = 31
NOWN = 33
EPS = 1e-6
TWO_PI = 2.0 * math.pi


class Res:
    __slots__ = ("name", "w", "r", "excl")

    def __init__(self, name, excl=False):
        self.name = name
        self.w = None
        self.r = set()
        self.excl = excl


class _Rec:
    def __init__(self):
        self.call = None

    def __getattr__(self, name):
        def f(*a, **k):
            self.call = (name, a, k)
            return self
        return f


def _fsize(ap):
    n = 1
    for d in list(ap.shape)[1:]:
        n *= int(d)
    return n


def _est(eng, fn):
    try:
        r = _Rec()
        fn(r)
        name, a, k = r.call
        if eng == "pe":
            rhs = k.get("rhs", k.get("identity"))
            n = _fsize(rhs) if name == "matmul" else 128
            return max(n, 64) / 2.4 + 25.0
        out = k.get("out", a[0] if a else None)
        f = _fsize(out)
        if eng == "act":
            return (f + 224) / 1.2
        if eng == "dve":
            if name == "reciprocal":
                return 165 + (6.2 * f if int(out.shape[0]) < 32 else f)
            return (f + 150) / 0.96
        return 200 + 3.4 * f
    except Exception:
        return 500.0


class _Op:
    __slots__ = ("q", "kind", "fns", "deps", "cost", "lat", "slot", "seq", "tag")


class Sched:
    ENG = ("pe", "act", "dve", "pool", "sp")

    def __init__(self, nc, stack):
        self.nc = nc
        self.stack = stack
        self.sem = {e: stack.enter_context(nc.semaphore("s_" + e)) for e in self.ENG}
        self.cnt = {e: 0 for e in self.ENG}
        self.seen = {e: {} for e in self.ENG}
        self.streams = {e: [] for e in self.ENG}
        self.dsem = {}
        self.dcnt = {}
        self.ops = []
        self.base = 0
        self.open_pe = None
        self.reorder = True

    def _record_deps(self, idx, reads, writes):
        deps = self.ops[idx].deps
        for r in reads:
            if r.w is not None and r.w >= self.base and r.w != idx:
                deps.add(r.w)
        for w in writes:
            if w.w is not None and w.w >= self.base and w.w != idx:
                deps.add(w.w)
            for t in w.r:
                if t >= self.base and t != idx:
                    deps.add(t)
        for r in reads:
            if r not in writes:
                r.r.add(idx)
        for w in writes:
            w.w = idx
            w.r = set()

    def op(self, eng, fn, reads=(), writes=(), inc=True, tag=None):
        ex = [r for r in reads if r.excl and r not in writes]
        if ex:
            writes = list(writes) + ex
        if eng == "pe" and self.open_pe is not None:
            idx = self.open_pe
            o = self.ops[idx]
            o.fns.append(fn)
            o.cost += _est(eng, fn)
        else:
            o = _Op()
            o.q, o.kind, o.fns, o.deps, o.cost, o.lat, o.slot, o.seq, o.tag = eng, "op", [fn], set(), _est(eng, fn), 0.0, None, None, tag
            idx = len(self.ops)
            self.ops.append(o)
        self._record_deps(idx, reads, writes)
        if eng == "pe":
            self.open_pe = None if inc else idx
        else:
            assert inc
        return idx

    def dma(self, eng, slot, fns, reads=(), writes=(), nbytes=262144):
        assert self.open_pe is None
        if slot not in self.dsem:
            self.dsem[slot] = self.stack.enter_context(self.nc.semaphore("d_" + slot))
            self.dcnt[slot] = 0
        o = _Op()
        o.q, o.kind, o.fns, o.deps, o.cost, o.slot, o.seq, o.tag = eng, "dma", list(fns), set(), 350.0 * len(fns), slot, None, None
        o.lat = 2000.0 + nbytes / 150.0
        idx = len(self.ops)
        self.ops.append(o)
        self._record_deps(idx, reads, writes)
        return idx

    def _wait(self, eng, toks):
        best = {}
        for t in toks:
            k, s, v = t
            if k not in best or best[k][2] < v:
                best[k] = t
        for k, (kk, s, v) in best.items():
            if self.seen[eng].get(k, 0) >= v:
                continue
            self.seen[eng][k] = v
            self.streams[eng].append(("wait", s, v))

    def _token(self, d):
        o = self.ops[d]
        if o.kind == "dma":
            return ("d_" + o.slot, self.dsem[o.slot], o.seq)
        return (o.q, self.sem[o.q], o.seq)

    def flush(self):
        assert self.open_pe is None
        ops, base = self.ops, self.base
        n = len(ops)
        if n == base:
            return
        if self.reorder:
            order = self._list_schedule(base, n)
        else:
            order = list(range(base, n))
        for i in order:
            o = ops[i]
            toks = []
            for d in o.deps:
                od = ops[d]
                if od.kind == "op" and od.q == "pe" and o.q == "pe" and o.kind == "op":
                    continue
                toks.append(self._token(d))
            self._wait(o.q, toks)
            if o.kind == "dma":
                for fn in o.fns:
                    self.dcnt[o.slot] += 16
                    self.streams[o.q].append(("dma", fn, self.dsem[o.slot]))
                o.seq = self.dcnt[o.slot]
            else:
                self.cnt[o.q] += 1
                o.seq = self.cnt[o.q]
                for j, fn in enumerate(o.fns):
                    self.streams[o.q].append(("op", fn, j == len(o.fns) - 1))
        self.base = n

    def _list_schedule(self, base, n):
        ops = self.ops
        indeg = {}
        succ = {}
        for i in range(base, n):
            dd = [d for d in ops[i].deps if d >= base]
            indeg[i] = len(dd)
            for d in dd:
                succ.setdefault(d, []).append(i)
        etime = {e: 0.0 for e in self.ENG}
        lasttag = {e: None for e in self.ENG}
        finish = {}
        rtime = {}
        ready = {e: [] for e in self.ENG}
        for i in range(base, n):
            if indeg[i] == 0:
                rtime[i] = 0.0
                ready[ops[i].q].append(i)
        order = []
        LOOK = 1 << 30
        while len(order) < n - base:
            bestk, besti = None, None
            for e in self.ENG:
                lst = ready[e]
                if not lst:
                    continue
                t = etime[e]
                cand, ck = None, None
                for i in lst:
                    st = rtime[i] if rtime[i] > t else t
                    k = (st, i)
                    if ck is None or k < ck:
                        cand, ck = i, k
                if bestk is None or ck < bestk:
                    bestk, besti = ck, cand
            i = besti
            o = ops[i]
            st = bestk[0]
            c = o.cost
            if o.tag is not None and o.q == "act":
                if lasttag["act"] is not None and lasttag["act"] != o.tag:
                    c += 1300.0
                lasttag["act"] = o.tag
            etime[o.q] = st + c
            finish[i] = st + c + (o.lat if o.kind == "dma" else 60.0)
            ready[o.q].remove(i)
            order.append(i)
            for j in succ.get(i, ()):
                indeg[j] -= 1
                rt = rtime.get(j, 0.0)
                if finish[i] > rt:
                    rtime[j] = finish[i]
                elif j not in rtime:
                    rtime[j] = rt
                if indeg[j] == 0:
                    ready[ops[j].q].append(j)
        self.est_span = max(etime.values())
        return order

    def barrier(self):
        self.flush()
        toks = [(e, self.sem[e], self.cnt[e]) for e in self.ENG if self.cnt[e] > 0]
        toks += [("d_" + s, self.dsem[s], self.dcnt[s]) for s in self.dsem if self.dcnt[s] > 0]
        for e in self.ENG:
            self._wait(e, toks)

    def emit(self):
        self.flush()
        nc = self.nc
        with nc.Block() as block:
            def run(eng, e):
                sem = self.sem[eng]
                for item in self.streams[eng]:
                    if item[0] == "wait":
                        e.wait_ge(item[1], item[2])
                    elif item[0] == "op":
                        ins = item[1](e)
                        if item[2]:
                            ins.then_inc(sem, 1)
                    else:
                        item[1](e).then_inc(item[2], 16)

            @block.tensor
            def _(e):
                run("pe", e)

            @block.scalar
            def _(e):
                run("act", e)

            @block.vector
            def _(e):
                run("dve", e)

            @block.gpsimd
            def _(e):
                run("pool", e)

            @block.sync
            def _(e):
                run("sp", e)


def bc_mid(a, k):
    return bass.AP(a.tensor, a.offset, [list(a.ap[0]), [0, k]] + [list(x) for x in a.ap[1:]])


def bc_last(a, m):
    return bass.AP(a.tensor, a.offset, [list(x) for x in a.ap] + [[0, m]])


class Arena:
    def __init__(self, t, total):
        self.t = t
        self.total = total
        self.off = 0

    def f32(self, n):
        assert self.off + n <= self.total, ("arena overflow", self.off, n, self.total)
        a = self.t[:, self.off:self.off + n]
        self.off += n
        return a

    def bf(self, n):
        w = (n + 1) // 2
        return self.f32(w).bitcast(BF16)[:, 0:n]


import os
KSTOP = os.environ.get('KSTOP', '')
KSUB = int(os.environ.get('KSUB', 0))


def build_program():
    nc = bass.Bass("TRN2", target_bir_lowering=False)
    din = {}

    def inp(name, shape, dt=F32):
        din[name] = nc.dram_tensor(name, list(shape), dt, kind="ExternalInput").ap()
        return din[name]

    xc = inp("xc", [NT * 128, 1024])
    posc = inp("posc", [128, NT], I32)
    valid = inp("valid", [128, NT])
    c_dt = inp("c_dt", [128, 1024])
    c_xi = inp("c_xi", [128, 512])
    c_zeta = inp("c_zeta", [128, 512])
    c_dec = inp("c_dec", [128, 4])
    c_invf = inp("c_invf", [128, 192])
    c_off = inp("c_off", [128, 192])
    c_mask = inp("c_mask", [128, 128])
    b_anw = inp("b_anw", [128, 1024])
    b_fnw = inp("b_fnw", [128, 1024])
    b_onw = inp("b_onw", [128, 1024])
    b_qnw = inp("b_qnw", [128, 256])
    b_kvnw = inp("b_kvnw", [128, 128])
    b_gnw = inp("b_gnw", [128, 512])
    c_cw = inp("c_cw", [128, 44 * 3])
    c_cb = inp("c_cb", [128, 44])
    w_in_l = inp("w_in_l", [128, 8 * 2464])
    w_uq_l = inp("w_uq_l", [128, 2 * 768])
    wk_l = inp("wk_l", [128, 512])
    wv_l = inp("wv_l", [128, 512])
    w_out_l = inp("w_out_l", [128, 8 * 1024])
    w_up_l = inp("w_up_l", [128, 44 * 1024])
    w_down_l = inp("w_down_l", [128, 22 * 1024])
    yout = nc.dram_tensor("yout", [4096, 1024], F32, kind="ExternalOutput").ap()

    s_wup = nc.dram_tensor("s_wup", [128, 44 * 1024], BF16).ap()
    s_wdown = nc.dram_tensor("s_wdown", [128, 22 * 1024], BF16).ap()
    s_wout = nc.dram_tensor("s_wout", [128, 8 * 1024], BF16).ap()
    s_kt = nc.dram_tensor("s_kt", [8, 96, NT * 128], BF16).ap()
    s_v = nc.dram_tensor("s_v", [8, 128, NT * 65], BF16).ap()
    s_qt = nc.dram_tensor("s_qt", [8, 96, NOWN * 128], BF16).ap()
    s_mix = nc.dram_tensor("s_mix", [NOWN, 128, 8, 128], BF16).ap()

    with ExitStack() as st:
        S = Sched(nc, st)
        TOT = 53000
        arena_t = st.enter_context(nc.sbuf_tensor("arena", [128, TOT], F32))
        ps = st.enter_context(nc.psum_tensor("ps", [128, 4096], F32))
        A = Arena(arena_t, TOT)

        def bank(i, n=512):
            return ps[:, i * 512:i * 512 + n]

        PB = [Res("psb%d" % i, excl=True) for i in range(8)]

        ident = A.bf(128)
        maskb = A.bf(128)
        R_smix = [Res("s_mix%d" % i) for i in range(NOWN)]
        R_ident, R_mask = Res("ident"), Res("mask")
        mark_persist = A.off

        S.op("pool", lambda e: e.memset(ident, 0.0), writes=[R_ident])
        S.op("pool", lambda e: e.affine_select(out=ident, in_=ident, pattern=[[-1, 128]], compare_op=ALU.not_equal,
                                               fill=1.0, base=0, channel_multiplier=1), reads=[R_ident], writes=[R_ident])

        w_in = A.bf(8 * 2464)
        w_in3 = w_in.rearrange("p (c f) -> p c f", c=8)
        w_uq = A.bf(2 * 768)
        w_uq3 = w_uq.rearrange("p (c f) -> p c f", c=2)
        wk = A.bf(512)
        wv = A.bf(512)
        R_win, R_wuq, R_wk, R_wv = Res("w_in"), Res("w_uq"), Res("wk"), Res("wv")
        stage = [A.f32(512) for _ in range(2)]
        stageb = [A.bf(512) for _ in range(2)]
        R_stage = [Res("stage0"), Res("stage1")]
        R_stageb = [Res("stageb0"), Res("stageb1")]
        R_scr = {k: Res(k) for k in ["s_wup", "s_wdown", "s_wout"]}
        pieces = []

        def add_pieces(src, ncols, dst_sb=None, dst_res=None, dst_dram=None, dram_res=None):
            c0 = 0
            while c0 < ncols:
                n = min(512, ncols - c0)
                pieces.append((src, c0, n, dst_sb, dst_res, dst_dram, dram_res))
                c0 += n

        def piece_load_cast(k):
            src, c0, n, dst_sb, dst_res, dst_dram, dram_res = pieces[k]
            i = k % 2
            S.dma("act", "stg%d" % i, [lambda e: e.dma_start(out=stage[i][:, 0:n], in_=src[:, c0:c0 + n])], writes=[R_stage[i]])
            if dst_sb is not None:
                S.op("pool", lambda e: e.tensor_copy(out=dst_sb[:, c0:c0 + n], in_=stage[i][:, 0:n]), reads=[R_stage[i]], writes=[dst_res])
            else:
                S.op("pool", lambda e: e.tensor_copy(out=stageb[i][:, 0:n], in_=stage[i][:, 0:n]), reads=[R_stage[i]], writes=[R_stageb[i]])

        def piece_store(k):
            src, c0, n, dst_sb, dst_res, dst_dram, dram_res = pieces[k]
            i = k % 2
            if dst_dram is not None:
                S.dma("act", "stb%d" % i, [lambda e: e.dma_start(out=dst_dram[:, c0:c0 + n], in_=stageb[i][:, 0:n])],
                      reads=[R_stageb[i]], writes=[Res("wscr")])

        add_pieces(w_in_l, 8 * 2464, dst_sb=w_in, dst_res=R_win)
        add_pieces(w_uq_l, 2 * 768, dst_sb=w_uq, dst_res=R_wuq)
        add_pieces(wk_l, 512, dst_sb=wk, dst_res=R_wk)
        add_pieces(wv_l, 512, dst_sb=wv, dst_res=R_wv)
        add_pieces(c_mask, 128, dst_sb=maskb, dst_res=R_mask)
        n_first = len(pieces)
        add_pieces(w_out_l, 8 * 1024, dst_dram=s_wout, dram_res=R_scr["s_wout"])
        add_pieces(w_down_l, 22 * 1024, dst_dram=s_wdown, dram_res=R_scr["s_wdown"])
        add_pieces(w_up_l, 44 * 1024, dst_dram=s_wup, dram_res=R_scr["s_wup"])
        for k in range(n_first):
            piece_load_cast(k)
        pk = [n_first, n_first]

        def cast_step(nload):
            for _ in range(nload):
                if pk[0] < len(pieces):
                    piece_load_cast(pk[0])
                    piece_store(pk[0])
                    pk[0] += 1
            pk[1] = pk[0]

        def load_const(src, n, name, dt=F32):
            a = A.f32(n)
            if dt is not F32:
                a = a.bitcast(dt)
            r = Res(name)
            S.dma("sp", "c_" + name, [lambda e: e.dma_start(out=a, in_=src)], writes=[r])
            return a, r

        dt_t, R_dt = load_const(c_dt, 1024, "dt")
        xi_t, R_xi = load_const(c_xi, 512, "xi")
        zeta_t, R_zeta = load_const(c_zeta, 512, "zeta")
        dec_t, R_dec = load_const(c_dec, 4, "dec")
        invf_t, R_invf = load_const(c_invf, 192, "invf")
        off_t, R_off = load_const(c_off, 192, "off")
        anw_t, R_anw = load_const(b_anw, 1024, "anw")
        qnw_t, R_qnw = load_const(b_qnw, 256, "qnw")
        kvnw_t, R_kvnw = load_const(b_kvnw, 128, "kvnw")
        gnw_t, R_gnw = load_const(b_gnw, 512, "gnw")
        posi, R_pos = load_const(posc, NT, "posi", I32)
        valid_t, R_valid = load_const(valid, NT, "valid")
        posf = A.f32(NT)
        S.op("dve", lambda e: e.tensor_copy(out=posf, in_=posi), reads=[R_pos], writes=[R_pos])

        TB = 8
        tab = A.f32(TB * 192)
        tab3 = tab.rearrange("p (n f) -> p n f", n=TB)
        R_tab = Res("tab")
        ttmp = A.f32(192)
        tti = A.f32(192).bitcast(I32)
        R_tt = Res("ttmp")

        def make_tables(n0):
            for n in range(n0, n0 + TB):
                S.op("dve", lambda e, n=n: e.scalar_tensor_tensor(out=ttmp, in0=invf_t, scalar=posf[:, n:n + 1], in1=off_t,
                                                                op0=ALU.mult, op1=ALU.add),
                     reads=[R_invf, R_off, R_pos], writes=[R_tt])
                S.op("dve", lambda e: e.tensor_copy(out=tti, in_=ttmp), reads=[R_tt], writes=[R_tt])
                S.op("dve", lambda e, n=n: e.tensor_tensor(out=tab3[:, n % TB, :], in0=ttmp, in1=tti, op=ALU.subtract),
                     reads=[R_tt], writes=[R_tab])
            S.op("act", lambda e: e.activation(out=tab, in_=tab, func=AF.Sin, scale=TWO_PI * (1.0 - 1e-6)),
                 reads=[R_tab], writes=[R_tab])

        xbuf = [A.f32(1024) for _ in range(2)]
        R_x = [Res("x0"), Res("x1")]
        R32 = A.f32(512)
        Rb = A.bf(512)
        R_R32, R_Rb = Res("R32"), Res("Rb")
        DB = {}
        for nm, kind, sz in [("st_small", "f", 64), ("hb", "b", 1024), ("hT", "b", 1024), ("qk_sb", "f", 1024), ("tmpA", "f", 1024),
                             ("tmpB", "f", 1024), ("qkr", "b", 1024), ("kz", "b", 512), ("vb", "b", 512), ("sg", "f", 512),
                             ("qT", "b", 1024), ("qxT", "b", 512), ("kT", "b", 512), ("sd", "b", 1024), ("gn1", "f", 512),
                             ("gn2", "f", 512), ("yb", "b", 512), ("yT", "b", 512), ("cqn", "b", 256), ("ckvn", "b", 128), ("kr", "b", 32),
                             ("cqnT", "b", 256), ("ckvnT", "b", 128), ("mt1", "f", 64), ("qb", "b", 768), ("lat_sb", "f", 416),
                             ("tA2", "f", 32), ("tB2", "f", 32), ("tA3", "f", 256), ("tB3", "f", 256)]:
            DB[nm] = [((A.f32(sz) if kind == "f" else A.bf(sz)), Res(nm + "_%d" % i)) for i in range(2)]
        kn_g = [A.bf(4 * 512) for _ in range(2)]
        kpe_g = [A.bf(512) for _ in range(2)]
        v_g = [A.bf(4 * 8 * 65) for _ in range(2)]
        q_g = [A.bf(8 * 512) for _ in range(2)]
        R_kng = [Res("kng0"), Res("kng1")]
        R_kpg = [Res("kpg0"), Res("kpg1")]
        R_vg = [Res("vg0"), Res("vg1")]
        R_qg = [Res("qg0"), Res("qg1")]
        R_skt, R_sv, R_sqt = Res("s_kt"), Res("s_v"), Res("s_qt")

        for i_ in range(2):
            S.op("dve", lambda e, i_=i_: e.memset(DB["qT"][i_][0], 0.0), writes=[DB["qT"][i_][1]])
        S.op("dve", lambda e: e.memset(R32, 0.0), writes=[R_R32])
        S.op("dve", lambda e: e.memset(Rb, 0.0), writes=[R_Rb])

        x_tiles = xc.rearrange("(n p) d -> n p d", p=128)

        def load_x(n):
            i = n % 2
            S.dma("sp", "x%d" % i, [lambda e, n=n, i=i: e.dma_start(out=xbuf[i], in_=x_tiles[n])], writes=[R_x[i]])

        if KSTOP == 'A0':
            npz = int(os.environ.get('KNP', 0))
            while pk[1] < min(len(pieces), n_first + npz):
                cast_step(2)
            S.barrier(); S.emit(); return nc
        load_x(0)

        def _tileA(n):
            cast_step(3)
            b = n % 2
            (st_small, R_ss), (hb, R_hb), (hT, R_hT), (qk_sb, R_qksb), (tmpA, R_tA), (tmpB, R_tB) = [DB[k][b] for k in ("st_small", "hb", "hT", "qk_sb", "tmpA", "tmpB")]
            (qkr, R_qkr), (kz, R_kz), (vb, R_vb), (sg, R_sg), (qT, R_qT), (qxT, R_qxT), (kT, R_kT) = [DB[k][b] for k in ("qkr", "kz", "vb", "sg", "qT", "qxT", "kT")]
            (sd, R_sd), (gn1, R_gn1), (gn2, R_gn2), (yb, R_yb), (yT, R_yT), (cqn, R_cqn), (ckvn, R_ckvn), (kr, R_kr) = [DB[k][b] for k in ("sd", "gn1", "gn2", "yb", "yT", "cqn", "ckvn", "kr")]
            (cqnT, R_cqnT), (ckvnT, R_ckvnT), (mt1, R_mt), (qb, R_qb) = [DB[k][b] for k in ("cqnT", "ckvnT", "mt1", "qb")]
            hT3 = hT.rearrange("p (c t) -> p c t", c=8)
            (lat_sb, R_lat), (tA2, R_tA2), (tB2, R_tB2), (tA3, R_tA3), (tB3, R_tB3) = [DB[k][b] for k in ("lat_sb", "tA2", "tB2", "tA3", "tB3")]
            full = n >= HALO
            g, gi = divmod(n, 4)
            gb = g % 2
            if n + 1 < NT:
                load_x(n + 1)
            xb_, Rx = xbuf[n % 2], R_x[n % 2]
            ss = st_small[:, 0:1]
            rstd = st_small[:, 1:2]
            S.op("act", lambda e, xb_=xb_: e.activation(out=hb, in_=xb_, func=AF.Square, accum_out=ss),
                 reads=[Rx], writes=[R_hb, R_ss])
            S.op("dve", lambda e: e.tensor_scalar(out=ss, in0=ss, scalar1=1.0 / 1024, scalar2=EPS, op0=ALU.mult, op1=ALU.add),
                 reads=[R_ss], writes=[R_ss])
            S.op("act", lambda e: e.activation(out=ss, in_=ss, func=AF.Sqrt), reads=[R_ss], writes=[R_ss])
            S.op("dve", lambda e: e.reciprocal(out=rstd, in_=ss), reads=[R_ss], writes=[R_ss])
            S.op("dve", lambda e, xb_=xb_: e.scalar_tensor_tensor(out=hb, in0=xb_, scalar=rstd, in1=anw_t, op0=ALU.mult, op1=ALU.mult),
                 reads=[Rx, R_ss, R_anw], writes=[R_hb])
            tb = bank(0).bitcast(BF16)
            tbh = bank(4).bitcast(BF16)
            for c in range(8):
                S.op("pe", lambda e, c=c: e.transpose(out=tbh[:, c * 128:(c + 1) * 128], in_=hb[:, c * 128:(c + 1) * 128], identity=ident),
                     reads=[R_hb, R_ident], writes=[PB[4]], inc=(c == 7))
            S.op("act", lambda e: e.activation(out=hT, in_=tbh, func=AF.Copy), reads=[PB[4]], writes=[R_hT])
            if KSUB == 1 and n >= HALO:
                return


            def proj(bk, col0, ncol, n_=None):
                for c in range(8):
                    S.op("pe", lambda e, c=c: e.matmul(bank(bk, ncol), lhsT=hT3[:, c, :], rhs=w_in3[:, c, col0:col0 + ncol],
                                                       start=(c == 0), stop=(c == 7)),
                         reads=[R_hT, R_win], writes=[PB[bk]], inc=(c == 7))

            if full:
                proj(1, 0, 512)
            proj(2, 512, 512)
            proj(3, 1024, 512)
            if full:
                proj(4, 1536, 512)
                S.op("act", lambda e: e.activation(out=qk_sb, in_=ps[:, 512:1536], func=AF.Copy), reads=[PB[1], PB[2]], writes=[R_qksb])
                proj(1, 2048, 416)
                S.op("act", lambda e: e.activation(out=lat_sb, in_=bank(1, 416), func=AF.Copy), reads=[PB[1]], writes=[R_lat])
            else:
                S.op("act", lambda e: e.activation(out=qk_sb[:, 512:1024], in_=ps[:, 1024:1536], func=AF.Copy), reads=[PB[2]], writes=[R_qksb])
                proj(1, 2304, 160)
                S.op("act", lambda e: e.activation(out=lat_sb[:, 0:160], in_=bank(1, 160), func=AF.Copy), reads=[PB[1]], writes=[R_lat])
            if KSUB == 2 and n >= HALO:
                return

            latoff = 0 if full else -256
            if n % TB == 0:
                make_tables(n)
            tabn = tab3[:, n % TB, :]
            cs_r, ss_r = tabn[:, 0:64], tabn[:, 64:128]
            cs_m, ss_m = tabn[:, 128:160], tabn[:, 160:192]

            def rope(src_ap, nh, hd, cs, sn, dstA, dstB, dst, reads, wres, RA=None, RB=None, add_eng="dve"):
                RA = R_tA if RA is None else RA
                RB = R_tB if RB is None else RB
                half = hd // 2
                x3 = src_ap.rearrange("p (h d) -> p h d", h=nh)
                sw = bass.AP(src_ap.tensor, src_ap.offset + half,
                             [list(src_ap.ap[0]), [hd, nh], [-half, 2], [1, half]])
                a3 = dstA.rearrange("p (h d) -> p h d", h=nh)
                b4 = dstB.rearrange("p (h a d) -> p h a d", h=nh, a=2)
                S.op("dve", lambda e: e.tensor_tensor(out=a3, in0=x3, in1=bc_mid(cs, nh), op=ALU.mult),
                     reads=reads + [R_tab], writes=[RA])
                S.op("dve", lambda e: e.tensor_tensor(out=b4, in0=sw, in1=bc_mid(sn.rearrange("p (a d) -> p a d", a=2), nh), op=ALU.mult),
                     reads=reads + [R_tab], writes=[RB])
                S.op(add_eng, lambda e: e.tensor_tensor(out=dst, in0=dstA, in1=dstB, op=ALU.add),
                     reads=[RA, RB], writes=[wres])

            if full:
                rope(qk_sb, 16, 64, cs_r, ss_r, tmpA, tmpB, qkr, [R_qksb], R_qkr, add_eng="pool")
            else:
                rope(qk_sb[:, 512:1024], 8, 64, cs_r, ss_r, tmpA[:, 0:512], tmpB[:, 0:512], qkr[:, 512:1024], [R_qksb], R_qkr, add_eng="pool")
            S.op("pool", lambda e: e.tensor_tensor(out=kz, in0=qkr[:, 512:1024], in1=zeta_t, op=ALU.mult),
                 reads=[R_qkr, R_zeta], writes=[R_kz])
            S.op("act", lambda e: e.activation(out=vb, in_=bank(3), func=AF.Copy), reads=[PB[3]], writes=[R_vb])
            if KSUB == 3 and n >= HALO:
                return


            if full:
                S.op("act", lambda e: e.activation(out=sg, in_=bank(4), func=AF.Silu), reads=[PB[4]], writes=[R_sg])
                for c in range(8):
                    S.op("pe", lambda e, c=c: e.transpose(out=tb[:, c * 128:(c + 1) * 128], in_=qkr[:, c * 128:(c + 1) * 128], identity=ident),
                         reads=[R_qkr, R_ident], writes=[PB[0]], inc=(c == 7))
                S.op("act", lambda e: e.activation(out=qT[0:64, 0:512], in_=tb[0:64, 0:512], func=AF.Copy), reads=[PB[0]], writes=[R_qT])
                S.op("act", lambda e: e.activation(out=qT[64:128, 512:1024], in_=tb[64:128, 0:512], func=AF.Copy), reads=[PB[0]], writes=[R_qT])
                S.op("dve", lambda e: e.tensor_tensor(out=qxT, in0=tb[:, 0:512], in1=xi_t, op=ALU.mult), reads=[PB[0], R_xi], writes=[R_qxT])
                S.op("act", lambda e: e.activation(out=kT, in_=tb[:, 512:1024], func=AF.Copy), reads=[PB[0]], writes=[R_kT])
                for h in range(8):
                    p_, a_ = divmod(h, 2)
                    rows = slice(a_ * 64, a_ * 64 + 64)
                    S.op("pe", lambda e, h=h, p_=p_, rows=rows: e.matmul(ps[:, 2560 + h * 128:2560 + (h + 1) * 128],
                                                                         lhsT=kT[:, p_ * 128:(p_ + 1) * 128],
                                                                         rhs=qT[:, (h % 2) * 512 + p_ * 128:(h % 2) * 512 + (p_ + 1) * 128],
                                                                         start=True, stop=True),
                         reads=[R_kT, R_qT], writes=[PB[5], PB[6]], inc=(h == 7))
                S.op("dve", lambda e: e.tensor_tensor(out=sd, in0=ps[:, 2560:3584], in1=dt_t, op=ALU.mult),
                     reads=[PB[5], PB[6], R_dt], writes=[R_sd])
                for h in range(8):
                    p_, a_ = divmod(h, 2)
                    rows = slice(a_ * 64, a_ * 64 + 64)
                    S.op("pe", lambda e, h=h: e.matmul(ps[:, 3584 + h * 64:3584 + (h + 1) * 64], lhsT=sd[:, h * 128:(h + 1) * 128],
                                                       rhs=vb[:, h * 64:(h + 1) * 64], start=True, stop=False),
                         reads=[R_sd, R_vb], writes=[PB[7]], inc=False)
                    S.op("pe", lambda e, h=h, p_=p_, rows=rows: e.matmul(ps[:, 3584 + h * 64:3584 + (h + 1) * 64],
                                                                         lhsT=qxT[:, p_ * 128:(p_ + 1) * 128],
                                                                         rhs=Rb[:, h * 64:(h + 1) * 64],
                                                                         start=False, stop=True),
                         reads=[R_qxT, R_Rb], writes=[PB[7]], inc=(h == 7))
            for p_ in range(4):
                S.op("pe", lambda e, p_=p_: e.matmul(ps[:, 2560 + p_ * 128:2560 + (p_ + 1) * 128], lhsT=kz[:, p_ * 128:(p_ + 1) * 128],
                                                     rhs=vb[:, p_ * 128:(p_ + 1) * 128], start=True, stop=True),
                     reads=[R_kz, R_vb], writes=[PB[5]], inc=(p_ == 3))
            for h in range(8):
                p_, a_ = divmod(h, 2)
                rows = slice(a_ * 64, a_ * 64 + 64)
                S.op("dve", lambda e, h=h, p_=p_, a_=a_, rows=rows: e.scalar_tensor_tensor(
                    out=R32[rows, h * 64:(h + 1) * 64], in0=R32[rows, h * 64:(h + 1) * 64], scalar=dec_t[rows, p_:p_ + 1],
                    in1=ps[rows, 2560 + p_ * 128 + a_ * 64:2560 + p_ * 128 + a_ * 64 + 64], op0=ALU.mult, op1=ALU.add),
                    reads=[PB[5], R_dec, R_R32], writes=[R_R32])
            S.op("act", lambda e: e.activation(out=Rb, in_=R32, func=AF.Copy), reads=[R_R32], writes=[R_Rb])
            if KSUB == 4 and n >= HALO:
                return


            if full:
                o3 = bank(7).rearrange("p (h d) -> p h d", h=8)
                s1, s2, mean, msq, var = (mt1[:, 0:8], mt1[:, 8:16], mt1[:, 16:24], mt1[:, 24:32], mt1[:, 32:40])
                S.op("dve", lambda e: e.tensor_reduce(out=s1, in_=o3, axis=AX.X, op=ALU.add), reads=[PB[7]], writes=[R_mt])
                S.op("act", lambda e: e.activation(out=gn1, in_=bank(7), func=AF.Square), reads=[PB[7]], writes=[R_gn1])
                S.op("dve", lambda e: e.tensor_reduce(out=s2, in_=gn1.rearrange("p (h d) -> p h d", h=8), axis=AX.X, op=ALU.add),
                     reads=[R_gn1], writes=[R_mt])
                S.op("dve", lambda e: e.tensor_scalar(out=mean, in0=s1, scalar1=1.0 / 64, scalar2=None, op0=ALU.mult), reads=[R_mt], writes=[R_mt])
                S.op("dve", lambda e: e.tensor_tensor(out=msq, in0=mean, in1=mean, op=ALU.mult), reads=[R_mt], writes=[R_mt])
                S.op("dve", lambda e: e.scalar_tensor_tensor(out=var, in0=s2, scalar=1.0 / 64, in1=msq, op0=ALU.mult, op1=ALU.subtract),
                     reads=[R_mt], writes=[R_mt])
                S.op("dve", lambda e: e.tensor_scalar(out=var, in0=var, scalar1=EPS, scalar2=None, op0=ALU.add), reads=[R_mt], writes=[R_mt])
                S.op("act", lambda e: e.activation(out=var, in_=var, func=AF.Sqrt), reads=[R_mt], writes=[R_mt])
                S.op("dve", lambda e: e.reciprocal(out=var, in_=var), reads=[R_mt], writes=[R_mt])
                g13 = gn1.rearrange("p (h d) -> p h d", h=8)
                S.op("dve", lambda e: e.tensor_tensor(out=g13, in0=o3, in1=bc_last(mean, 64), op=ALU.subtract),
                     reads=[PB[7], R_mt], writes=[R_gn1])
                S.op("dve", lambda e: e.tensor_tensor(out=g13, in0=g13, in1=bc_last(var, 64), op=ALU.mult), reads=[R_gn1, R_mt], writes=[R_gn1])
                S.op("pool", lambda e: e.tensor_tensor(out=gn2, in0=sg, in1=gnw_t, op=ALU.mult), reads=[R_sg, R_gnw], writes=[R_gn2])
                S.op("dve", lambda e: e.tensor_tensor(out=yb, in0=gn1, in1=gn2, op=ALU.mult), reads=[R_gn1, R_gn2], writes=[R_yb])
                for c in range(4):
                    S.op("pe", lambda e, c=c: e.transpose(out=tb[:, c * 128:(c + 1) * 128], in_=yb[:, c * 128:(c + 1) * 128], identity=ident),
                         reads=[R_yb, R_ident], writes=[PB[0]], inc=(c == 3))
                m = n - HALO
                S.op("act", lambda e: e.activation(out=yT, in_=tb[:, 0:512], func=AF.Copy), reads=[PB[0]], writes=[R_yT])
                S.dma("sp", "ymr%d" % b, [lambda e: e.dma_start(out=s_mix[m, :, 0:4, :], in_=yT.rearrange("p (c t) -> p c t", c=4))],
                      reads=[R_yT], writes=[R_smix[m]], nbytes=131072)

            lat = lat_sb
            ckv_ap = lat[:, 256 + latoff:384 + latoff]
            kpe_ap = lat[:, 384 + latoff:416 + latoff]
            ssq = st_small[:, 4:6]
            rq = st_small[:, 6:8]
            if full:
                S.op("act", lambda e: e.activation(out=cqn, in_=lat[:, 0:256], func=AF.Square, accum_out=ssq[:, 0:1]),
                     reads=[R_lat], writes=[R_cqn, R_ss])
            S.op("act", lambda e: e.activation(out=ckvn, in_=ckv_ap, func=AF.Square, accum_out=ssq[:, 1:2]),
                 reads=[R_lat], writes=[R_ckvn, R_ss])
            if full:
                S.op("dve", lambda e: e.tensor_scalar(out=ssq[:, 0:1], in0=ssq[:, 0:1], scalar1=1.0 / 256, scalar2=EPS, op0=ALU.mult, op1=ALU.add),
                     reads=[R_ss], writes=[R_ss])
            S.op("dve", lambda e: e.tensor_scalar(out=ssq[:, 1:2], in0=ssq[:, 1:2], scalar1=1.0 / 128, scalar2=EPS, op0=ALU.mult, op1=ALU.add),
                 reads=[R_ss], writes=[R_ss])
            lo = 0 if full else 1
            S.op("act", lambda e, lo=lo: e.activation(out=ssq[:, lo:2], in_=ssq[:, lo:2], func=AF.Sqrt), reads=[R_ss], writes=[R_ss])
            S.op("dve", lambda e, lo=lo: e.reciprocal(out=rq[:, lo:2], in_=ssq[:, lo:2]), reads=[R_ss], writes=[R_ss])
            if full:
                S.op("dve", lambda e: e.scalar_tensor_tensor(out=cqn, in0=lat[:, 0:256], scalar=rq[:, 0:1], in1=qnw_t, op0=ALU.mult, op1=ALU.mult),
                     reads=[R_lat, R_ss, R_qnw], writes=[R_cqn])
            S.op("dve", lambda e: e.scalar_tensor_tensor(out=ckvn, in0=ckv_ap, scalar=rq[:, 1:2], in1=kvnw_t, op0=ALU.mult, op1=ALU.mult),
                 reads=[R_lat, R_ss, R_kvnw], writes=[R_ckvn])
            rope(kpe_ap, 1, 32, cs_m, ss_m, tA2, tB2, kr, [R_lat], R_kr, RA=R_tA2, RB=R_tB2)
            if KSUB == 5 and n >= HALO:
                return

            S.op("pe", lambda e: e.transpose(out=tb[:, 0:128], in_=ckvn, identity=ident), reads=[R_ckvn, R_ident], writes=[PB[0]], inc=False)
            S.op("pe", lambda e: e.transpose(out=tb[0:32, 128:256], in_=kr, identity=ident), reads=[R_kr, R_ident], writes=[PB[0]], inc=not full)
            if full:
                for c in range(2):
                    S.op("pe", lambda e, c=c: e.transpose(out=tb[:, 256 + c * 128:256 + (c + 1) * 128], in_=cqn[:, c * 128:(c + 1) * 128], identity=ident),
                         reads=[R_cqn, R_ident], writes=[PB[0]], inc=(c == 1))
            S.op("act", lambda e: e.activation(out=ckvnT, in_=tb[:, 0:128], func=AF.Copy), reads=[PB[0]], writes=[R_ckvnT])
            S.op("act", lambda e, gb=gb, gi=gi: e.activation(out=kpe_g[gb][0:32, gi * 128:(gi + 1) * 128], in_=tb[0:32, 128:256], func=AF.Copy),
                 reads=[PB[0]], writes=[R_kpg[gb]])
            if KSUB == 6 and n >= HALO:
                return

            if full:
                S.op("act", lambda e: e.activation(out=cqnT, in_=tb[:, 256:512], func=AF.Copy), reads=[PB[0]], writes=[R_cqnT])
            for p_ in range(4):
                S.op("pe", lambda e, p_=p_: e.matmul(ps[:, 3072 + p_ * 128:3072 + (p_ + 1) * 128], lhsT=wk[:, p_ * 128:(p_ + 1) * 128], rhs=ckvnT,
                                                     start=True, stop=True), reads=[R_wk, R_ckvnT], writes=[PB[6]], inc=(p_ == 3))
            kng3 = kn_g[gb].rearrange("p (a k) -> p a k", a=4)
            S.op("act", lambda e, gi=gi, kng3=kng3: e.activation(out=kng3[:, :, gi * 128:(gi + 1) * 128], in_=bank(6).rearrange("p (a k) -> p a k", a=4), func=AF.Copy),
                 reads=[PB[6]], writes=[R_kng[gb]])
            S.op("pe", lambda e: e.matmul(bank(5), lhsT=ckvnT, rhs=wv, start=True, stop=True), reads=[R_wv, R_ckvnT], writes=[PB[5]])
            vg4 = v_g[gb].rearrange("p (h t e) -> p h t e", h=8, t=4)
            S.op("dve", lambda e, gi=gi, vg4=vg4: e.tensor_copy(out=vg4[:, :, gi, 0:64], in_=bank(5).rearrange("p (h e) -> p h e", h=8)),
                 reads=[PB[5]], writes=[R_vg[gb]])
            S.op("dve", lambda e, gi=gi, vg4=vg4, n=n: e.tensor_copy(out=vg4[:, :, gi, 64:65], in_=bc_mid(valid_t[:, n:n + 1], 8)),
                 reads=[R_valid], writes=[R_vg[gb]])
            if KSUB == 7 and n >= HALO:
                return

            if full:
                for hf in range(2):
                    for c in range(2):
                        S.op("pe", lambda e, hf=hf, c=c: e.matmul(ps[:, (5 + hf) * 512:(5 + hf) * 512 + 384], lhsT=cqnT[:, c * 128:(c + 1) * 128],
                                                                  rhs=w_uq3[:, c, hf * 384:(hf + 1) * 384], start=(c == 0), stop=(c == 1)),
                             reads=[R_cqnT, R_wuq], writes=[PB[5 + hf]], inc=(c == 1))
                qb3 = qb.rearrange("p (h d) -> p h d", h=8)
                for hf in range(2):
                    src = ps[:, (5 + hf) * 512:(5 + hf) * 512 + 384]
                    s3 = src.rearrange("p (h d) -> p h d", h=4)
                    S.op("act", lambda e, hf=hf, s3=s3: e.activation(out=qb3[:, hf * 4:(hf + 1) * 4, 0:64], in_=s3[:, :, 0:64], func=AF.Copy),
                         reads=[PB[5 + hf]], writes=[R_qb])
                    x3 = s3[:, :, 64:96]
                    sw = bass.AP(src.tensor, src.offset + 64 + 16, [list(src.ap[0]), [96, 4], [-16, 2], [1, 16]])
                    a3 = tA3[:, hf * 128:(hf + 1) * 128].rearrange("p (h d) -> p h d", h=4)
                    b4 = tB3[:, hf * 128:(hf + 1) * 128].rearrange("p (h a d) -> p h a d", h=4, a=2)
                    S.op("dve", lambda e, x3=x3, a3=a3: e.tensor_tensor(out=a3, in0=x3, in1=bc_mid(cs_m, 4), op=ALU.mult),
                         reads=[PB[5 + hf], R_tab], writes=[R_tA3])
                    S.op("dve", lambda e, sw=sw, b4=b4: e.tensor_tensor(out=b4, in0=sw, in1=bc_mid(ss_m.rearrange("p (a d) -> p a d", a=2), 4), op=ALU.mult),
                         reads=[PB[5 + hf], R_tab], writes=[R_tB3])
                    S.op("dve", lambda e, hf=hf, a3=a3: e.tensor_tensor(out=qb3[:, hf * 4:(hf + 1) * 4, 64:96], in0=a3,
                                                                        in1=tB3[:, hf * 128:(hf + 1) * 128].rearrange("p (h d) -> p h d", h=4), op=ALU.add),
                         reads=[R_tA3, R_tB3], writes=[R_qb])
                for h in range(8):
                    S.op("pe", lambda e, h=h: e.transpose(out=tb[0:96, h * 128:(h + 1) * 128], in_=qb[:, h * 96:(h + 1) * 96], identity=ident),
                         reads=[R_qb, R_ident], writes=[PB[0]], inc=(h == 7))
                mg, mi = divmod(n - HALO, 4)
                qg3 = q_g[mg % 2].rearrange("p (h t) -> p h t", h=8)
                S.op("act", lambda e, mi=mi, qg3=qg3: e.activation(out=qg3[0:96, :, mi * 128:(mi + 1) * 128],
                                                                   in_=tb[0:96, :].rearrange("p (h t) -> p h t", h=8), func=AF.Copy),
                     reads=[PB[0]], writes=[R_qg[mg % 2]])
                if mi == 3 or n == NT - 1:
                    ntl = mi + 1
                    S.dma("sp", "sq%d" % (mg % 2),
                          [lambda e, mg=mg, ntl=ntl, qg3=qg3: e.dma_start(
                              out=s_qt[:, :, mg * 512:mg * 512 + ntl * 128].rearrange("h r t -> r h t"),
                              in_=qg3[0:96, :, 0:ntl * 128])],
                          reads=[R_qg[mg % 2]], writes=[Res("sqt")])
            if gi == 3:
                fns = []
                for a_ in range(2):
                    fns.append(lambda e, a_=a_, g=g, kng3=kng3: e.dma_start(
                        out=s_kt[:, 0:64, g * 512:(g + 1) * 512].rearrange("(p a) r k -> a r p k", a=2)[a_],
                        in_=kng3[a_ * 64:(a_ + 1) * 64, :, :]))
                fns.append(lambda e, g=g, gb=gb: e.dma_start(
                    out=s_kt[:, 64:96, g * 512:(g + 1) * 512].rearrange("h r k -> r h k"),
                    in_=bc_mid(kpe_g[gb][0:32, :], 8)))
                S.dma("sp", "sk%d" % gb, fns, reads=[R_kng[gb], R_kpg[gb]], writes=[Res("skt")])
                S.dma("sp", "sv%d" % gb,
                      [lambda e, g=g, vg4=vg4: e.dma_start(
                          out=s_v[:, :, g * 4 * 65:(g + 1) * 4 * 65].rearrange("h p (t e) -> p h t e", t=4),
                          in_=vg4)],
                      reads=[R_vg[gb]], writes=[Res("sv")])

        for n in range(int(os.environ.get('KNT', NT))):
            _tileA(n)
        while pk[1] < len(pieces):
            cast_step(2)

        S.barrier()
        if KSTOP == 'A':
            S.emit(); return nc
        A.off = mark_persist
        ytmp = [A.bf(512) for _ in range(2)]
        R_ytmp = [Res("ytmp0"), Res("ytmp1")]
        QT = [A.bf(NOWN * 128) for _ in range(2)]
        KT = [A.bf(NT * 128) for _ in range(2)]
        VV = [A.bf(NT * 65) for _ in range(2)]
        R_Q, R_K, R_V = [Res("Q0"), Res("Q1")], [Res("K0"), Res("K1")], [Res("V0"), Res("V1")]
        PT = [A.bf(1024) for _ in range(3)]
        R_PT = [Res("PT%d" % i) for i in range(3)]
        rrow = A.f32(512)
        R_rrow = Res("rrow")
        ones_t = A.f32(64)
        R_ones = Res("ones")
        bcs = A.f32(512)
        R_bcs = Res("bcs")
        S.op("dve", lambda e: e.memset(ones_t, 1.0), writes=[R_ones])
        scale = (64 + 32) ** -0.5

        def load_head(h):
            i = h % 2
            S.dma("sp", "lq%d" % i, [lambda e: e.dma_start(out=QT[i][0:96, :], in_=s_qt[h])], reads=[R_sqt], writes=[R_Q[i]])
            S.dma("sp", "lk%d" % i, [lambda e: e.dma_start(out=KT[i][0:96, :], in_=s_kt[h])], reads=[R_skt], writes=[R_K[i]])
            S.dma("sp", "lv%d" % i, [lambda e: e.dma_start(out=VV[i], in_=s_v[h])], reads=[R_sv], writes=[R_V[i]])

        load_head(0)

        groups = []
        blk_id = 0
        for h in range(8):
            qblocks = [(0, 128, [(kt, 0) for kt in range(HALO)] + [(HALO, 0)], HALO)]
            for j in range(8):
                kts = [(kt, 0) for kt in range(32 + 4 * j)] + [(32 + 4 * j + m, 128 * m) for m in range(4)]
                qblocks.append((128 + 512 * j, 512, kts, 32 + 4 * j))
            for (q0, qw, kts, diag0) in qblocks:
                npairs = (len(kts) + 1) // 2
                for gidx in range(npairs):
                    groups.append(dict(h=h, i=h % 2, q0=q0, qw=qw, pair=kts[2 * gidx:2 * gidx + 2], diag0=diag0,
                                       ob=4 + blk_id % 2, first=(gidx == 0), last=(gidx == npairs - 1),
                                       sb=(len(groups) % 2) * 2, pt=len(groups) % 3, yi=blk_id % 2,
                                       newhead=(gidx == 0 and q0 == 0)))
                blk_id += 1

        def emit_qk(g):
            i, q0, qw, sb_ = g["i"], g["q0"], g["qw"], g["sb"]
            if g["newhead"] and g["h"] + 1 < 8:
                load_head(g["h"] + 1)
            for u, (kt, c0) in enumerate(g["pair"]):
                dst = ps[:, (sb_ + u) * 512 + c0:(sb_ + u) * 512 + qw]
                isdiag = kt >= g["diag0"]
                S.op("pe", lambda e, kt=kt, c0=c0, dst=dst, isdiag=isdiag: e.matmul(
                    dst, lhsT=KT[i][0:96, kt * 128:(kt + 1) * 128], rhs=QT[i][0:96, q0 + c0:q0 + qw], start=True, stop=not isdiag),
                    reads=[R_K[i], R_Q[i]], writes=[PB[sb_ + u]], inc=not isdiag)
                if isdiag:
                    S.op("pe", lambda e, dst=dst: e.matmul(dst[:, 0:128], lhsT=ident, rhs=maskb, start=False, stop=True),
                         reads=[R_ident, R_mask], writes=[PB[sb_ + u]])

        def emit_exp(g):
            qw, sb_, pair = g["qw"], g["sb"], g["pair"]
            pt, Rpt = PT[g["pt"]], R_PT[g["pt"]]
            if len(pair) == 2 and pair[0][1] == 0 and pair[1][1] == 0 and qw == 512:
                S.op("act", lambda e: e.activation(out=pt, in_=ps[:, sb_ * 512:sb_ * 512 + 1024], func=AF.Exp, scale=scale),
                     reads=[PB[sb_], PB[sb_ + 1]], writes=[Rpt])
            else:
                for u, (kt, c0) in enumerate(pair):
                    S.op("act", lambda e, u=u, c0=c0: e.activation(
                        out=pt[:, u * 512 + c0:u * 512 + qw], in_=ps[:, (sb_ + u) * 512 + c0:(sb_ + u) * 512 + qw], func=AF.Exp, scale=scale),
                        reads=[PB[sb_ + u]], writes=[Rpt])

        def emit_pv(g):
            i, qw, ob, pair = g["i"], g["qw"], g["ob"], g["pair"]
            pt, Rpt = PT[g["pt"]], R_PT[g["pt"]]
            V3 = VV[i].rearrange("p (t e) -> p t e", t=NT)
            for u, (kt, c0) in enumerate(pair):
                first = g["first"] and u == 0
                last = g["last"] and u == len(pair) - 1
                S.op("pe", lambda e, kt=kt, c0=c0, u=u, first=first, last=last: e.matmul(
                    ps[0:65, ob * 512 + c0:ob * 512 + qw], lhsT=V3[:, kt, :], rhs=pt[:, u * 512 + c0:u * 512 + qw], start=first, stop=last),
                    reads=[R_V[i], Rpt], writes=[PB[ob]], inc=(u == len(pair) - 1))

        def emit_norm(g):
            h, q0, qw, ob, yi_ = g["h"], g["q0"], g["qw"], g["ob"], g["yi"]
            S.op("dve", lambda e: e.tensor_scalar(out=rrow[64:65, 0:qw], in0=ps[64:65, ob * 512:ob * 512 + qw], scalar1=1e-30, scalar2=None, op0=ALU.max),
                 reads=[PB[ob]], writes=[R_rrow])
            S.op("dve", lambda e: e.reciprocal(out=rrow[64:65, 0:qw], in_=rrow[64:65, 0:qw]), reads=[R_rrow], writes=[R_rrow])
            S.op("pe", lambda e: e.matmul(ps[0:64, 6 * 512:6 * 512 + qw], lhsT=ones_t[64:65, 0:64], rhs=rrow[64:65, 0:qw], start=True, stop=True),
                 reads=[R_ones, R_rrow], writes=[PB[6]])
            S.op("dve", lambda e: e.tensor_copy(out=bcs[0:64, 0:qw], in_=ps[0:64, 6 * 512:6 * 512 + qw]), reads=[PB[6]], writes=[R_bcs])
            S.op("dve", lambda e: e.tensor_tensor(out=ytmp[yi_][0:64, 0:qw], in0=ps[0:64, ob * 512:ob * 512 + qw], in1=bcs[0:64, 0:qw], op=ALU.mult),
                 reads=[PB[ob], R_bcs], writes=[R_ytmp[yi_]])
            m0_, nt_ = q0 // 128, qw // 128
            S.dma("sp", "ym%d" % yi_, [lambda e: e.dma_start(
                out=s_mix[m0_:m0_ + nt_, (h % 2) * 64:(h % 2) * 64 + 64, 4 + h // 2, :].rearrange("m r t -> r m t"),
                in_=ytmp[yi_][0:64, 0:qw].rearrange("r (m t) -> r m t", m=nt_))],
                reads=[R_ytmp[yi_]], writes=R_smix[m0_:m0_ + nt_], nbytes=65536)

        pend_norm = None
        emit_qk(groups[0])
        for gi_, g in enumerate(groups):
            emit_exp(g)
            if gi_ + 1 < len(groups):
                emit_qk(groups[gi_ + 1])
            emit_pv(g)
            if pend_norm is not None:
                emit_norm(pend_norm)
                pend_norm = None
            if g["last"]:
                pend_norm = g
        if pend_norm is not None:
            emit_norm(pend_norm)

        S.barrier()
        if KSTOP == 'AB':
            S.emit(); return nc
        A.off = mark_persist
        wout = A.bf(8 * 1024)
        wout3 = wout.rearrange("p (c f) -> p c f", c=8)
        wdown = A.bf(22 * 1024)
        wdown3 = wdown.rearrange("p (c f) -> p c f", c=22)
        R_wout, R_wdown = Res("wout"), Res("wdown")
        S.dma("sp", "wc1", [lambda e: e.dma_start(out=wout, in_=s_wout)], reads=[R_scr["s_wout"]], writes=[R_wout])
        S.dma("sp", "wc2", [lambda e: e.dma_start(out=wdown, in_=s_wdown)], reads=[R_scr["s_wdown"]], writes=[R_wdown])
        fnw_t, R_fnw = load_const(b_fnw, 1024, "fnw")
        onw_t, R_onw = load_const(b_onw, 1024, "onw")
        cw_t, R_cw = load_const(c_cw, 132, "cw")
        cw3 = cw_t.rearrange("p (c j) -> p c j", c=44)
        cb_t, R_cb = load_const(c_cb, 44, "cb")
        mixb = [A.bf(1024) for _ in range(3)]
        R_mixb = [Res("mixb%d" % i) for i in range(3)]
        NWU = 4
        wupb = [A.bf(1024) for _ in range(NWU)]
        R_wupb = [Res("wup%d" % i) for i in range(NWU)]
        xb2 = [A.f32(1024) for _ in range(3)]
        R_xb2 = [Res("xb2_%d" % i) for i in range(3)]
        x1_ = A.f32(4 * 1024)
        x1 = [x1_, x1_]
        R_x1_ = [Res("x1_%d" % i) for i in range(4)]
        R_x1 = [R_x1_, R_x1_]
        h2b = [A.bf(1024) for _ in range(2)]
        R_h2b = [Res("h2b0"), Res("h2b1")]
        jnk = A.bf(1024)
        R_jnk = Res("jnk")
        h2T = [A.bf(8 * 512) for _ in range(2)]
        R_h2T = [Res("h2T0"), Res("h2T1")]
        gT = [A.bf(22 * 512) for _ in range(2)]
        R_gT = [Res("gT0"), Res("gT1")]
        ubuf = [A.f32(514) for _ in range(2)]
        R_ub = [Res("ub0"), Res("ub1")]
        acc = [[A.f32(512) for _ in range(2)] for _ in range(2)]
        R_acc = [[Res("acc%d%d" % (h_, p_)) for p_ in range(2)] for h_ in range(2)]
        carry = A.f32(44 * 2)
        carry3 = carry.rearrange("p (c j) -> p c j", c=44)
        R_carry = [Res("carry%d" % i) for i in range(44)]
        st2 = [A.f32(8) for _ in range(2)]
        R_st2 = [Res("st2_0"), Res("st2_1")]
        ybuf = xb2
        R_yb2 = R_xb2
        S.op("dve", lambda e: e.memset(carry, 0.0), writes=R_carry)
        wupi = [0]
        xli = [0]
        ybi = [0]
        y_tiles = yout.rearrange("(n p) d -> n p d", p=128)

        blocks = [(0, 1)] + [(1 + 4 * j, 4) for j in range(8)]
        wuc = [0]

        def _blockC(bi, m0, ntl):
            W = ntl * 128
            pb = bi % 2
            x13 = x1[pb].rearrange("p (t d) -> p t d", t=4)
            Rx1 = R_x1[pb]
            h2T3 = h2T[pb].rearrange("p (c t) -> p c t", c=8)
            gT3 = gT[pb].rearrange("p (c t) -> p c t", c=22)
            tb = bank(7).bitcast(BF16)

            def _s1(t):
                m = m0 + t
                xi_ = xli[0] % 3
                xli[0] += 1
                S.dma("sp", "xc%d" % xi_, [lambda e: e.dma_start(out=xb2[xi_], in_=x_tiles[HALO + m])], writes=[R_xb2[xi_]], nbytes=524288)
                mi_ = m % 3
                S.dma("sp", "mx%d" % mi_, [lambda e: e.dma_start(out=mixb[mi_].rearrange("p (c t) -> p c t", c=8), in_=s_mix[m])],
                      reads=[R_smix[m]], writes=[R_mixb[mi_]])
                mix3 = mixb[mi_].rearrange("p (c t) -> p c t", c=8)
                for hf in range(2):
                    for c in range(8):
                        S.op("pe", lambda e, c=c, hf=hf: e.matmul(bank(hf), lhsT=mix3[:, c, :], rhs=wout3[:, c, hf * 512:(hf + 1) * 512],
                                                                  start=(c == 0), stop=(c == 7)),
                             reads=[R_mixb[mi_], R_wout], writes=[PB[hf]], inc=(c == 7))
                S.op("dve", lambda e: e.tensor_tensor(out=x13[:, t, :], in0=ps[:, 0:1024], in1=xb2[xi_], op=ALU.add),
                     reads=[PB[0], PB[1], R_xb2[xi_]], writes=[Rx1[t]])
                tp = t % 2
                hb_, Rhb = h2b[tp], R_h2b[tp]
                ss, rstd, Rst = st2[tp][:, 0:1], st2[tp][:, 1:2], R_st2[tp]
                S.op("act", lambda e: e.activation(out=hb_, in_=x13[:, t, :], func=AF.Square, accum_out=ss), reads=[Rx1[t]], writes=[Rhb, Rst])
                S.op("dve", lambda e: e.tensor_scalar(out=ss, in0=ss, scalar1=1.0 / 1024, scalar2=EPS, op0=ALU.mult, op1=ALU.add), reads=[Rst], writes=[Rst])
                S.op("act", lambda e: e.activation(out=ss, in_=ss, func=AF.Sqrt), reads=[Rst], writes=[Rst])
                S.op("dve", lambda e: e.reciprocal(out=rstd, in_=ss), reads=[Rst], writes=[Rst])
                S.op("dve", lambda e: e.scalar_tensor_tensor(out=hb_, in0=x13[:, t, :], scalar=rstd, in1=fnw_t, op0=ALU.mult, op1=ALU.mult),
                     reads=[Rx1[t], Rst, R_fnw], writes=[Rhb])
                for c in range(8):
                    S.op("pe", lambda e, c=c: e.transpose(out=tb[:, c * 128:(c + 1) * 128], in_=hb_[:, c * 128:(c + 1) * 128], identity=ident),
                         reads=[Rhb, R_ident], writes=[PB[7]], inc=(c == 7))
                S.op("act", lambda e: e.activation(out=h2T3[:, :, t * 128:(t + 1) * 128], in_=tb.rearrange("p (c t) -> p c t", c=8), func=AF.Copy),
                     reads=[PB[7]], writes=[R_h2T[pb]])

            for t in range(ntl):
                _s1(t)

            def _chunk(fc, half):
                cidx = fc + 22 * half
                fp = fc % 2
                wi = wupi[0] % NWU
                wupi[0] += 1
                S.dma("sp", "wu%d" % wi, [lambda e: e.dma_start(out=wupb[wi], in_=s_wup[:, cidx * 1024:(cidx + 1) * 1024])],
                      writes=[R_wupb[wi]])
                bk = 2 + wuc[0] % 3
                wuc[0] += 1
                w3 = wupb[wi].rearrange("p (c f) -> p c f", c=8)
                for c in range(8):
                    S.op("pe", lambda e, c=c: e.matmul(bank(bk, W), lhsT=w3[:, c, :], rhs=h2T3[:, c, 0:W], start=(c == 0), stop=(c == 7)),
                         reads=[R_wupb[wi], R_h2T[pb]], writes=[PB[bk]], inc=(c == 7))
                Rc = R_carry[cidx]
                if bi == 0:
                    S.op("dve", lambda e: e.tensor_copy(out=carry3[:, cidx, :], in_=bank(bk, W)[:, W - 2:W]), reads=[PB[bk]], writes=[Rc])
                    return
                ac, Rac = acc[half][fp], R_acc[half][fp]
                if half == 0:
                    ub, Rub = ubuf[fp], R_ub[fp]
                    S.op("act", lambda e: e.activation(out=ub[:, 0:2], in_=carry3[:, cidx, :], func=AF.Copy), reads=[Rc], writes=[Rub])
                    S.op("act", lambda e: e.activation(out=ub[:, 2:2 + W], in_=bank(bk, W), func=AF.Copy), reads=[PB[bk]], writes=[Rub])
                    S.op("act", lambda e: e.activation(out=ac[:, 0:W], in_=bank(bk, W), func=AF.Identity,
                                                       scale=cw3[:, cidx, 2:3], bias=cb_t[:, cidx:cidx + 1]),
                         reads=[PB[bk], R_cw, R_cb], writes=[Rac])
                    S.op("dve", lambda e: e.tensor_copy(out=carry3[:, cidx, :], in_=ub[:, W:W + 2]), reads=[Rub], writes=[Rc])
                    S.op("dve", lambda e: e.scalar_tensor_tensor(out=ac[:, 0:W], in0=ub[:, 1:1 + W], scalar=cw3[:, cidx, 1:2], in1=ac[:, 0:W],
                                                                 op0=ALU.mult, op1=ALU.add), reads=[Rub, Rac, R_cw], writes=[Rac])
                    S.op("dve", lambda e: e.scalar_tensor_tensor(out=ac[:, 0:W], in0=ub[:, 0:W], scalar=cw3[:, cidx, 0:1], in1=ac[:, 0:W],
                                                                 op0=ALU.mult, op1=ALU.add), reads=[Rub, Rac, R_cw], writes=[Rac])
                    S.op("act", lambda e: e.activation(out=ac[:, 0:W], in_=ac[:, 0:W], func=AF.Silu), reads=[Rac], writes=[Rac])
                else:
                    pu = bank(bk, W)
                    S.op("act", lambda e: e.activation(out=ac[:, 0:W], in_=pu, func=AF.Identity,
                                                       scale=cw3[:, cidx, 2:3], bias=cb_t[:, cidx:cidx + 1]),
                         reads=[PB[bk], R_cw, R_cb], writes=[Rac])
                    S.op("dve", lambda e: e.scalar_tensor_tensor(out=ac[:, 1:W], in0=pu[:, 0:W - 1], scalar=cw3[:, cidx, 1:2], in1=ac[:, 1:W],
                                                                 op0=ALU.mult, op1=ALU.add), reads=[PB[bk], Rac, R_cw], writes=[Rac])
                    S.op("dve", lambda e: e.scalar_tensor_tensor(out=ac[:, 2:W], in0=pu[:, 0:W - 2], scalar=cw3[:, cidx, 0:1], in1=ac[:, 2:W],
                                                                 op0=ALU.mult, op1=ALU.add), reads=[PB[bk], Rac, R_cw], writes=[Rac])
                    S.op("dve", lambda e: e.scalar_tensor_tensor(out=ac[:, 0:1], in0=carry3[:, cidx, 1:2], scalar=cw3[:, cidx, 1:2], in1=ac[:, 0:1],
                                                                 op0=ALU.mult, op1=ALU.add), reads=[Rc, Rac, R_cw], writes=[Rac])
                    S.op("dve", lambda e: e.scalar_tensor_tensor(out=ac[:, 0:2], in0=carry3[:, cidx, 0:2], scalar=cw3[:, cidx, 0:1], in1=ac[:, 0:2],
                                                                 op0=ALU.mult, op1=ALU.add), reads=[Rc, Rac, R_cw], writes=[Rac])
                    S.op("dve", lambda e: e.tensor_copy(out=carry3[:, cidx, :], in_=pu[:, W - 2:W]), reads=[PB[bk]], writes=[Rc])
                    S.op("pool", lambda e: e.tensor_tensor(out=gT3[:, fc, 0:W], in0=acc[0][fp][:, 0:W], in1=ac[:, 0:W], op=ALU.mult),
                         reads=[R_acc[0][fp], Rac], writes=[R_gT[pb]])

            for fc in range(22):
                for half in range(2):
                    _chunk(fc, half)
            if bi == 0:
                return

            def _s3(t):
                m = m0 + t
                for hf in range(2):
                    for fc in range(22):
                        S.op("pe", lambda e, fc=fc, hf=hf: e.matmul(bank(5 + hf), lhsT=gT3[:, fc, t * 128:(t + 1) * 128],
                                                                    rhs=wdown3[:, fc, hf * 512:(hf + 1) * 512], start=(fc == 0), stop=(fc == 21)),
                             reads=[R_gT[pb], R_wdown], writes=[PB[5 + hf]], inc=(fc == 21))
                yi = ybi[0] % 3
                ybi[0] += 1
                yb_, Ryb = ybuf[yi], R_yb2[yi]
                S.op("dve", lambda e: e.tensor_tensor(out=x13[:, t, :], in0=ps[:, 2560:3584], in1=x13[:, t, :], op=ALU.add),
                     reads=[PB[5], PB[6], Rx1[t]], writes=[Rx1[t]])
                tp = t % 2
                ss2, rstd2, Rst = st2[tp][:, 2:3], st2[tp][:, 3:4], R_st2[tp]
                S.op("act", lambda e: e.activation(out=jnk, in_=x13[:, t, :], func=AF.Square, accum_out=ss2), reads=[Rx1[t]], writes=[R_jnk, Rst])
                S.op("dve", lambda e: e.tensor_scalar(out=ss2, in0=ss2, scalar1=1.0 / 1024, scalar2=EPS, op0=ALU.mult, op1=ALU.add), reads=[Rst], writes=[Rst])
                S.op("act", lambda e: e.activation(out=ss2, in_=ss2, func=AF.Sqrt), reads=[Rst], writes=[Rst])
                S.op("dve", lambda e: e.reciprocal(out=rstd2, in_=ss2), reads=[Rst], writes=[Rst])
                S.op("dve", lambda e: e.scalar_tensor_tensor(out=yb_, in0=x13[:, t, :], scalar=rstd2, in1=onw_t, op0=ALU.mult, op1=ALU.mult),
                     reads=[Rx1[t], Rst, R_onw], writes=[Ryb])
                S.dma("sp", "yo%d" % yi, [lambda e: e.dma_start(out=y_tiles[m - 1], in_=yb_)], reads=[Ryb], nbytes=524288)

            for t in range(ntl):
                _s3(t)

        for bi, (m0, ntl) in enumerate(blocks):
            _blockC(bi, m0, ntl)

        S.barrier()
        S.emit()
    return nc


def _consts():
    H, C = 8, 128
    lg = np.log1p(-np.power(2.0, -5.0 - np.arange(H, dtype=np.float64)))
    idx = np.arange(C, dtype=np.float64)
    diff = idx[None, :] - idx[:, None]
    dt = np.where(diff[:, None, :] >= 0, np.exp(lg[None, :, None] * np.maximum(diff[:, None, :], 0.0)), 0.0) / 8.0
    c_dt = dt.reshape(128, 1024).astype(np.float32)
    xi = np.exp(lg[:, None] * (idx[None, :] + 1.0))
    c_xi = np.zeros((128, 4, 128))
    for p in range(4):
        for a in range(2):
            c_xi[a * 64:(a + 1) * 64, p, :] = xi[2 * p + a][None, :]
    c_xi = c_xi.reshape(128, 512).astype(np.float32)
    zeta = np.exp(lg[:, None] * (C - 1.0 - idx[None, :])) / 8.0
    c_zeta = np.repeat(zeta.T[:, :, None], 64, axis=2).reshape(128, 512).astype(np.float32)
    dec = np.exp(lg * C)
    c_dec = np.zeros((128, 4))
    for p in range(4):
        c_dec[0:64, p] = dec[2 * p]
        c_dec[64:128, p] = dec[2 * p + 1]
    c_dec = c_dec.astype(np.float32)
    fr = (10000.0 ** (-np.arange(0, 64, 2, dtype=np.float32) / np.float32(64))).astype(np.float32)
    fm = (10000.0 ** (-np.arange(0, 32, 2, dtype=np.float32) / np.float32(32))).astype(np.float32)
    invf = np.concatenate([fr, fr, fr, fr, fm, fm, fm, fm]).astype(np.float64) / (2 * np.pi)
    off = np.concatenate([np.full(64, 0.25), np.full(32, 0.5), np.zeros(32), np.full(32, 0.25), np.full(16, 0.5), np.zeros(16)])
    c_invf = np.broadcast_to(invf[None, :], (128, 192)).astype(np.float32).copy()
    c_off = np.broadcast_to(off[None, :], (128, 192)).astype(np.float32).copy()
    k = np.arange(128)
    c_mask = np.where(k[None, :] < k[:, None], -30000.0, 0.0).astype(np.float32)
    return dict(c_dt=c_dt, c_xi=c_xi, c_zeta=c_zeta, c_dec=c_dec, c_invf=c_invf, c_off=c_off, c_mask=c_mask)


def _bc(v, n=128):
    return np.ascontiguousarray(np.broadcast_to(np.asarray(v, np.float32)[None, :], (n, v.shape[0])))


_PROG = None


def kernel(x, positions, attn_norm_w, w_in, ret_gn_w, mla_q_norm_w, w_uq, mla_kv_norm_w, w_ukv,
           w_out, ffn_norm_w, w_up, conv_w, conv_b, w_down, final_norm_w):
    global _PROG
    x = np.asarray(x, np.float32)
    positions = np.asarray(positions, np.int32)
    shared = _consts()
    shared["b_anw"] = _bc(np.asarray(attn_norm_w)[0])
    shared["b_fnw"] = _bc(np.asarray(ffn_norm_w)[0])
    shared["b_onw"] = _bc(np.asarray(final_norm_w))
    shared["b_qnw"] = _bc(np.asarray(mla_q_norm_w)[0])
    shared["b_kvnw"] = _bc(np.asarray(mla_kv_norm_w)[0])
    shared["b_gnw"] = _bc(np.asarray(ret_gn_w)[0])
    cw = np.asarray(conv_w, np.float32)[0]
    shared["c_cw"] = np.ascontiguousarray(cw.reshape(3, 44, 128).transpose(2, 1, 0)).reshape(128, 132)
    shared["c_cb"] = np.ascontiguousarray(np.asarray(conv_b, np.float32)[0].reshape(44, 128).T)
    shared["w_in_l"] = np.ascontiguousarray(np.asarray(w_in, np.float32)[0].reshape(8, 128, 2464).transpose(1, 0, 2)).reshape(128, -1)
    shared["w_uq_l"] = np.ascontiguousarray(np.asarray(w_uq, np.float32)[0].reshape(2, 128, 768).transpose(1, 0, 2)).reshape(128, -1)
    wukv = np.asarray(w_ukv, np.float32)[0].reshape(128, 8, 128)
    shared["wk_l"] = np.ascontiguousarray(wukv[:, :, 0:64]).reshape(128, 512)
    shared["wv_l"] = np.ascontiguousarray(wukv[:, :, 64:128]).reshape(128, 512)
    wo = np.asarray(w_out, np.float32)[0]
    shared["w_out_l"] = np.ascontiguousarray(wo.reshape(8, 128, 1024).transpose(1, 0, 2)).reshape(128, -1)
    wu = np.asarray(w_up, np.float32)[0]
    shared["w_up_l"] = np.ascontiguousarray(wu.reshape(8, 128, 44, 128).transpose(1, 2, 0, 3)).reshape(128, -1)
    wd = np.asarray(w_down, np.float32)[0]
    shared["w_down_l"] = np.ascontiguousarray(wd.reshape(22, 128, 1024).transpose(1, 0, 2)).reshape(128, -1)

    in_maps = []
    for c in range(8):
        b, z = divmod(c, 2)
        m = dict(shared)
        if z == 1:
            xcore = x[b]
            pc = positions[b]
            vd = np.ones(8192, np.float32)
        else:
            xcore = np.concatenate([np.zeros((4096, 1024), np.float32), x[b, :4096]], axis=0)
            pc = np.concatenate([np.zeros(4096, np.int32), positions[b, :4096]])
            vd = np.concatenate([np.zeros(4096, np.float32), np.ones(4096, np.float32)])
        m["xc"] = np.ascontiguousarray(xcore)
        m["posc"] = np.ascontiguousarray(pc.reshape(NT, 128).T)
        m["valid"] = np.ascontiguousarray(vd.reshape(NT, 128).T)
        in_maps.append(m)
    if _PROG is None:
        _PROG = build_program()
    res = run_bass_kernel_spmd(_PROG, in_maps, core_ids=list(range(8)))
    out = np.empty((4, 8192, 1024), np.float32)
    for c in range(8):
        b, z = divmod(c, 2)
        out[b, z * 4096:(z + 1) * 4096] = res.results[c]["yout"]
    return out
```

```python
import math
import os
from contextlib import ExitStack
import numpy as np
import concourse.bass as bass
import concourse.mybir as mybir
from concourse.bass_utils import run_bass_kernel_spmd

F32 = mybir.dt.float32
BF16 = mybir.dt.bfloat16
I32 = mybir.dt.int32
ALU = mybir.AluOpType
AF = mybir.ActivationFunctionType
AX = mybir.AxisListType

NT = 64
HALO = 31
NOWN = 33
EPS = 1e-6
TWO_PI = 2.0 * math.pi


class Res:
    __slots__ = ("name", "w", "r", "excl")

    def __init__(self, name, excl=False):
        self.name = name
        self.w = None
        self.r = set()
        self.excl = excl


class _Rec:
    def __init__(self):
        self.call = None

    def __getattr__(self, name):
        def f(*a, **k):
            self.call = (name, a, k)
            return self
        return f


def _fsize(ap):
    n = 1
    for d in list(ap.shape)[1:]:
        n *= int(d)
    return n


_TAGGED = ("Sqrt", "Silu", "Sin", "Exp")


def _act_tag(fn):
    try:
        r = _Rec()
        fn(r)
        name, a, k = r.call
        f = str(k.get("func", ""))
        for t in _TAGGED:
            if f.endswith(t):
                return t
    except Exception:
        pass
    return None


def _est(eng, fn):
    try:
        r = _Rec()
        fn(r)
        name, a, k = r.call
        if eng == "pe":
            rhs = k.get("rhs", k.get("identity"))
            n = _fsize(rhs) if name == "matmul" else 128
            return max(n, 64) / 2.4 + 90.0
        out = k.get("out", a[0] if a else None)
        f = _fsize(out)
        if eng == "act":
            return (f + 224) / 1.2
        if eng == "dve":
            if name == "reciprocal":
                return 165 + (6.2 * f if int(out.shape[0]) < 32 else f)
            return (f + 150) / 0.96
        return 200 + (3.4 if name == 'tensor_copy' else 2.2) * f
    except Exception:
        return 500.0


class _Op:
    __slots__ = ("q", "kind", "fns", "deps", "cost", "lat", "slot", "seq", "tag")


class Sched:
    ENG = ("pe", "act", "dve", "pool", "sp")

    def __init__(self, nc, stack):
        self.nc = nc
        self.stack = stack
        self.sem = {e: stack.enter_context(nc.semaphore("s_" + e)) for e in self.ENG}
        self.cnt = {e: 0 for e in self.ENG}
        self.seen = {e: {} for e in self.ENG}
        self.streams = {e: [] for e in self.ENG}
        self.dsem = {}
        self.dcnt = {}
        self.ops = []
        self.base = 0
        self.open_pe = None
        self.reorder = True
        self.prio = bool(int(os.environ.get('KPRIO', '1')))

    def _record_deps(self, idx, reads, writes):
        deps = self.ops[idx].deps
        for r in reads:
            if r.w is not None and r.w >= self.base and r.w != idx:
                deps.add(r.w)
        for w in writes:
            if w.w is not None and w.w >= self.base and w.w != idx:
                deps.add(w.w)
            for t in w.r:
                if t >= self.base and t != idx:
                    deps.add(t)
        for r in reads:
            if r not in writes:
                r.r.add(idx)
        for w in writes:
            w.w = idx
            w.r = set()

    def op(self, eng, fn, reads=(), writes=(), inc=True, tag=None):
        ex = [r for r in reads if r.excl and r not in writes]
        if ex:
            writes = list(writes) + ex
        if eng == "pe" and self.open_pe is not None:
            idx = self.open_pe
            o = self.ops[idx]
            o.fns.append(fn)
            o.cost += _est(eng, fn)
        else:
            o = _Op()
            o.q, o.kind, o.fns, o.deps, o.cost, o.lat, o.slot, o.seq, o.tag = eng, "op", [fn], set(), _est(eng, fn), 0.0, None, None, (_act_tag(fn) if eng == "act" else None)
            idx = len(self.ops)
            self.ops.append(o)
        self._record_deps(idx, reads, writes)
        if eng == "pe":
            self.open_pe = None if inc else idx
        else:
            assert inc
        return idx

    def dma(self, eng, slot, fns, reads=(), writes=(), nbytes=262144):
        assert self.open_pe is None
        if slot not in self.dsem:
            self.dsem[slot] = self.stack.enter_context(self.nc.semaphore("d_" + slot))
            self.dcnt[slot] = 0
        o = _Op()
        o.q, o.kind, o.fns, o.deps, o.cost, o.slot, o.seq, o.tag = eng, "dma", list(fns), set(), 350.0 * len(fns), slot, None, None
        o.lat = 2000.0 + nbytes / 150.0
        idx = len(self.ops)
        self.ops.append(o)
        self._record_deps(idx, reads, writes)
        return idx

    def _wait(self, eng, toks):
        best = {}
        for t in toks:
            k, s, v = t
            if k not in best or best[k][2] < v:
                best[k] = t
        for k, (kk, s, v) in best.items():
            if self.seen[eng].get(k, 0) >= v:
                continue
            self.seen[eng][k] = v
            self.streams[eng].append(("wait", s, v))

    def _token(self, d):
        o = self.ops[d]
        if o.kind == "dma":
            return ("d_" + o.slot, self.dsem[o.slot], o.seq)
        return (o.q, self.sem[o.q], o.seq)

    def flush(self):
        assert self.open_pe is None
        ops, base = self.ops, self.base
        n = len(ops)
        if n == base:
            return
        if self.reorder:
            order = self._list_schedule(base, n)
        else:
            order = list(range(base, n))
        for i in order:
            o = ops[i]
            toks = []
            for d in o.deps:
                od = ops[d]
                if od.kind == "op" and od.q == "pe" and o.q == "pe" and o.kind == "op":
                    continue
                toks.append(self._token(d))
            self._wait(o.q, toks)
            if o.kind == "dma":
                for fn in o.fns:
                    self.dcnt[o.slot] += 16
                    self.streams[o.q].append(("dma", fn, self.dsem[o.slot]))
                o.seq = self.dcnt[o.slot]
            else:
                self.cnt[o.q] += 1
                o.seq = self.cnt[o.q]
                for j, fn in enumerate(o.fns):
                    self.streams[o.q].append(("op", fn, j == len(o.fns) - 1))
        self.base = n

    def _list_schedule(self, base, n):
        ops = self.ops
        indeg = {}
        succ = {}
        for i in range(base, n):
            dd = [d for d in ops[i].deps if d >= base]
            indeg[i] = len(dd)
            for d in dd:
                succ.setdefault(d, []).append(i)
        blev = {}
        for i in range(n - 1, base - 1, -1):
            o = ops[i]
            m = 0.0
            for j in succ.get(i, ()):
                if blev[j] > m:
                    m = blev[j]
            blev[i] = m + o.cost + (o.lat if o.kind == "dma" else 60.0)
        etime = {e: 0.0 for e in self.ENG}
        lasttag = {e: None for e in self.ENG}
        finish = {}
        rtime = {}
        ready = {e: [] for e in self.ENG}
        for i in range(base, n):
            if indeg[i] == 0:
                rtime[i] = 0.0
                ready[ops[i].q].append(i)
        order = []
        PRIO = self.prio
        while len(order) < n - base:
            bestk, besti = None, None
            for e in self.ENG:
                lst = ready[e]
                if not lst:
                    continue
                t = etime[e]
                cand, ck = None, None
                for i in lst:
                    st = rtime[i] if rtime[i] > t else t
                    if e == "act" and ops[i].tag is not None and lasttag["act"] not in (None, ops[i].tag):
                        st += 1300.0
                    k = (st, -blev[i], i) if PRIO else (st, i)
                    if ck is None or k < ck:
                        cand, ck = i, k
                if bestk is None or ck < bestk:
                    bestk, besti = ck, cand
            i = besti
            o = ops[i]
            st = bestk[0]
            c = o.cost
            if o.tag is not None and o.q == "act":
                lasttag["act"] = o.tag
            etime[o.q] = st + c
            finish[i] = st + c + (o.lat if o.kind == "dma" else 60.0)
            ready[o.q].remove(i)
            order.append(i)
            for j in succ.get(i, ()):
                indeg[j] -= 1
                rt = rtime.get(j, 0.0)
                if finish[i] > rt:
                    rtime[j] = finish[i]
                elif j not in rtime:
                    rtime[j] = rt
                if indeg[j] == 0:
                    ready[ops[j].q].append(j)
        self.est_span = max(etime.values())
        return order

    def barrier(self):
        self.flush()
        toks = [(e, self.sem[e], self.cnt[e]) for e in self.ENG if self.cnt[e] > 0]
        toks += [("d_" + s, self.dsem[s], self.dcnt[s]) for s in self.dsem if self.dcnt[s] > 0]
        for e in self.ENG:
            self._wait(e, toks)

    def emit(self):
        self.flush()
        nc = self.nc
        with nc.Block() as block:
            def run(eng, e):
                sem = self.sem[eng]
                for item in self.streams[eng]:
                    if item[0] == "wait":
                        e.wait_ge(item[1], item[2])
                    elif item[0] == "op":
                        ins = item[1](e)
                        if item[2]:
                            ins.then_inc(sem, 1)
                    else:
                        item[1](e).then_inc(item[2], 16)

            @block.tensor
            def _(e):
                run("pe", e)

            @block.scalar
            def _(e):
                run("act", e)

            @block.vector
            def _(e):
                run("dve", e)

            @block.gpsimd
            def _(e):
                run("pool", e)

            @block.sync
            def _(e):
                run("sp", e)


def bc_mid(a, k):
    return bass.AP(a.tensor, a.offset, [list(a.ap[0]), [0, k]] + [list(x) for x in a.ap[1:]])


def bc_last(a, m):
    return bass.AP(a.tensor, a.offset, [list(x) for x in a.ap] + [[0, m]])


class Arena:
    def __init__(self, t, total):
        self.t = t
        self.total = total
        self.off = 0

    def f32(self, n):
        assert self.off + n <= self.total, ("arena overflow", self.off, n, self.total)
        a = self.t[:, self.off:self.off + n]
        self.off += n
        return a

    def bf(self, n):
        w = (n + 1) // 2
        return self.f32(w).bitcast(BF16)[:, 0:n]


import os
KSTOP = os.environ.get('KSTOP', '')
KSUB = int(os.environ.get('KSUB', 0))


def build_program():
    nc = bass.Bass("TRN2", target_bir_lowering=False)
    din = {}

    def inp(name, shape, dt=F32):
        din[name] = nc.dram_tensor(name, list(shape), dt, kind="ExternalInput").ap()
        return din[name]

    xc = inp("xc", [NT * 128, 1024])
    posc = inp("posc", [128, NT], I32)
    valid = inp("valid", [128, NT])
    c_dt = inp("c_dt", [128, 1024])
    c_xi = inp("c_xi", [128, 512])
    c_zeta = inp("c_zeta", [128, 512])
    c_dec = inp("c_dec", [128, 4])
    c_invf = inp("c_invf", [128, 192])
    c_off = inp("c_off", [128, 192])
    c_mask = inp("c_mask", [128, 128])
    b_anw = inp("b_anw", [128, 1024])
    b_fnw = inp("b_fnw", [128, 1024])
    b_onw = inp("b_onw", [128, 1024])
    b_qnw = inp("b_qnw", [128, 256])
    b_kvnw = inp("b_kvnw", [128, 128])
    b_gnw = inp("b_gnw", [128, 512])
    c_cw = inp("c_cw", [128, 44 * 3])
    c_cb = inp("c_cb", [128, 44])
    w_in_l = inp("w_in_l", [128, 8 * 2464])
    w_uq_l = inp("w_uq_l", [128, 2 * 768])
    wk_l = inp("wk_l", [128, 512])
    wv_l = inp("wv_l", [128, 512])
    w_out_l = inp("w_out_l", [128, 8 * 1024])
    w_up_l = inp("w_up_l", [128, 44 * 1024])
    w_down_l = inp("w_down_l", [128, 22 * 1024])
    yout = nc.dram_tensor("yout", [4096, 1024], F32, kind="ExternalOutput").ap()

    s_wup = nc.dram_tensor("s_wup", [128, 44 * 1024], BF16).ap()
    s_wdown = nc.dram_tensor("s_wdown", [128, 22 * 1024], BF16).ap()
    s_wout = nc.dram_tensor("s_wout", [128, 8 * 1024], BF16).ap()
    s_kt = nc.dram_tensor("s_kt", [8, 96, NT * 128], BF16).ap()
    s_v = nc.dram_tensor("s_v", [8, 128, NT * 65], BF16).ap()
    s_qt = nc.dram_tensor("s_qt", [8, 96, NOWN * 128], BF16).ap()
    s_mix = nc.dram_tensor("s_mix", [NOWN, 128, 8, 128], BF16).ap()

    with ExitStack() as st:
        S = Sched(nc, st)
        TOT = 53000
        arena_t = st.enter_context(nc.sbuf_tensor("arena", [128, TOT], F32))
        ps = st.enter_context(nc.psum_tensor("ps", [128, 4096], F32))
        A = Arena(arena_t, TOT)

        def bank(i, n=512):
            return ps[:, i * 512:i * 512 + n]

        PB = [Res("psb%d" % i, excl=True) for i in range(8)]

        ident = A.bf(128)
        maskb = A.bf(128)
        R_smix = [Res("s_mix%d" % i) for i in range(NOWN)]
        R_ident, R_mask = Res("ident"), Res("mask")
        mark_persist = A.off

        S.op("pool", lambda e: e.memset(ident, 0.0), writes=[R_ident])
        S.op("pool", lambda e: e.affine_select(out=ident, in_=ident, pattern=[[-1, 128]], compare_op=ALU.not_equal,
                                               fill=1.0, base=0, channel_multiplier=1), reads=[R_ident], writes=[R_ident])

        w_in = A.bf(8 * 2464)
        w_in3 = w_in.rearrange("p (c f) -> p c f", c=8)
        w_uq = A.bf(2 * 768)
        w_uq3 = w_uq.rearrange("p (c f) -> p c f", c=2)
        wk = A.bf(512)
        wv = A.bf(512)
        R_win, R_wuq, R_wk, R_wv = Res("w_in"), Res("w_uq"), Res("wk"), Res("wv")
        stage = [A.f32(512) for _ in range(2)]
        stageb = [A.bf(512) for _ in range(2)]
        R_stage = [Res("stage0"), Res("stage1")]
        R_stageb = [Res("stageb0"), Res("stageb1")]
        R_scr = {k: Res(k) for k in ["s_wup", "s_wdown", "s_wout"]}
        pieces = []

        def add_pieces(src, ncols, dst_sb=None, dst_res=None, dst_dram=None, dram_res=None):
            c0 = 0
            while c0 < ncols:
                n = min(512, ncols - c0)
                pieces.append((src, c0, n, dst_sb, dst_res, dst_dram, dram_res))
                c0 += n

        def piece_load_cast(k):
            src, c0, n, dst_sb, dst_res, dst_dram, dram_res = pieces[k]
            i = k % 2
            S.dma("act", "stg%d" % i, [lambda e: e.dma_start(out=stage[i][:, 0:n], in_=src[:, c0:c0 + n])], writes=[R_stage[i]])
            if dst_sb is not None:
                S.op("pool", lambda e: e.tensor_copy(out=dst_sb[:, c0:c0 + n], in_=stage[i][:, 0:n]), reads=[R_stage[i]], writes=[dst_res])
            else:
                S.op("pool", lambda e: e.tensor_copy(out=stageb[i][:, 0:n], in_=stage[i][:, 0:n]), reads=[R_stage[i]], writes=[R_stageb[i]])

        def piece_store(k):
            src, c0, n, dst_sb, dst_res, dst_dram, dram_res = pieces[k]
            i = k % 2
            if dst_dram is not None:
                S.dma("act", "stb%d" % i, [lambda e: e.dma_start(out=dst_dram[:, c0:c0 + n], in_=stageb[i][:, 0:n])],
                      reads=[R_stageb[i]], writes=[Res("wscr")])

        add_pieces(w_in_l, 8 * 2464, dst_sb=w_in, dst_res=R_win)
        add_pieces(w_uq_l, 2 * 768, dst_sb=w_uq, dst_res=R_wuq)
        add_pieces(wk_l, 512, dst_sb=wk, dst_res=R_wk)
        add_pieces(wv_l, 512, dst_sb=wv, dst_res=R_wv)
        add_pieces(c_mask, 128, dst_sb=maskb, dst_res=R_mask)
        n_first = len(pieces)
        add_pieces(w_out_l, 8 * 1024, dst_dram=s_wout, dram_res=R_scr["s_wout"])
        add_pieces(w_down_l, 22 * 1024, dst_dram=s_wdown, dram_res=R_scr["s_wdown"])
        add_pieces(w_up_l, 44 * 1024, dst_dram=s_wup, dram_res=R_scr["s_wup"])
        for k in range(n_first):
            piece_load_cast(k)
        pk = [n_first, n_first]

        def cast_step(nload):
            for _ in range(nload):
                if pk[0] < len(pieces):
                    piece_load_cast(pk[0])
                    piece_store(pk[0])
                    pk[0] += 1
            pk[1] = pk[0]

        def load_const(src, n, name, dt=F32):
            a = A.f32(n)
            if dt is not F32:
                a = a.bitcast(dt)
            r = Res(name)
            S.dma("sp", "c_" + name, [lambda e: e.dma_start(out=a, in_=src)], writes=[r])
            return a, r

        dt_t, R_dt = load_const(c_dt, 1024, "dt")
        xi_t, R_xi = load_const(c_xi, 512, "xi")
        zeta_t, R_zeta = load_const(c_zeta, 512, "zeta")
        dec_t, R_dec = load_const(c_dec, 4, "dec")
        invf_t, R_invf = load_const(c_invf, 192, "invf")
        off_t, R_off = load_const(c_off, 192, "off")
        anw_t, R_anw = load_const(b_anw, 1024, "anw")
        qnw_t, R_qnw = load_const(b_qnw, 256, "qnw")
        kvnw_t, R_kvnw = load_const(b_kvnw, 128, "kvnw")
        gnw_t, R_gnw = load_const(b_gnw, 512, "gnw")
        posi, R_pos = load_const(posc, NT, "posi", I32)
        valid_t, R_valid = load_const(valid, NT, "valid")
        posf = A.f32(NT)
        S.op("dve", lambda e: e.tensor_copy(out=posf, in_=posi), reads=[R_pos], writes=[R_pos])

        TB = 8
        tab = A.f32(TB * 192)
        tab3 = tab.rearrange("p (n f) -> p n f", n=TB)
        R_tab = Res("tab")
        ttmp = A.f32(192)
        tti = A.f32(192).bitcast(I32)
        R_tt = Res("ttmp")

        def make_tables(n0):
            for n in range(n0, n0 + TB):
                S.op("dve", lambda e, n=n: e.scalar_tensor_tensor(out=ttmp, in0=invf_t, scalar=posf[:, n:n + 1], in1=off_t,
                                                                op0=ALU.mult, op1=ALU.add),
                     reads=[R_invf, R_off, R_pos], writes=[R_tt])
                S.op("dve", lambda e: e.tensor_copy(out=tti, in_=ttmp), reads=[R_tt], writes=[R_tt])
                S.op("dve", lambda e, n=n: e.tensor_tensor(out=tab3[:, n % TB, :], in0=ttmp, in1=tti, op=ALU.subtract),
                     reads=[R_tt], writes=[R_tab])
            S.op("act", lambda e: e.activation(out=tab, in_=tab, func=AF.Sin, scale=TWO_PI * (1.0 - 1e-6)),
                 reads=[R_tab], writes=[R_tab])

        xbuf = [A.f32(1024) for _ in range(2)]
        R_x = [Res("x0"), Res("x1")]
        R32 = A.f32(512)
        Rb = A.bf(512)
        R_R32, R_Rb = Res("R32"), Res("Rb")
        DB = {}
        for nm, kind, sz in [("st_small", "f", 64), ("hb", "b", 1024), ("hT", "b", 1024), ("qk_sb", "f", 1024), ("tmpA", "f", 1024),
                             ("tmpB", "f", 1024), ("qkr", "b", 1024), ("kz", "b", 512), ("vb", "b", 512), ("sg", "f", 512),
                             ("qT", "b", 1024), ("qxT", "b", 512), ("kT", "b", 512), ("sd", "b", 1024), ("gn1", "f", 512),
                             ("gn2", "f", 512), ("yb", "b", 512), ("yT", "b", 512), ("cqn", "b", 256), ("ckvn", "b", 128), ("kr", "b", 32),
                             ("cqnT", "b", 256), ("ckvnT", "b", 128), ("mt1", "f", 64), ("qb", "b", 768), ("lat_sb", "f", 416),
                             ("tA2", "f", 32), ("tB2", "f", 32), ("tA3", "f", 256), ("tB3", "f", 256)]:
            DB[nm] = [((A.f32(sz) if kind == "f" else A.bf(sz)), Res(nm + "_%d" % i)) for i in range(2)]
        kn_g = [A.bf(4 * 512) for _ in range(2)]
        kpe_g = [A.bf(512) for _ in range(2)]
        v_g = [A.bf(4 * 8 * 65) for _ in range(2)]
        q_g = [A.bf(8 * 512) for _ in range(2)]
        R_kng = [Res("kng0"), Res("kng1")]
        R_kpg = [Res("kpg0"), Res("kpg1")]
        R_vg = [Res("vg0"), Res("vg1")]
        R_qg = [Res("qg0"), Res("qg1")]
        R_skt, R_sv, R_sqt = Res("s_kt"), Res("s_v"), Res("s_qt")

        for i_ in range(2):
            S.op("dve", lambda e, i_=i_: e.memset(DB["qT"][i_][0], 0.0), writes=[DB["qT"][i_][1]])
        S.op("dve", lambda e: e.memset(R32, 0.0), writes=[R_R32])
        S.op("dve", lambda e: e.memset(Rb, 0.0), writes=[R_Rb])

        x_tiles = xc.rearrange("(n p) d -> n p d", p=128)

        def load_x(n):
            i = n % 2
            S.dma("sp", "x%d" % i, [lambda e, n=n, i=i: e.dma_start(out=xbuf[i], in_=x_tiles[n])], writes=[R_x[i]])

        if KSTOP == 'A0':
            npz = int(os.environ.get('KNP', 0))
            while pk[1] < min(len(pieces), n_first + npz):
                cast_step(2)
            S.barrier(); S.emit(); return nc
        load_x(0)

        def _tileA(n):
            cast_step(3)
            b = n % 2
            (st_small, R_ss), (hb, R_hb), (hT, R_hT), (qk_sb, R_qksb), (tmpA, R_tA), (tmpB, R_tB) = [DB[k][b] for k in ("st_small", "hb", "hT", "qk_sb", "tmpA", "tmpB")]
            (qkr, R_qkr), (kz, R_kz), (vb, R_vb), (sg, R_sg), (qT, R_qT), (qxT, R_qxT), (kT, R_kT) = [DB[k][b] for k in ("qkr", "kz", "vb", "sg", "qT", "qxT", "kT")]
            (sd, R_sd), (gn1, R_gn1), (gn2, R_gn2), (yb, R_yb), (yT, R_yT), (cqn, R_cqn), (ckvn, R_ckvn), (kr, R_kr) = [DB[k][b] for k in ("sd", "gn1", "gn2", "yb", "yT", "cqn", "ckvn", "kr")]
            (cqnT, R_cqnT), (ckvnT, R_ckvnT), (mt1, R_mt), (qb, R_qb) = [DB[k][b] for k in ("cqnT", "ckvnT", "mt1", "qb")]
            hT3 = hT.rearrange("p (c t) -> p c t", c=8)
            (lat_sb, R_lat), (tA2, R_tA2), (tB2, R_tB2), (tA3, R_tA3), (tB3, R_tB3) = [DB[k][b] for k in ("lat_sb", "tA2", "tB2", "tA3", "tB3")]
            full = n >= HALO
            g, gi = divmod(n, 4)
            gb = g % 2
            if n + 1 < NT:
                load_x(n + 1)
            xb_, Rx = xbuf[n % 2], R_x[n % 2]
            ss = st_small[:, 0:1]
            rstd = st_small[:, 1:2]
            S.op("act", lambda e, xb_=xb_: e.activation(out=hb, in_=xb_, func=AF.Square, accum_out=ss),
                 reads=[Rx], writes=[R_hb, R_ss])
            S.op("dve", lambda e: e.tensor_scalar(out=ss, in0=ss, scalar1=1.0 / 1024, scalar2=EPS, op0=ALU.mult, op1=ALU.add),
                 reads=[R_ss], writes=[R_ss])
            S.op("act", lambda e: e.activation(out=ss, in_=ss, func=AF.Sqrt), reads=[R_ss], writes=[R_ss])
            S.op("dve", lambda e: e.reciprocal(out=rstd, in_=ss), reads=[R_ss], writes=[R_ss])
            S.op("dve", lambda e, xb_=xb_: e.scalar_tensor_tensor(out=hb, in0=xb_, scalar=rstd, in1=anw_t, op0=ALU.mult, op1=ALU.mult),
                 reads=[Rx, R_ss, R_anw], writes=[R_hb])
            tb = bank(0).bitcast(BF16)
            tbh = bank(4).bitcast(BF16)
            for c in range(8):
                S.op("pe", lambda e, c=c: e.transpose(out=tbh[:, c * 128:(c + 1) * 128], in_=hb[:, c * 128:(c + 1) * 128], identity=ident),
                     reads=[R_hb, R_ident], writes=[PB[4]], inc=(c == 7))
            S.op("act", lambda e: e.activation(out=hT, in_=tbh, func=AF.Copy), reads=[PB[4]], writes=[R_hT])
            if KSUB == 1 and n >= HALO:
                return


            def proj(bk, col0, ncol, n_=None):
                for c in range(8):
                    S.op("pe", lambda e, c=c: e.matmul(bank(bk, ncol), lhsT=hT3[:, c, :], rhs=w_in3[:, c, col0:col0 + ncol],
                                                       start=(c == 0), stop=(c == 7)),
                         reads=[R_hT, R_win], writes=[PB[bk]], inc=(c == 7))

            if full:
                proj(1, 0, 512)
            proj(2, 512, 512)
            proj(3, 1024, 512)
            if full:
                proj(4, 1536, 512)
                S.op("act", lambda e: e.activation(out=qk_sb, in_=ps[:, 512:1536], func=AF.Copy), reads=[PB[1], PB[2]], writes=[R_qksb])
                proj(1, 2048, 416)
                S.op("act", lambda e: e.activation(out=lat_sb, in_=bank(1, 416), func=AF.Copy), reads=[PB[1]], writes=[R_lat])
            else:
                S.op("act", lambda e: e.activation(out=qk_sb[:, 512:1024], in_=ps[:, 1024:1536], func=AF.Copy), reads=[PB[2]], writes=[R_qksb])
                proj(1, 2304, 160)
                S.op("act", lambda e: e.activation(out=lat_sb[:, 0:160], in_=bank(1, 160), func=AF.Copy), reads=[PB[1]], writes=[R_lat])
            if KSUB == 2 and n >= HALO:
                return

            latoff = 0 if full else -256
            if n % TB == 0:
                make_tables(n)
            tabn = tab3[:, n % TB, :]
            cs_r, ss_r = tabn[:, 0:64], tabn[:, 64:128]
            cs_m, ss_m = tabn[:, 128:160], tabn[:, 160:192]

            def rope(src_ap, nh, hd, cs, sn, dstA, dstB, dst, reads, wres, RA=None, RB=None, add_eng="dve"):
                RA = R_tA if RA is None else RA
                RB = R_tB if RB is None else RB
                half = hd // 2
                x3 = src_ap.rearrange("p (h d) -> p h d", h=nh)
                sw = bass.AP(src_ap.tensor, src_ap.offset + half,
                             [list(src_ap.ap[0]), [hd, nh], [-half, 2], [1, half]])
                a3 = dstA.rearrange("p (h d) -> p h d", h=nh)
                b4 = dstB.rearrange("p (h a d) -> p h a d", h=nh, a=2)
                S.op("dve", lambda e: e.tensor_tensor(out=a3, in0=x3, in1=bc_mid(cs, nh), op=ALU.mult),
                     reads=reads + [R_tab], writes=[RA])
                S.op("dve", lambda e: e.tensor_tensor(out=b4, in0=sw, in1=bc_mid(sn.rearrange("p (a d) -> p a d", a=2), nh), op=ALU.mult),
                     reads=reads + [R_tab], writes=[RB])
                S.op(add_eng, lambda e: e.tensor_tensor(out=dst, in0=dstA, in1=dstB, op=ALU.add),
                     reads=[RA, RB], writes=[wres])

            if full:
                rope(qk_sb, 16, 64, cs_r, ss_r, tmpA, tmpB, qkr, [R_qksb], R_qkr, add_eng="pool")
            else:
                rope(qk_sb[:, 512:1024], 8, 64, cs_r, ss_r, tmpA[:, 0:512], tmpB[:, 0:512], qkr[:, 512:1024], [R_qksb], R_qkr, add_eng="pool")
            S.op("pool", lambda e: e.tensor_tensor(out=kz, in0=qkr[:, 512:1024], in1=zeta_t, op=ALU.mult),
                 reads=[R_qkr, R_zeta], writes=[R_kz])
            S.op("act", lambda e: e.activation(out=vb, in_=bank(3), func=AF.Copy), reads=[PB[3]], writes=[R_vb])
            if KSUB == 3 and n >= HALO:
                return


            if full:
                S.op("act", lambda e: e.activation(out=sg, in_=bank(4), func=AF.Silu), reads=[PB[4]], writes=[R_sg])
                for c in range(8):
                    S.op("pe", lambda e, c=c: e.transpose(out=tb[:, c * 128:(c + 1) * 128], in_=qkr[:, c * 128:(c + 1) * 128], identity=ident),
                         reads=[R_qkr, R_ident], writes=[PB[0]], inc=(c == 7))
                S.op("act", lambda e: e.activation(out=qT[0:64, 0:512], in_=tb[0:64, 0:512], func=AF.Copy), reads=[PB[0]], writes=[R_qT])
                S.op("act", lambda e: e.activation(out=qT[64:128, 512:1024], in_=tb[64:128, 0:512], func=AF.Copy), reads=[PB[0]], writes=[R_qT])
                S.op("dve", lambda e: e.tensor_tensor(out=qxT, in0=tb[:, 0:512], in1=xi_t, op=ALU.mult), reads=[PB[0], R_xi], writes=[R_qxT])
                S.op("act", lambda e: e.activation(out=kT, in_=tb[:, 512:1024], func=AF.Copy), reads=[PB[0]], writes=[R_kT])
                for h in range(8):
                    p_, a_ = divmod(h, 2)
                    rows = slice(a_ * 64, a_ * 64 + 64)
                    S.op("pe", lambda e, h=h, p_=p_, rows=rows: e.matmul(ps[:, 2560 + h * 128:2560 + (h + 1) * 128],
                                                                         lhsT=kT[:, p_ * 128:(p_ + 1) * 128],
                                                                         rhs=qT[:, (h % 2) * 512 + p_ * 128:(h % 2) * 512 + (p_ + 1) * 128],
                                                                         start=True, stop=True),
                         reads=[R_kT, R_qT], writes=[PB[5], PB[6]], inc=(h == 7))
                S.op("dve", lambda e: e.tensor_tensor(out=sd, in0=ps[:, 2560:3584], in1=dt_t, op=ALU.mult),
                     reads=[PB[5], PB[6], R_dt], writes=[R_sd])
                for h in range(8):
                    p_, a_ = divmod(h, 2)
                    rows = slice(a_ * 64, a_ * 64 + 64)
                    S.op("pe", lambda e, h=h: e.matmul(ps[:, 3584 + h * 64:3584 + (h + 1) * 64], lhsT=sd[:, h * 128:(h + 1) * 128],
                                                       rhs=vb[:, h * 64:(h + 1) * 64], start=True, stop=False),
                         reads=[R_sd, R_vb], writes=[PB[7]], inc=False)
                    S.op("pe", lambda e, h=h, p_=p_, rows=rows: e.matmul(ps[:, 3584 + h * 64:3584 + (h + 1) * 64],
                                                                         lhsT=qxT[:, p_ * 128:(p_ + 1) * 128],
                                                                         rhs=Rb[:, h * 64:(h + 1) * 64],
                                                                         start=False, stop=True),
                         reads=[R_qxT, R_Rb], writes=[PB[7]], inc=(h == 7))
            for p_ in range(4):
                S.op("pe", lambda e, p_=p_: e.matmul(ps[:, 2560 + p_ * 128:2560 + (p_ + 1) * 128], lhsT=kz[:, p_ * 128:(p_ + 1) * 128],
                                                     rhs=vb[:, p_ * 128:(p_ + 1) * 128], start=True, stop=True),
                     reads=[R_kz, R_vb], writes=[PB[5]], inc=(p_ == 3))
            for h in range(8):
                p_, a_ = divmod(h, 2)
                rows = slice(a_ * 64, a_ * 64 + 64)
                S.op("dve", lambda e, h=h, p_=p_, a_=a_, rows=rows: e.scalar_tensor_tensor(
                    out=R32[rows, h * 64:(h + 1) * 64], in0=R32[rows, h * 64:(h + 1) * 64], scalar=dec_t[rows, p_:p_ + 1],
                    in1=ps[rows, 2560 + p_ * 128 + a_ * 64:2560 + p_ * 128 + a_ * 64 + 64], op0=ALU.mult, op1=ALU.add),
                    reads=[PB[5], R_dec, R_R32], writes=[R_R32])
            S.op("act", lambda e: e.activation(out=Rb, in_=R32, func=AF.Copy), reads=[R_R32], writes=[R_Rb])
            if KSUB == 4 and n >= HALO:
                return


            if full:
                o3 = bank(7).rearrange("p (h d) -> p h d", h=8)
                s1, s2, mean, msq, var = (mt1[:, 0:8], mt1[:, 8:16], mt1[:, 16:24], mt1[:, 24:32], mt1[:, 32:40])
                S.op("dve", lambda e: e.tensor_reduce(out=s1, in_=o3, axis=AX.X, op=ALU.add), reads=[PB[7]], writes=[R_mt])
                S.op("act", lambda e: e.activation(out=gn1, in_=bank(7), func=AF.Square), reads=[PB[7]], writes=[R_gn1])
                S.op("dve", lambda e: e.tensor_reduce(out=s2, in_=gn1.rearrange("p (h d) -> p h d", h=8), axis=AX.X, op=ALU.add),
                     reads=[R_gn1], writes=[R_mt])
                S.op("dve", lambda e: e.tensor_scalar(out=mean, in0=s1, scalar1=1.0 / 64, scalar2=None, op0=ALU.mult), reads=[R_mt], writes=[R_mt])
                S.op("dve", lambda e: e.tensor_tensor(out=msq, in0=mean, in1=mean, op=ALU.mult), reads=[R_mt], writes=[R_mt])
                S.op("dve", lambda e: e.scalar_tensor_tensor(out=var, in0=s2, scalar=1.0 / 64, in1=msq, op0=ALU.mult, op1=ALU.subtract),
                     reads=[R_mt], writes=[R_mt])
                S.op("dve", lambda e: e.tensor_scalar(out=var, in0=var, scalar1=EPS, scalar2=None, op0=ALU.add), reads=[R_mt], writes=[R_mt])
                S.op("act", lambda e: e.activation(out=var, in_=var, func=AF.Sqrt), reads=[R_mt], writes=[R_mt])
                S.op("dve", lambda e: e.reciprocal(out=var, in_=var), reads=[R_mt], writes=[R_mt])
                g13 = gn1.rearrange("p (h d) -> p h d", h=8)
                S.op("dve", lambda e: e.tensor_tensor(out=g13, in0=o3, in1=bc_last(mean, 64), op=ALU.subtract),
                     reads=[PB[7], R_mt], writes=[R_gn1])
                S.op("dve", lambda e: e.tensor_tensor(out=g13, in0=g13, in1=bc_last(var, 64), op=ALU.mult), reads=[R_gn1, R_mt], writes=[R_gn1])
                S.op("pool", lambda e: e.tensor_tensor(out=gn2, in0=sg, in1=gnw_t, op=ALU.mult), reads=[R_sg, R_gnw], writes=[R_gn2])
                S.op("dve", lambda e: e.tensor_tensor(out=yb, in0=gn1, in1=gn2, op=ALU.mult), reads=[R_gn1, R_gn2], writes=[R_yb])
                for c in range(4):
                    S.op("pe", lambda e, c=c: e.transpose(out=tb[:, c * 128:(c + 1) * 128], in_=yb[:, c * 128:(c + 1) * 128], identity=ident),
                         reads=[R_yb, R_ident], writes=[PB[0]], inc=(c == 3))
                m = n - HALO
                S.op("act", lambda e: e.activation(out=yT, in_=tb[:, 0:512], func=AF.Copy), reads=[PB[0]], writes=[R_yT])
                S.dma("sp", "ymr%d" % b, [lambda e: e.dma_start(out=s_mix[m, :, 0:4, :], in_=yT.rearrange("p (c t) -> p c t", c=4))],
                      reads=[R_yT], writes=[R_smix[m]], nbytes=131072)

            lat = lat_sb
            ckv_ap = lat[:, 256 + latoff:384 + latoff]
            kpe_ap = lat[:, 384 + latoff:416 + latoff]
            ssq = st_small[:, 4:6]
            rq = st_small[:, 6:8]
            if full:
                S.op("act", lambda e: e.activation(out=cqn, in_=lat[:, 0:256], func=AF.Square, accum_out=ssq[:, 0:1]),
                     reads=[R_lat], writes=[R_cqn, R_ss])
            S.op("act", lambda e: e.activation(out=ckvn, in_=ckv_ap, func=AF.Square, accum_out=ssq[:, 1:2]),
                 reads=[R_lat], writes=[R_ckvn, R_ss])
            if full:
                S.op("dve", lambda e: e.tensor_scalar(out=ssq[:, 0:1], in0=ssq[:, 0:1], scalar1=1.0 / 256, scalar2=EPS, op0=ALU.mult, op1=ALU.add),
                     reads=[R_ss], writes=[R_ss])
            S.op("dve", lambda e: e.tensor_scalar(out=ssq[:, 1:2], in0=ssq[:, 1:2], scalar1=1.0 / 128, scalar2=EPS, op0=ALU.mult, op1=ALU.add),
                 reads=[R_ss], writes=[R_ss])
            lo = 0 if full else 1
            S.op("act", lambda e, lo=lo: e.activation(out=ssq[:, lo:2], in_=ssq[:, lo:2], func=AF.Sqrt), reads=[R_ss], writes=[R_ss])
            S.op("dve", lambda e, lo=lo: e.reciprocal(out=rq[:, lo:2], in_=ssq[:, lo:2]), reads=[R_ss], writes=[R_ss])
            if full:
                S.op("dve", lambda e: e.scalar_tensor_tensor(out=cqn, in0=lat[:, 0:256], scalar=rq[:, 0:1], in1=qnw_t, op0=ALU.mult, op1=ALU.mult),
                     reads=[R_lat, R_ss, R_qnw], writes=[R_cqn])
            S.op("dve", lambda e: e.scalar_tensor_tensor(out=ckvn, in0=ckv_ap, scalar=rq[:, 1:2], in1=kvnw_t, op0=ALU.mult, op1=ALU.mult),
                 reads=[R_lat, R_ss, R_kvnw], writes=[R_ckvn])
            rope(kpe_ap, 1, 32, cs_m, ss_m, tA2, tB2, kr, [R_lat], R_kr, RA=R_tA2, RB=R_tB2)
            if KSUB == 5 and n >= HALO:
                return

            S.op("pe", lambda e: e.transpose(out=tb[:, 0:128], in_=ckvn, identity=ident), reads=[R_ckvn, R_ident], writes=[PB[0]], inc=False)
            S.op("pe", lambda e: e.transpose(out=tb[0:32, 128:256], in_=kr, identity=ident), reads=[R_kr, R_ident], writes=[PB[0]], inc=not full)
            if full:
                for c in range(2):
                    S.op("pe", lambda e, c=c: e.transpose(out=tb[:, 256 + c * 128:256 + (c + 1) * 128], in_=cqn[:, c * 128:(c + 1) * 128], identity=ident),
                         reads=[R_cqn, R_ident], writes=[PB[0]], inc=(c == 1))
            S.op("act", lambda e: e.activation(out=ckvnT, in_=tb[:, 0:128], func=AF.Copy), reads=[PB[0]], writes=[R_ckvnT])
            S.op("act", lambda e, gb=gb, gi=gi: e.activation(out=kpe_g[gb][0:32, gi * 128:(gi + 1) * 128], in_=tb[0:32, 128:256], func=AF.Copy),
                 reads=[PB[0]], writes=[R_kpg[gb]])
            if KSUB == 6 and n >= HALO:
                return

            if full:
                S.op("act", lambda e: e.activation(out=cqnT, in_=tb[:, 256:512], func=AF.Copy), reads=[PB[0]], writes=[R_cqnT])
            for p_ in range(4):
                S.op("pe", lambda e, p_=p_: e.matmul(ps[:, 3072 + p_ * 128:3072 + (p_ + 1) * 128], lhsT=wk[:, p_ * 128:(p_ + 1) * 128], rhs=ckvnT,
                                                     start=True, stop=True), reads=[R_wk, R_ckvnT], writes=[PB[6]], inc=(p_ == 3))
            kng3 = kn_g[gb].rearrange("p (a k) -> p a k", a=4)
            S.op("act", lambda e, gi=gi, kng3=kng3: e.activation(out=kng3[:, :, gi * 128:(gi + 1) * 128], in_=bank(6).rearrange("p (a k) -> p a k", a=4), func=AF.Copy),
                 reads=[PB[6]], writes=[R_kng[gb]])
            S.op("pe", lambda e: e.matmul(bank(5), lhsT=ckvnT, rhs=wv, start=True, stop=True), reads=[R_wv, R_ckvnT], writes=[PB[5]])
            vg4 = v_g[gb].rearrange("p (h t e) -> p h t e", h=8, t=4)
            S.op("dve", lambda e, gi=gi, vg4=vg4: e.tensor_copy(out=vg4[:, :, gi, 0:64], in_=bank(5).rearrange("p (h e) -> p h e", h=8)),
                 reads=[PB[5]], writes=[R_vg[gb]])
            S.op("dve", lambda e, gi=gi, vg4=vg4, n=n: e.tensor_copy(out=vg4[:, :, gi, 64:65], in_=bc_mid(valid_t[:, n:n + 1], 8)),
                 reads=[R_valid], writes=[R_vg[gb]])
            if KSUB == 7 and n >= HALO:
                return

            if full:
                for hf in range(2):
                    for c in range(2):
                        S.op("pe", lambda e, hf=hf, c=c: e.matmul(ps[:, (5 + hf) * 512:(5 + hf) * 512 + 384], lhsT=cqnT[:, c * 128:(c + 1) * 128],
                                                                  rhs=w_uq3[:, c, hf * 384:(hf + 1) * 384], start=(c == 0), stop=(c == 1)),
                             reads=[R_cqnT, R_wuq], writes=[PB[5 + hf]], inc=(c == 1))
                qb3 = qb.rearrange("p (h d) -> p h d", h=8)
                for hf in range(2):
                    src = ps[:, (5 + hf) * 512:(5 + hf) * 512 + 384]
                    s3 = src.rearrange("p (h d) -> p h d", h=4)
                    S.op("act", lambda e, hf=hf, s3=s3: e.activation(out=qb3[:, hf * 4:(hf + 1) * 4, 0:64], in_=s3[:, :, 0:64], func=AF.Copy),
                         reads=[PB[5 + hf]], writes=[R_qb])
                    x3 = s3[:, :, 64:96]
                    sw = bass.AP(src.tensor, src.offset + 64 + 16, [list(src.ap[0]), [96, 4], [-16, 2], [1, 16]])
                    a3 = tA3[:, hf * 128:(hf + 1) * 128].rearrange("p (h d) -> p h d", h=4)
                    b4 = tB3[:, hf * 128:(hf + 1) * 128].rearrange("p (h a d) -> p h a d", h=4, a=2)
                    S.op("dve", lambda e, x3=x3, a3=a3: e.tensor_tensor(out=a3, in0=x3, in1=bc_mid(cs_m, 4), op=ALU.mult),
                         reads=[PB[5 + hf], R_tab], writes=[R_tA3])
                    S.op("dve", lambda e, sw=sw, b4=b4: e.tensor_tensor(out=b4, in0=sw, in1=bc_mid(ss_m.rearrange("p (a d) -> p a d", a=2), 4), op=ALU.mult),
                         reads=[PB[5 + hf], R_tab], writes=[R_tB3])
                    S.op("dve", lambda e, hf=hf, a3=a3: e.tensor_tensor(out=qb3[:, hf * 4:(hf + 1) * 4, 64:96], in0=a3,
                                                                        in1=tB3[:, hf * 128:(hf + 1) * 128].rearrange("p (h d) -> p h d", h=4), op=ALU.add),
                         reads=[R_tA3, R_tB3], writes=[R_qb])
                for h in range(8):
                    S.op("pe", lambda e, h=h: e.transpose(out=tb[0:96, h * 128:(h + 1) * 128], in_=qb[:, h * 96:(h + 1) * 96], identity=ident),
                         reads=[R_qb, R_ident], writes=[PB[0]], inc=(h == 7))
                mg, mi = divmod(n - HALO, 4)
                qg3 = q_g[mg % 2].rearrange("p (h t) -> p h t", h=8)
                S.op("act", lambda e, mi=mi, qg3=qg3: e.activation(out=qg3[0:96, :, mi * 128:(mi + 1) * 128],
                                                                   in_=tb[0:96, :].rearrange("p (h t) -> p h t", h=8), func=AF.Copy),
                     reads=[PB[0]], writes=[R_qg[mg % 2]])
                if mi == 3 or n == NT - 1:
                    ntl = mi + 1
                    S.dma("sp", "sq%d" % (mg % 2),
                          [lambda e, mg=mg, ntl=ntl, qg3=qg3: e.dma_start(
                              out=s_qt[:, :, mg * 512:mg * 512 + ntl * 128].rearrange("h r t -> r h t"),
                              in_=qg3[0:96, :, 0:ntl * 128])],
                          reads=[R_qg[mg % 2]], writes=[Res("sqt")])
            if gi == 3:
                fns = []
                for a_ in range(2):
                    fns.append(lambda e, a_=a_, g=g, kng3=kng3: e.dma_start(
                        out=s_kt[:, 0:64, g * 512:(g + 1) * 512].rearrange("(p a) r k -> a r p k", a=2)[a_],
                        in_=kng3[a_ * 64:(a_ + 1) * 64, :, :]))
                fns.append(lambda e, g=g, gb=gb: e.dma_start(
                    out=s_kt[:, 64:96, g * 512:(g + 1) * 512].rearrange("h r k -> r h k"),
                    in_=bc_mid(kpe_g[gb][0:32, :], 8)))
                S.dma("sp", "sk%d" % gb, fns, reads=[R_kng[gb], R_kpg[gb]], writes=[Res("skt")])
                S.dma("sp", "sv%d" % gb,
                      [lambda e, g=g, vg4=vg4: e.dma_start(
                          out=s_v[:, :, g * 4 * 65:(g + 1) * 4 * 65].rearrange("h p (t e) -> p h t e", t=4),
                          in_=vg4)],
                      reads=[R_vg[gb]], writes=[Res("sv")])

        for n in range(int(os.environ.get('KNT', NT))):
            _tileA(n)
        while pk[1] < len(pieces):
            cast_step(2)

        S.barrier()
        if KSTOP == 'A':
            S.emit(); return nc
        A.off = mark_persist
        ytmp = [A.bf(512) for _ in range(2)]
        R_ytmp = [Res("ytmp0"), Res("ytmp1")]
        QT = [A.bf(NOWN * 128) for _ in range(2)]
        KT = [A.bf(NT * 128) for _ in range(2)]
        VV = [A.bf(NT * 65) for _ in range(2)]
        R_Q, R_K, R_V = [Res("Q0"), Res("Q1")], [Res("K0"), Res("K1")], [Res("V0"), Res("V1")]
        PT = [A.bf(1024) for _ in range(3)]
        R_PT = [Res("PT%d" % i) for i in range(3)]
        rrow = A.f32(512)
        R_rrow = Res("rrow")
        ones_t = A.f32(64)
        R_ones = Res("ones")
        bcs = A.f32(512)
        R_bcs = Res("bcs")
        S.op("dve", lambda e: e.memset(ones_t, 1.0), writes=[R_ones])
        scale = (64 + 32) ** -0.5

        def load_head(h):
            i = h % 2
            S.dma("sp", "lq%d" % i, [lambda e: e.dma_start(out=QT[i][0:96, :], in_=s_qt[h])], reads=[R_sqt], writes=[R_Q[i]])
            S.dma("sp", "lk%d" % i, [lambda e: e.dma_start(out=KT[i][0:96, :], in_=s_kt[h])], reads=[R_skt], writes=[R_K[i]])
            S.dma("sp", "lv%d" % i, [lambda e: e.dma_start(out=VV[i], in_=s_v[h])], reads=[R_sv], writes=[R_V[i]])

        load_head(0)

        groups = []
        blk_id = 0
        for h in range(8):
            qblocks = [(0, 128, [(kt, 0) for kt in range(HALO)] + [(HALO, 0)], HALO)]
            for j in range(8):
                kts = [(kt, 0) for kt in range(32 + 4 * j)] + [(32 + 4 * j + m, 128 * m) for m in range(4)]
                qblocks.append((128 + 512 * j, 512, kts, 32 + 4 * j))
            for (q0, qw, kts, diag0) in qblocks:
                npairs = (len(kts) + 1) // 2
                for gidx in range(npairs):
                    groups.append(dict(h=h, i=h % 2, q0=q0, qw=qw, pair=kts[2 * gidx:2 * gidx + 2], diag0=diag0,
                                       ob=4 + blk_id % 2, first=(gidx == 0), last=(gidx == npairs - 1),
                                       sb=(len(groups) % 2) * 2, pt=len(groups) % 3, yi=blk_id % 2,
                                       newhead=(gidx == 0 and q0 == 0)))
                blk_id += 1

        def emit_qk(g):
            i, q0, qw, sb_ = g["i"], g["q0"], g["qw"], g["sb"]
            if g["newhead"] and g["h"] + 1 < 8:
                load_head(g["h"] + 1)
            for u, (kt, c0) in enumerate(g["pair"]):
                dst = ps[:, (sb_ + u) * 512 + c0:(sb_ + u) * 512 + qw]
                isdiag = kt >= g["diag0"]
                S.op("pe", lambda e, kt=kt, c0=c0, dst=dst, isdiag=isdiag: e.matmul(
                    dst, lhsT=KT[i][0:96, kt * 128:(kt + 1) * 128], rhs=QT[i][0:96, q0 + c0:q0 + qw], start=True, stop=not isdiag),
                    reads=[R_K[i], R_Q[i]], writes=[PB[sb_ + u]], inc=not isdiag)
                if isdiag:
                    S.op("pe", lambda e, dst=dst: e.matmul(dst[:, 0:128], lhsT=ident, rhs=maskb, start=False, stop=True),
                         reads=[R_ident, R_mask], writes=[PB[sb_ + u]])

        def emit_exp(g):
            qw, sb_, pair = g["qw"], g["sb"], g["pair"]
            pt, Rpt = PT[g["pt"]], R_PT[g["pt"]]
            if len(pair) == 2 and pair[0][1] == 0 and pair[1][1] == 0 and qw == 512:
                S.op("act", lambda e: e.activation(out=pt, in_=ps[:, sb_ * 512:sb_ * 512 + 1024], func=AF.Exp, scale=scale),
                     reads=[PB[sb_], PB[sb_ + 1]], writes=[Rpt])
            else:
                for u, (kt, c0) in enumerate(pair):
                    S.op("act", lambda e, u=u, c0=c0: e.activation(
                        out=pt[:, u * 512 + c0:u * 512 + qw], in_=ps[:, (sb_ + u) * 512 + c0:(sb_ + u) * 512 + qw], func=AF.Exp, scale=scale),
                        reads=[PB[sb_ + u]], writes=[Rpt])

        def emit_pv(g):
            i, qw, ob, pair = g["i"], g["qw"], g["ob"], g["pair"]
            pt, Rpt = PT[g["pt"]], R_PT[g["pt"]]
            V3 = VV[i].rearrange("p (t e) -> p t e", t=NT)
            for u, (kt, c0) in enumerate(pair):
                first = g["first"] and u == 0
                last = g["last"] and u == len(pair) - 1
                S.op("pe", lambda e, kt=kt, c0=c0, u=u, first=first, last=last: e.matmul(
                    ps[0:65, ob * 512 + c0:ob * 512 + qw], lhsT=V3[:, kt, :], rhs=pt[:, u * 512 + c0:u * 512 + qw], start=first, stop=last),
                    reads=[R_V[i], Rpt], writes=[PB[ob]], inc=(u == len(pair) - 1))

        def emit_norm(g):
            h, q0, qw, ob, yi_ = g["h"], g["q0"], g["qw"], g["ob"], g["yi"]
            S.op("dve", lambda e: e.tensor_scalar(out=rrow[64:65, 0:qw], in0=ps[64:65, ob * 512:ob * 512 + qw], scalar1=1e-30, scalar2=None, op0=ALU.max),
                 reads=[PB[ob]], writes=[R_rrow])
            S.op("dve", lambda e: e.reciprocal(out=rrow[64:65, 0:qw], in_=rrow[64:65, 0:qw]), reads=[R_rrow], writes=[R_rrow])
            S.op("pe", lambda e: e.matmul(ps[0:64, 6 * 512:6 * 512 + qw], lhsT=ones_t[64:65, 0:64], rhs=rrow[64:65, 0:qw], start=True, stop=True),
                 reads=[R_ones, R_rrow], writes=[PB[6]])
            S.op("dve", lambda e: e.tensor_copy(out=bcs[0:64, 0:qw], in_=ps[0:64, 6 * 512:6 * 512 + qw]), reads=[PB[6]], writes=[R_bcs])
            S.op("dve", lambda e: e.tensor_tensor(out=ytmp[yi_][0:64, 0:qw], in0=ps[0:64, ob * 512:ob * 512 + qw], in1=bcs[0:64, 0:qw], op=ALU.mult),
                 reads=[PB[ob], R_bcs], writes=[R_ytmp[yi_]])
            m0_, nt_ = q0 // 128, qw // 128
            S.dma("sp", "ym%d" % yi_, [lambda e: e.dma_start(
                out=s_mix[m0_:m0_ + nt_, (h % 2) * 64:(h % 2) * 64 + 64, 4 + h // 2, :].rearrange("m r t -> r m t"),
                in_=ytmp[yi_][0:64, 0:qw].rearrange("r (m t) -> r m t", m=nt_))],
                reads=[R_ytmp[yi_]], writes=R_smix[m0_:m0_ + nt_], nbytes=65536)

        pend_norm = None
        emit_qk(groups[0])
        for gi_, g in enumerate(groups):
            emit_exp(g)
            if gi_ + 1 < len(groups):
                emit_qk(groups[gi_ + 1])
            emit_pv(g)
            if pend_norm is not None:
                emit_norm(pend_norm)
                pend_norm = None
            if g["last"]:
                pend_norm = g
        if pend_norm is not None:
            emit_norm(pend_norm)

        S.barrier()
        if KSTOP == 'AB':
            S.emit(); return nc
        A.off = mark_persist
        wout = A.bf(8 * 1024)
        wout3 = wout.rearrange("p (c f) -> p c f", c=8)
        wdown = A.bf(22 * 1024)
        wdown3 = wdown.rearrange("p (c f) -> p c f", c=22)
        R_wout, R_wdown = Res("wout"), Res("wdown")
        S.dma("sp", "wc1", [lambda e: e.dma_start(out=wout, in_=s_wout)], reads=[R_scr["s_wout"]], writes=[R_wout])
        S.dma("sp", "wc2", [lambda e: e.dma_start(out=wdown, in_=s_wdown)], reads=[R_scr["s_wdown"]], writes=[R_wdown])
        fnw_t, R_fnw = load_const(b_fnw, 1024, "fnw")
        onw_t, R_onw = load_const(b_onw, 1024, "onw")
        cw_t, R_cw = load_const(c_cw, 132, "cw")
        cw3 = cw_t.rearrange("p (c j) -> p c j", c=44)
        cb_t, R_cb = load_const(c_cb, 44, "cb")
        mixb = [A.bf(1024) for _ in range(3)]
        R_mixb = [Res("mixb%d" % i) for i in range(3)]
        NWU = 3
        wupb = [A.bf(2048) for _ in range(NWU)]
        R_wupb = [Res("wup%d" % i) for i in range(NWU)]
        xb2 = [A.f32(1024) for _ in range(3)]
        R_xb2 = [Res("xb2_%d" % i) for i in range(3)]
        x1_ = A.f32(4 * 1024)
        x1 = [x1_, x1_]
        R_x1_ = [Res("x1_%d" % i) for i in range(4)]
        R_x1 = [R_x1_, R_x1_]
        h2b = [A.bf(1024) for _ in range(2)]
        R_h2b = [Res("h2b0"), Res("h2b1")]
        jnk = A.bf(1024)
        R_jnk = Res("jnk")
        h2T = [A.bf(8 * 512) for _ in range(2)]
        R_h2T = [Res("h2T0"), Res("h2T1")]
        gT = [A.bf(22 * 512) for _ in range(2)]
        R_gT = [Res("gT0"), Res("gT1")]
        ubuf = [A.f32(514) for _ in range(2)]
        R_ub = [Res("ub0"), Res("ub1")]
        acc = [[A.f32(512) for _ in range(2)] for _ in range(2)]
        R_acc = [[Res("acc%d%d" % (h_, p_)) for p_ in range(2)] for h_ in range(2)]
        carry = A.f32(44 * 2)
        carry3 = carry.rearrange("p (c j) -> p c j", c=44)
        R_carry = [Res("carry%d" % i) for i in range(44)]
        st2 = [A.f32(8) for _ in range(2)]
        R_st2 = [Res("st2_0"), Res("st2_1")]
        ybuf = xb2
        R_yb2 = R_xb2
        S.op("dve", lambda e: e.memset(carry, 0.0), writes=R_carry)
        wupi = [0]
        xli = [0]
        ybi = [0]
        y_tiles = yout.rearrange("(n p) d -> n p d", p=128)

        blocks = [(0, 1)] + [(1 + 4 * j, 4) for j in range(8)]
        wuc = [0]

        def _blockC(bi, m0, ntl):
            W = ntl * 128
            pb = bi % 2
            x13 = x1[pb].rearrange("p (t d) -> p t d", t=4)
            Rx1 = R_x1[pb]
            h2T3 = h2T[pb].rearrange("p (c t) -> p c t", c=8)
            gT3 = gT[pb].rearrange("p (c t) -> p c t", c=22)
            tb = bank(7).bitcast(BF16)

            def _s1(t):
                m = m0 + t
                xi_ = xli[0] % 3
                xli[0] += 1
                S.dma("sp", "xc%d" % xi_, [lambda e: e.dma_start(out=xb2[xi_], in_=x_tiles[HALO + m])], writes=[R_xb2[xi_]], nbytes=524288)
                mi_ = m % 3
                S.dma("sp", "mx%d" % mi_, [lambda e: e.dma_start(out=mixb[mi_].rearrange("p (c t) -> p c t", c=8), in_=s_mix[m])],
                      reads=[R_smix[m]], writes=[R_mixb[mi_]])
                mix3 = mixb[mi_].rearrange("p (c t) -> p c t", c=8)
                for hf in range(2):
                    for c in range(8):
                        S.op("pe", lambda e, c=c, hf=hf: e.matmul(bank(hf), lhsT=mix3[:, c, :], rhs=wout3[:, c, hf * 512:(hf + 1) * 512],
                                                                  start=(c == 0), stop=(c == 7)),
                             reads=[R_mixb[mi_], R_wout], writes=[PB[hf]], inc=(c == 7))
                S.op("dve", lambda e: e.tensor_tensor(out=x13[:, t, :], in0=ps[:, 0:1024], in1=xb2[xi_], op=ALU.add),
                     reads=[PB[0], PB[1], R_xb2[xi_]], writes=[Rx1[t]])
                tp = t % 2
                hb_, Rhb = h2b[tp], R_h2b[tp]
                ss, rstd, Rst = st2[tp][:, 0:1], st2[tp][:, 1:2], R_st2[tp]
                S.op("act", lambda e: e.activation(out=hb_, in_=x13[:, t, :], func=AF.Square, accum_out=ss), reads=[Rx1[t]], writes=[Rhb, Rst])
                S.op("dve", lambda e: e.tensor_scalar(out=ss, in0=ss, scalar1=1.0 / 1024, scalar2=EPS, op0=ALU.mult, op1=ALU.add), reads=[Rst], writes=[Rst])
                S.op("act", lambda e: e.activation(out=ss, in_=ss, func=AF.Sqrt), reads=[Rst], writes=[Rst])
                S.op("dve", lambda e: e.reciprocal(out=rstd, in_=ss), reads=[Rst], writes=[Rst])
                S.op("dve", lambda e: e.scalar_tensor_tensor(out=hb_, in0=x13[:, t, :], scalar=rstd, in1=fnw_t, op0=ALU.mult, op1=ALU.mult),
                     reads=[Rx1[t], Rst, R_fnw], writes=[Rhb])
                for c in range(8):
                    S.op("pe", lambda e, c=c: e.transpose(out=tb[:, c * 128:(c + 1) * 128], in_=hb_[:, c * 128:(c + 1) * 128], identity=ident),
                         reads=[Rhb, R_ident], writes=[PB[7]], inc=(c == 7))
                S.op("act", lambda e: e.activation(out=h2T3[:, :, t * 128:(t + 1) * 128], in_=tb.rearrange("p (c t) -> p c t", c=8), func=AF.Copy),
                     reads=[PB[7]], writes=[R_h2T[pb]])

            for t in range(ntl):
                _s1(t)

            def _chunk(fc, half):
                cidx = fc + 22 * half
                fp = fc % 2
                if half == 0:
                    wupi[0] += 1
                wi = wupi[0] % NWU
                if half == 0:
                    S.dma("sp", "wu%d" % wi, [lambda e: e.dma_start(out=wupb[wi], in_=s_wup[:, fc * 2048:(fc + 1) * 2048])],
                          writes=[R_wupb[wi]], nbytes=524288)
                bk = 2 + wuc[0] % 3
                wuc[0] += 1
                w3 = wupb[wi][:, half * 1024:(half + 1) * 1024].rearrange("p (c f) -> p c f", c=8)
                for c in range(8):
                    S.op("pe", lambda e, c=c: e.matmul(bank(bk, W), lhsT=w3[:, c, :], rhs=h2T3[:, c, 0:W], start=(c == 0), stop=(c == 7)),
                         reads=[R_wupb[wi], R_h2T[pb]], writes=[PB[bk]], inc=(c == 7))
                Rc = R_carry[cidx]
                if bi == 0:
                    S.op("dve", lambda e: e.tensor_copy(out=carry3[:, cidx, :], in_=bank(bk, W)[:, W - 2:W]), reads=[PB[bk]], writes=[Rc])
                    return
                ac, Rac = acc[half][fp], R_acc[half][fp]
                if half == 0:
                    ub, Rub = ubuf[fp], R_ub[fp]
                    S.op("act", lambda e: e.activation(out=ub[:, 0:2], in_=carry3[:, cidx, :], func=AF.Copy), reads=[Rc], writes=[Rub])
                    S.op("act", lambda e: e.activation(out=ub[:, 2:2 + W], in_=bank(bk, W), func=AF.Copy), reads=[PB[bk]], writes=[Rub])
                    S.op("act", lambda e: e.activation(out=ac[:, 0:W], in_=bank(bk, W), func=AF.Identity,
                                                       scale=cw3[:, cidx, 2:3], bias=cb_t[:, cidx:cidx + 1]),
                         reads=[PB[bk], R_cw, R_cb], writes=[Rac])
                    S.op("dve", lambda e: e.tensor_copy(out=carry3[:, cidx, :], in_=ub[:, W:W + 2]), reads=[Rub], writes=[Rc])
                    S.op("dve", lambda e: e.scalar_tensor_tensor(out=ac[:, 0:W], in0=ub[:, 1:1 + W], scalar=cw3[:, cidx, 1:2], in1=ac[:, 0:W],
                                                                 op0=ALU.mult, op1=ALU.add), reads=[Rub, Rac, R_cw], writes=[Rac])
                    S.op("dve", lambda e: e.scalar_tensor_tensor(out=ac[:, 0:W], in0=ub[:, 0:W], scalar=cw3[:, cidx, 0:1], in1=ac[:, 0:W],
                                                                 op0=ALU.mult, op1=ALU.add), reads=[Rub, Rac, R_cw], writes=[Rac])
                    S.op("act", lambda e: e.activation(out=ac[:, 0:W], in_=ac[:, 0:W], func=AF.Silu), reads=[Rac], writes=[Rac])
                else:
                    pu = bank(bk, W)
                    S.op("act", lambda e: e.activation(out=ac[:, 0:W], in_=pu, func=AF.Identity,
                                                       scale=cw3[:, cidx, 2:3], bias=cb_t[:, cidx:cidx + 1]),
                         reads=[PB[bk], R_cw, R_cb], writes=[Rac])
                    S.op("dve", lambda e: e.scalar_tensor_tensor(out=ac[:, 1:W], in0=pu[:, 0:W - 1], scalar=cw3[:, cidx, 1:2], in1=ac[:, 1:W],
                                                                 op0=ALU.mult, op1=ALU.add), reads=[PB[bk], Rac, R_cw], writes=[Rac])
                    S.op("dve", lambda e: e.scalar_tensor_tensor(out=ac[:, 2:W], in0=pu[:, 0:W - 2], scalar=cw3[:, cidx, 0:1], in1=ac[:, 2:W],
                                                                 op0=ALU.mult, op1=ALU.add), reads=[PB[bk], Rac, R_cw], writes=[Rac])
                    S.op("dve", lambda e: e.scalar_tensor_tensor(out=ac[:, 0:1], in0=carry3[:, cidx, 1:2], scalar=cw3[:, cidx, 1:2], in1=ac[:, 0:1],
                                                                 op0=ALU.mult, op1=ALU.add), reads=[Rc, Rac, R_cw], writes=[Rac])
                    S.op("dve", lambda e: e.scalar_tensor_tensor(out=ac[:, 0:2], in0=carry3[:, cidx, 0:2], scalar=cw3[:, cidx, 0:1], in1=ac[:, 0:2],
                                                                 op0=ALU.mult, op1=ALU.add), reads=[Rc, Rac, R_cw], writes=[Rac])
                    S.op("dve", lambda e: e.tensor_copy(out=carry3[:, cidx, :], in_=pu[:, W - 2:W]), reads=[PB[bk]], writes=[Rc])
                    S.op("pool", lambda e: e.tensor_tensor(out=gT3[:, fc, 0:W], in0=acc[0][fp][:, 0:W], in1=ac[:, 0:W], op=ALU.mult),
                         reads=[R_acc[0][fp], Rac], writes=[R_gT[pb]])

            for fc in range(22):
                for half in range(2):
                    _chunk(fc, half)
            if bi == 0:
                return

            def _s3(t):
                m = m0 + t
                for hf in range(2):
                    for fc in range(22):
                        S.op("pe", lambda e, fc=fc, hf=hf: e.matmul(bank(5 + hf), lhsT=gT3[:, fc, t * 128:(t + 1) * 128],
                                                                    rhs=wdown3[:, fc, hf * 512:(hf + 1) * 512], start=(fc == 0), stop=(fc == 21)),
                             reads=[R_gT[pb], R_wdown], writes=[PB[5 + hf]], inc=(fc == 21))
                yi = ybi[0] % 3
                ybi[0] += 1
                yb_, Ryb = ybuf[yi], R_yb2[yi]
                S.op("dve", lambda e: e.tensor_tensor(out=x13[:, t, :], in0=ps[:, 2560:3584], in1=x13[:, t, :], op=ALU.add),
                     reads=[PB[5], PB[6], Rx1[t]], writes=[Rx1[t]])
                tp = t % 2
                ss2, rstd2, Rst = st2[tp][:, 2:3], st2[tp][:, 3:4], R_st2[tp]
                S.op("act", lambda e: e.activation(out=jnk, in_=x13[:, t, :], func=AF.Square, accum_out=ss2), reads=[Rx1[t]], writes=[R_jnk, Rst])
                S.op("dve", lambda e: e.tensor_scalar(out=ss2, in0=ss2, scalar1=1.0 / 1024, scalar2=EPS, op0=ALU.mult, op1=ALU.add), reads=[Rst], writes=[Rst])
                S.op("act", lambda e: e.activation(out=ss2, in_=ss2, func=AF.Sqrt), reads=[Rst], writes=[Rst])
                S.op("dve", lambda e: e.reciprocal(out=rstd2, in_=ss2), reads=[Rst], writes=[Rst])
                S.op("dve", lambda e: e.scalar_tensor_tensor(out=yb_, in0=x13[:, t, :], scalar=rstd2, in1=onw_t, op0=ALU.mult, op1=ALU.mult),
                     reads=[Rx1[t], Rst, R_onw], writes=[Ryb])
                S.dma("sp", "yo%d" % yi, [lambda e: e.dma_start(out=y_tiles[m - 1], in_=yb_)], reads=[Ryb], nbytes=524288)

            for t in range(ntl):
                _s3(t)

        for bi, (m0, ntl) in enumerate(blocks):
            _blockC(bi, m0, ntl)

        S.barrier()
        S.emit()
    return nc


def _consts():
    H, C = 8, 128
    lg = np.log1p(-np.power(2.0, -5.0 - np.arange(H, dtype=np.float64)))
    idx = np.arange(C, dtype=np.float64)
    diff = idx[None, :] - idx[:, None]
    dt = np.where(diff[:, None, :] >= 0, np.exp(lg[None, :, None] * np.maximum(diff[:, None, :], 0.0)), 0.0) / 8.0
    c_dt = dt.reshape(128, 1024).astype(np.float32)
    xi = np.exp(lg[:, None] * (idx[None, :] + 1.0))
    c_xi = np.zeros((128, 4, 128))
    for p in range(4):
        for a in range(2):
            c_xi[a * 64:(a + 1) * 64, p, :] = xi[2 * p + a][None, :]
    c_xi = c_xi.reshape(128, 512).astype(np.float32)
    zeta = np.exp(lg[:, None] * (C - 1.0 - idx[None, :])) / 8.0
    c_zeta = np.repeat(zeta.T[:, :, None], 64, axis=2).reshape(128, 512).astype(np.float32)
    dec = np.exp(lg * C)
    c_dec = np.zeros((128, 4))
    for p in range(4):
        c_dec[0:64, p] = dec[2 * p]
        c_dec[64:128, p] = dec[2 * p + 1]
    c_dec = c_dec.astype(np.float32)
    fr = (10000.0 ** (-np.arange(0, 64, 2, dtype=np.float32) / np.float32(64))).astype(np.float32)
    fm = (10000.0 ** (-np.arange(0, 32, 2, dtype=np.float32) / np.float32(32))).astype(np.float32)
    invf = np.concatenate([fr, fr, fr, fr, fm, fm, fm, fm]).astype(np.float64) / (2 * np.pi)
    off = np.concatenate([np.full(64, 0.25), np.full(32, 0.5), np.zeros(32), np.full(32, 0.25), np.full(16, 0.5), np.zeros(16)])
    c_invf = np.broadcast_to(invf[None, :], (128, 192)).astype(np.float32).copy()
    c_off = np.broadcast_to(off[None, :], (128, 192)).astype(np.float32).copy()
    k = np.arange(128)
    c_mask = np.where(k[None, :] < k[:, None], -30000.0, 0.0).astype(np.float32)
    return dict(c_dt=c_dt, c_xi=c_xi, c_zeta=c_zeta, c_dec=c_dec, c_invf=c_invf, c_off=c_off, c_mask=c_mask)


def _bc(v, n=128):
    return np.ascontiguousarray(np.broadcast_to(np.asarray(v, np.float32)[None, :], (n, v.shape[0])))


_PROG = None


def kernel(x, positions, attn_norm_w, w_in, ret_gn_w, mla_q_norm_w, w_uq, mla_kv_norm_w, w_ukv,
           w_out, ffn_norm_w, w_up, conv_w, conv_b, w_down, final_norm_w):
    global _PROG
    x = np.asarray(x, np.float32)
    positions = np.asarray(positions, np.int32)
    shared = _consts()
    shared["b_anw"] = _bc(np.asarray(attn_norm_w)[0])
    shared["b_fnw"] = _bc(np.asarray(ffn_norm_w)[0])
    shared["b_onw"] = _bc(np.asarray(final_norm_w))
    shared["b_qnw"] = _bc(np.asarray(mla_q_norm_w)[0])
    shared["b_kvnw"] = _bc(np.asarray(mla_kv_norm_w)[0])
    shared["b_gnw"] = _bc(np.asarray(ret_gn_w)[0])
    cw = np.asarray(conv_w, np.float32)[0]
    shared["c_cw"] = np.ascontiguousarray(cw.reshape(3, 44, 128).transpose(2, 1, 0)).reshape(128, 132)
    shared["c_cb"] = np.ascontiguousarray(np.asarray(conv_b, np.float32)[0].reshape(44, 128).T)
    shared["w_in_l"] = np.ascontiguousarray(np.asarray(w_in, np.float32)[0].reshape(8, 128, 2464).transpose(1, 0, 2)).reshape(128, -1)
    shared["w_uq_l"] = np.ascontiguousarray(np.asarray(w_uq, np.float32)[0].reshape(2, 128, 768).transpose(1, 0, 2)).reshape(128, -1)
    wukv = np.asarray(w_ukv, np.float32)[0].reshape(128, 8, 128)
    shared["wk_l"] = np.ascontiguousarray(wukv[:, :, 0:64]).reshape(128, 512)
    shared["wv_l"] = np.ascontiguousarray(wukv[:, :, 64:128]).reshape(128, 512)
    wo = np.asarray(w_out, np.float32)[0]
    shared["w_out_l"] = np.ascontiguousarray(wo.reshape(8, 128, 1024).transpose(1, 0, 2)).reshape(128, -1)
    wu = np.asarray(w_up, np.float32)[0]
    shared["w_up_l"] = np.ascontiguousarray(wu.reshape(8, 128, 2, 22, 128).transpose(1, 3, 2, 0, 4)).reshape(128, -1)
    wd = np.asarray(w_down, np.float32)[0]
    shared["w_down_l"] = np.ascontiguousarray(wd.reshape(22, 128, 1024).transpose(1, 0, 2)).reshape(128, -1)

    in_maps = []
    for c in range(8):
        b, z = divmod(c, 2)
        m = dict(shared)
        if z == 1:
            xcore = x[b]
            pc = positions[b]
            vd = np.ones(8192, np.float32)
        else:
            xcore = np.concatenate([np.zeros((4096, 1024), np.float32), x[b, :4096]], axis=0)
            pc = np.concatenate([np.zeros(4096, np.int32), positions[b, :4096]])
            vd = np.concatenate([np.zeros(4096, np.float32), np.ones(4096, np.float32)])
        m["xc"] = np.ascontiguousarray(xcore)
        m["posc"] = np.ascontiguousarray(pc.reshape(NT, 128).T)
        m["valid"] = np.ascontiguousarray(vd.reshape(NT, 128).T)
        in_maps.append(m)
    if _PROG is None:
        _PROG = build_program()
    res = run_bass_kernel_spmd(_PROG, in_maps, core_ids=list(range(8)))
    out = np.empty((4, 8192, 1024), np.float32)
    for c in range(8):
        b, z = divmod(c, 2)
        out[b, z * 4096:(z + 1) * 4096] = res.results[c]["yout"]
    return out
```

```python
import math
import os
from contextlib import ExitStack
import numpy as np
import concourse.bass as bass
import concourse.mybir as mybir
from concourse.bass_utils import run_bass_kernel_spmd

F32 = mybir.dt.float32
BF16 = mybir.dt.bfloat16
I32 = mybir.dt.int32
ALU = mybir.AluOpType
AF = mybir.ActivationFunctionType
AX = mybir.AxisListType

NT = 64
HALO = 31
NOWN = 33
EPS = 1e-6
TWO_PI = 2.0 * math.pi


class Res:
    __slots__ = ("name", "w", "r", "excl")

    def __init__(self, name, excl=False):
        self.name = name
        self.w = None
        self.r = set()
        self.excl = excl


class _Rec:
    def __init__(self):
        self.call = None

    def __getattr__(self, name):
        def f(*a, **k):
            self.call = (name, a, k)
            return self
        return f


def _fsize(ap):
    n = 1
    for d in list(ap.shape)[1:]:
        n *= int(d)
    return n


_TAGGED = ("Sqrt", "Silu", "Sin", "Exp")


def _act_tag(fn):
    try:
        r = _Rec()
        fn(r)
        name, a, k = r.call
        f = str(k.get("func", ""))
        for t in _TAGGED:
            if f.endswith(t):
                return t
    except Exception:
        pass
    return None


def _est(eng, fn):
    try:
        r = _Rec()
        fn(r)
        name, a, k = r.call
        if eng == "pe":
            rhs = k.get("rhs", k.get("identity"))
            n = _fsize(rhs) if name == "matmul" else 128
            return max(n, 64) / 2.4 + 90.0
        out = k.get("out", a[0] if a else None)
        f = _fsize(out)
        if eng == "act":
            return (f + 224) / 1.2
        if eng == "dve":
            if name == "reciprocal":
                return 165 + (6.2 * f if int(out.shape[0]) < 32 else f)
            return (f + 150) / 0.96
        return 200 + (3.4 if name == 'tensor_copy' else 2.2) * f
    except Exception:
        return 500.0


class _Op:
    __slots__ = ("q", "kind", "fns", "deps", "cost", "lat", "slot", "seq", "tag")


class Sched:
    ENG = ("pe", "act", "dve", "pool", "sp")

    def __init__(self, nc, stack):
        self.nc = nc
        self.stack = stack
        self.sem = {e: stack.enter_context(nc.semaphore("s_" + e)) for e in self.ENG}
        self.cnt = {e: 0 for e in self.ENG}
        self.seen = {e: {} for e in self.ENG}
        self.streams = {e: [] for e in self.ENG}
        self.dsem = {}
        self.dcnt = {}
        self.ops = []
        self.base = 0
        self.open_pe = None
        self.reorder = True
        self.prio = bool(int(os.environ.get('KPRIO', '1')))

    def _record_deps(self, idx, reads, writes):
        deps = self.ops[idx].deps
        for r in reads:
            if r.w is not None and r.w >= self.base and r.w != idx:
                deps.add(r.w)
        for w in writes:
            if w.w is not None and w.w >= self.base and w.w != idx:
                deps.add(w.w)
            for t in w.r:
                if t >= self.base and t != idx:
                    deps.add(t)
        for r in reads:
            if r not in writes:
                r.r.add(idx)
        for w in writes:
            w.w = idx
            w.r = set()

    def op(self, eng, fn, reads=(), writes=(), inc=True, tag=None):
        ex = [r for r in reads if r.excl and r not in writes]
        if ex:
            writes = list(writes) + ex
        if eng == "pe" and self.open_pe is not None:
            idx = self.open_pe
            o = self.ops[idx]
            o.fns.append(fn)
            o.cost += _est(eng, fn)
        else:
            o = _Op()
            o.q, o.kind, o.fns, o.deps, o.cost, o.lat, o.slot, o.seq, o.tag = eng, "op", [fn], set(), _est(eng, fn), 0.0, None, None, (_act_tag(fn) if eng == "act" else None)
            idx = len(self.ops)
            self.ops.append(o)
        self._record_deps(idx, reads, writes)
        if eng == "pe":
            self.open_pe = None if inc else idx
        else:
            assert inc
        return idx

    def dma(self, eng, slot, fns, reads=(), writes=(), nbytes=262144):
        assert self.open_pe is None
        if slot not in self.dsem:
            self.dsem[slot] = self.stack.enter_context(self.nc.semaphore("d_" + slot))
            self.dcnt[slot] = 0
        o = _Op()
        o.q, o.kind, o.fns, o.deps, o.cost, o.slot, o.seq, o.tag = eng, "dma", list(fns), set(), 350.0 * len(fns), slot, None, None
        o.lat = 2000.0 + nbytes / 150.0
        idx = len(self.ops)
        self.ops.append(o)
        self._record_deps(idx, reads, writes)
        return idx

    def _wait(self, eng, toks):
        best = {}
        for t in toks:
            k, s, v = t
            if k not in best or best[k][2] < v:
                best[k] = t
        for k, (kk, s, v) in best.items():
            if self.seen[eng].get(k, 0) >= v:
                continue
            self.seen[eng][k] = v
            self.streams[eng].append(("wait", s, v))

    def _token(self, d):
        o = self.ops[d]
        if o.kind == "dma":
            return ("d_" + o.slot, self.dsem[o.slot], o.seq)
        return (o.q, self.sem[o.q], o.seq)

    def flush(self):
        assert self.open_pe is None
        ops, base = self.ops, self.base
        n = len(ops)
        if n == base:
            return
        if self.reorder:
            order = self._list_schedule(base, n)
        else:
            order = list(range(base, n))
        for i in order:
            o = ops[i]
            toks = []
            for d in o.deps:
                od = ops[d]
                if od.kind == "op" and od.q == "pe" and o.q == "pe" and o.kind == "op":
                    continue
                toks.append(self._token(d))
            self._wait(o.q, toks)
            if o.kind == "dma":
                for fn in o.fns:
                    self.dcnt[o.slot] += 16
                    self.streams[o.q].append(("dma", fn, self.dsem[o.slot]))
                o.seq = self.dcnt[o.slot]
            else:
                self.cnt[o.q] += 1
                o.seq = self.cnt[o.q]
                for j, fn in enumerate(o.fns):
                    self.streams[o.q].append(("op", fn, j == len(o.fns) - 1))
        self.base = n

    def _list_schedule(self, base, n):
        ops = self.ops
        indeg = {}
        succ = {}
        for i in range(base, n):
            dd = [d for d in ops[i].deps if d >= base]
            indeg[i] = len(dd)
            for d in dd:
                succ.setdefault(d, []).append(i)
        blev = {}
        for i in range(n - 1, base - 1, -1):
            o = ops[i]
            m = 0.0
            for j in succ.get(i, ()):
                if blev[j] > m:
                    m = blev[j]
            blev[i] = m + o.cost + (o.lat if o.kind == "dma" else 60.0)
        etime = {e: 0.0 for e in self.ENG}
        lasttag = {e: None for e in self.ENG}
        finish = {}
        rtime = {}
        ready = {e: [] for e in self.ENG}
        for i in range(base, n):
            if indeg[i] == 0:
                rtime[i] = 0.0
                ready[ops[i].q].append(i)
        order = []
        PRIO = self.prio
        while len(order) < n - base:
            bestk, besti = None, None
            for e in self.ENG:
                lst = ready[e]
                if not lst:
                    continue
                t = etime[e]
                cand, ck = None, None
                for i in lst:
                    st = rtime[i] if rtime[i] > t else t
                    if e == "act" and ops[i].tag is not None and lasttag["act"] not in (None, ops[i].tag):
                        st += 1300.0
                    k = (st, -blev[i], i) if PRIO else (st, i)
                    if ck is None or k < ck:
                        cand, ck = i, k
                if bestk is None or ck < bestk:
                    bestk, besti = ck, cand
            i = besti
            o = ops[i]
            st = bestk[0]
            c = o.cost
            if o.tag is not None and o.q == "act":
                lasttag["act"] = o.tag
            etime[o.q] = st + c
            finish[i] = st + c + (o.lat if o.kind == "dma" else 60.0)
            ready[o.q].remove(i)
            order.append(i)
            for j in succ.get(i, ()):
                indeg[j] -= 1
                rt = rtime.get(j, 0.0)
                if finish[i] > rt:
                    rtime[j] = finish[i]
                elif j not in rtime:
                    rtime[j] = rt
                if indeg[j] == 0:
                    ready[ops[j].q].append(j)
        self.est_span = max(etime.values())
        return order

    def barrier(self):
        self.flush()
        toks = [(e, self.sem[e], self.cnt[e]) for e in self.ENG if self.cnt[e] > 0]
        toks += [("d_" + s, self.dsem[s], self.dcnt[s]) for s in self.dsem if self.dcnt[s] > 0]
        for e in self.ENG:
            self._wait(e, toks)

    def emit(self):
        self.flush()
        nc = self.nc
        with nc.Block() as block:
            def run(eng, e):
                sem = self.sem[eng]
                for item in self.streams[eng]:
                    if item[0] == "wait":
                        e.wait_ge(item[1], item[2])
                    elif item[0] == "op":
                        ins = item[1](e)
                        if item[2]:
                            ins.then_inc(sem, 1)
                    else:
                        item[1](e).then_inc(item[2], 16)

            @block.tensor
            def _(e):
                run("pe", e)

            @block.scalar
            def _(e):
                run("act", e)

            @block.vector
            def _(e):
                run("dve", e)

            @block.gpsimd
            def _(e):
                run("pool", e)

            @block.sync
            def _(e):
                run("sp", e)


def bc_mid(a, k):
    return bass.AP(a.tensor, a.offset, [list(a.ap[0]), [0, k]] + [list(x) for x in a.ap[1:]])


def bc_last(a, m):
    return bass.AP(a.tensor, a.offset, [list(x) for x in a.ap] + [[0, m]])


class Arena:
    def __init__(self, t, total):
        self.t = t
        self.total = total
        self.off = 0

    def f32(self, n):
        assert self.off + n <= self.total, ("arena overflow", self.off, n, self.total)
        a = self.t[:, self.off:self.off + n]
        self.off += n
        return a

    def bf(self, n):
        w = (n + 1) // 2
        return self.f32(w).bitcast(BF16)[:, 0:n]


import os
KSTOP = os.environ.get('KSTOP', '')
KSUB = int(os.environ.get('KSUB', 0))


def build_program():
    nc = bass.Bass("TRN2", target_bir_lowering=False)
    din = {}

    def inp(name, shape, dt=F32):
        din[name] = nc.dram_tensor(name, list(shape), dt, kind="ExternalInput").ap()
        return din[name]

    xc = inp("xc", [NT * 128, 1024])
    posc = inp("posc", [128, NT], I32)
    valid = inp("valid", [128, NT])
    c_dt = inp("c_dt", [128, 1024])
    c_xi = inp("c_xi", [128, 512])
    c_zeta = inp("c_zeta", [128, 512])
    c_dec = inp("c_dec", [128, 4])
    c_invf = inp("c_invf", [128, 192])
    c_off = inp("c_off", [128, 192])
    c_mask = inp("c_mask", [128, 128])
    b_anw = inp("b_anw", [128, 1024])
    b_fnw = inp("b_fnw", [128, 1024])
    b_onw = inp("b_onw", [128, 1024])
    b_qnw = inp("b_qnw", [128, 256])
    b_kvnw = inp("b_kvnw", [128, 128])
    b_gnw = inp("b_gnw", [128, 512])
    c_cw = inp("c_cw", [128, 44 * 3])
    c_cb = inp("c_cb", [128, 44])
    w_in_l = inp("w_in_l", [128, 8 * 2464])
    w_uq_l = inp("w_uq_l", [128, 2 * 768])
    wk_l = inp("wk_l", [128, 512])
    wv_l = inp("wv_l", [128, 512])
    w_out_l = inp("w_out_l", [128, 8 * 1024])
    w_up_l = inp("w_up_l", [128, 44 * 1024])
    w_down_l = inp("w_down_l", [128, 22 * 1024])
    yout = nc.dram_tensor("yout", [4096, 1024], F32, kind="ExternalOutput").ap()

    s_wup = nc.dram_tensor("s_wup", [128, 44 * 1024], BF16).ap()
    s_wdown = nc.dram_tensor("s_wdown", [128, 22 * 1024], BF16).ap()
    s_wout = nc.dram_tensor("s_wout", [128, 8 * 1024], BF16).ap()
    s_kt = nc.dram_tensor("s_kt", [8, 96, NT * 128], BF16).ap()
    s_v = nc.dram_tensor("s_v", [8, 128, NT * 65], BF16).ap()
    s_qt = nc.dram_tensor("s_qt", [8, 96, NOWN * 128], BF16).ap()
    s_mix = nc.dram_tensor("s_mix", [NOWN, 128, 8, 128], BF16).ap()

    with ExitStack() as st:
        S = Sched(nc, st)
        TOT = 53000
        arena_t = st.enter_context(nc.sbuf_tensor("arena", [128, TOT], F32))
        ps = st.enter_context(nc.psum_tensor("ps", [128, 4096], F32))
        A = Arena(arena_t, TOT)

        def bank(i, n=512):
            return ps[:, i * 512:i * 512 + n]

        PB = [Res("psb%d" % i, excl=True) for i in range(8)]

        ident = A.bf(128)
        maskb = A.bf(128)
        R_smix = [Res("s_mix%d" % i) for i in range(NOWN)]
        R_ident, R_mask = Res("ident"), Res("mask")
        mark_persist = A.off

        S.op("pool", lambda e: e.memset(ident, 0.0), writes=[R_ident])
        S.op("pool", lambda e: e.affine_select(out=ident, in_=ident, pattern=[[-1, 128]], compare_op=ALU.not_equal,
                                               fill=1.0, base=0, channel_multiplier=1), reads=[R_ident], writes=[R_ident])

        w_in = A.bf(8 * 2464)
        w_in3 = w_in.rearrange("p (c f) -> p c f", c=8)
        w_uq = A.bf(2 * 768)
        w_uq3 = w_uq.rearrange("p (c f) -> p c f", c=2)
        wk = A.bf(512)
        wv = A.bf(512)
        R_win, R_wuq, R_wk, R_wv = Res("w_in"), Res("w_uq"), Res("wk"), Res("wv")
        stage = [A.f32(512) for _ in range(2)]
        stageb = [A.bf(512) for _ in range(2)]
        R_stage = [Res("stage0"), Res("stage1")]
        R_stageb = [Res("stageb0"), Res("stageb1")]
        R_scr = {k: Res(k) for k in ["s_wup", "s_wdown", "s_wout"]}
        pieces = []

        def add_pieces(src, ncols, dst_sb=None, dst_res=None, dst_dram=None, dram_res=None):
            c0 = 0
            while c0 < ncols:
                n = min(512, ncols - c0)
                pieces.append((src, c0, n, dst_sb, dst_res, dst_dram, dram_res))
                c0 += n

        def piece_load_cast(k):
            src, c0, n, dst_sb, dst_res, dst_dram, dram_res = pieces[k]
            i = k % 2
            S.dma("act", "stg%d" % i, [lambda e: e.dma_start(out=stage[i][:, 0:n], in_=src[:, c0:c0 + n])], writes=[R_stage[i]])
            if dst_sb is not None:
                S.op("pool", lambda e: e.tensor_copy(out=dst_sb[:, c0:c0 + n], in_=stage[i][:, 0:n]), reads=[R_stage[i]], writes=[dst_res])
            else:
                S.op("pool", lambda e: e.tensor_copy(out=stageb[i][:, 0:n], in_=stage[i][:, 0:n]), reads=[R_stage[i]], writes=[R_stageb[i]])

        def piece_store(k):
            src, c0, n, dst_sb, dst_res, dst_dram, dram_res = pieces[k]
            i = k % 2
            if dst_dram is not None:
                S.dma("act", "stb%d" % i, [lambda e: e.dma_start(out=dst_dram[:, c0:c0 + n], in_=stageb[i][:, 0:n])],
                      reads=[R_stageb[i]], writes=[Res("wscr")])

        add_pieces(w_in_l, 8 * 2464, dst_sb=w_in, dst_res=R_win)
        add_pieces(w_uq_l, 2 * 768, dst_sb=w_uq, dst_res=R_wuq)
        add_pieces(wk_l, 512, dst_sb=wk, dst_res=R_wk)
        add_pieces(wv_l, 512, dst_sb=wv, dst_res=R_wv)
        add_pieces(c_mask, 128, dst_sb=maskb, dst_res=R_mask)
        n_first = len(pieces)
        add_pieces(w_out_l, 8 * 1024, dst_dram=s_wout, dram_res=R_scr["s_wout"])
        add_pieces(w_down_l, 22 * 1024, dst_dram=s_wdown, dram_res=R_scr["s_wdown"])
        add_pieces(w_up_l, 44 * 1024, dst_dram=s_wup, dram_res=R_scr["s_wup"])
        for k in range(n_first):
            piece_load_cast(k)
        pk = [n_first, n_first]

        def cast_step(nload):
            for _ in range(nload):
                if pk[0] < len(pieces):
                    piece_load_cast(pk[0])
                    piece_store(pk[0])
                    pk[0] += 1
            pk[1] = pk[0]

        def load_const(src, n, name, dt=F32):
            a = A.f32(n)
            if dt is not F32:
                a = a.bitcast(dt)
            r = Res(name)
            S.dma("sp", "c_" + name, [lambda e: e.dma_start(out=a, in_=src)], writes=[r])
            return a, r

        dt_t, R_dt = load_const(c_dt, 1024, "dt")
        xi_t, R_xi = load_const(c_xi, 512, "xi")
        zeta_t, R_zeta = load_const(c_zeta, 512, "zeta")
        dec_t, R_dec = load_const(c_dec, 4, "dec")
        invf_t, R_invf = load_const(c_invf, 192, "invf")
        off_t, R_off = load_const(c_off, 192, "off")
        anw_t, R_anw = load_const(b_anw, 1024, "anw")
        qnw_t, R_qnw = load_const(b_qnw, 256, "qnw")
        kvnw_t, R_kvnw = load_const(b_kvnw, 128, "kvnw")
        gnw_t, R_gnw = load_const(b_gnw, 512, "gnw")
        posi, R_pos = load_const(posc, NT, "posi", I32)
        valid_t, R_valid = load_const(valid, NT, "valid")
        posf = A.f32(NT)
        S.op("dve", lambda e: e.tensor_copy(out=posf, in_=posi), reads=[R_pos], writes=[R_pos])

        TB = 8
        tab = A.f32(TB * 192)
        tab3 = tab.rearrange("p (n f) -> p n f", n=TB)
        R_tab = Res("tab")
        ttmp = A.f32(192)
        tti = A.f32(192).bitcast(I32)
        R_tt = Res("ttmp")

        def make_tables(n0):
            for n in range(n0, n0 + TB):
                S.op("dve", lambda e, n=n: e.scalar_tensor_tensor(out=ttmp, in0=invf_t, scalar=posf[:, n:n + 1], in1=off_t,
                                                                op0=ALU.mult, op1=ALU.add),
                     reads=[R_invf, R_off, R_pos], writes=[R_tt])
                S.op("dve", lambda e: e.tensor_copy(out=tti, in_=ttmp), reads=[R_tt], writes=[R_tt])
                S.op("dve", lambda e, n=n: e.tensor_tensor(out=tab3[:, n % TB, :], in0=ttmp, in1=tti, op=ALU.subtract),
                     reads=[R_tt], writes=[R_tab])
            S.op("act", lambda e: e.activation(out=tab, in_=tab, func=AF.Sin, scale=TWO_PI * (1.0 - 1e-6)),
                 reads=[R_tab], writes=[R_tab])

        xbuf = [A.f32(1024) for _ in range(2)]
        R_x = [Res("x0"), Res("x1")]
        R32 = A.f32(512)
        Rb = A.bf(512)
        R_R32, R_Rb = Res("R32"), Res("Rb")
        DB = {}
        for nm, kind, sz in [("st_small", "f", 64), ("hb", "b", 1024), ("hT", "b", 1024), ("qk_sb", "f", 1024), ("tmpA", "f", 1024),
                             ("tmpB", "f", 1024), ("qkr", "b", 1024), ("kz", "b", 512), ("vb", "b", 512), ("sg", "f", 512),
                             ("qT", "b", 1024), ("qxT", "b", 512), ("kT", "b", 512), ("sd", "b", 1024), ("gn1", "f", 512),
                             ("gn2", "f", 512), ("yb", "b", 512), ("yT", "b", 512), ("cqn", "b", 256), ("ckvn", "b", 128), ("kr", "b", 32),
                             ("cqnT", "b", 256), ("ckvnT", "b", 128), ("mt1", "f", 64), ("qb", "b", 768), ("lat_sb", "f", 416),
                             ("tA2", "f", 32), ("tB2", "f", 32), ("tA3", "f", 256), ("tB3", "f", 256)]:
            DB[nm] = [((A.f32(sz) if kind == "f" else A.bf(sz)), Res(nm + "_%d" % i)) for i in range(2)]
        kn_g = [A.bf(4 * 512) for _ in range(2)]
        kpe_g = [A.bf(512) for _ in range(2)]
        v_g = [A.bf(4 * 8 * 65) for _ in range(2)]
        q_g = [A.bf(8 * 512) for _ in range(2)]
        R_kng = [Res("kng0"), Res("kng1")]
        R_kpg = [Res("kpg0"), Res("kpg1")]
        R_vg = [Res("vg0"), Res("vg1")]
        R_qg = [Res("qg0"), Res("qg1")]
        R_skt, R_sv, R_sqt = Res("s_kt"), Res("s_v"), Res("s_qt")

        for i_ in range(2):
            S.op("dve", lambda e, i_=i_: e.memset(DB["qT"][i_][0], 0.0), writes=[DB["qT"][i_][1]])
        S.op("dve", lambda e: e.memset(R32, 0.0), writes=[R_R32])
        S.op("dve", lambda e: e.memset(Rb, 0.0), writes=[R_Rb])

        x_tiles = xc.rearrange("(n p) d -> n p d", p=128)

        def load_x(n):
            i = n % 2
            S.dma("sp", "x%d" % i, [lambda e, n=n, i=i: e.dma_start(out=xbuf[i], in_=x_tiles[n])], writes=[R_x[i]])

        if KSTOP == 'A0':
            npz = int(os.environ.get('KNP', 0))
            while pk[1] < min(len(pieces), n_first + npz):
                cast_step(2)
            S.barrier(); S.emit(); return nc
        load_x(0)

        def _tileA(n):
            cast_step(3)
            b = n % 2
            (st_small, R_ss), (hb, R_hb), (hT, R_hT), (qk_sb, R_qksb), (tmpA, R_tA), (tmpB, R_tB) = [DB[k][b] for k in ("st_small", "hb", "hT", "qk_sb", "tmpA", "tmpB")]
            (qkr, R_qkr), (kz, R_kz), (vb, R_vb), (sg, R_sg), (qT, R_qT), (qxT, R_qxT), (kT, R_kT) = [DB[k][b] for k in ("qkr", "kz", "vb", "sg", "qT", "qxT", "kT")]
            (sd, R_sd), (gn1, R_gn1), (gn2, R_gn2), (yb, R_yb), (yT, R_yT), (cqn, R_cqn), (ckvn, R_ckvn), (kr, R_kr) = [DB[k][b] for k in ("sd", "gn1", "gn2", "yb", "yT", "cqn", "ckvn", "kr")]
            (cqnT, R_cqnT), (ckvnT, R_ckvnT), (mt1, R_mt), (qb, R_qb) = [DB[k][b] for k in ("cqnT", "ckvnT", "mt1", "qb")]
            hT3 = hT.rearrange("p (c t) -> p c t", c=8)
            (lat_sb, R_lat), (tA2, R_tA2), (tB2, R_tB2), (tA3, R_tA3), (tB3, R_tB3) = [DB[k][b] for k in ("lat_sb", "tA2", "tB2", "tA3", "tB3")]
            full = n >= HALO
            g, gi = divmod(n, 4)
            gb = g % 2
            if n + 1 < NT:
                load_x(n + 1)
            xb_, Rx = xbuf[n % 2], R_x[n % 2]
            ss = st_small[:, 0:1]
            rstd = st_small[:, 1:2]
            S.op("act", lambda e, xb_=xb_: e.activation(out=hb, in_=xb_, func=AF.Square, accum_out=ss),
                 reads=[Rx], writes=[R_hb, R_ss])
            S.op("dve", lambda e: e.tensor_scalar(out=ss, in0=ss, scalar1=1.0 / 1024, scalar2=EPS, op0=ALU.mult, op1=ALU.add),
                 reads=[R_ss], writes=[R_ss])
            S.op("act", lambda e: e.activation(out=ss, in_=ss, func=AF.Sqrt), reads=[R_ss], writes=[R_ss])
            S.op("dve", lambda e: e.reciprocal(out=rstd, in_=ss), reads=[R_ss], writes=[R_ss])
            S.op("dve", lambda e, xb_=xb_: e.scalar_tensor_tensor(out=hb, in0=xb_, scalar=rstd, in1=anw_t, op0=ALU.mult, op1=ALU.mult),
                 reads=[Rx, R_ss, R_anw], writes=[R_hb])
            tb = bank(0).bitcast(BF16)
            tbh = bank(4).bitcast(BF16)
            for c in range(8):
                S.op("pe", lambda e, c=c: e.transpose(out=tbh[:, c * 128:(c + 1) * 128], in_=hb[:, c * 128:(c + 1) * 128], identity=ident),
                     reads=[R_hb, R_ident], writes=[PB[4]], inc=(c == 7))
            S.op("act", lambda e: e.activation(out=hT, in_=tbh, func=AF.Copy), reads=[PB[4]], writes=[R_hT])
            if KSUB == 1 and n >= HALO:
                return


            def proj(bk, col0, ncol, n_=None):
                for c in range(8):
                    S.op("pe", lambda e, c=c: e.matmul(bank(bk, ncol), lhsT=hT3[:, c, :], rhs=w_in3[:, c, col0:col0 + ncol],
                                                       start=(c == 0), stop=(c == 7)),
                         reads=[R_hT, R_win], writes=[PB[bk]], inc=(c == 7))

            if full:
                proj(1, 0, 512)
            proj(2, 512, 512)
            proj(3, 1024, 512)
            if full:
                proj(4, 1536, 512)
                S.op("act", lambda e: e.activation(out=qk_sb, in_=ps[:, 512:1536], func=AF.Copy), reads=[PB[1], PB[2]], writes=[R_qksb])
                proj(1, 2048, 416)
                S.op("act", lambda e: e.activation(out=lat_sb, in_=bank(1, 416), func=AF.Copy), reads=[PB[1]], writes=[R_lat])
            else:
                S.op("act", lambda e: e.activation(out=qk_sb[:, 512:1024], in_=ps[:, 1024:1536], func=AF.Copy), reads=[PB[2]], writes=[R_qksb])
                proj(1, 2304, 160)
                S.op("act", lambda e: e.activation(out=lat_sb[:, 0:160], in_=bank(1, 160), func=AF.Copy), reads=[PB[1]], writes=[R_lat])
            if KSUB == 2 and n >= HALO:
                return

            latoff = 0 if full else -256
            if n % TB == 0:
                make_tables(n)
            tabn = tab3[:, n % TB, :]
            cs_r, ss_r = tabn[:, 0:64], tabn[:, 64:128]
            cs_m, ss_m = tabn[:, 128:160], tabn[:, 160:192]

            def rope(src_ap, nh, hd, cs, sn, dstA, dstB, dst, reads, wres, RA=None, RB=None, add_eng="dve"):
                RA = R_tA if RA is None else RA
                RB = R_tB if RB is None else RB
                half = hd // 2
                x3 = src_ap.rearrange("p (h d) -> p h d", h=nh)
                sw = bass.AP(src_ap.tensor, src_ap.offset + half,
                             [list(src_ap.ap[0]), [hd, nh], [-half, 2], [1, half]])
                a3 = dstA.rearrange("p (h d) -> p h d", h=nh)
                b4 = dstB.rearrange("p (h a d) -> p h a d", h=nh, a=2)
                S.op("dve", lambda e: e.tensor_tensor(out=a3, in0=x3, in1=bc_mid(cs, nh), op=ALU.mult),
                     reads=reads + [R_tab], writes=[RA])
                S.op("dve", lambda e: e.tensor_tensor(out=b4, in0=sw, in1=bc_mid(sn.rearrange("p (a d) -> p a d", a=2), nh), op=ALU.mult),
                     reads=reads + [R_tab], writes=[RB])
                S.op(add_eng, lambda e: e.tensor_tensor(out=dst, in0=dstA, in1=dstB, op=ALU.add),
                     reads=[RA, RB], writes=[wres])

            if full:
                rope(qk_sb, 16, 64, cs_r, ss_r, tmpA, tmpB, qkr, [R_qksb], R_qkr, add_eng="dve")
            else:
                rope(qk_sb[:, 512:1024], 8, 64, cs_r, ss_r, tmpA[:, 0:512], tmpB[:, 0:512], qkr[:, 512:1024], [R_qksb], R_qkr, add_eng="dve")
            S.op("dve", lambda e: e.tensor_tensor(out=kz, in0=qkr[:, 512:1024], in1=zeta_t, op=ALU.mult),
                 reads=[R_qkr, R_zeta], writes=[R_kz])
            S.op("act", lambda e: e.activation(out=vb, in_=bank(3), func=AF.Copy), reads=[PB[3]], writes=[R_vb])
            if KSUB == 3 and n >= HALO:
                return


            if full:
                S.op("act", lambda e: e.activation(out=sg, in_=bank(4), func=AF.Silu), reads=[PB[4]], writes=[R_sg])
                for c in range(8):
                    S.op("pe", lambda e, c=c: e.transpose(out=tb[:, c * 128:(c + 1) * 128], in_=qkr[:, c * 128:(c + 1) * 128], identity=ident),
                         reads=[R_qkr, R_ident], writes=[PB[0]], inc=(c == 7))
                S.op("act", lambda e: e.activation(out=qT[0:64, 0:512], in_=tb[0:64, 0:512], func=AF.Copy), reads=[PB[0]], writes=[R_qT])
                S.op("act", lambda e: e.activation(out=qT[64:128, 512:1024], in_=tb[64:128, 0:512], func=AF.Copy), reads=[PB[0]], writes=[R_qT])
                S.op("dve", lambda e: e.tensor_tensor(out=qxT, in0=tb[:, 0:512], in1=xi_t, op=ALU.mult), reads=[PB[0], R_xi], writes=[R_qxT])
                S.op("act", lambda e: e.activation(out=kT, in_=tb[:, 512:1024], func=AF.Copy), reads=[PB[0]], writes=[R_kT])
                for h in range(8):
                    p_, a_ = divmod(h, 2)
                    rows = slice(a_ * 64, a_ * 64 + 64)
                    S.op("pe", lambda e, h=h, p_=p_, rows=rows: e.matmul(ps[:, 2560 + h * 128:2560 + (h + 1) * 128],
                                                                         lhsT=kT[:, p_ * 128:(p_ + 1) * 128],
                                                                         rhs=qT[:, (h % 2) * 512 + p_ * 128:(h % 2) * 512 + (p_ + 1) * 128],
                                                                         start=True, stop=True),
                         reads=[R_kT, R_qT], writes=[PB[5], PB[6]], inc=(h == 7))
                S.op("dve", lambda e: e.tensor_tensor(out=sd, in0=ps[:, 2560:3584], in1=dt_t, op=ALU.mult),
                     reads=[PB[5], PB[6], R_dt], writes=[R_sd])
                for h in range(8):
                    p_, a_ = divmod(h, 2)
                    rows = slice(a_ * 64, a_ * 64 + 64)
                    S.op("pe", lambda e, h=h: e.matmul(ps[:, 3584 + h * 64:3584 + (h + 1) * 64], lhsT=sd[:, h * 128:(h + 1) * 128],
                                                       rhs=vb[:, h * 64:(h + 1) * 64], start=True, stop=False),
                         reads=[R_sd, R_vb], writes=[PB[7]], inc=False)
                    S.op("pe", lambda e, h=h, p_=p_, rows=rows: e.matmul(ps[:, 3584 + h * 64:3584 + (h + 1) * 64],
                                                                         lhsT=qxT[:, p_ * 128:(p_ + 1) * 128],
                                                                         rhs=Rb[:, h * 64:(h + 1) * 64],
                                                                         start=False, stop=True),
                         reads=[R_qxT, R_Rb], writes=[PB[7]], inc=(h == 7))
            for p_ in range(4):
                S.op("pe", lambda e, p_=p_: e.matmul(ps[:, 2560 + p_ * 128:2560 + (p_ + 1) * 128], lhsT=kz[:, p_ * 128:(p_ + 1) * 128],
                                                     rhs=vb[:, p_ * 128:(p_ + 1) * 128], start=True, stop=True),
                     reads=[R_kz, R_vb], writes=[PB[5]], inc=(p_ == 3))
            for h in range(8):
                p_, a_ = divmod(h, 2)
                rows = slice(a_ * 64, a_ * 64 + 64)
                S.op("dve", lambda e, h=h, p_=p_, a_=a_, rows=rows: e.scalar_tensor_tensor(
                    out=R32[rows, h * 64:(h + 1) * 64], in0=R32[rows, h * 64:(h + 1) * 64], scalar=dec_t[rows, p_:p_ + 1],
                    in1=ps[rows, 2560 + p_ * 128 + a_ * 64:2560 + p_ * 128 + a_ * 64 + 64], op0=ALU.mult, op1=ALU.add),
                    reads=[PB[5], R_dec, R_R32], writes=[R_R32])
            S.op("act", lambda e: e.activation(out=Rb, in_=R32, func=AF.Copy), reads=[R_R32], writes=[R_Rb])
            if KSUB == 4 and n >= HALO:
                return


            if full:
                o3 = bank(7).rearrange("p (h d) -> p h d", h=8)
                s1, s2, mean, msq, var = (mt1[:, 0:8], mt1[:, 8:16], mt1[:, 16:24], mt1[:, 24:32], mt1[:, 32:40])
                S.op("dve", lambda e: e.tensor_reduce(out=s1, in_=o3, axis=AX.X, op=ALU.add), reads=[PB[7]], writes=[R_mt])
                S.op("act", lambda e: e.activation(out=gn1, in_=bank(7), func=AF.Square), reads=[PB[7]], writes=[R_gn1])
                S.op("dve", lambda e: e.tensor_reduce(out=s2, in_=gn1.rearrange("p (h d) -> p h d", h=8), axis=AX.X, op=ALU.add),
                     reads=[R_gn1], writes=[R_mt])
                S.op("dve", lambda e: e.tensor_scalar(out=mean, in0=s1, scalar1=1.0 / 64, scalar2=None, op0=ALU.mult), reads=[R_mt], writes=[R_mt])
                S.op("dve", lambda e: e.tensor_tensor(out=msq, in0=mean, in1=mean, op=ALU.mult), reads=[R_mt], writes=[R_mt])
                S.op("dve", lambda e: e.scalar_tensor_tensor(out=var, in0=s2, scalar=1.0 / 64, in1=msq, op0=ALU.mult, op1=ALU.subtract),
                     reads=[R_mt], writes=[R_mt])
                S.op("dve", lambda e: e.tensor_scalar(out=var, in0=var, scalar1=EPS, scalar2=None, op0=ALU.add), reads=[R_mt], writes=[R_mt])
                S.op("act", lambda e: e.activation(out=var, in_=var, func=AF.Sqrt), reads=[R_mt], writes=[R_mt])
                S.op("dve", lambda e: e.reciprocal(out=var, in_=var), reads=[R_mt], writes=[R_mt])
                g13 = gn1.rearrange("p (h d) -> p h d", h=8)
                S.op("dve", lambda e: e.tensor_tensor(out=g13, in0=o3, in1=bc_last(mean, 64), op=ALU.subtract),
                     reads=[PB[7], R_mt], writes=[R_gn1])
                S.op("dve", lambda e: e.tensor_tensor(out=g13, in0=g13, in1=bc_last(var, 64), op=ALU.mult), reads=[R_gn1, R_mt], writes=[R_gn1])
                S.op("pool", lambda e: e.tensor_tensor(out=gn2, in0=sg, in1=gnw_t, op=ALU.mult), reads=[R_sg, R_gnw], writes=[R_gn2])
                S.op("dve", lambda e: e.tensor_tensor(out=yb, in0=gn1, in1=gn2, op=ALU.mult), reads=[R_gn1, R_gn2], writes=[R_yb])
                for c in range(4):
                    S.op("pe", lambda e, c=c: e.transpose(out=tb[:, c * 128:(c + 1) * 128], in_=yb[:, c * 128:(c + 1) * 128], identity=ident),
                         reads=[R_yb, R_ident], writes=[PB[0]], inc=(c == 3))
                m = n - HALO
                S.op("act", lambda e: e.activation(out=yT, in_=tb[:, 0:512], func=AF.Copy), reads=[PB[0]], writes=[R_yT])
                S.dma("sp", "ymr%d" % b, [lambda e: e.dma_start(out=s_mix[m, :, 0:4, :], in_=yT.rearrange("p (c t) -> p c t", c=4))],
                      reads=[R_yT], writes=[R_smix[m]], nbytes=131072)

            lat = lat_sb
            ckv_ap = lat[:, 256 + latoff:384 + latoff]
            kpe_ap = lat[:, 384 + latoff:416 + latoff]
            ssq = st_small[:, 4:6]
            rq = st_small[:, 6:8]
            if full:
                S.op("act", lambda e: e.activation(out=cqn, in_=lat[:, 0:256], func=AF.Square, accum_out=ssq[:, 0:1]),
                     reads=[R_lat], writes=[R_cqn, R_ss])
            S.op("act", lambda e: e.activation(out=ckvn, in_=ckv_ap, func=AF.Square, accum_out=ssq[:, 1:2]),
                 reads=[R_lat], writes=[R_ckvn, R_ss])
            if full:
                S.op("dve", lambda e: e.tensor_scalar(out=ssq[:, 0:1], in0=ssq[:, 0:1], scalar1=1.0 / 256, scalar2=EPS, op0=ALU.mult, op1=ALU.add),
                     reads=[R_ss], writes=[R_ss])
            S.op("dve", lambda e: e.tensor_scalar(out=ssq[:, 1:2], in0=ssq[:, 1:2], scalar1=1.0 / 128, scalar2=EPS, op0=ALU.mult, op1=ALU.add),
                 reads=[R_ss], writes=[R_ss])
            lo = 0 if full else 1
            S.op("act", lambda e, lo=lo: e.activation(out=ssq[:, lo:2], in_=ssq[:, lo:2], func=AF.Sqrt), reads=[R_ss], writes=[R_ss])
            S.op("dve", lambda e, lo=lo: e.reciprocal(out=rq[:, lo:2], in_=ssq[:, lo:2]), reads=[R_ss], writes=[R_ss])
            if full:
                S.op("dve", lambda e: e.scalar_tensor_tensor(out=cqn, in0=lat[:, 0:256], scalar=rq[:, 0:1], in1=qnw_t, op0=ALU.mult, op1=ALU.mult),
                     reads=[R_lat, R_ss, R_qnw], writes=[R_cqn])
            S.op("dve", lambda e: e.scalar_tensor_tensor(out=ckvn, in0=ckv_ap, scalar=rq[:, 1:2], in1=kvnw_t, op0=ALU.mult, op1=ALU.mult),
                 reads=[R_lat, R_ss, R_kvnw], writes=[R_ckvn])
            rope(kpe_ap, 1, 32, cs_m, ss_m, tA2, tB2, kr, [R_lat], R_kr, RA=R_tA2, RB=R_tB2)
            if KSUB == 5 and n >= HALO:
                return

            S.op("pe", lambda e: e.transpose(out=tb[:, 0:128], in_=ckvn, identity=ident), reads=[R_ckvn, R_ident], writes=[PB[0]], inc=False)
            S.op("pe", lambda e: e.transpose(out=tb[0:32, 128:256], in_=kr, identity=ident), reads=[R_kr, R_ident], writes=[PB[0]], inc=not full)
            if full:
                for c in range(2):
                    S.op("pe", lambda e, c=c: e.transpose(out=tb[:, 256 + c * 128:256 + (c + 1) * 128], in_=cqn[:, c * 128:(c + 1) * 128], identity=ident),
                         reads=[R_cqn, R_ident], writes=[PB[0]], inc=(c == 1))
            S.op("act", lambda e: e.activation(out=ckvnT, in_=tb[:, 0:128], func=AF.Copy), reads=[PB[0]], writes=[R_ckvnT])
            S.op("act", lambda e, gb=gb, gi=gi: e.activation(out=kpe_g[gb][0:32, gi * 128:(gi + 1) * 128], in_=tb[0:32, 128:256], func=AF.Copy),
                 reads=[PB[0]], writes=[R_kpg[gb]])
            if KSUB == 6 and n >= HALO:
                return

            if full:
                S.op("act", lambda e: e.activation(out=cqnT, in_=tb[:, 256:512], func=AF.Copy), reads=[PB[0]], writes=[R_cqnT])
            for p_ in range(4):
                S.op("pe", lambda e, p_=p_: e.matmul(ps[:, 3072 + p_ * 128:3072 + (p_ + 1) * 128], lhsT=wk[:, p_ * 128:(p_ + 1) * 128], rhs=ckvnT,
                                                     start=True, stop=True), reads=[R_wk, R_ckvnT], writes=[PB[6]], inc=(p_ == 3))
            kng3 = kn_g[gb].rearrange("p (a k) -> p a k", a=4)
            S.op("act", lambda e, gi=gi, kng3=kng3: e.activation(out=kng3[:, :, gi * 128:(gi + 1) * 128], in_=bank(6).rearrange("p (a k) -> p a k", a=4), func=AF.Copy),
                 reads=[PB[6]], writes=[R_kng[gb]])
            S.op("pe", lambda e: e.matmul(bank(5), lhsT=ckvnT, rhs=wv, start=True, stop=True), reads=[R_wv, R_ckvnT], writes=[PB[5]])
            vg4 = v_g[gb].rearrange("p (h t e) -> p h t e", h=8, t=4)
            S.op("act", lambda e, gi=gi, vg4=vg4: e.activation(out=vg4[:, :, gi, 0:64], in_=bank(5).rearrange("p (h e) -> p h e", h=8), func=AF.Copy),
                 reads=[PB[5]], writes=[R_vg[gb]])
            S.op("dve", lambda e, gi=gi, vg4=vg4, n=n: e.tensor_copy(out=vg4[:, :, gi, 64:65], in_=bc_mid(valid_t[:, n:n + 1], 8)),
                 reads=[R_valid], writes=[R_vg[gb]])
            if KSUB == 7 and n >= HALO:
                return

            if full:
                for hf in range(2):
                    for c in range(2):
                        S.op("pe", lambda e, hf=hf, c=c: e.matmul(ps[:, (5 + hf) * 512:(5 + hf) * 512 + 384], lhsT=cqnT[:, c * 128:(c + 1) * 128],
                                                                  rhs=w_uq3[:, c, hf * 384:(hf + 1) * 384], start=(c == 0), stop=(c == 1)),
                             reads=[R_cqnT, R_wuq], writes=[PB[5 + hf]], inc=(c == 1))
                qb3 = qb.rearrange("p (h d) -> p h d", h=8)
                for hf in range(2):
                    src = ps[:, (5 + hf) * 512:(5 + hf) * 512 + 384]
                    s3 = src.rearrange("p (h d) -> p h d", h=4)
                    S.op("act", lambda e, hf=hf, s3=s3: e.activation(out=qb3[:, hf * 4:(hf + 1) * 4, 0:64], in_=s3[:, :, 0:64], func=AF.Copy),
                         reads=[PB[5 + hf]], writes=[R_qb])
                    x3 = s3[:, :, 64:96]
                    sw = bass.AP(src.tensor, src.offset + 64 + 16, [list(src.ap[0]), [96, 4], [-16, 2], [1, 16]])
                    a3 = tA3[:, hf * 128:(hf + 1) * 128].rearrange("p (h d) -> p h d", h=4)
                    b4 = tB3[:, hf * 128:(hf + 1) * 128].rearrange("p (h a d) -> p h a d", h=4, a=2)
                    S.op("dve", lambda e, x3=x3, a3=a3: e.tensor_tensor(out=a3, in0=x3, in1=bc_mid(cs_m, 4), op=ALU.mult),
                         reads=[PB[5 + hf], R_tab], writes=[R_tA3])
                    S.op("dve", lambda e, sw=sw, b4=b4: e.tensor_tensor(out=b4, in0=sw, in1=bc_mid(ss_m.rearrange("p (a d) -> p a d", a=2), 4), op=ALU.mult),
                         reads=[PB[5 + hf], R_tab], writes=[R_tB3])
                    S.op("dve", lambda e, hf=hf, a3=a3: e.tensor_tensor(out=qb3[:, hf * 4:(hf + 1) * 4, 64:96], in0=a3,
                                                                        in1=tB3[:, hf * 128:(hf + 1) * 128].rearrange("p (h d) -> p h d", h=4), op=ALU.add),
                         reads=[R_tA3, R_tB3], writes=[R_qb])
                for h in range(8):
                    S.op("pe", lambda e, h=h: e.transpose(out=tb[0:96, h * 128:(h + 1) * 128], in_=qb[:, h * 96:(h + 1) * 96], identity=ident),
                         reads=[R_qb, R_ident], writes=[PB[0]], inc=(h == 7))
                mg, mi = divmod(n - HALO, 4)
                qg3 = q_g[mg % 2].rearrange("p (h t) -> p h t", h=8)
                S.op("act", lambda e, mi=mi, qg3=qg3: e.activation(out=qg3[0:96, :, mi * 128:(mi + 1) * 128],
                                                                   in_=tb[0:96, :].rearrange("p (h t) -> p h t", h=8), func=AF.Copy),
                     reads=[PB[0]], writes=[R_qg[mg % 2]])
                if mi == 3 or n == NT - 1:
                    ntl = mi + 1
                    S.dma("sp", "sq%d" % (mg % 2),
                          [lambda e, mg=mg, ntl=ntl, qg3=qg3: e.dma_start(
                              out=s_qt[:, :, mg * 512:mg * 512 + ntl * 128].rearrange("h r t -> r h t"),
                              in_=qg3[0:96, :, 0:ntl * 128])],
                          reads=[R_qg[mg % 2]], writes=[Res("sqt")])
            if gi == 3:
                fns = []
                for a_ in range(2):
                    fns.append(lambda e, a_=a_, g=g, kng3=kng3: e.dma_start(
                        out=s_kt[:, 0:64, g * 512:(g + 1) * 512].rearrange("(p a) r k -> a r p k", a=2)[a_],
                        in_=kng3[a_ * 64:(a_ + 1) * 64, :, :]))
                fns.append(lambda e, g=g, gb=gb: e.dma_start(
                    out=s_kt[:, 64:96, g * 512:(g + 1) * 512].rearrange("h r k -> r h k"),
                    in_=bc_mid(kpe_g[gb][0:32, :], 8)))
                S.dma("sp", "sk%d" % gb, fns, reads=[R_kng[gb], R_kpg[gb]], writes=[Res("skt")])
                S.dma("sp", "sv%d" % gb,
                      [lambda e, g=g, vg4=vg4: e.dma_start(
                          out=s_v[:, :, g * 4 * 65:(g + 1) * 4 * 65].rearrange("h p (t e) -> p h t e", t=4),
                          in_=vg4)],
                      reads=[R_vg[gb]], writes=[Res("sv")])

        for n in range(int(os.environ.get('KNT', NT))):
            _tileA(n)
        while pk[1] < len(pieces):
            cast_step(2)

        S.barrier()
        if KSTOP == 'A':
            S.emit(); return nc
        A.off = mark_persist
        ytmp = [A.bf(512) for _ in range(2)]
        R_ytmp = [Res("ytmp0"), Res("ytmp1")]
        QT = [A.bf(NOWN * 128) for _ in range(2)]
        KT = [A.bf(NT * 128) for _ in range(2)]
        VV = [A.bf(NT * 65) for _ in range(2)]
        R_Q, R_K, R_V = [Res("Q0"), Res("Q1")], [Res("K0"), Res("K1")], [Res("V0"), Res("V1")]
        PT = [A.bf(1024) for _ in range(3)]
        R_PT = [Res("PT%d" % i) for i in range(3)]
        rrow = A.f32(512)
        R_rrow = Res("rrow")
        ones_t = A.f32(64)
        R_ones = Res("ones")
        bcs = A.f32(512)
        R_bcs = Res("bcs")
        S.op("dve", lambda e: e.memset(ones_t, 1.0), writes=[R_ones])
        scale = (64 + 32) ** -0.5

        def load_head(h):
            i = h % 2
            S.dma("sp", "lq%d" % i, [lambda e: e.dma_start(out=QT[i][0:96, :], in_=s_qt[h])], reads=[R_sqt], writes=[R_Q[i]])
            S.dma("sp", "lk%d" % i, [lambda e: e.dma_start(out=KT[i][0:96, :], in_=s_kt[h])], reads=[R_skt], writes=[R_K[i]])
            S.dma("sp", "lv%d" % i, [lambda e: e.dma_start(out=VV[i], in_=s_v[h])], reads=[R_sv], writes=[R_V[i]])

        load_head(0)

        groups = []
        blk_id = 0
        for h in range(8):
            qblocks = [(0, 128, [(kt, 0) for kt in range(HALO)] + [(HALO, 0)], HALO)]
            for j in range(8):
                kts = [(kt, 0) for kt in range(32 + 4 * j)] + [(32 + 4 * j + m, 128 * m) for m in range(4)]
                qblocks.append((128 + 512 * j, 512, kts, 32 + 4 * j))
            for (q0, qw, kts, diag0) in qblocks:
                npairs = (len(kts) + 1) // 2
                for gidx in range(npairs):
                    groups.append(dict(h=h, i=h % 2, q0=q0, qw=qw, pair=kts[2 * gidx:2 * gidx + 2], diag0=diag0,
                                       ob=4 + blk_id % 2, first=(gidx == 0), last=(gidx == npairs - 1),
                                       sb=(len(groups) % 2) * 2, pt=len(groups) % 3, yi=blk_id % 2,
                                       newhead=(gidx == 0 and q0 == 0)))
                blk_id += 1

        def emit_qk(g):
            i, q0, qw, sb_ = g["i"], g["q0"], g["qw"], g["sb"]
            if g["newhead"] and g["h"] + 1 < 8:
                load_head(g["h"] + 1)
            for u, (kt, c0) in enumerate(g["pair"]):
                dst = ps[:, (sb_ + u) * 512 + c0:(sb_ + u) * 512 + qw]
                isdiag = kt >= g["diag0"]
                S.op("pe", lambda e, kt=kt, c0=c0, dst=dst, isdiag=isdiag: e.matmul(
                    dst, lhsT=KT[i][0:96, kt * 128:(kt + 1) * 128], rhs=QT[i][0:96, q0 + c0:q0 + qw], start=True, stop=not isdiag),
                    reads=[R_K[i], R_Q[i]], writes=[PB[sb_ + u]], inc=not isdiag)
                if isdiag:
                    S.op("pe", lambda e, dst=dst: e.matmul(dst[:, 0:128], lhsT=ident, rhs=maskb, start=False, stop=True),
                         reads=[R_ident, R_mask], writes=[PB[sb_ + u]])

        def emit_exp(g):
            qw, sb_, pair = g["qw"], g["sb"], g["pair"]
            pt, Rpt = PT[g["pt"]], R_PT[g["pt"]]
            if len(pair) == 2 and pair[0][1] == 0 and pair[1][1] == 0 and qw == 512:
                S.op("act", lambda e: e.activation(out=pt, in_=ps[:, sb_ * 512:sb_ * 512 + 1024], func=AF.Exp, scale=scale),
                     reads=[PB[sb_], PB[sb_ + 1]], writes=[Rpt])
            else:
                for u, (kt, c0) in enumerate(pair):
                    S.op("act", lambda e, u=u, c0=c0: e.activation(
                        out=pt[:, u * 512 + c0:u * 512 + qw], in_=ps[:, (sb_ + u) * 512 + c0:(sb_ + u) * 512 + qw], func=AF.Exp, scale=scale),
                        reads=[PB[sb_ + u]], writes=[Rpt])

        def emit_pv(g):
            i, qw, ob, pair = g["i"], g["qw"], g["ob"], g["pair"]
            pt, Rpt = PT[g["pt"]], R_PT[g["pt"]]
            V3 = VV[i].rearrange("p (t e) -> p t e", t=NT)
            for u, (kt, c0) in enumerate(pair):
                first = g["first"] and u == 0
                last = g["last"] and u == len(pair) - 1
                S.op("pe", lambda e, kt=kt, c0=c0, u=u, first=first, last=last: e.matmul(
                    ps[0:65, ob * 512 + c0:ob * 512 + qw], lhsT=V3[:, kt, :], rhs=pt[:, u * 512 + c0:u * 512 + qw], start=first, stop=last),
                    reads=[R_V[i], Rpt], writes=[PB[ob]], inc=(u == len(pair) - 1))

        def emit_norm(g):
            h, q0, qw, ob, yi_ = g["h"], g["q0"], g["qw"], g["ob"], g["yi"]
            S.op("dve", lambda e: e.tensor_scalar(out=rrow[64:65, 0:qw], in0=ps[64:65, ob * 512:ob * 512 + qw], scalar1=1e-30, scalar2=None, op0=ALU.max),
                 reads=[PB[ob]], writes=[R_rrow])
            S.op("dve", lambda e: e.reciprocal(out=rrow[64:65, 0:qw], in_=rrow[64:65, 0:qw]), reads=[R_rrow], writes=[R_rrow])
            S.op("pe", lambda e: e.matmul(ps[0:64, 6 * 512:6 * 512 + qw], lhsT=ones_t[64:65, 0:64], rhs=rrow[64:65, 0:qw], start=True, stop=True),
                 reads=[R_ones, R_rrow], writes=[PB[6]])
            S.op("dve", lambda e: e.tensor_copy(out=bcs[0:64, 0:qw], in_=ps[0:64, 6 * 512:6 * 512 + qw]), reads=[PB[6]], writes=[R_bcs])
            S.op("dve", lambda e: e.tensor_tensor(out=ytmp[yi_][0:64, 0:qw], in0=ps[0:64, ob * 512:ob * 512 + qw], in1=bcs[0:64, 0:qw], op=ALU.mult),
                 reads=[PB[ob], R_bcs], writes=[R_ytmp[yi_]])
            m0_, nt_ = q0 // 128, qw // 128
            S.dma("sp", "ym%d" % yi_, [lambda e: e.dma_start(
                out=s_mix[m0_:m0_ + nt_, (h % 2) * 64:(h % 2) * 64 + 64, 4 + h // 2, :].rearrange("m r t -> r m t"),
                in_=ytmp[yi_][0:64, 0:qw].rearrange("r (m t) -> r m t", m=nt_))],
                reads=[R_ytmp[yi_]], writes=R_smix[m0_:m0_ + nt_], nbytes=65536)

        pend_norm = None
        emit_qk(groups[0])
        for gi_, g in enumerate(groups):
            emit_exp(g)
            if gi_ + 1 < len(groups):
                emit_qk(groups[gi_ + 1])
            emit_pv(g)
            if pend_norm is not None:
                emit_norm(pend_norm)
                pend_norm = None
            if g["last"]:
                pend_norm = g
        if pend_norm is not None:
            emit_norm(pend_norm)

        S.barrier()
        if KSTOP == 'AB':
            S.emit(); return nc
        A.off = mark_persist
        wout = A.bf(8 * 1024)
        wout3 = wout.rearrange("p (c f) -> p c f", c=8)
        wdown = A.bf(22 * 1024)
        wdown3 = wdown.rearrange("p (c f) -> p c f", c=22)
        R_wout, R_wdown = Res("wout"), Res("wdown")
        S.dma("sp", "wc1", [lambda e: e.dma_start(out=wout, in_=s_wout)], reads=[R_scr["s_wout"]], writes=[R_wout])
        S.dma("sp", "wc2", [lambda e: e.dma_start(out=wdown, in_=s_wdown)], reads=[R_scr["s_wdown"]], writes=[R_wdown])
        fnw_t, R_fnw = load_const(b_fnw, 1024, "fnw")
        onw_t, R_onw = load_const(b_onw, 1024, "onw")
        cw_t, R_cw = load_const(c_cw, 132, "cw")
        cw3 = cw_t.rearrange("p (c j) -> p c j", c=44)
        cb_t, R_cb = load_const(c_cb, 44, "cb")
        mixb = [A.bf(1024) for _ in range(3)]
        R_mixb = [Res("mixb%d" % i) for i in range(3)]
        NWU = 3
        wupb = [A.bf(2048) for _ in range(NWU)]
        R_wupb = [Res("wup%d" % i) for i in range(NWU)]
        xb2 = [A.f32(1024) for _ in range(3)]
        R_xb2 = [Res("xb2_%d" % i) for i in range(3)]
        x1_ = A.f32(4 * 1024)
        x1 = [x1_, x1_]
        R_x1_ = [Res("x1_%d" % i) for i in range(4)]
        R_x1 = [R_x1_, R_x1_]
        h2b = [A.bf(1024) for _ in range(2)]
        R_h2b = [Res("h2b0"), Res("h2b1")]
        jnk = A.bf(1024)
        R_jnk = Res("jnk")
        h2T = [A.bf(8 * 512) for _ in range(2)]
        R_h2T = [Res("h2T0"), Res("h2T1")]
        gT = [A.bf(22 * 512) for _ in range(2)]
        R_gT = [Res("gT0"), Res("gT1")]
        ubuf = [A.f32(514) for _ in range(2)]
        R_ub = [Res("ub0"), Res("ub1")]
        acc = [[A.f32(512) for _ in range(2)] for _ in range(2)]
        R_acc = [[Res("acc%d%d" % (h_, p_)) for p_ in range(2)] for h_ in range(2)]
        carry = A.f32(44 * 2)
        carry3 = carry.rearrange("p (c j) -> p c j", c=44)
        R_carry = [Res("carry%d" % i) for i in range(44)]
        st2 = [A.f32(8) for _ in range(2)]
        R_st2 = [Res("st2_0"), Res("st2_1")]
        ybuf = xb2
        R_yb2 = R_xb2
        S.op("dve", lambda e: e.memset(carry, 0.0), writes=R_carry)
        wupi = [0]
        xli = [0]
        ybi = [0]
        y_tiles = yout.rearrange("(n p) d -> n p d", p=128)

        blocks = [(0, 1)] + [(1 + 4 * j, 4) for j in range(8)]
        wuc = [0]

        def _blockC(bi, m0, ntl):
            W = ntl * 128
            pb = bi % 2
            x13 = x1[pb].rearrange("p (t d) -> p t d", t=4)
            Rx1 = R_x1[pb]
            h2T3 = h2T[pb].rearrange("p (c t) -> p c t", c=8)
            gT3 = gT[pb].rearrange("p (c t) -> p c t", c=22)
            tb = bank(7).bitcast(BF16)

            def _s1(t):
                m = m0 + t
                xi_ = xli[0] % 3
                xli[0] += 1
                S.dma("sp", "xc%d" % xi_, [lambda e: e.dma_start(out=xb2[xi_], in_=x_tiles[HALO + m])], writes=[R_xb2[xi_]], nbytes=524288)
                mi_ = m % 3
                S.dma("sp", "mx%d" % mi_, [lambda e: e.dma_start(out=mixb[mi_].rearrange("p (c t) -> p c t", c=8), in_=s_mix[m])],
                      reads=[R_smix[m]], writes=[R_mixb[mi_]])
                mix3 = mixb[mi_].rearrange("p (c t) -> p c t", c=8)
                for hf in range(2):
                    for c in range(8):
                        S.op("pe", lambda e, c=c, hf=hf: e.matmul(bank(hf), lhsT=mix3[:, c, :], rhs=wout3[:, c, hf * 512:(hf + 1) * 512],
                                                                  start=(c == 0), stop=(c == 7)),
                             reads=[R_mixb[mi_], R_wout], writes=[PB[hf]], inc=(c == 7))
                S.op("dve", lambda e: e.tensor_tensor(out=x13[:, t, :], in0=ps[:, 0:1024], in1=xb2[xi_], op=ALU.add),
                     reads=[PB[0], PB[1], R_xb2[xi_]], writes=[Rx1[t]])
                tp = t % 2
                hb_, Rhb = h2b[tp], R_h2b[tp]
                ss, rstd, Rst = st2[tp][:, 0:1], st2[tp][:, 1:2], R_st2[tp]
                S.op("act", lambda e: e.activation(out=hb_, in_=x13[:, t, :], func=AF.Square, accum_out=ss), reads=[Rx1[t]], writes=[Rhb, Rst])
                S.op("dve", lambda e: e.tensor_scalar(out=ss, in0=ss, scalar1=1.0 / 1024, scalar2=EPS, op0=ALU.mult, op1=ALU.add), reads=[Rst], writes=[Rst])
                S.op("act", lambda e: e.activation(out=ss, in_=ss, func=AF.Sqrt), reads=[Rst], writes=[Rst])
                S.op("dve", lambda e: e.reciprocal(out=rstd, in_=ss), reads=[Rst], writes=[Rst])
                S.op("dve", lambda e: e.scalar_tensor_tensor(out=hb_, in0=x13[:, t, :], scalar=rstd, in1=fnw_t, op0=ALU.mult, op1=ALU.mult),
                     reads=[Rx1[t], Rst, R_fnw], writes=[Rhb])
                for c in range(8):
                    S.op("pe", lambda e, c=c: e.transpose(out=tb[:, c * 128:(c + 1) * 128], in_=hb_[:, c * 128:(c + 1) * 128], identity=ident),
                         reads=[Rhb, R_ident], writes=[PB[7]], inc=(c == 7))
                S.op("act", lambda e: e.activation(out=h2T3[:, :, t * 128:(t + 1) * 128], in_=tb.rearrange("p (c t) -> p c t", c=8), func=AF.Copy),
                     reads=[PB[7]], writes=[R_h2T[pb]])

            for t in range(ntl):
                _s1(t)

            def _chunk(fc, half):
                cidx = fc + 22 * half
                fp = fc % 2
                if half == 0:
                    wupi[0] += 1
                wi = wupi[0] % NWU
                if half == 0:
                    S.dma("sp", "wu%d" % wi, [lambda e: e.dma_start(out=wupb[wi], in_=s_wup[:, fc * 2048:(fc + 1) * 2048])],
                          writes=[R_wupb[wi]], nbytes=524288)
                bk = 2 + wuc[0] % 3
                wuc[0] += 1
                w3 = wupb[wi][:, half * 1024:(half + 1) * 1024].rearrange("p (c f) -> p c f", c=8)
                for c in range(8):
                    S.op("pe", lambda e, c=c: e.matmul(bank(bk, W), lhsT=w3[:, c, :], rhs=h2T3[:, c, 0:W], start=(c == 0), stop=(c == 7)),
                         reads=[R_wupb[wi], R_h2T[pb]], writes=[PB[bk]], inc=(c == 7))
                Rc = R_carry[cidx]
                if bi == 0:
                    S.op("dve", lambda e: e.tensor_copy(out=carry3[:, cidx, :], in_=bank(bk, W)[:, W - 2:W]), reads=[PB[bk]], writes=[Rc])
                    return
                ac, Rac = acc[half][fp], R_acc[half][fp]
                if half == 0:
                    ub, Rub = ubuf[fp], R_ub[fp]
                    S.op("act", lambda e: e.activation(out=ub[:, 0:2], in_=carry3[:, cidx, :], func=AF.Copy), reads=[Rc], writes=[Rub])
                    S.op("act", lambda e: e.activation(out=ub[:, 2:2 + W], in_=bank(bk, W), func=AF.Copy), reads=[PB[bk]], writes=[Rub])
                    S.op("act", lambda e: e.activation(out=ac[:, 0:W], in_=bank(bk, W), func=AF.Identity,
                                                       scale=cw3[:, cidx, 2:3], bias=cb_t[:, cidx:cidx + 1]),
                         reads=[PB[bk], R_cw, R_cb], writes=[Rac])
                    S.op("dve", lambda e: e.tensor_copy(out=carry3[:, cidx, :], in_=ub[:, W:W + 2]), reads=[Rub], writes=[Rc])
                    S.op("dve", lambda e: e.scalar_tensor_tensor(out=ac[:, 0:W], in0=ub[:, 1:1 + W], scalar=cw3[:, cidx, 1:2], in1=ac[:, 0:W],
                                                                 op0=ALU.mult, op1=ALU.add), reads=[Rub, Rac, R_cw], writes=[Rac])
                    S.op("dve", lambda e: e.scalar_tensor_tensor(out=ac[:, 0:W], in0=ub[:, 0:W], scalar=cw3[:, cidx, 0:1], in1=ac[:, 0:W],
                                                                 op0=ALU.mult, op1=ALU.add), reads=[Rub, Rac, R_cw], writes=[Rac])
                    S.op("act", lambda e: e.activation(out=ac[:, 0:W], in_=ac[:, 0:W], func=AF.Silu), reads=[Rac], writes=[Rac])
                else:
                    pu = bank(bk, W)
                    S.op("act", lambda e: e.activation(out=ac[:, 0:W], in_=pu, func=AF.Identity,
                                                       scale=cw3[:, cidx, 2:3], bias=cb_t[:, cidx:cidx + 1]),
                         reads=[PB[bk], R_cw, R_cb], writes=[Rac])
                    S.op("dve", lambda e: e.scalar_tensor_tensor(out=ac[:, 1:W], in0=pu[:, 0:W - 1], scalar=cw3[:, cidx, 1:2], in1=ac[:, 1:W],
                                                                 op0=ALU.mult, op1=ALU.add), reads=[PB[bk], Rac, R_cw], writes=[Rac])
                    S.op("dve", lambda e: e.scalar_tensor_tensor(out=ac[:, 2:W], in0=pu[:, 0:W - 2], scalar=cw3[:, cidx, 0:1], in1=ac[:, 2:W],
                                                                 op0=ALU.mult, op1=ALU.add), reads=[PB[bk], Rac, R_cw], writes=[Rac])
                    S.op("dve", lambda e: e.scalar_tensor_tensor(out=ac[:, 0:1], in0=carry3[:, cidx, 1:2], scalar=cw3[:, cidx, 1:2], in1=ac[:, 0:1],
                                                                 op0=ALU.mult, op1=ALU.add), reads=[Rc, Rac, R_cw], writes=[Rac])
                    S.op("dve", lambda e: e.scalar_tensor_tensor(out=ac[:, 0:2], in0=carry3[:, cidx, 0:2], scalar=cw3[:, cidx, 0:1], in1=ac[:, 0:2],
                                                                 op0=ALU.mult, op1=ALU.add), reads=[Rc, Rac, R_cw], writes=[Rac])
                    S.op("dve", lambda e: e.tensor_copy(out=carry3[:, cidx, :], in_=pu[:, W - 2:W]), reads=[PB[bk]], writes=[Rc])
                    S.op("pool", lambda e: e.tensor_tensor(out=gT3[:, fc, 0:W], in0=acc[0][fp][:, 0:W], in1=ac[:, 0:W], op=ALU.mult),
                         reads=[R_acc[0][fp], Rac], writes=[R_gT[pb]])

            for fc in range(22):
                for half in range(2):
                    _chunk(fc, half)
            if bi == 0:
                return

            def _s3(t):
                m = m0 + t
                for hf in range(2):
                    for fc in range(22):
                        S.op("pe", lambda e, fc=fc, hf=hf: e.matmul(bank(5 + hf), lhsT=gT3[:, fc, t * 128:(t + 1) * 128],
                                                                    rhs=wdown3[:, fc, hf * 512:(hf + 1) * 512], start=(fc == 0), stop=(fc == 21)),
                             reads=[R_gT[pb], R_wdown], writes=[PB[5 + hf]], inc=(fc == 21))
                yi = ybi[0] % 3
                ybi[0] += 1
                yb_, Ryb = ybuf[yi], R_yb2[yi]
                S.op("dve", lambda e: e.tensor_tensor(out=x13[:, t, :], in0=ps[:, 2560:3584], in1=x13[:, t, :], op=ALU.add),
                     reads=[PB[5], PB[6], Rx1[t]], writes=[Rx1[t]])
                tp = t % 2
                ss2, rstd2, Rst = st2[tp][:, 2:3], st2[tp][:, 3:4], R_st2[tp]
                S.op("act", lambda e: e.activation(out=jnk, in_=x13[:, t, :], func=AF.Square, accum_out=ss2), reads=[Rx1[t]], writes=[R_jnk, Rst])
                S.op("dve", lambda e: e.tensor_scalar(out=ss2, in0=ss2, scalar1=1.0 / 1024, scalar2=EPS, op0=ALU.mult, op1=ALU.add), reads=[Rst], writes=[Rst])
                S.op("act", lambda e: e.activation(out=ss2, in_=ss2, func=AF.Sqrt), reads=[Rst], writes=[Rst])
                S.op("dve", lambda e: e.reciprocal(out=rstd2, in_=ss2), reads=[Rst], writes=[Rst])
                S.op("dve", lambda e: e.scalar_tensor_tensor(out=yb_, in0=x13[:, t, :], scalar=rstd2, in1=onw_t, op0=ALU.mult, op1=ALU.mult),
                     reads=[Rx1[t], Rst, R_onw], writes=[Ryb])
                S.dma("sp", "yo%d" % yi, [lambda e: e.dma_start(out=y_tiles[m - 1], in_=yb_)], reads=[Ryb], nbytes=524288)

            for t in range(ntl):
                _s3(t)

        for bi, (m0, ntl) in enumerate(blocks):
            _blockC(bi, m0, ntl)

        S.barrier()
        S.emit()
    return nc


def _consts():
    H, C = 8, 128
    lg = np.log1p(-np.power(2.0, -5.0 - np.arange(H, dtype=np.float64)))
    idx = np.arange(C, dtype=np.float64)
    diff = idx[None, :] - idx[:, None]
    dt = np.where(diff[:, None, :] >= 0, np.exp(lg[None, :, None] * np.maximum(diff[:, None, :], 0.0)), 0.0) / 8.0
    c_dt = dt.reshape(128, 1024).astype(np.float32)
    xi = np.exp(lg[:, None] * (idx[None, :] + 1.0))
    c_xi = np.zeros((128, 4, 128))
    for p in range(4):
        for a in range(2):
            c_xi[a * 64:(a + 1) * 64, p, :] = xi[2 * p + a][None, :]
    c_xi = c_xi.reshape(128, 512).astype(np.float32)
    zeta = np.exp(lg[:, None] * (C - 1.0 - idx[None, :])) / 8.0
    c_zeta = np.repeat(zeta.T[:, :, None], 64, axis=2).reshape(128, 512).astype(np.float32)
    dec = np.exp(lg * C)
    c_dec = np.zeros((128, 4))
    for p in range(4):
        c_dec[0:64, p] = dec[2 * p]
        c_dec[64:128, p] = dec[2 * p + 1]
    c_dec = c_dec.astype(np.float32)
    fr = (10000.0 ** (-np.arange(0, 64, 2, dtype=np.float32) / np.float32(64))).astype(np.float32)
    fm = (10000.0 ** (-np.arange(0, 32, 2, dtype=np.float32) / np.float32(32))).astype(np.float32)
    invf = np.concatenate([fr, fr, fr, fr, fm, fm, fm, fm]).astype(np.float64) / (2 * np.pi)
    off = np.concatenate([np.full(64, 0.25), np.full(32, 0.5), np.zeros(32), np.full(32, 0.25), np.full(16, 0.5), np.zeros(16)])
    c_invf = np.broadcast_to(invf[None, :], (128, 192)).astype(np.float32).copy()
    c_off = np.broadcast_to(off[None, :], (128, 192)).astype(np.float32).copy()
    k = np.arange(128)
    c_mask = np.where(k[None, :] < k[:, None], -30000.0, 0.0).astype(np.float32)
    return dict(c_dt=c_dt, c_xi=c_xi, c_zeta=c_zeta, c_dec=c_dec, c_invf=c_invf, c_off=c_off, c_mask=c_mask)


def _bc(v, n=128):
    return np.ascontiguousarray(np.broadcast_to(np.asarray(v, np.float32)[None, :], (n, v.shape[0])))


_PROG = None


def kernel(x, positions, attn_norm_w, w_in, ret_gn_w, mla_q_norm_w, w_uq, mla_kv_norm_w, w_ukv,
           w_out, ffn_norm_w, w_up, conv_w, conv_b, w_down, final_norm_w):
    global _PROG
    x = np.asarray(x, np.float32)
    positions = np.asarray(positions, np.int32)
    shared = _consts()
    shared["b_anw"] = _bc(np.asarray(attn_norm_w)[0])
    shared["b_fnw"] = _bc(np.asarray(ffn_norm_w)[0])
    shared["b_onw"] = _bc(np.asarray(final_norm_w))
    shared["b_qnw"] = _bc(np.asarray(mla_q_norm_w)[0])
    shared["b_kvnw"] = _bc(np.asarray(mla_kv_norm_w)[0])
    shared["b_gnw"] = _bc(np.asarray(ret_gn_w)[0])
    cw = np.asarray(conv_w, np.float32)[0]
    shared["c_cw"] = np.ascontiguousarray(cw.reshape(3, 44, 128).transpose(2, 1, 0)).reshape(128, 132)
    shared["c_cb"] = np.ascontiguousarray(np.asarray(conv_b, np.float32)[0].reshape(44, 128).T)
    shared["w_in_l"] = np.ascontiguousarray(np.asarray(w_in, np.float32)[0].reshape(8, 128, 2464).transpose(1, 0, 2)).reshape(128, -1)
    shared["w_uq_l"] = np.ascontiguousarray(np.asarray(w_uq, np.float32)[0].reshape(2, 128, 768).transpose(1, 0, 2)).reshape(128, -1)
    wukv = np.asarray(w_ukv, np.float32)[0].reshape(128, 8, 128)
    shared["wk_l"] = np.ascontiguousarray(wukv[:, :, 0:64]).reshape(128, 512)
    shared["wv_l"] = np.ascontiguousarray(wukv[:, :, 64:128]).reshape(128, 512)
    wo = np.asarray(w_out, np.float32)[0]
    shared["w_out_l"] = np.ascontiguousarray(wo.reshape(8, 128, 1024).transpose(1, 0, 2)).reshape(128, -1)
    wu = np.asarray(w_up, np.float32)[0]
    shared["w_up_l"] = np.ascontiguousarray(wu.reshape(8, 128, 2, 22, 128).transpose(1, 3, 2, 0, 4)).reshape(128, -1)
    wd = np.asarray(w_down, np.float32)[0]
    shared["w_down_l"] = np.ascontiguousarray(wd.reshape(22, 128, 1024).transpose(1, 0, 2)).reshape(128, -1)

    in_maps = []
    for c in range(8):
        b, z = divmod(c, 2)
        m = dict(shared)
        if z == 1:
            xcore = x[b]
            pc = positions[b]
            vd = np.ones(8192, np.float32)
        else:
            xcore = np.concatenate([np.zeros((4096, 1024), np.float32), x[b, :4096]], axis=0)
            pc = np.concatenate([np.zeros(4096, np.int32), positions[b, :4096]])
            vd = np.concatenate([np.zeros(4096, np.float32), np.ones(4096, np.float32)])
        m["xc"] = np.ascontiguousarray(xcore)
        m["posc"] = np.ascontiguousarray(pc.reshape(NT, 128).T)
        m["valid"] = np.ascontiguousarray(vd.reshape(NT, 128).T)
        in_maps.append(m)
    if _PROG is None:
        _PROG = build_program()
    res = run_bass_kernel_spmd(_PROG, in_maps, core_ids=list(range(8)))
    out = np.empty((4, 8192, 1024), np.float32)
    for c in range(8):
        b, z = divmod(c, 2)
        out[b, z * 4096:(z + 1) * 4096] = res.results[c]["yout"]
    return out
```

```python
import math
import os
from contextlib import ExitStack
import numpy as np
import concourse.bass as bass
import concourse.mybir as mybir
from concourse.bass_utils import run_bass_kernel_spmd

F32 = mybir.dt.float32
BF16 = mybir.dt.bfloat16
I32 = mybir.dt.int32
ALU = mybir.AluOpType
AF = mybir.ActivationFunctionType
AX = mybir.AxisListType

NT = 64
HALO = 31
NOWN = 33
EPS = 1e-6
TWO_PI = 2.0 * math.pi


class Res:
    __slots__ = ("name", "w", "r", "excl")

    def __init__(self, name, excl=False):
        self.name = name
        self.w = None
        self.r = set()
        self.excl = excl


class _Rec:
    def __init__(self):
        self.call = None

    def __getattr__(self, name):
        def f(*a, **k):
            self.call = (name, a, k)
            return self
        return f


def _fsize(ap):
    n = 1
    for d in list(ap.shape)[1:]:
        n *= int(d)
    return n


_TAGGED = ("Sqrt", "Silu", "Sin", "Exp")


def _act_tag(fn):
    try:
        r = _Rec()
        fn(r)
        name, a, k = r.call
        f = str(k.get("func", ""))
        for t in _TAGGED:
            if f.endswith(t):
                return t
    except Exception:
        pass
    return None


def _est(eng, fn):
    try:
        r = _Rec()
        fn(r)
        name, a, k = r.call
        if eng == "pe":
            rhs = k.get("rhs", k.get("identity"))
            n = _fsize(rhs) if name == "matmul" else 128
            return max(n, 64) / 2.4 + 90.0
        out = k.get("out", a[0] if a else None)
        f = _fsize(out)
        if eng == "act":
            return (f + 224) / 1.2
        if eng == "dve":
            if name == "reciprocal":
                return 165 + (6.2 * f if int(out.shape[0]) < 32 else f)
            return (f + 150) / 0.96
        return 200 + (3.4 if name == 'tensor_copy' else 2.2) * f
    except Exception:
        return 500.0


class _Op:
    __slots__ = ("q", "kind", "fns", "deps", "cost", "lat", "slot", "seq", "tag")


class Sched:
    ENG = ("pe", "act", "dve", "pool", "sp")

    def __init__(self, nc, stack):
        self.nc = nc
        self.stack = stack
        self.sem = {e: stack.enter_context(nc.semaphore("s_" + e)) for e in self.ENG}
        self.cnt = {e: 0 for e in self.ENG}
        self.seen = {e: {} for e in self.ENG}
        self.streams = {e: [] for e in self.ENG}
        self.dsem = {}
        self.dcnt = {}
        self.ops = []
        self.base = 0
        self.open_pe = None
        self.reorder = True
        self.prio = bool(int(os.environ.get('KPRIO', '1')))

    def _record_deps(self, idx, reads, writes):
        deps = self.ops[idx].deps
        for r in reads:
            if r.w is not None and r.w >= self.base and r.w != idx:
                deps.add(r.w)
        for w in writes:
            if w.w is not None and w.w >= self.base and w.w != idx:
                deps.add(w.w)
            for t in w.r:
                if t >= self.base and t != idx:
                    deps.add(t)
        for r in reads:
            if r not in writes:
                r.r.add(idx)
        for w in writes:
            w.w = idx
            w.r = set()

    def op(self, eng, fn, reads=(), writes=(), inc=True, tag=None):
        ex = [r for r in reads if r.excl and r not in writes]
        if ex:
            writes = list(writes) + ex
        if eng == "pe" and self.open_pe is not None:
            idx = self.open_pe
            o = self.ops[idx]
            o.fns.append(fn)
            o.cost += _est(eng, fn)
        else:
            o = _Op()
            o.q, o.kind, o.fns, o.deps, o.cost, o.lat, o.slot, o.seq, o.tag = eng, "op", [fn], set(), _est(eng, fn), 0.0, None, None, (_act_tag(fn) if eng == "act" else None)
            idx = len(self.ops)
            self.ops.append(o)
        self._record_deps(idx, reads, writes)
        if eng == "pe":
            self.open_pe = None if inc else idx
        else:
            assert inc
        return idx

    def dma(self, eng, slot, fns, reads=(), writes=(), nbytes=262144):
        assert self.open_pe is None
        if slot not in self.dsem:
            self.dsem[slot] = self.stack.enter_context(self.nc.semaphore("d_" + slot))
            self.dcnt[slot] = 0
        o = _Op()
        o.q, o.kind, o.fns, o.deps, o.cost, o.slot, o.seq, o.tag = eng, "dma", list(fns), set(), 350.0 * len(fns), slot, None, None
        o.lat = 2000.0 + nbytes / 150.0
        idx = len(self.ops)
        self.ops.append(o)
        self._record_deps(idx, reads, writes)
        return idx

    def _wait(self, eng, toks):
        best = {}
        for t in toks:
            k, s, v = t
            if k not in best or best[k][2] < v:
                best[k] = t
        for k, (kk, s, v) in best.items():
            if self.seen[eng].get(k, 0) >= v:
                continue
            self.seen[eng][k] = v
            self.streams[eng].append(("wait", s, v))

    def _token(self, d):
        o = self.ops[d]
        if o.kind == "dma":
            return ("d_" + o.slot, self.dsem[o.slot], o.seq)
        return (o.q, self.sem[o.q], o.seq)

    def flush(self):
        assert self.open_pe is None
        ops, base = self.ops, self.base
        n = len(ops)
        if n == base:
            return
        if self.reorder:
            order = self._list_schedule(base, n)
        else:
            order = list(range(base, n))
        for i in order:
            o = ops[i]
            toks = []
            for d in o.deps:
                od = ops[d]
                if od.kind == "op" and od.q == "pe" and o.q == "pe" and o.kind == "op":
                    continue
                toks.append(self._token(d))
            self._wait(o.q, toks)
            if o.kind == "dma":
                for fn in o.fns:
                    self.dcnt[o.slot] += 16
                    self.streams[o.q].append(("dma", fn, self.dsem[o.slot]))
                o.seq = self.dcnt[o.slot]
            else:
                self.cnt[o.q] += 1
                o.seq = self.cnt[o.q]
                for j, fn in enumerate(o.fns):
                    self.streams[o.q].append(("op", fn, j == len(o.fns) - 1))
        self.base = n

    def _list_schedule(self, base, n):
        ops = self.ops
        indeg = {}
        succ = {}
        for i in range(base, n):
            dd = [d for d in ops[i].deps if d >= base]
            indeg[i] = len(dd)
            for d in dd:
                succ.setdefault(d, []).append(i)
        blev = {}
        for i in range(n - 1, base - 1, -1):
            o = ops[i]
            m = 0.0
            for j in succ.get(i, ()):
                if blev[j] > m:
                    m = blev[j]
            blev[i] = m + o.cost + (o.lat if o.kind == "dma" else 60.0)
        etime = {e: 0.0 for e in self.ENG}
        lasttag = {e: None for e in self.ENG}
        finish = {}
        rtime = {}
        ready = {e: [] for e in self.ENG}
        for i in range(base, n):
            if indeg[i] == 0:
                rtime[i] = 0.0
                ready[ops[i].q].append(i)
        order = []
        PRIO = self.prio
        while len(order) < n - base:
            bestk, besti = None, None
            for e in self.ENG:
                lst = ready[e]
                if not lst:
                    continue
                t = etime[e]
                cand, ck = None, None
                for i in lst:
                    st = rtime[i] if rtime[i] > t else t
                    if e == "act" and ops[i].tag is not None and lasttag["act"] not in (None, ops[i].tag):
                        st += 1300.0
                    k = (st, -blev[i], i) if PRIO else (st, i)
                    if ck is None or k < ck:
                        cand, ck = i, k
                if bestk is None or ck < bestk:
                    bestk, besti = ck, cand
            i = besti
            o = ops[i]
            st = bestk[0]
            c = o.cost
            if o.tag is not None and o.q == "act":
                lasttag["act"] = o.tag
            etime[o.q] = st + c
            finish[i] = st + c + (o.lat if o.kind == "dma" else 60.0)
            ready[o.q].remove(i)
            order.append(i)
            for j in succ.get(i, ()):
                indeg[j] -= 1
                rt = rtime.get(j, 0.0)
                if finish[i] > rt:
                    rtime[j] = finish[i]
                elif j not in rtime:
                    rtime[j] = rt
                if indeg[j] == 0:
                    ready[ops[j].q].append(j)
        self.est_span = max(etime.values())
        return order

    def barrier(self):
        self.flush()
        toks = [(e, self.sem[e], self.cnt[e]) for e in self.ENG if self.cnt[e] > 0]
        toks += [("d_" + s, self.dsem[s], self.dcnt[s]) for s in self.dsem if self.dcnt[s] > 0]
        for e in self.ENG:
            self._wait(e, toks)

    def emit(self):
        self.flush()
        nc = self.nc
        with nc.Block() as block:
            def run(eng, e):
                sem = self.sem[eng]
                for item in self.streams[eng]:
                    if item[0] == "wait":
                        e.wait_ge(item[1], item[2])
                    elif item[0] == "op":
                        ins = item[1](e)
                        if item[2]:
                            ins.then_inc(sem, 1)
                    else:
                        item[1](e).then_inc(item[2], 16)

            @block.tensor
            def _(e):
                run("pe", e)

            @block.scalar
            def _(e):
                run("act", e)

            @block.vector
            def _(e):
                run("dve", e)

            @block.gpsimd
            def _(e):
                run("pool", e)

            @block.sync
            def _(e):
                run("sp", e)


def bc_mid(a, k):
    return bass.AP(a.tensor, a.offset, [list(a.ap[0]), [0, k]] + [list(x) for x in a.ap[1:]])


def bc_last(a, m):
    return bass.AP(a.tensor, a.offset, [list(x) for x in a.ap] + [[0, m]])


class Arena:
    def __init__(self, t, total):
        self.t = t
        self.total = total
        self.off = 0

    def f32(self, n):
        assert self.off + n <= self.total, ("arena overflow", self.off, n, self.total)
        a = self.t[:, self.off:self.off + n]
        self.off += n
        return a

    def bf(self, n):
        w = (n + 1) // 2
        return self.f32(w).bitcast(BF16)[:, 0:n]


import os
KSTOP = os.environ.get('KSTOP', '')
KSUB = int(os.environ.get('KSUB', 0))


def build_program():
    nc = bass.Bass("TRN2", target_bir_lowering=False)
    din = {}

    def inp(name, shape, dt=F32):
        din[name] = nc.dram_tensor(name, list(shape), dt, kind="ExternalInput").ap()
        return din[name]

    xc = inp("xc", [NT * 128, 1024])
    posc = inp("posc", [128, NT], I32)
    valid = inp("valid", [128, NT])
    c_dt = inp("c_dt", [128, 1024])
    c_xi = inp("c_xi", [128, 512])
    c_zeta = inp("c_zeta", [128, 512])
    c_dec = inp("c_dec", [128, 4])
    c_invf = inp("c_invf", [128, 192])
    c_off = inp("c_off", [128, 192])
    c_mask = inp("c_mask", [128, 128])
    b_anw = inp("b_anw", [128, 1024])
    b_fnw = inp("b_fnw", [128, 1024])
    b_onw = inp("b_onw", [128, 1024])
    b_qnw = inp("b_qnw", [128, 256])
    b_kvnw = inp("b_kvnw", [128, 128])
    b_gnw = inp("b_gnw", [128, 512])
    c_cw = inp("c_cw", [128, 44 * 3])
    c_cb = inp("c_cb", [128, 44])
    w_in_l = inp("w_in_l", [128, 8 * 2464])
    w_uq_l = inp("w_uq_l", [128, 2 * 768])
    wk_l = inp("wk_l", [128, 512])
    wv_l = inp("wv_l", [128, 512])
    w_out_l = inp("w_out_l", [128, 8 * 1024])
    w_up_l = inp("w_up_l", [128, 44 * 1024])
    w_down_l = inp("w_down_l", [128, 22 * 1024])
    yout = nc.dram_tensor("yout", [4096, 1024], F32, kind="ExternalOutput").ap()

    s_wup = nc.dram_tensor("s_wup", [128, 44 * 1024], BF16).ap()
    s_wdown = nc.dram_tensor("s_wdown", [128, 22 * 1024], BF16).ap()
    s_wout = nc.dram_tensor("s_wout", [128, 8 * 1024], BF16).ap()
    s_kt = nc.dram_tensor("s_kt", [8, 96, NT * 128], BF16).ap()
    s_v = nc.dram_tensor("s_v", [8, 128, NT * 65], BF16).ap()
    s_qt = nc.dram_tensor("s_qt", [8, 96, NOWN * 128], BF16).ap()
    s_mix = nc.dram_tensor("s_mix", [NOWN, 128, 8, 128], BF16).ap()

    with ExitStack() as st:
        S = Sched(nc, st)
        TOT = 53000
        arena_t = st.enter_context(nc.sbuf_tensor("arena", [128, TOT], F32))
        ps = st.enter_context(nc.psum_tensor("ps", [128, 4096], F32))
        A = Arena(arena_t, TOT)

        def bank(i, n=512):
            return ps[:, i * 512:i * 512 + n]

        PB = [Res("psb%d" % i, excl=True) for i in range(8)]

        ident = A.bf(128)
        maskb = A.bf(128)
        R_smix = [Res("s_mix%d" % i) for i in range(NOWN)]
        R_ident, R_mask = Res("ident"), Res("mask")
        mark_persist = A.off

        S.op("pool", lambda e: e.memset(ident, 0.0), writes=[R_ident])
        S.op("pool", lambda e: e.affine_select(out=ident, in_=ident, pattern=[[-1, 128]], compare_op=ALU.not_equal,
                                               fill=1.0, base=0, channel_multiplier=1), reads=[R_ident], writes=[R_ident])

        w_in = A.bf(8 * 2464)
        w_in3 = w_in.rearrange("p (c f) -> p c f", c=8)
        w_uq = A.bf(2 * 768)
        w_uq3 = w_uq.rearrange("p (c f) -> p c f", c=2)
        wk = A.bf(512)
        wv = A.bf(512)
        R_win, R_wuq, R_wk, R_wv = Res("w_in"), Res("w_uq"), Res("wk"), Res("wv")
        stage = [A.f32(512) for _ in range(2)]
        stageb = [A.bf(512) for _ in range(2)]
        R_stage = [Res("stage0"), Res("stage1")]
        R_stageb = [Res("stageb0"), Res("stageb1")]
        R_scr = {k: Res(k) for k in ["s_wup", "s_wdown", "s_wout"]}
        pieces = []

        def add_pieces(src, ncols, dst_sb=None, dst_res=None, dst_dram=None, dram_res=None):
            c0 = 0
            while c0 < ncols:
                n = min(512, ncols - c0)
                pieces.append((src, c0, n, dst_sb, dst_res, dst_dram, dram_res))
                c0 += n

        def piece_load_cast(k):
            src, c0, n, dst_sb, dst_res, dst_dram, dram_res = pieces[k]
            i = k % 2
            S.dma("act", "stg%d" % i, [lambda e: e.dma_start(out=stage[i][:, 0:n], in_=src[:, c0:c0 + n])], writes=[R_stage[i]])
            if dst_sb is not None:
                S.op("pool", lambda e: e.tensor_copy(out=dst_sb[:, c0:c0 + n], in_=stage[i][:, 0:n]), reads=[R_stage[i]], writes=[dst_res])
            else:
                S.op("pool", lambda e: e.tensor_copy(out=stageb[i][:, 0:n], in_=stage[i][:, 0:n]), reads=[R_stage[i]], writes=[R_stageb[i]])

        def piece_store(k):
            src, c0, n, dst_sb, dst_res, dst_dram, dram_res = pieces[k]
            i = k % 2
            if dst_dram is not None:
                S.dma("act", "stb%d" % i, [lambda e: e.dma_start(out=dst_dram[:, c0:c0 + n], in_=stageb[i][:, 0:n])],
                      reads=[R_stageb[i]], writes=[Res("wscr")])

        add_pieces(w_in_l, 8 * 2464, dst_sb=w_in, dst_res=R_win)
        add_pieces(w_uq_l, 2 * 768, dst_sb=w_uq, dst_res=R_wuq)
        add_pieces(wk_l, 512, dst_sb=wk, dst_res=R_wk)
        add_pieces(wv_l, 512, dst_sb=wv, dst_res=R_wv)
        add_pieces(c_mask, 128, dst_sb=maskb, dst_res=R_mask)
        n_first = len(pieces)
        add_pieces(w_out_l, 8 * 1024, dst_dram=s_wout, dram_res=R_scr["s_wout"])
        add_pieces(w_down_l, 22 * 1024, dst_dram=s_wdown, dram_res=R_scr["s_wdown"])
        add_pieces(w_up_l, 44 * 1024, dst_dram=s_wup, dram_res=R_scr["s_wup"])
        for k in range(n_first):
            piece_load_cast(k)
        pk = [n_first, n_first]

        def cast_step(nload):
            for _ in range(nload):
                if pk[0] < len(pieces):
                    piece_load_cast(pk[0])
                    piece_store(pk[0])
                    pk[0] += 1
            pk[1] = pk[0]

        def load_const(src, n, name, dt=F32):
            a = A.f32(n)
            if dt is not F32:
                a = a.bitcast(dt)
            r = Res(name)
            S.dma("sp", "c_" + name, [lambda e: e.dma_start(out=a, in_=src)], writes=[r])
            return a, r

        dt_t, R_dt = load_const(c_dt, 1024, "dt")
        xi_t, R_xi = load_const(c_xi, 512, "xi")
        zeta_t, R_zeta = load_const(c_zeta, 512, "zeta")
        dec_t, R_dec = load_const(c_dec, 4, "dec")
        invf_t, R_invf = load_const(c_invf, 192, "invf")
        off_t, R_off = load_const(c_off, 192, "off")
        anw_t, R_anw = load_const(b_anw, 1024, "anw")
        qnw_t, R_qnw = load_const(b_qnw, 256, "qnw")
        kvnw_t, R_kvnw = load_const(b_kvnw, 128, "kvnw")
        gnw_t, R_gnw = load_const(b_gnw, 512, "gnw")
        posi, R_pos = load_const(posc, NT, "posi", I32)
        valid_t, R_valid = load_const(valid, NT, "valid")
        posf = A.f32(NT)
        S.op("dve", lambda e: e.tensor_copy(out=posf, in_=posi), reads=[R_pos], writes=[R_pos])

        TB = 8
        tab = A.f32(TB * 192)
        tab3 = tab.rearrange("p (n f) -> p n f", n=TB)
        R_tab = Res("tab")
        ttmp = A.f32(192)
        tti = A.f32(192).bitcast(I32)
        R_tt = Res("ttmp")

        def make_tables(n0):
            for n in range(n0, n0 + TB):
                S.op("dve", lambda e, n=n: e.scalar_tensor_tensor(out=ttmp, in0=invf_t, scalar=posf[:, n:n + 1], in1=off_t,
                                                                op0=ALU.mult, op1=ALU.add),
                     reads=[R_invf, R_off, R_pos], writes=[R_tt])
                S.op("dve", lambda e: e.tensor_copy(out=tti, in_=ttmp), reads=[R_tt], writes=[R_tt])
                S.op("dve", lambda e, n=n: e.tensor_tensor(out=tab3[:, n % TB, :], in0=ttmp, in1=tti, op=ALU.subtract),
                     reads=[R_tt], writes=[R_tab])
            S.op("act", lambda e: e.activation(out=tab, in_=tab, func=AF.Sin, scale=TWO_PI * (1.0 - 1e-6)),
                 reads=[R_tab], writes=[R_tab])

        xbuf = [A.f32(1024) for _ in range(2)]
        R_x = [Res("x0"), Res("x1")]
        R32 = A.f32(512)
        Rb = A.bf(512)
        R_R32, R_Rb = Res("R32"), Res("Rb")
        DB = {}
        R_ssm2 = [Res("ssm0"), Res("ssm1")]
        for nm, kind, sz in [("st_small", "f", 64), ("hb", "b", 1024), ("hT", "b", 1024), ("qk_sb", "f", 1024), ("tmpA", "f", 1024),
                             ("tmpB", "f", 1024), ("qkr", "b", 1024), ("kz", "b", 512), ("vb", "b", 512), ("sg", "f", 512),
                             ("qT", "b", 1024), ("qxT", "b", 512), ("kT", "b", 512), ("sd", "b", 1024), ("gn1", "f", 512),
                             ("gn2", "f", 512), ("yb", "b", 512), ("yT", "b", 512), ("cqn", "b", 256), ("ckvn", "b", 128), ("kr", "b", 32),
                             ("cqnT", "b", 256), ("ckvnT", "b", 128), ("mt1", "f", 64), ("qb", "b", 768), ("lat_sb", "f", 416),
                             ("tA2", "f", 32), ("tB2", "f", 32), ("tA3", "f", 256), ("tB3", "f", 256)]:
            DB[nm] = [((A.f32(sz) if kind == "f" else A.bf(sz)), Res(nm + "_%d" % i)) for i in range(2)]
        kn_g = [A.bf(4 * 512) for _ in range(2)]
        kpe_g = [A.bf(512) for _ in range(2)]
        v_g = [A.bf(4 * 8 * 65) for _ in range(2)]
        q_g = [A.bf(8 * 512) for _ in range(2)]
        R_kng = [Res("kng0"), Res("kng1")]
        R_kpg = [Res("kpg0"), Res("kpg1")]
        R_vg = [Res("vg0"), Res("vg1")]
        R_qg = [Res("qg0"), Res("qg1")]
        R_skt, R_sv, R_sqt = Res("s_kt"), Res("s_v"), Res("s_qt")

        for i_ in range(2):
            S.op("dve", lambda e, i_=i_: e.memset(DB["qT"][i_][0], 0.0), writes=[DB["qT"][i_][1]])
        S.op("dve", lambda e: e.memset(R32, 0.0), writes=[R_R32])
        S.op("dve", lambda e: e.memset(Rb, 0.0), writes=[R_Rb])

        x_tiles = xc.rearrange("(n p) d -> n p d", p=128)

        def load_x(n):
            i = n % 2
            S.dma("sp", "x%d" % i, [lambda e, n=n, i=i: e.dma_start(out=xbuf[i], in_=x_tiles[n])], writes=[R_x[i]])

        if KSTOP == 'A0':
            npz = int(os.environ.get('KNP', 0))
            while pk[1] < min(len(pieces), n_first + npz):
                cast_step(2)
            S.barrier(); S.emit(); return nc
        load_x(0)

        def _tileA(n):
            cast_step(3)
            b = n % 2
            (st_small, R_ss), (hb, R_hb), (hT, R_hT), (qk_sb, R_qksb), (tmpA, R_tA), (tmpB, R_tB) = [DB[k][b] for k in ("st_small", "hb", "hT", "qk_sb", "tmpA", "tmpB")]
            (qkr, R_qkr), (kz, R_kz), (vb, R_vb), (sg, R_sg), (qT, R_qT), (qxT, R_qxT), (kT, R_kT) = [DB[k][b] for k in ("qkr", "kz", "vb", "sg", "qT", "qxT", "kT")]
            (sd, R_sd), (gn1, R_gn1), (gn2, R_gn2), (yb, R_yb), (yT, R_yT), (cqn, R_cqn), (ckvn, R_ckvn), (kr, R_kr) = [DB[k][b] for k in ("sd", "gn1", "gn2", "yb", "yT", "cqn", "ckvn", "kr")]
            (cqnT, R_cqnT), (ckvnT, R_ckvnT), (mt1, R_mt), (qb, R_qb) = [DB[k][b] for k in ("cqnT", "ckvnT", "mt1", "qb")]
            hT3 = hT.rearrange("p (c t) -> p c t", c=8)
            R_ssm = R_ssm2[b]
            (lat_sb, R_lat), (tA2, R_tA2), (tB2, R_tB2), (tA3, R_tA3), (tB3, R_tB3) = [DB[k][b] for k in ("lat_sb", "tA2", "tB2", "tA3", "tB3")]
            full = n >= HALO
            g, gi = divmod(n, 4)
            gb = g % 2
            if n + 1 < NT:
                load_x(n + 1)
            xb_, Rx = xbuf[n % 2], R_x[n % 2]
            ss = st_small[:, 0:1]
            rstd = st_small[:, 1:2]
            S.op("act", lambda e, xb_=xb_: e.activation(out=hb, in_=xb_, func=AF.Square, accum_out=ss),
                 reads=[Rx], writes=[R_hb, R_ss])
            S.op("dve", lambda e: e.tensor_scalar(out=ss, in0=ss, scalar1=1.0 / 1024, scalar2=EPS, op0=ALU.mult, op1=ALU.add),
                 reads=[R_ss], writes=[R_ss])
            S.op("act", lambda e: e.activation(out=ss, in_=ss, func=AF.Sqrt), reads=[R_ss], writes=[R_ss])
            S.op("dve", lambda e: e.reciprocal(out=rstd, in_=ss), reads=[R_ss], writes=[R_ss])
            S.op("dve", lambda e, xb_=xb_: e.scalar_tensor_tensor(out=hb, in0=xb_, scalar=rstd, in1=anw_t, op0=ALU.mult, op1=ALU.mult),
                 reads=[Rx, R_ss, R_anw], writes=[R_hb])
            tb = bank(0).bitcast(BF16)
            tbh = bank(4).bitcast(BF16)
            for c in range(8):
                S.op("pe", lambda e, c=c: e.transpose(out=tbh[:, c * 128:(c + 1) * 128], in_=hb[:, c * 128:(c + 1) * 128], identity=ident),
                     reads=[R_hb, R_ident], writes=[PB[4]], inc=(c == 7))
            S.op("act", lambda e: e.activation(out=hT, in_=tbh, func=AF.Copy), reads=[PB[4]], writes=[R_hT])
            if KSUB == 1 and n >= HALO:
                return


            def proj(bk, col0, ncol, n_=None):
                for c in range(8):
                    S.op("pe", lambda e, c=c: e.matmul(bank(bk, ncol), lhsT=hT3[:, c, :], rhs=w_in3[:, c, col0:col0 + ncol],
                                                       start=(c == 0), stop=(c == 7)),
                         reads=[R_hT, R_win], writes=[PB[bk]], inc=(c == 7))

            if full:
                proj(1, 0, 512)
            proj(2, 512, 512)
            proj(3, 1024, 512)
            if full:
                proj(4, 1536, 512)
                S.op("act", lambda e: e.activation(out=qk_sb, in_=ps[:, 512:1536], func=AF.Copy), reads=[PB[1], PB[2]], writes=[R_qksb])
                proj(1, 2048, 416)
                S.op("act", lambda e: e.activation(out=lat_sb, in_=bank(1, 416), func=AF.Copy), reads=[PB[1]], writes=[R_lat])
            else:
                S.op("act", lambda e: e.activation(out=qk_sb[:, 512:1024], in_=ps[:, 1024:1536], func=AF.Copy), reads=[PB[2]], writes=[R_qksb])
                proj(1, 2304, 160)
                S.op("act", lambda e: e.activation(out=lat_sb[:, 0:160], in_=bank(1, 160), func=AF.Copy), reads=[PB[1]], writes=[R_lat])
            if KSUB == 2 and n >= HALO:
                return

            latoff = 0 if full else -256
            if n % TB == 0:
                make_tables(n)
            tabn = tab3[:, n % TB, :]
            cs_r, ss_r = tabn[:, 0:64], tabn[:, 64:128]
            cs_m, ss_m = tabn[:, 128:160], tabn[:, 160:192]

            def rope(src_ap, nh, hd, cs, sn, dstA, dstB, dst, reads, wres, RA=None, RB=None, add_eng="dve"):
                RA = R_tA if RA is None else RA
                RB = R_tB if RB is None else RB
                half = hd // 2
                x3 = src_ap.rearrange("p (h d) -> p h d", h=nh)
                sw = bass.AP(src_ap.tensor, src_ap.offset + half,
                             [list(src_ap.ap[0]), [hd, nh], [-half, 2], [1, half]])
                a3 = dstA.rearrange("p (h d) -> p h d", h=nh)
                b4 = dstB.rearrange("p (h a d) -> p h a d", h=nh, a=2)
                S.op("dve", lambda e: e.tensor_tensor(out=a3, in0=x3, in1=bc_mid(cs, nh), op=ALU.mult),
                     reads=reads + [R_tab], writes=[RA])
                S.op("dve", lambda e: e.tensor_tensor(out=b4, in0=sw, in1=bc_mid(sn.rearrange("p (a d) -> p a d", a=2), nh), op=ALU.mult),
                     reads=reads + [R_tab], writes=[RB])
                S.op(add_eng, lambda e: e.tensor_tensor(out=dst, in0=dstA, in1=dstB, op=ALU.add),
                     reads=[RA, RB], writes=[wres])

            if full:
                rope(qk_sb, 16, 64, cs_r, ss_r, tmpA, tmpB, qkr, [R_qksb], R_qkr, add_eng="dve")
            else:
                rope(qk_sb[:, 512:1024], 8, 64, cs_r, ss_r, tmpA[:, 0:512], tmpB[:, 0:512], qkr[:, 512:1024], [R_qksb], R_qkr, add_eng="dve")
            S.op("dve", lambda e: e.tensor_tensor(out=kz, in0=qkr[:, 512:1024], in1=zeta_t, op=ALU.mult),
                 reads=[R_qkr, R_zeta], writes=[R_kz])
            S.op("act", lambda e: e.activation(out=vb, in_=bank(3), func=AF.Copy), reads=[PB[3]], writes=[R_vb])
            if KSUB == 3 and n >= HALO:
                return


            if full:
                S.op("act", lambda e: e.activation(out=sg, in_=bank(4), func=AF.Silu), reads=[PB[4]], writes=[R_sg])
                for c in range(8):
                    S.op("pe", lambda e, c=c: e.transpose(out=tb[:, c * 128:(c + 1) * 128], in_=qkr[:, c * 128:(c + 1) * 128], identity=ident),
                         reads=[R_qkr, R_ident], writes=[PB[0]], inc=(c == 7))
                S.op("act", lambda e: e.activation(out=qT[0:64, 0:512], in_=tb[0:64, 0:512], func=AF.Copy), reads=[PB[0]], writes=[R_qT])
                S.op("act", lambda e: e.activation(out=qT[64:128, 512:1024], in_=tb[64:128, 0:512], func=AF.Copy), reads=[PB[0]], writes=[R_qT])
                S.op("dve", lambda e: e.tensor_tensor(out=qxT, in0=tb[:, 0:512], in1=xi_t, op=ALU.mult), reads=[PB[0], R_xi], writes=[R_qxT])
                S.op("act", lambda e: e.activation(out=kT, in_=tb[:, 512:1024], func=AF.Copy), reads=[PB[0]], writes=[R_kT])
                for h in range(8):
                    p_, a_ = divmod(h, 2)
                    rows = slice(a_ * 64, a_ * 64 + 64)
                    S.op("pe", lambda e, h=h, p_=p_, rows=rows: e.matmul(ps[:, 2560 + h * 128:2560 + (h + 1) * 128],
                                                                         lhsT=kT[:, p_ * 128:(p_ + 1) * 128],
                                                                         rhs=qT[:, (h % 2) * 512 + p_ * 128:(h % 2) * 512 + (p_ + 1) * 128],
                                                                         start=True, stop=True),
                         reads=[R_kT, R_qT], writes=[PB[5], PB[6]], inc=(h == 7))
                S.op("dve", lambda e: e.tensor_tensor(out=sd, in0=ps[:, 2560:3584], in1=dt_t, op=ALU.mult),
                     reads=[PB[5], PB[6], R_dt], writes=[R_sd])
                for h in range(8):
                    p_, a_ = divmod(h, 2)
                    rows = slice(a_ * 64, a_ * 64 + 64)
                    S.op("pe", lambda e, h=h: e.matmul(ps[:, 3584 + h * 64:3584 + (h + 1) * 64], lhsT=sd[:, h * 128:(h + 1) * 128],
                                                       rhs=vb[:, h * 64:(h + 1) * 64], start=True, stop=False),
                         reads=[R_sd, R_vb], writes=[PB[7]], inc=False)
                    S.op("pe", lambda e, h=h, p_=p_, rows=rows: e.matmul(ps[:, 3584 + h * 64:3584 + (h + 1) * 64],
                                                                         lhsT=qxT[:, p_ * 128:(p_ + 1) * 128],
                                                                         rhs=Rb[:, h * 64:(h + 1) * 64],
                                                                         start=False, stop=True),
                         reads=[R_qxT, R_Rb], writes=[PB[7]], inc=(h == 7))
            for p_ in range(4):
                S.op("pe", lambda e, p_=p_: e.matmul(ps[:, 2560 + p_ * 128:2560 + (p_ + 1) * 128], lhsT=kz[:, p_ * 128:(p_ + 1) * 128],
                                                     rhs=vb[:, p_ * 128:(p_ + 1) * 128], start=True, stop=True),
                     reads=[R_kz, R_vb], writes=[PB[5]], inc=(p_ == 3))
            for h in range(8):
                p_, a_ = divmod(h, 2)
                rows = slice(a_ * 64, a_ * 64 + 64)
                S.op("dve", lambda e, h=h, p_=p_, a_=a_, rows=rows: e.scalar_tensor_tensor(
                    out=R32[rows, h * 64:(h + 1) * 64], in0=R32[rows, h * 64:(h + 1) * 64], scalar=dec_t[rows, p_:p_ + 1],
                    in1=ps[rows, 2560 + p_ * 128 + a_ * 64:2560 + p_ * 128 + a_ * 64 + 64], op0=ALU.mult, op1=ALU.add),
                    reads=[PB[5], R_dec, R_R32], writes=[R_R32])
            S.op("act", lambda e: e.activation(out=Rb, in_=R32, func=AF.Copy), reads=[R_R32], writes=[R_Rb])
            if KSUB == 4 and n >= HALO:
                return


            if full:
                o3 = bank(7).rearrange("p (h d) -> p h d", h=8)
                s1, s2, mean, msq, var = (mt1[:, 0:8], mt1[:, 8:16], mt1[:, 16:24], mt1[:, 24:32], mt1[:, 32:40])
                S.op("dve", lambda e: e.tensor_reduce(out=s1, in_=o3, axis=AX.X, op=ALU.add), reads=[PB[7]], writes=[R_mt])
                S.op("act", lambda e: e.activation(out=gn1, in_=bank(7), func=AF.Square), reads=[PB[7]], writes=[R_gn1])
                S.op("dve", lambda e: e.tensor_reduce(out=s2, in_=gn1.rearrange("p (h d) -> p h d", h=8), axis=AX.X, op=ALU.add),
                     reads=[R_gn1], writes=[R_mt])
                S.op("dve", lambda e: e.tensor_scalar(out=mean, in0=s1, scalar1=1.0 / 64, scalar2=None, op0=ALU.mult), reads=[R_mt], writes=[R_mt])
                S.op("dve", lambda e: e.tensor_tensor(out=msq, in0=mean, in1=mean, op=ALU.mult), reads=[R_mt], writes=[R_mt])
                S.op("dve", lambda e: e.scalar_tensor_tensor(out=var, in0=s2, scalar=1.0 / 64, in1=msq, op0=ALU.mult, op1=ALU.subtract),
                     reads=[R_mt], writes=[R_mt])
                S.op("dve", lambda e: e.tensor_scalar(out=var, in0=var, scalar1=EPS, scalar2=None, op0=ALU.add), reads=[R_mt], writes=[R_mt])
                S.op("act", lambda e: e.activation(out=var, in_=var, func=AF.Sqrt), reads=[R_mt], writes=[R_mt])
                S.op("dve", lambda e: e.reciprocal(out=var, in_=var), reads=[R_mt], writes=[R_mt])
                g13 = gn1.rearrange("p (h d) -> p h d", h=8)
                S.op("dve", lambda e: e.tensor_tensor(out=g13, in0=o3, in1=bc_last(mean, 64), op=ALU.subtract),
                     reads=[PB[7], R_mt], writes=[R_gn1])
                S.op("dve", lambda e: e.tensor_tensor(out=g13, in0=g13, in1=bc_last(var, 64), op=ALU.mult), reads=[R_gn1, R_mt], writes=[R_gn1])
                S.op("pool", lambda e: e.tensor_tensor(out=gn2, in0=sg, in1=gnw_t, op=ALU.mult), reads=[R_sg, R_gnw], writes=[R_gn2])
                S.op("dve", lambda e: e.tensor_tensor(out=yb, in0=gn1, in1=gn2, op=ALU.mult), reads=[R_gn1, R_gn2], writes=[R_yb])
                for c in range(4):
                    S.op("pe", lambda e, c=c: e.transpose(out=tb[:, c * 128:(c + 1) * 128], in_=yb[:, c * 128:(c + 1) * 128], identity=ident),
                         reads=[R_yb, R_ident], writes=[PB[0]], inc=(c == 3))
                m = n - HALO
                S.op("act", lambda e: e.activation(out=yT, in_=tb[:, 0:512], func=AF.Copy), reads=[PB[0]], writes=[R_yT])
                S.dma("sp", "ymr%d" % b, [lambda e: e.dma_start(out=s_mix[m, :, 0:4, :], in_=yT.rearrange("p (c t) -> p c t", c=4))],
                      reads=[R_yT], writes=[R_smix[m]], nbytes=131072)

            lat = lat_sb
            ckv_ap = lat[:, 256 + latoff:384 + latoff]
            kpe_ap = lat[:, 384 + latoff:416 + latoff]
            ssq = st_small[:, 4:6]
            rq = st_small[:, 6:8]
            if full:
                S.op("act", lambda e: e.activation(out=cqn, in_=lat[:, 0:256], func=AF.Square, accum_out=ssq[:, 0:1]),
                     reads=[R_lat], writes=[R_cqn, R_ssm])
            S.op("act", lambda e: e.activation(out=ckvn, in_=ckv_ap, func=AF.Square, accum_out=ssq[:, 1:2]),
                 reads=[R_lat], writes=[R_ckvn, R_ssm])
            if full:
                S.op("dve", lambda e: e.tensor_scalar(out=ssq[:, 0:1], in0=ssq[:, 0:1], scalar1=1.0 / 256, scalar2=EPS, op0=ALU.mult, op1=ALU.add),
                     reads=[R_ssm], writes=[R_ssm])
            S.op("dve", lambda e: e.tensor_scalar(out=ssq[:, 1:2], in0=ssq[:, 1:2], scalar1=1.0 / 128, scalar2=EPS, op0=ALU.mult, op1=ALU.add),
                 reads=[R_ssm], writes=[R_ssm])
            lo = 0 if full else 1
            S.op("act", lambda e, lo=lo: e.activation(out=ssq[:, lo:2], in_=ssq[:, lo:2], func=AF.Sqrt), reads=[R_ssm], writes=[R_ssm])
            S.op("dve", lambda e, lo=lo: e.reciprocal(out=rq[:, lo:2], in_=ssq[:, lo:2]), reads=[R_ssm], writes=[R_ssm])
            if full:
                S.op("dve", lambda e: e.scalar_tensor_tensor(out=cqn, in0=lat[:, 0:256], scalar=rq[:, 0:1], in1=qnw_t, op0=ALU.mult, op1=ALU.mult),
                     reads=[R_lat, R_ssm, R_qnw], writes=[R_cqn])
            S.op("dve", lambda e: e.scalar_tensor_tensor(out=ckvn, in0=ckv_ap, scalar=rq[:, 1:2], in1=kvnw_t, op0=ALU.mult, op1=ALU.mult),
                 reads=[R_lat, R_ssm, R_kvnw], writes=[R_ckvn])
            rope(kpe_ap, 1, 32, cs_m, ss_m, tA2, tB2, kr, [R_lat], R_kr, RA=R_tA2, RB=R_tB2)
            if KSUB == 5 and n >= HALO:
                return

            S.op("pe", lambda e: e.transpose(out=tb[:, 0:128], in_=ckvn, identity=ident), reads=[R_ckvn, R_ident], writes=[PB[0]], inc=False)
            S.op("pe", lambda e: e.transpose(out=tb[0:32, 128:256], in_=kr, identity=ident), reads=[R_kr, R_ident], writes=[PB[0]], inc=not full)
            if full:
                for c in range(2):
                    S.op("pe", lambda e, c=c: e.transpose(out=tb[:, 256 + c * 128:256 + (c + 1) * 128], in_=cqn[:, c * 128:(c + 1) * 128], identity=ident),
                         reads=[R_cqn, R_ident], writes=[PB[0]], inc=(c == 1))
            S.op("act", lambda e: e.activation(out=ckvnT, in_=tb[:, 0:128], func=AF.Copy), reads=[PB[0]], writes=[R_ckvnT])
            S.op("act", lambda e, gb=gb, gi=gi: e.activation(out=kpe_g[gb][0:32, gi * 128:(gi + 1) * 128], in_=tb[0:32, 128:256], func=AF.Copy),
                 reads=[PB[0]], writes=[R_kpg[gb]])
            if KSUB == 6 and n >= HALO:
                return

            if full:
                S.op("act", lambda e: e.activation(out=cqnT, in_=tb[:, 256:512], func=AF.Copy), reads=[PB[0]], writes=[R_cqnT])
            for p_ in range(4):
                S.op("pe", lambda e, p_=p_: e.matmul(ps[:, 3072 + p_ * 128:3072 + (p_ + 1) * 128], lhsT=wk[:, p_ * 128:(p_ + 1) * 128], rhs=ckvnT,
                                                     start=True, stop=True), reads=[R_wk, R_ckvnT], writes=[PB[6]], inc=(p_ == 3))
            kng3 = kn_g[gb].rearrange("p (a k) -> p a k", a=4)
            S.op("act", lambda e, gi=gi, kng3=kng3: e.activation(out=kng3[:, :, gi * 128:(gi + 1) * 128], in_=bank(6).rearrange("p (a k) -> p a k", a=4), func=AF.Copy),
                 reads=[PB[6]], writes=[R_kng[gb]])
            S.op("pe", lambda e: e.matmul(bank(5), lhsT=ckvnT, rhs=wv, start=True, stop=True), reads=[R_wv, R_ckvnT], writes=[PB[5]])
            vg4 = v_g[gb].rearrange("p (h t e) -> p h t e", h=8, t=4)
            S.op("act", lambda e, gi=gi, vg4=vg4: e.activation(out=vg4[:, :, gi, 0:64], in_=bank(5).rearrange("p (h e) -> p h e", h=8), func=AF.Copy),
                 reads=[PB[5]], writes=[R_vg[gb]])
            S.op("dve", lambda e, gi=gi, vg4=vg4, n=n: e.tensor_copy(out=vg4[:, :, gi, 64:65], in_=bc_mid(valid_t[:, n:n + 1], 8)),
                 reads=[R_valid], writes=[R_vg[gb]])
            if KSUB == 7 and n >= HALO:
                return

            if full:
                for hf in range(2):
                    for c in range(2):
                        S.op("pe", lambda e, hf=hf, c=c: e.matmul(ps[:, (5 + hf) * 512:(5 + hf) * 512 + 384], lhsT=cqnT[:, c * 128:(c + 1) * 128],
                                                                  rhs=w_uq3[:, c, hf * 384:(hf + 1) * 384], start=(c == 0), stop=(c == 1)),
                             reads=[R_cqnT, R_wuq], writes=[PB[5 + hf]], inc=(c == 1))
                qb3 = qb.rearrange("p (h d) -> p h d", h=8)
                for hf in range(2):
                    src = ps[:, (5 + hf) * 512:(5 + hf) * 512 + 384]
                    s3 = src.rearrange("p (h d) -> p h d", h=4)
                    S.op("act", lambda e, hf=hf, s3=s3: e.activation(out=qb3[:, hf * 4:(hf + 1) * 4, 0:64], in_=s3[:, :, 0:64], func=AF.Copy),
                         reads=[PB[5 + hf]], writes=[R_qb])
                    x3 = s3[:, :, 64:96]
                    sw = bass.AP(src.tensor, src.offset + 64 + 16, [list(src.ap[0]), [96, 4], [-16, 2], [1, 16]])
                    a3 = tA3[:, hf * 128:(hf + 1) * 128].rearrange("p (h d) -> p h d", h=4)
                    b4 = tB3[:, hf * 128:(hf + 1) * 128].rearrange("p (h a d) -> p h a d", h=4, a=2)
                    S.op("dve", lambda e, x3=x3, a3=a3: e.tensor_tensor(out=a3, in0=x3, in1=bc_mid(cs_m, 4), op=ALU.mult),
                         reads=[PB[5 + hf], R_tab], writes=[R_tA3])
                    S.op("dve", lambda e, sw=sw, b4=b4: e.tensor_tensor(out=b4, in0=sw, in1=bc_mid(ss_m.rearrange("p (a d) -> p a d", a=2), 4), op=ALU.mult),
                         reads=[PB[5 + hf], R_tab], writes=[R_tB3])
                    S.op("dve", lambda e, hf=hf, a3=a3: e.tensor_tensor(out=qb3[:, hf * 4:(hf + 1) * 4, 64:96], in0=a3,
                                                                        in1=tB3[:, hf * 128:(hf + 1) * 128].rearrange("p (h d) -> p h d", h=4), op=ALU.add),
                         reads=[R_tA3, R_tB3], writes=[R_qb])
                for h in range(8):
                    S.op("pe", lambda e, h=h: e.transpose(out=tb[0:96, h * 128:(h + 1) * 128], in_=qb[:, h * 96:(h + 1) * 96], identity=ident),
                         reads=[R_qb, R_ident], writes=[PB[0]], inc=(h == 7))
                mg, mi = divmod(n - HALO, 4)
                qg3 = q_g[mg % 2].rearrange("p (h t) -> p h t", h=8)
                S.op("act", lambda e, mi=mi, qg3=qg3: e.activation(out=qg3[0:96, :, mi * 128:(mi + 1) * 128],
                                                                   in_=tb[0:96, :].rearrange("p (h t) -> p h t", h=8), func=AF.Copy),
                     reads=[PB[0]], writes=[R_qg[mg % 2]])
                if mi == 3 or n == NT - 1:
                    ntl = mi + 1
                    S.dma("sp", "sq%d" % (mg % 2),
                          [lambda e, mg=mg, ntl=ntl, qg3=qg3: e.dma_start(
                              out=s_qt[:, :, mg * 512:mg * 512 + ntl * 128].rearrange("h r t -> r h t"),
                              in_=qg3[0:96, :, 0:ntl * 128])],
                          reads=[R_qg[mg % 2]], writes=[Res("sqt")])
            if gi == 3:
                fns = []
                for a_ in range(2):
                    fns.append(lambda e, a_=a_, g=g, kng3=kng3: e.dma_start(
                        out=s_kt[:, 0:64, g * 512:(g + 1) * 512].rearrange("(p a) r k -> a r p k", a=2)[a_],
                        in_=kng3[a_ * 64:(a_ + 1) * 64, :, :]))
                fns.append(lambda e, g=g, gb=gb: e.dma_start(
                    out=s_kt[:, 64:96, g * 512:(g + 1) * 512].rearrange("h r k -> r h k"),
                    in_=bc_mid(kpe_g[gb][0:32, :], 8)))
                S.dma("sp", "sk%d" % gb, fns, reads=[R_kng[gb], R_kpg[gb]], writes=[Res("skt")])
                S.dma("sp", "sv%d" % gb,
                      [lambda e, g=g, vg4=vg4: e.dma_start(
                          out=s_v[:, :, g * 4 * 65:(g + 1) * 4 * 65].rearrange("h p (t e) -> p h t e", t=4),
                          in_=vg4)],
                      reads=[R_vg[gb]], writes=[Res("sv")])

        for n in range(int(os.environ.get('KNT', NT))):
            _tileA(n)
        while pk[1] < len(pieces):
            cast_step(2)

        S.barrier()
        if KSTOP == 'A':
            S.emit(); return nc
        A.off = mark_persist
        ytmp = [A.bf(512) for _ in range(2)]
        R_ytmp = [Res("ytmp0"), Res("ytmp1")]
        QT = [A.bf(NOWN * 128) for _ in range(2)]
        KT = [A.bf(NT * 128) for _ in range(2)]
        VV = [A.bf(NT * 65) for _ in range(2)]
        R_Q, R_K, R_V = [Res("Q0"), Res("Q1")], [Res("K0"), Res("K1")], [Res("V0"), Res("V1")]
        PT = [A.bf(1024) for _ in range(3)]
        R_PT = [Res("PT%d" % i) for i in range(3)]
        rrow = A.f32(512)
        R_rrow = Res("rrow")
        ones_t = A.f32(64)
        R_ones = Res("ones")
        bcs = A.f32(512)
        R_bcs = Res("bcs")
        S.op("dve", lambda e: e.memset(ones_t, 1.0), writes=[R_ones])
        scale = (64 + 32) ** -0.5

        def load_head(h):
            i = h % 2
            S.dma("sp", "lq%d" % i, [lambda e: e.dma_start(out=QT[i][0:96, :], in_=s_qt[h])], reads=[R_sqt], writes=[R_Q[i]])
            S.dma("sp", "lk%d" % i, [lambda e: e.dma_start(out=KT[i][0:96, :], in_=s_kt[h])], reads=[R_skt], writes=[R_K[i]])
            S.dma("sp", "lv%d" % i, [lambda e: e.dma_start(out=VV[i], in_=s_v[h])], reads=[R_sv], writes=[R_V[i]])

        load_head(0)

        groups = []
        blk_id = 0
        for h in range(8):
            qblocks = [(0, 128, [(kt, 0) for kt in range(HALO)] + [(HALO, 0)], HALO)]
            for j in range(8):
                kts = [(kt, 0) for kt in range(32 + 4 * j)] + [(32 + 4 * j + m, 128 * m) for m in range(4)]
                qblocks.append((128 + 512 * j, 512, kts, 32 + 4 * j))
            for (q0, qw, kts, diag0) in qblocks:
                npairs = (len(kts) + 1) // 2
                for gidx in range(npairs):
                    groups.append(dict(h=h, i=h % 2, q0=q0, qw=qw, pair=kts[2 * gidx:2 * gidx + 2], diag0=diag0,
                                       ob=4 + blk_id % 2, first=(gidx == 0), last=(gidx == npairs - 1),
                                       sb=(len(groups) % 2) * 2, pt=len(groups) % 3, yi=blk_id % 2,
                                       newhead=(gidx == 0 and q0 == 0)))
                blk_id += 1

        def emit_qk(g):
            i, q0, qw, sb_ = g["i"], g["q0"], g["qw"], g["sb"]
            if g["newhead"] and g["h"] + 1 < 8:
                load_head(g["h"] + 1)
            for u, (kt, c0) in enumerate(g["pair"]):
                dst = ps[:, (sb_ + u) * 512 + c0:(sb_ + u) * 512 + qw]
                isdiag = kt >= g["diag0"]
                S.op("pe", lambda e, kt=kt, c0=c0, dst=dst, isdiag=isdiag: e.matmul(
                    dst, lhsT=KT[i][0:96, kt * 128:(kt + 1) * 128], rhs=QT[i][0:96, q0 + c0:q0 + qw], start=True, stop=not isdiag),
                    reads=[R_K[i], R_Q[i]], writes=[PB[sb_ + u]], inc=not isdiag)
                if isdiag:
                    S.op("pe", lambda e, dst=dst: e.matmul(dst[:, 0:128], lhsT=ident, rhs=maskb, start=False, stop=True),
                         reads=[R_ident, R_mask], writes=[PB[sb_ + u]])

        def emit_exp(g):
            qw, sb_, pair = g["qw"], g["sb"], g["pair"]
            pt, Rpt = PT[g["pt"]], R_PT[g["pt"]]
            if len(pair) == 2 and pair[0][1] == 0 and pair[1][1] == 0 and qw == 512:
                S.op("act", lambda e: e.activation(out=pt, in_=ps[:, sb_ * 512:sb_ * 512 + 1024], func=AF.Exp, scale=scale),
                     reads=[PB[sb_], PB[sb_ + 1]], writes=[Rpt])
            else:
                for u, (kt, c0) in enumerate(pair):
                    S.op("act", lambda e, u=u, c0=c0: e.activation(
                        out=pt[:, u * 512 + c0:u * 512 + qw], in_=ps[:, (sb_ + u) * 512 + c0:(sb_ + u) * 512 + qw], func=AF.Exp, scale=scale),
                        reads=[PB[sb_ + u]], writes=[Rpt])

        def emit_pv(g):
            i, qw, ob, pair = g["i"], g["qw"], g["ob"], g["pair"]
            pt, Rpt = PT[g["pt"]], R_PT[g["pt"]]
            V3 = VV[i].rearrange("p (t e) -> p t e", t=NT)
            for u, (kt, c0) in enumerate(pair):
                first = g["first"] and u == 0
                last = g["last"] and u == len(pair) - 1
                S.op("pe", lambda e, kt=kt, c0=c0, u=u, first=first, last=last: e.matmul(
                    ps[0:65, ob * 512 + c0:ob * 512 + qw], lhsT=V3[:, kt, :], rhs=pt[:, u * 512 + c0:u * 512 + qw], start=first, stop=last),
                    reads=[R_V[i], Rpt], writes=[PB[ob]], inc=(u == len(pair) - 1))

        def emit_norm(g):
            h, q0, qw, ob, yi_ = g["h"], g["q0"], g["qw"], g["ob"], g["yi"]
            S.op("dve", lambda e: e.tensor_scalar(out=rrow[64:65, 0:qw], in0=ps[64:65, ob * 512:ob * 512 + qw], scalar1=1e-30, scalar2=None, op0=ALU.max),
                 reads=[PB[ob]], writes=[R_rrow])
            S.op("dve", lambda e: e.reciprocal(out=rrow[64:65, 0:qw], in_=rrow[64:65, 0:qw]), reads=[R_rrow], writes=[R_rrow])
            S.op("pe", lambda e: e.matmul(ps[0:64, 6 * 512:6 * 512 + qw], lhsT=ones_t[64:65, 0:64], rhs=rrow[64:65, 0:qw], start=True, stop=True),
                 reads=[R_ones, R_rrow], writes=[PB[6]])
            S.op("dve", lambda e: e.tensor_copy(out=bcs[0:64, 0:qw], in_=ps[0:64, 6 * 512:6 * 512 + qw]), reads=[PB[6]], writes=[R_bcs])
            S.op("dve", lambda e: e.tensor_tensor(out=ytmp[yi_][0:64, 0:qw], in0=ps[0:64, ob * 512:ob * 512 + qw], in1=bcs[0:64, 0:qw], op=ALU.mult),
                 reads=[PB[ob], R_bcs], writes=[R_ytmp[yi_]])
            m0_, nt_ = q0 // 128, qw // 128
            S.dma("sp", "ym%d" % yi_, [lambda e: e.dma_start(
                out=s_mix[m0_:m0_ + nt_, (h % 2) * 64:(h % 2) * 64 + 64, 4 + h // 2, :].rearrange("m r t -> r m t"),
                in_=ytmp[yi_][0:64, 0:qw].rearrange("r (m t) -> r m t", m=nt_))],
                reads=[R_ytmp[yi_]], writes=R_smix[m0_:m0_ + nt_], nbytes=65536)

        pend_norm = None
        emit_qk(groups[0])
        for gi_, g in enumerate(groups):
            emit_exp(g)
            if gi_ + 1 < len(groups):
                emit_qk(groups[gi_ + 1])
            emit_pv(g)
            if pend_norm is not None:
                emit_norm(pend_norm)
                pend_norm = None
            if g["last"]:
                pend_norm = g
        if pend_norm is not None:
            emit_norm(pend_norm)

        S.barrier()
        if KSTOP == 'AB':
            S.emit(); return nc
        A.off = mark_persist
        wout = A.bf(8 * 1024)
        wout3 = wout.rearrange("p (c f) -> p c f", c=8)
        wdown = A.bf(22 * 1024)
        wdown3 = wdown.rearrange("p (c f) -> p c f", c=22)
        R_wout, R_wdown = Res("wout"), Res("wdown")
        S.dma("sp", "wc1", [lambda e: e.dma_start(out=wout, in_=s_wout)], reads=[R_scr["s_wout"]], writes=[R_wout])
        S.dma("sp", "wc2", [lambda e: e.dma_start(out=wdown, in_=s_wdown)], reads=[R_scr["s_wdown"]], writes=[R_wdown])
        fnw_t, R_fnw = load_const(b_fnw, 1024, "fnw")
        onw_t, R_onw = load_const(b_onw, 1024, "onw")
        cw_t, R_cw = load_const(c_cw, 132, "cw")
        cw3 = cw_t.rearrange("p (c j) -> p c j", c=44)
        cb_t, R_cb = load_const(c_cb, 44, "cb")
        mixb = [A.bf(1024) for _ in range(2)]
        R_mixb = [Res("mixb%d" % i) for i in range(2)]
        NWU = 3
        wupb = [A.bf(2048) for _ in range(NWU)]
        R_wupb = [Res("wup%d" % i) for i in range(NWU)]
        xb2 = [A.f32(1024) for _ in range(2)]
        R_xb2 = [Res("xb2_%d" % i) for i in range(2)]
        jnk = A.bf(1024)
        R_jnk = Res("jnk")
        x1 = [A.f32(4 * 1024) for _ in range(2)]
        R_x1 = [[Res("x1_%d_%d" % (j, i)) for i in range(4)] for j in range(2)]
        h2b = [A.bf(1024) for _ in range(2)]
        R_h2b = [Res("h2b0"), Res("h2b1")]
        h2T = [A.bf(8 * 512) for _ in range(2)]
        R_h2T = [Res("h2T0"), Res("h2T1")]
        gT = [A.bf(22 * 512) for _ in range(2)]
        R_gT = [Res("gT0"), Res("gT1")]
        ubuf = [A.f32(514) for _ in range(2)]
        R_ub = [Res("ub0"), Res("ub1")]
        acc = [[A.f32(512) for _ in range(2)] for _ in range(2)]
        R_acc = [[Res("acc%d%d" % (h_, p_)) for p_ in range(2)] for h_ in range(2)]
        carry = A.f32(44 * 2)
        carry3 = carry.rearrange("p (c j) -> p c j", c=44)
        R_carry = [Res("carry%d" % i) for i in range(44)]
        st2 = [A.f32(8) for _ in range(2)]
        R_st2 = [Res("st2_0"), Res("st2_1")]
        R_st3 = [Res("st3_0"), Res("st3_1")]
        ybuf = xb2
        R_yb2 = R_xb2
        S.op("dve", lambda e: e.memset(carry, 0.0), writes=R_carry)
        wupi = [0]
        xli = [0]
        ybi = [0]
        y_tiles = yout.rearrange("(n p) d -> n p d", p=128)

        blocks = [(0, 1)] + [(1 + 4 * j, 4) for j in range(8)]
        wuc = [0]

        def _blockC(bi, m0, ntl):
            W = ntl * 128
            pb = bi % 2
            x13 = x1[pb].rearrange("p (t d) -> p t d", t=4)
            Rx1 = R_x1[pb]
            h2T3 = h2T[pb].rearrange("p (c t) -> p c t", c=8)
            gT3 = gT[pb].rearrange("p (c t) -> p c t", c=22)
            tb = bank(7).bitcast(BF16)

            def _s1(t):
                m = m0 + t
                xi_ = xli[0] % 2
                xli[0] += 1
                S.dma("sp", "xc%d" % xi_, [lambda e: e.dma_start(out=xb2[xi_], in_=x_tiles[HALO + m])], writes=[R_xb2[xi_]], nbytes=524288)
                mi_ = m % 2
                S.dma("sp", "mx%d" % mi_, [lambda e: e.dma_start(out=mixb[mi_].rearrange("p (c t) -> p c t", c=8), in_=s_mix[m])],
                      reads=[R_smix[m]], writes=[R_mixb[mi_]])
                mix3 = mixb[mi_].rearrange("p (c t) -> p c t", c=8)
                for hf in range(2):
                    for c in range(8):
                        S.op("pe", lambda e, c=c, hf=hf: e.matmul(bank(hf), lhsT=mix3[:, c, :], rhs=wout3[:, c, hf * 512:(hf + 1) * 512],
                                                                  start=(c == 0), stop=(c == 7)),
                             reads=[R_mixb[mi_], R_wout], writes=[PB[hf]], inc=(c == 7))
                S.op("dve", lambda e: e.tensor_tensor(out=x13[:, t, :], in0=ps[:, 0:1024], in1=xb2[xi_], op=ALU.add),
                     reads=[PB[0], PB[1], R_xb2[xi_]], writes=[Rx1[t]])
                tp = t % 2
                hb_, Rhb = h2b[tp], R_h2b[tp]
                ss, rstd, Rst = st2[tp][:, 0:1], st2[tp][:, 1:2], R_st2[tp]
                S.op("act", lambda e: e.activation(out=hb_, in_=x13[:, t, :], func=AF.Square, accum_out=ss), reads=[Rx1[t]], writes=[Rhb, Rst])
                S.op("dve", lambda e: e.tensor_scalar(out=ss, in0=ss, scalar1=1.0 / 1024, scalar2=EPS, op0=ALU.mult, op1=ALU.add), reads=[Rst], writes=[Rst])
                S.op("act", lambda e: e.activation(out=ss, in_=ss, func=AF.Sqrt), reads=[Rst], writes=[Rst])
                S.op("dve", lambda e: e.reciprocal(out=rstd, in_=ss), reads=[Rst], writes=[Rst])
                S.op("dve", lambda e: e.scalar_tensor_tensor(out=hb_, in0=x13[:, t, :], scalar=rstd, in1=fnw_t, op0=ALU.mult, op1=ALU.mult),
                     reads=[Rx1[t], Rst, R_fnw], writes=[Rhb])
                for c in range(8):
                    S.op("pe", lambda e, c=c: e.transpose(out=tb[:, c * 128:(c + 1) * 128], in_=hb_[:, c * 128:(c + 1) * 128], identity=ident),
                         reads=[Rhb, R_ident], writes=[PB[7]], inc=(c == 7))
                S.op("act", lambda e: e.activation(out=h2T3[:, :, t * 128:(t + 1) * 128], in_=tb.rearrange("p (c t) -> p c t", c=8), func=AF.Copy),
                     reads=[PB[7]], writes=[R_h2T[pb]])

            for t in range(ntl):
                _s1(t)

            def _chunk(fc, half):
                cidx = fc + 22 * half
                fp = fc % 2
                if half == 0:
                    wupi[0] += 1
                wi = wupi[0] % NWU
                if half == 0:
                    S.dma("sp", "wu%d" % wi, [lambda e: e.dma_start(out=wupb[wi], in_=s_wup[:, fc * 2048:(fc + 1) * 2048])],
                          writes=[R_wupb[wi]], nbytes=524288)
                bk = 2 + wuc[0] % 3
                wuc[0] += 1
                w3 = wupb[wi][:, half * 1024:(half + 1) * 1024].rearrange("p (c f) -> p c f", c=8)
                for c in range(8):
                    S.op("pe", lambda e, c=c: e.matmul(bank(bk, W), lhsT=w3[:, c, :], rhs=h2T3[:, c, 0:W], start=(c == 0), stop=(c == 7)),
                         reads=[R_wupb[wi], R_h2T[pb]], writes=[PB[bk]], inc=(c == 7))
                Rc = R_carry[cidx]
                if bi == 0:
                    S.op("dve", lambda e: e.tensor_copy(out=carry3[:, cidx, :], in_=bank(bk, W)[:, W - 2:W]), reads=[PB[bk]], writes=[Rc])
                    return
                ac, Rac = acc[half][fp], R_acc[half][fp]
                if half == 0:
                    ub, Rub = ubuf[fp], R_ub[fp]
                    S.op("act", lambda e: e.activation(out=ub[:, 0:2], in_=carry3[:, cidx, :], func=AF.Copy), reads=[Rc], writes=[Rub])
                    S.op("act", lambda e: e.activation(out=ub[:, 2:2 + W], in_=bank(bk, W), func=AF.Copy), reads=[PB[bk]], writes=[Rub])
                    S.op("act", lambda e: e.activation(out=ac[:, 0:W], in_=bank(bk, W), func=AF.Identity,
                                                       scale=cw3[:, cidx, 2:3], bias=cb_t[:, cidx:cidx + 1]),
                         reads=[PB[bk], R_cw, R_cb], writes=[Rac])
                    S.op("dve", lambda e: e.tensor_copy(out=carry3[:, cidx, :], in_=ub[:, W:W + 2]), reads=[Rub], writes=[Rc])
                    S.op("dve", lambda e: e.scalar_tensor_tensor(out=ac[:, 0:W], in0=ub[:, 1:1 + W], scalar=cw3[:, cidx, 1:2], in1=ac[:, 0:W],
                                                                 op0=ALU.mult, op1=ALU.add), reads=[Rub, Rac, R_cw], writes=[Rac])
                    S.op("dve", lambda e: e.scalar_tensor_tensor(out=ac[:, 0:W], in0=ub[:, 0:W], scalar=cw3[:, cidx, 0:1], in1=ac[:, 0:W],
                                                                 op0=ALU.mult, op1=ALU.add), reads=[Rub, Rac, R_cw], writes=[Rac])
                    S.op("act", lambda e: e.activation(out=ac[:, 0:W], in_=ac[:, 0:W], func=AF.Silu), reads=[Rac], writes=[Rac])
                else:
                    pu = bank(bk, W)
                    S.op("act", lambda e: e.activation(out=ac[:, 0:W], in_=pu, func=AF.Identity,
                                                       scale=cw3[:, cidx, 2:3], bias=cb_t[:, cidx:cidx + 1]),
                         reads=[PB[bk], R_cw, R_cb], writes=[Rac])
                    S.op("dve", lambda e: e.scalar_tensor_tensor(out=ac[:, 1:W], in0=pu[:, 0:W - 1], scalar=cw3[:, cidx, 1:2], in1=ac[:, 1:W],
                                                                 op0=ALU.mult, op1=ALU.add), reads=[PB[bk], Rac, R_cw], writes=[Rac])
                    S.op("dve", lambda e: e.scalar_tensor_tensor(out=ac[:, 2:W], in0=pu[:, 0:W - 2], scalar=cw3[:, cidx, 0:1], in1=ac[:, 2:W],
                                                                 op0=ALU.mult, op1=ALU.add), reads=[PB[bk], Rac, R_cw], writes=[Rac])
                    S.op("dve", lambda e: e.scalar_tensor_tensor(out=ac[:, 0:1], in0=carry3[:, cidx, 1:2], scalar=cw3[:, cidx, 1:2], in1=ac[:, 0:1],
                                                                 op0=ALU.mult, op1=ALU.add), reads=[Rc, Rac, R_cw], writes=[Rac])
                    S.op("dve", lambda e: e.scalar_tensor_tensor(out=ac[:, 0:2], in0=carry3[:, cidx, 0:2], scalar=cw3[:, cidx, 0:1], in1=ac[:, 0:2],
                                                                 op0=ALU.mult, op1=ALU.add), reads=[Rc, Rac, R_cw], writes=[Rac])
                    S.op("dve", lambda e: e.tensor_copy(out=carry3[:, cidx, :], in_=pu[:, W - 2:W]), reads=[PB[bk]], writes=[Rc])
                    S.op("pool", lambda e: e.tensor_tensor(out=gT3[:, fc, 0:W], in0=acc[0][fp][:, 0:W], in1=ac[:, 0:W], op=ALU.mult),
                         reads=[R_acc[0][fp], Rac], writes=[R_gT[pb]])

            for fc in range(22):
                for half in range(2):
                    _chunk(fc, half)
            if bi == 0:
                return

            def _s3(t):
                m = m0 + t
                for hf in range(2):
                    for fc in range(22):
                        S.op("pe", lambda e, fc=fc, hf=hf: e.matmul(bank(5 + hf), lhsT=gT3[:, fc, t * 128:(t + 1) * 128],
                                                                    rhs=wdown3[:, fc, hf * 512:(hf + 1) * 512], start=(fc == 0), stop=(fc == 21)),
                             reads=[R_gT[pb], R_wdown], writes=[PB[5 + hf]], inc=(fc == 21))
                S.op("dve", lambda e: e.tensor_tensor(out=x13[:, t, :], in0=ps[:, 2560:3584], in1=x13[:, t, :], op=ALU.add),
                     reads=[PB[5], PB[6], Rx1[t]], writes=[Rx1[t]])
                tp = t % 2
                ss2, rstd2, Rst = st2[tp][:, 2:3], st2[tp][:, 3:4], R_st3[tp]
                S.op("act", lambda e: e.activation(out=jnk, in_=x13[:, t, :], func=AF.Square, accum_out=ss2), reads=[Rx1[t]], writes=[R_jnk, Rst])
                S.op("dve", lambda e: e.tensor_scalar(out=ss2, in0=ss2, scalar1=1.0 / 1024, scalar2=EPS, op0=ALU.mult, op1=ALU.add), reads=[Rst], writes=[Rst])
                S.op("act", lambda e: e.activation(out=ss2, in_=ss2, func=AF.Sqrt), reads=[Rst], writes=[Rst])
                S.op("dve", lambda e: e.reciprocal(out=rstd2, in_=ss2), reads=[Rst], writes=[Rst])
                S.op("dve", lambda e: e.scalar_tensor_tensor(out=x13[:, t, :], in0=x13[:, t, :], scalar=rstd2, in1=onw_t, op0=ALU.mult, op1=ALU.mult),
                     reads=[Rx1[t], Rst, R_onw], writes=[Rx1[t]])
                S.dma("sp", "yo%d_%d" % (pb, t), [lambda e: e.dma_start(out=y_tiles[m - 1], in_=x13[:, t, :])], reads=[Rx1[t]], nbytes=524288)

            for t in range(ntl):
                _s3(t)

        for bi, (m0, ntl) in enumerate(blocks):
            _blockC(bi, m0, ntl)

        S.barrier()
        S.emit()
    return nc


def _consts():
    H, C = 8, 128
    lg = np.log1p(-np.power(2.0, -5.0 - np.arange(H, dtype=np.float64)))
    idx = np.arange(C, dtype=np.float64)
    diff = idx[None, :] - idx[:, None]
    dt = np.where(diff[:, None, :] >= 0, np.exp(lg[None, :, None] * np.maximum(diff[:, None, :], 0.0)), 0.0) / 8.0
    c_dt = dt.reshape(128, 1024).astype(np.float32)
    xi = np.exp(lg[:, None] * (idx[None, :] + 1.0))
    c_xi = np.zeros((128, 4, 128))
    for p in range(4):
        for a in range(2):
            c_xi[a * 64:(a + 1) * 64, p, :] = xi[2 * p + a][None, :]
    c_xi = c_xi.reshape(128, 512).astype(np.float32)
    zeta = np.exp(lg[:, None] * (C - 1.0 - idx[None, :])) / 8.0
    c_zeta = np.repeat(zeta.T[:, :, None], 64, axis=2).reshape(128, 512).astype(np.float32)
    dec = np.exp(lg * C)
    c_dec = np.zeros((128, 4))
    for p in range(4):
        c_dec[0:64, p] = dec[2 * p]
        c_dec[64:128, p] = dec[2 * p + 1]
    c_dec = c_dec.astype(np.float32)
    fr = (10000.0 ** (-np.arange(0, 64, 2, dtype=np.float32) / np.float32(64))).astype(np.float32)
    fm = (10000.0 ** (-np.arange(0, 32, 2, dtype=np.float32) / np.float32(32))).astype(np.float32)
    invf = np.concatenate([fr, fr, fr, fr, fm, fm, fm, fm]).astype(np.float64) / (2 * np.pi)
    off = np.concatenate([np.full(64, 0.25), np.full(32, 0.5), np.zeros(32), np.full(32, 0.25), np.full(16, 0.5), np.zeros(16)])
    c_invf = np.broadcast_to(invf[None, :], (128, 192)).astype(np.float32).copy()
    c_off = np.broadcast_to(off[None, :], (128, 192)).astype(np.float32).copy()
    k = np.arange(128)
    c_mask = np.where(k[None, :] < k[:, None], -30000.0, 0.0).astype(np.float32)
    return dict(c_dt=c_dt, c_xi=c_xi, c_zeta=c_zeta, c_dec=c_dec, c_invf=c_invf, c_off=c_off, c_mask=c_mask)


def _bc(v, n=128):
    return np.ascontiguousarray(np.broadcast_to(np.asarray(v, np.float32)[None, :], (n, v.shape[0])))


_PROG = None


def kernel(x, positions, attn_norm_w, w_in, ret_gn_w, mla_q_norm_w, w_uq, mla_kv_norm_w, w_ukv,
           w_out, ffn_norm_w, w_up, conv_w, conv_b, w_down, final_norm_w):
    global _PROG
    x = np.asarray(x, np.float32)
    positions = np.asarray(positions, np.int32)
    shared = _consts()
    shared["b_anw"] = _bc(np.asarray(attn_norm_w)[0])
    shared["b_fnw"] = _bc(np.asarray(ffn_norm_w)[0])
    shared["b_onw"] = _bc(np.asarray(final_norm_w))
    shared["b_qnw"] = _bc(np.asarray(mla_q_norm_w)[0])
    shared["b_kvnw"] = _bc(np.asarray(mla_kv_norm_w)[0])
    shared["b_gnw"] = _bc(np.asarray(ret_gn_w)[0])
    cw = np.asarray(conv_w, np.float32)[0]
    shared["c_cw"] = np.ascontiguousarray(cw.reshape(3, 44, 128).transpose(2, 1, 0)).reshape(128, 132)
    shared["c_cb"] = np.ascontiguousarray(np.asarray(conv_b, np.float32)[0].reshape(44, 128).T)
    shared["w_in_l"] = np.ascontiguousarray(np.asarray(w_in, np.float32)[0].reshape(8, 128, 2464).transpose(1, 0, 2)).reshape(128, -1)
    shared["w_uq_l"] = np.ascontiguousarray(np.asarray(w_uq, np.float32)[0].reshape(2, 128, 768).transpose(1, 0, 2)).reshape(128, -1)
    wukv = np.asarray(w_ukv, np.float32)[0].reshape(128, 8, 128)
    shared["wk_l"] = np.ascontiguousarray(wukv[:, :, 0:64]).reshape(128, 512)
    shared["wv_l"] = np.ascontiguousarray(wukv[:, :, 64:128]).reshape(128, 512)
    wo = np.asarray(w_out, np.float32)[0]
    shared["w_out_l"] = np.ascontiguousarray(wo.reshape(8, 128, 1024).transpose(1, 0, 2)).reshape(128, -1)
    wu = np.asarray(w_up, np.float32)[0]
    shared["w_up_l"] = np.ascontiguousarray(wu.reshape(8, 128, 2, 22, 128).transpose(1, 3, 2, 0, 4)).reshape(128, -1)
    wd = np.asarray(w_down, np.float32)[0]
    shared["w_down_l"] = np.ascontiguousarray(wd.reshape(22, 128, 1024).transpose(1, 0, 2)).reshape(128, -1)

    in_maps = []
    for c in range(8):
        b, z = divmod(c, 2)
        m = dict(shared)
        if z == 1:
            xcore = x[b]
            pc = positions[b]
            vd = np.ones(8192, np.float32)
        else:
            xcore = np.concatenate([np.zeros((4096, 1024), np.float32), x[b, :4096]], axis=0)
            pc = np.concatenate([np.zeros(4096, np.int32), positions[b, :4096]])
            vd = np.concatenate([np.zeros(4096, np.float32), np.ones(4096, np.float32)])
        m["xc"] = np.ascontiguousarray(xcore)
        m["posc"] = np.ascontiguousarray(pc.reshape(NT, 128).T)
        m["valid"] = np.ascontiguousarray(vd.reshape(NT, 128).T)
        in_maps.append(m)
    if _PROG is None:
        _PROG = build_program()
    res = run_bass_kernel_spmd(_PROG, in_maps, core_ids=list(range(8)))
    out = np.empty((4, 8192, 1024), np.float32)
    for c in range(8):
        b, z = divmod(c, 2)
        out[b, z * 4096:(z + 1) * 4096] = res.results[c]["yout"]
    return out
```

```python
import math
import os
from contextlib import ExitStack
import numpy as np
import concourse.bass as bass
import concourse.mybir as mybir
from concourse.bass_utils import run_bass_kernel_spmd

F32 = mybir.dt.float32
BF16 = mybir.dt.bfloat16
I32 = mybir.dt.int32
ALU = mybir.AluOpType
AF = mybir.ActivationFunctionType
AX = mybir.AxisListType

NT = 64
HALO = 31
NOWN = 33
EPS = 1e-6
TWO_PI = 2.0 * math.pi


class Res:
    __slots__ = ("name", "w", "r", "excl")

    def __init__(self, name, excl=False):
        self.name = name
        self.w = None
        self.r = set()
        self.excl = excl


class _Rec:
    def __init__(self):
        self.call = None

    def __getattr__(self, name):
        def f(*a, **k):
            self.call = (name, a, k)
            return self
        return f


def _fsize(ap):
    n = 1
    for d in list(ap.shape)[1:]:
        n *= int(d)
    return n


_TAGGED = ("Sqrt", "Silu", "Sin", "Exp")


def _act_tag(fn):
    try:
        r = _Rec()
        fn(r)
        name, a, k = r.call
        f = str(k.get("func", ""))
        for t in _TAGGED:
            if f.endswith(t):
                return t
    except Exception:
        pass
    return None


def _est(eng, fn):
    try:
        r = _Rec()
        fn(r)
        name, a, k = r.call
        if eng == "pe":
            rhs = k.get("rhs", k.get("identity"))
            n = _fsize(rhs) if name == "matmul" else 128
            return max(n, 64) / 2.4 + 90.0
        out = k.get("out", a[0] if a else None)
        f = _fsize(out)
        if eng == "act":
            return (f + 224) / 1.2
        if eng == "dve":
            if name == "reciprocal":
                return 165 + (6.2 * f if int(out.shape[0]) < 32 else f)
            return (f + 150) / 0.96
        return 200 + (3.4 if name == 'tensor_copy' else 2.2) * f
    except Exception:
        return 500.0


class _Op:
    __slots__ = ("q", "kind", "fns", "deps", "cost", "lat", "slot", "seq", "tag")


class Sched:
    ENG = ("pe", "act", "dve", "pool", "sp")

    def __init__(self, nc, stack):
        self.nc = nc
        self.stack = stack
        self.sem = {e: stack.enter_context(nc.semaphore("s_" + e)) for e in self.ENG}
        self.cnt = {e: 0 for e in self.ENG}
        self.seen = {e: {} for e in self.ENG}
        self.streams = {e: [] for e in self.ENG}
        self.dsem = {}
        self.dcnt = {}
        self.ops = []
        self.base = 0
        self.open_pe = None
        self.reorder = True
        self.prio = bool(int(os.environ.get('KPRIO', '1')))

    def _record_deps(self, idx, reads, writes):
        deps = self.ops[idx].deps
        for r in reads:
            if r.w is not None and r.w >= self.base and r.w != idx:
                deps.add(r.w)
        for w in writes:
            if w.w is not None and w.w >= self.base and w.w != idx:
                deps.add(w.w)
            for t in w.r:
                if t >= self.base and t != idx:
                    deps.add(t)
        for r in reads:
            if r not in writes:
                r.r.add(idx)
        for w in writes:
            w.w = idx
            w.r = set()

    def op(self, eng, fn, reads=(), writes=(), inc=True, tag=None):
        ex = [r for r in reads if r.excl and r not in writes]
        if ex:
            writes = list(writes) + ex
        if eng == "pe" and self.open_pe is not None:
            idx = self.open_pe
            o = self.ops[idx]
            o.fns.append(fn)
            o.cost += _est(eng, fn)
        else:
            o = _Op()
            o.q, o.kind, o.fns, o.deps, o.cost, o.lat, o.slot, o.seq, o.tag = eng, "op", [fn], set(), _est(eng, fn), 0.0, None, None, (_act_tag(fn) if eng == "act" else None)
            idx = len(self.ops)
            self.ops.append(o)
        self._record_deps(idx, reads, writes)
        if eng == "pe":
            self.open_pe = None if inc else idx
        else:
            assert inc
        return idx

    def dma(self, eng, slot, fns, reads=(), writes=(), nbytes=262144):
        assert self.open_pe is None
        if slot not in self.dsem:
            self.dsem[slot] = self.stack.enter_context(self.nc.semaphore("d_" + slot))
            self.dcnt[slot] = 0
        o = _Op()
        o.q, o.kind, o.fns, o.deps, o.cost, o.slot, o.seq, o.tag = eng, "dma", list(fns), set(), 350.0 * len(fns), slot, None, None
        o.lat = 2000.0 + nbytes / 150.0
        idx = len(self.ops)
        self.ops.append(o)
        self._record_deps(idx, reads, writes)
        return idx

    def _wait(self, eng, toks):
        best = {}
        for t in toks:
            k, s, v = t
            if k not in best or best[k][2] < v:
                best[k] = t
        for k, (kk, s, v) in best.items():
            if self.seen[eng].get(k, 0) >= v:
                continue
            self.seen[eng][k] = v
            self.streams[eng].append(("wait", s, v))

    def _token(self, d):
        o = self.ops[d]
        if o.kind == "dma":
            return ("d_" + o.slot, self.dsem[o.slot], o.seq)
        return (o.q, self.sem[o.q], o.seq)

    def flush(self):
        assert self.open_pe is None
        ops, base = self.ops, self.base
        n = len(ops)
        if n == base:
            return
        if self.reorder:
            order = self._list_schedule(base, n)
        else:
            order = list(range(base, n))
        for i in order:
            o = ops[i]
            toks = []
            for d in o.deps:
                od = ops[d]
                if od.kind == "op" and od.q == "pe" and o.q == "pe" and o.kind == "op":
                    continue
                toks.append(self._token(d))
            self._wait(o.q, toks)
            if o.kind == "dma":
                for fn in o.fns:
                    self.dcnt[o.slot] += 16
                    self.streams[o.q].append(("dma", fn, self.dsem[o.slot]))
                o.seq = self.dcnt[o.slot]
            else:
                self.cnt[o.q] += 1
                o.seq = self.cnt[o.q]
                for j, fn in enumerate(o.fns):
                    self.streams[o.q].append(("op", fn, j == len(o.fns) - 1))
        self.base = n

    def _list_schedule(self, base, n):
        ops = self.ops
        indeg = {}
        succ = {}
        for i in range(base, n):
            dd = [d for d in ops[i].deps if d >= base]
            indeg[i] = len(dd)
            for d in dd:
                succ.setdefault(d, []).append(i)
        blev = {}
        for i in range(n - 1, base - 1, -1):
            o = ops[i]
            m = 0.0
            for j in succ.get(i, ()):
                if blev[j] > m:
                    m = blev[j]
            blev[i] = m + o.cost + (o.lat if o.kind == "dma" else 60.0)
        etime = {e: 0.0 for e in self.ENG}
        lasttag = {e: None for e in self.ENG}
        finish = {}
        rtime = {}
        ready = {e: [] for e in self.ENG}
        for i in range(base, n):
            if indeg[i] == 0:
                rtime[i] = 0.0
                ready[ops[i].q].append(i)
        order = []
        PRIO = self.prio
        while len(order) < n - base:
            bestk, besti = None, None
            for e in self.ENG:
                lst = ready[e]
                if not lst:
                    continue
                t = etime[e]
                cand, ck = None, None
                for i in lst:
                    st = rtime[i] if rtime[i] > t else t
                    if e == "act" and ops[i].tag is not None and lasttag["act"] not in (None, ops[i].tag):
                        st += 1300.0
                    k = (st, -blev[i], i) if PRIO else (st, i)
                    if ck is None or k < ck:
                        cand, ck = i, k
                if bestk is None or ck < bestk:
                    bestk, besti = ck, cand
            i = besti
            o = ops[i]
            st = bestk[0]
            c = o.cost
            if o.tag is not None and o.q == "act":
                lasttag["act"] = o.tag
            etime[o.q] = st + c
            finish[i] = st + c + (o.lat if o.kind == "dma" else 60.0)
            ready[o.q].remove(i)
            order.append(i)
            for j in succ.get(i, ()):
                indeg[j] -= 1
                rt = rtime.get(j, 0.0)
                if finish[i] > rt:
                    rtime[j] = finish[i]
                elif j not in rtime:
                    rtime[j] = rt
                if indeg[j] == 0:
                    ready[ops[j].q].append(j)
        self.est_span = max(etime.values())
        return order

    def barrier(self):
        self.flush()
        toks = [(e, self.sem[e], self.cnt[e]) for e in self.ENG if self.cnt[e] > 0]
        toks += [("d_" + s, self.dsem[s], self.dcnt[s]) for s in self.dsem if self.dcnt[s] > 0]
        for e in self.ENG:
            self._wait(e, toks)

    def emit(self):
        self.flush()
        nc = self.nc
        with nc.Block() as block:
            def run(eng, e):
                sem = self.sem[eng]
                for item in self.streams[eng]:
                    if item[0] == "wait":
                        e.wait_ge(item[1], item[2])
                    elif item[0] == "op":
                        ins = item[1](e)
                        if item[2]:
                            ins.then_inc(sem, 1)
                    else:
                        item[1](e).then_inc(item[2], 16)

            @block.tensor
            def _(e):
                run("pe", e)

            @block.scalar
            def _(e):
                run("act", e)

            @block.vector
            def _(e):
                run("dve", e)

            @block.gpsimd
            def _(e):
                run("pool", e)

            @block.sync
            def _(e):
                run("sp", e)


def bc_mid(a, k):
    return bass.AP(a.tensor, a.offset, [list(a.ap[0]), [0, k]] + [list(x) for x in a.ap[1:]])


def bc_last(a, m):
    return bass.AP(a.tensor, a.offset, [list(x) for x in a.ap] + [[0, m]])


class Arena:
    def __init__(self, t, total):
        self.t = t
        self.total = total
        self.off = 0

    def f32(self, n):
        assert self.off + n <= self.total, ("arena overflow", self.off, n, self.total)
        a = self.t[:, self.off:self.off + n]
        self.off += n
        return a

    def bf(self, n):
        w = (n + 1) // 2
        return self.f32(w).bitcast(BF16)[:, 0:n]


import os
KSTOP = os.environ.get('KSTOP', '')
KSUB = int(os.environ.get('KSUB', 0))


def build_program():
    nc = bass.Bass("TRN2", target_bir_lowering=False)
    din = {}

    def inp(name, shape, dt=F32):
        din[name] = nc.dram_tensor(name, list(shape), dt, kind="ExternalInput").ap()
        return din[name]

    xc = inp("xc", [NT * 128, 1024])
    posc = inp("posc", [128, NT], I32)
    valid = inp("valid", [128, NT])
    c_dt = inp("c_dt", [128, 1024])
    c_xi = inp("c_xi", [128, 512])
    c_zeta = inp("c_zeta", [128, 512])
    c_dec = inp("c_dec", [128, 4])
    c_invf = inp("c_invf", [128, 192])
    c_off = inp("c_off", [128, 192])
    c_mask = inp("c_mask", [128, 128])
    b_anw = inp("b_anw", [128, 1024])
    b_fnw = inp("b_fnw", [128, 1024])
    b_onw = inp("b_onw", [128, 1024])
    b_qnw = inp("b_qnw", [128, 256])
    b_kvnw = inp("b_kvnw", [128, 128])
    b_gnw = inp("b_gnw", [128, 512])
    c_cw = inp("c_cw", [128, 44 * 3])
    c_cb = inp("c_cb", [128, 44])
    w_in_l = inp("w_in_l", [128, 8 * 2464])
    w_uq_l = inp("w_uq_l", [128, 2 * 768])
    wk_l = inp("wk_l", [128, 512])
    wv_l = inp("wv_l", [128, 512])
    w_out_l = inp("w_out_l", [128, 8 * 1024])
    w_up_l = inp("w_up_l", [128, 44 * 1024])
    w_down_l = inp("w_down_l", [128, 22 * 1024])
    yout = nc.dram_tensor("yout", [4096, 1024], F32, kind="ExternalOutput").ap()

    s_wup = nc.dram_tensor("s_wup", [128, 44 * 1024], BF16).ap()
    s_wdown = nc.dram_tensor("s_wdown", [128, 22 * 1024], BF16).ap()
    s_wout = nc.dram_tensor("s_wout", [128, 8 * 1024], BF16).ap()
    s_kt = nc.dram_tensor("s_kt", [8, 96, NT * 128], BF16).ap()
    s_v = nc.dram_tensor("s_v", [8, 128, NT * 65], BF16).ap()
    s_qt = nc.dram_tensor("s_qt", [8, 96, NOWN * 128], BF16).ap()
    s_mix = nc.dram_tensor("s_mix", [NOWN, 128, 8, 128], BF16).ap()

    with ExitStack() as st:
        S = Sched(nc, st)
        TOT = 53000
        arena_t = st.enter_context(nc.sbuf_tensor("arena", [128, TOT], F32))
        ps = st.enter_context(nc.psum_tensor("ps", [128, 4096], F32))
        A = Arena(arena_t, TOT)

        def bank(i, n=512):
            return ps[:, i * 512:i * 512 + n]

        PB = [Res("psb%d" % i, excl=True) for i in range(8)]

        ident = A.bf(128)
        maskb = A.bf(128)
        R_smix = [Res("s_mix%d" % i) for i in range(NOWN)]
        R_ident, R_mask = Res("ident"), Res("mask")
        mark_persist = A.off

        S.op("pool", lambda e: e.memset(ident, 0.0), writes=[R_ident])
        S.op("pool", lambda e: e.affine_select(out=ident, in_=ident, pattern=[[-1, 128]], compare_op=ALU.not_equal,
                                               fill=1.0, base=0, channel_multiplier=1), reads=[R_ident], writes=[R_ident])

        w_in = A.bf(8 * 2464)
        w_in3 = w_in.rearrange("p (c f) -> p c f", c=8)
        w_uq = A.bf(2 * 768)
        w_uq3 = w_uq.rearrange("p (c f) -> p c f", c=2)
        wk = A.bf(512)
        wv = A.bf(512)
        R_win, R_wuq, R_wk, R_wv = Res("w_in"), Res("w_uq"), Res("wk"), Res("wv")
        stage = [A.f32(512) for _ in range(2)]
        stageb = [A.bf(512) for _ in range(2)]
        R_stage = [Res("stage0"), Res("stage1")]
        R_stageb = [Res("stageb0"), Res("stageb1")]
        R_scr = {k: Res(k) for k in ["s_wup", "s_wdown", "s_wout"]}
        pieces = []

        def add_pieces(src, ncols, dst_sb=None, dst_res=None, dst_dram=None, dram_res=None):
            c0 = 0
            while c0 < ncols:
                n = min(512, ncols - c0)
                pieces.append((src, c0, n, dst_sb, dst_res, dst_dram, dram_res))
                c0 += n

        def piece_load_cast(k):
            src, c0, n, dst_sb, dst_res, dst_dram, dram_res = pieces[k]
            i = k % 2
            S.dma("act", "stg%d" % i, [lambda e: e.dma_start(out=stage[i][:, 0:n], in_=src[:, c0:c0 + n])], writes=[R_stage[i]])
            if dst_sb is not None:
                S.op("pool", lambda e: e.tensor_copy(out=dst_sb[:, c0:c0 + n], in_=stage[i][:, 0:n]), reads=[R_stage[i]], writes=[dst_res])
            else:
                S.op("pool", lambda e: e.tensor_copy(out=stageb[i][:, 0:n], in_=stage[i][:, 0:n]), reads=[R_stage[i]], writes=[R_stageb[i]])

        def piece_store(k):
            src, c0, n, dst_sb, dst_res, dst_dram, dram_res = pieces[k]
            i = k % 2
            if dst_dram is not None:
                S.dma("act", "stb%d" % i, [lambda e: e.dma_start(out=dst_dram[:, c0:c0 + n], in_=stageb[i][:, 0:n])],
                      reads=[R_stageb[i]], writes=[Res("wscr")])

        add_pieces(w_in_l, 8 * 2464, dst_sb=w_in, dst_res=R_win)
        add_pieces(w_uq_l, 2 * 768, dst_sb=w_uq, dst_res=R_wuq)
        add_pieces(wk_l, 512, dst_sb=wk, dst_res=R_wk)
        add_pieces(wv_l, 512, dst_sb=wv, dst_res=R_wv)
        add_pieces(c_mask, 128, dst_sb=maskb, dst_res=R_mask)
        n_first = len(pieces)
        add_pieces(w_out_l, 8 * 1024, dst_dram=s_wout, dram_res=R_scr["s_wout"])
        add_pieces(w_down_l, 22 * 1024, dst_dram=s_wdown, dram_res=R_scr["s_wdown"])
        add_pieces(w_up_l, 44 * 1024, dst_dram=s_wup, dram_res=R_scr["s_wup"])
        for k in range(n_first):
            piece_load_cast(k)
        pk = [n_first, n_first]

        def cast_step(nload):
            for _ in range(nload):
                if pk[0] < len(pieces):
                    piece_load_cast(pk[0])
                    piece_store(pk[0])
                    pk[0] += 1
            pk[1] = pk[0]

        def load_const(src, n, name, dt=F32):
            a = A.f32(n)
            if dt is not F32:
                a = a.bitcast(dt)
            r = Res(name)
            S.dma("sp", "c_" + name, [lambda e: e.dma_start(out=a, in_=src)], writes=[r])
            return a, r

        dt_t, R_dt = load_const(c_dt, 1024, "dt")
        xi_t, R_xi = load_const(c_xi, 512, "xi")
        zeta_t, R_zeta = load_const(c_zeta, 512, "zeta")
        dec_t, R_dec = load_const(c_dec, 4, "dec")
        invf_t, R_invf = load_const(c_invf, 192, "invf")
        off_t, R_off = load_const(c_off, 192, "off")
        anw_t, R_anw = load_const(b_anw, 1024, "anw")
        qnw_t, R_qnw = load_const(b_qnw, 256, "qnw")
        kvnw_t, R_kvnw = load_const(b_kvnw, 128, "kvnw")
        gnw_t, R_gnw = load_const(b_gnw, 512, "gnw")
        posi, R_pos = load_const(posc, NT, "posi", I32)
        valid_t, R_valid = load_const(valid, NT, "valid")
        posf = A.f32(NT)
        S.op("dve", lambda e: e.tensor_copy(out=posf, in_=posi), reads=[R_pos], writes=[R_pos])

        TB = 4
        tabs = [A.f32(TB * 192) for _ in range(2)]
        R_tabs = [Res("tab0"), Res("tab1")]
        ttmp = A.f32(192)
        tti = A.f32(192).bitcast(I32)
        R_tt = Res("ttmp")

        def make_tables(n0):
            k = (n0 // TB) % 2
            tab, R_tab_ = tabs[k], R_tabs[k]
            tab3_ = tab.rearrange("p (n f) -> p n f", n=TB)
            for n in range(n0, n0 + TB):
                S.op("dve", lambda e, n=n: e.scalar_tensor_tensor(out=ttmp, in0=invf_t, scalar=posf[:, n:n + 1], in1=off_t,
                                                                op0=ALU.mult, op1=ALU.add),
                     reads=[R_invf, R_off, R_pos], writes=[R_tt])
                S.op("dve", lambda e: e.tensor_copy(out=tti, in_=ttmp), reads=[R_tt], writes=[R_tt])
                S.op("dve", lambda e, n=n: e.tensor_tensor(out=tab3_[:, n % TB, :], in0=ttmp, in1=tti, op=ALU.subtract),
                     reads=[R_tt], writes=[R_tab_])
            S.op("act", lambda e: e.activation(out=tab, in_=tab, func=AF.Sin, scale=TWO_PI * (1.0 - 1e-6)),
                 reads=[R_tab_], writes=[R_tab_])

        xbuf = [A.f32(1024) for _ in range(2)]
        R_x = [Res("x0"), Res("x1")]
        R32 = A.f32(512)
        Rb = A.bf(512)
        R_R32, R_Rb = Res("R32"), Res("Rb")
        DB = {}
        R_ssm2 = [Res("ssm0"), Res("ssm1")]
        for nm, kind, sz in [("st_small", "f", 64), ("hb", "b", 1024), ("hT", "b", 1024), ("qk_sb", "f", 1024), ("tmpA", "f", 1024),
                             ("tmpB", "f", 1024), ("qkr", "b", 1024), ("kz", "b", 512), ("vb", "b", 512), ("sg", "f", 512),
                             ("qT", "b", 1024), ("qxT", "b", 512), ("kT", "b", 512), ("sd", "b", 1024), ("gn1", "f", 512),
                             ("gn2", "f", 512), ("yb", "b", 512), ("yT", "b", 512), ("cqn", "b", 256), ("ckvn", "b", 128), ("kr", "b", 32),
                             ("cqnT", "b", 256), ("ckvnT", "b", 128), ("mt1", "f", 64), ("qb", "b", 768), ("lat_sb", "f", 416),
                             ("tA2", "f", 32), ("tB2", "f", 32), ("tA3", "f", 256), ("tB3", "f", 256)]:
            DB[nm] = [((A.f32(sz) if kind == "f" else A.bf(sz)), Res(nm + "_%d" % i)) for i in range(2)]
        kn_g = [A.bf(4 * 512) for _ in range(2)]
        kpe_g = [A.bf(512) for _ in range(2)]
        v_g = [A.bf(4 * 8 * 65) for _ in range(2)]
        q_g = [A.bf(8 * 512) for _ in range(2)]
        R_kng = [Res("kng0"), Res("kng1")]
        R_kpg = [Res("kpg0"), Res("kpg1")]
        R_vg = [Res("vg0"), Res("vg1")]
        R_qg = [Res("qg0"), Res("qg1")]
        R_skt, R_sv, R_sqt = Res("s_kt"), Res("s_v"), Res("s_qt")

        for i_ in range(2):
            S.op("dve", lambda e, i_=i_: e.memset(DB["qT"][i_][0], 0.0), writes=[DB["qT"][i_][1]])
        S.op("dve", lambda e: e.memset(R32, 0.0), writes=[R_R32])
        S.op("dve", lambda e: e.memset(Rb, 0.0), writes=[R_Rb])

        x_tiles = xc.rearrange("(n p) d -> n p d", p=128)

        def load_x(n):
            i = n % 2
            S.dma("sp", "x%d" % i, [lambda e, n=n, i=i: e.dma_start(out=xbuf[i], in_=x_tiles[n])], writes=[R_x[i]])

        if KSTOP == 'A0':
            npz = int(os.environ.get('KNP', 0))
            while pk[1] < min(len(pieces), n_first + npz):
                cast_step(2)
            S.barrier(); S.emit(); return nc
        load_x(0)

        def _tileA(n):
            cast_step(3)
            b = n % 2
            (st_small, R_ss), (hb, R_hb), (hT, R_hT), (qk_sb, R_qksb), (tmpA, R_tA), (tmpB, R_tB) = [DB[k][b] for k in ("st_small", "hb", "hT", "qk_sb", "tmpA", "tmpB")]
            (qkr, R_qkr), (kz, R_kz), (vb, R_vb), (sg, R_sg), (qT, R_qT), (qxT, R_qxT), (kT, R_kT) = [DB[k][b] for k in ("qkr", "kz", "vb", "sg", "qT", "qxT", "kT")]
            (sd, R_sd), (gn1, R_gn1), (gn2, R_gn2), (yb, R_yb), (yT, R_yT), (cqn, R_cqn), (ckvn, R_ckvn), (kr, R_kr) = [DB[k][b] for k in ("sd", "gn1", "gn2", "yb", "yT", "cqn", "ckvn", "kr")]
            (cqnT, R_cqnT), (ckvnT, R_ckvnT), (mt1, R_mt), (qb, R_qb) = [DB[k][b] for k in ("cqnT", "ckvnT", "mt1", "qb")]
            hT3 = hT.rearrange("p (c t) -> p c t", c=8)
            R_ssm = R_ssm2[b]
            (lat_sb, R_lat), (tA2, R_tA2), (tB2, R_tB2), (tA3, R_tA3), (tB3, R_tB3) = [DB[k][b] for k in ("lat_sb", "tA2", "tB2", "tA3", "tB3")]
            full = n >= HALO
            g, gi = divmod(n, 4)
            gb = g % 2
            if n + 1 < NT:
                load_x(n + 1)
            xb_, Rx = xbuf[n % 2], R_x[n % 2]
            ss = st_small[:, 0:1]
            rstd = st_small[:, 1:2]
            S.op("act", lambda e, xb_=xb_: e.activation(out=hb, in_=xb_, func=AF.Square, accum_out=ss),
                 reads=[Rx], writes=[R_hb, R_ss])
            S.op("dve", lambda e: e.tensor_scalar(out=ss, in0=ss, scalar1=1.0 / 1024, scalar2=EPS, op0=ALU.mult, op1=ALU.add),
                 reads=[R_ss], writes=[R_ss])
            S.op("act", lambda e: e.activation(out=ss, in_=ss, func=AF.Sqrt), reads=[R_ss], writes=[R_ss])
            S.op("dve", lambda e: e.reciprocal(out=rstd, in_=ss), reads=[R_ss], writes=[R_ss])
            S.op("dve", lambda e, xb_=xb_: e.scalar_tensor_tensor(out=hb, in0=xb_, scalar=rstd, in1=anw_t, op0=ALU.mult, op1=ALU.mult),
                 reads=[Rx, R_ss, R_anw], writes=[R_hb])
            tb = bank(0).bitcast(BF16)
            tbh = bank(4).bitcast(BF16)
            for c in range(8):
                S.op("pe", lambda e, c=c: e.transpose(out=tbh[:, c * 128:(c + 1) * 128], in_=hb[:, c * 128:(c + 1) * 128], identity=ident),
                     reads=[R_hb, R_ident], writes=[PB[4]], inc=(c == 7))
            S.op("act", lambda e: e.activation(out=hT, in_=tbh, func=AF.Copy), reads=[PB[4]], writes=[R_hT])
            if KSUB == 1 and n >= HALO:
                return


            def proj(bk, col0, ncol, n_=None):
                for c in range(8):
                    S.op("pe", lambda e, c=c: e.matmul(bank(bk, ncol), lhsT=hT3[:, c, :], rhs=w_in3[:, c, col0:col0 + ncol],
                                                       start=(c == 0), stop=(c == 7)),
                         reads=[R_hT, R_win], writes=[PB[bk]], inc=(c == 7))

            if full:
                proj(1, 0, 512)
            proj(2, 512, 512)
            proj(3, 1024, 512)
            if full:
                proj(4, 1536, 512)
                S.op("act", lambda e: e.activation(out=qk_sb, in_=ps[:, 512:1536], func=AF.Copy), reads=[PB[1], PB[2]], writes=[R_qksb])
                proj(1, 2048, 416)
                S.op("act", lambda e: e.activation(out=lat_sb, in_=bank(1, 416), func=AF.Copy), reads=[PB[1]], writes=[R_lat])
            else:
                S.op("act", lambda e: e.activation(out=qk_sb[:, 512:1024], in_=ps[:, 1024:1536], func=AF.Copy), reads=[PB[2]], writes=[R_qksb])
                proj(1, 2304, 160)
                S.op("act", lambda e: e.activation(out=lat_sb[:, 0:160], in_=bank(1, 160), func=AF.Copy), reads=[PB[1]], writes=[R_lat])
            if KSUB == 2 and n >= HALO:
                return

            latoff = 0 if full else -256
            if n % TB == 0:
                make_tables(n)
            R_tab = R_tabs[(n // TB) % 2]
            tabn = tabs[(n // TB) % 2].rearrange("p (n f) -> p n f", n=TB)[:, n % TB, :]
            cs_r, ss_r = tabn[:, 0:64], tabn[:, 64:128]
            cs_m, ss_m = tabn[:, 128:160], tabn[:, 160:192]

            def rope(src_ap, nh, hd, cs, sn, dstA, dstB, dst, reads, wres, RA=None, RB=None, add_eng="dve"):
                RA = R_tA if RA is None else RA
                RB = R_tB if RB is None else RB
                half = hd // 2
                x3 = src_ap.rearrange("p (h d) -> p h d", h=nh)
                sw = bass.AP(src_ap.tensor, src_ap.offset + half,
                             [list(src_ap.ap[0]), [hd, nh], [-half, 2], [1, half]])
                a3 = dstA.rearrange("p (h d) -> p h d", h=nh)
                b4 = dstB.rearrange("p (h a d) -> p h a d", h=nh, a=2)
                S.op("dve", lambda e: e.tensor_tensor(out=a3, in0=x3, in1=bc_mid(cs, nh), op=ALU.mult),
                     reads=reads + [R_tab], writes=[RA])
                S.op("dve", lambda e: e.tensor_tensor(out=b4, in0=sw, in1=bc_mid(sn.rearrange("p (a d) -> p a d", a=2), nh), op=ALU.mult),
                     reads=reads + [R_tab], writes=[RB])
                S.op(add_eng, lambda e: e.tensor_tensor(out=dst, in0=dstA, in1=dstB, op=ALU.add),
                     reads=[RA, RB], writes=[wres])

            if full:
                rope(qk_sb, 16, 64, cs_r, ss_r, tmpA, tmpB, qkr, [R_qksb], R_qkr, add_eng="dve")
            else:
                rope(qk_sb[:, 512:1024], 8, 64, cs_r, ss_r, tmpA[:, 0:512], tmpB[:, 0:512], qkr[:, 512:1024], [R_qksb], R_qkr, add_eng="dve")
            S.op("dve", lambda e: e.tensor_tensor(out=kz, in0=qkr[:, 512:1024], in1=zeta_t, op=ALU.mult),
                 reads=[R_qkr, R_zeta], writes=[R_kz])
            S.op("act", lambda e: e.activation(out=vb, in_=bank(3), func=AF.Copy), reads=[PB[3]], writes=[R_vb])
            if KSUB == 3 and n >= HALO:
                return


            if full:
                S.op("act", lambda e: e.activation(out=sg, in_=bank(4), func=AF.Silu), reads=[PB[4]], writes=[R_sg])
                for c in range(8):
                    S.op("pe", lambda e, c=c: e.transpose(out=tb[:, c * 128:(c + 1) * 128], in_=qkr[:, c * 128:(c + 1) * 128], identity=ident),
                         reads=[R_qkr, R_ident], writes=[PB[0]], inc=(c == 7))
                S.op("act", lambda e: e.activation(out=qT[0:64, 0:512], in_=tb[0:64, 0:512], func=AF.Copy), reads=[PB[0]], writes=[R_qT])
                S.op("act", lambda e: e.activation(out=qT[64:128, 512:1024], in_=tb[64:128, 0:512], func=AF.Copy), reads=[PB[0]], writes=[R_qT])
                S.op("dve", lambda e: e.tensor_tensor(out=qxT, in0=tb[:, 0:512], in1=xi_t, op=ALU.mult), reads=[PB[0], R_xi], writes=[R_qxT])
                S.op("act", lambda e: e.activation(out=kT, in_=tb[:, 512:1024], func=AF.Copy), reads=[PB[0]], writes=[R_kT])
                for h in range(8):
                    p_, a_ = divmod(h, 2)
                    rows = slice(a_ * 64, a_ * 64 + 64)
                    S.op("pe", lambda e, h=h, p_=p_, rows=rows: e.matmul(ps[:, 2560 + h * 128:2560 + (h + 1) * 128],
                                                                         lhsT=kT[:, p_ * 128:(p_ + 1) * 128],
                                                                         rhs=qT[:, (h % 2) * 512 + p_ * 128:(h % 2) * 512 + (p_ + 1) * 128],
                                                                         start=True, stop=True),
                         reads=[R_kT, R_qT], writes=[PB[5], PB[6]], inc=(h == 7))
                S.op("dve", lambda e: e.tensor_tensor(out=sd, in0=ps[:, 2560:3584], in1=dt_t, op=ALU.mult),
                     reads=[PB[5], PB[6], R_dt], writes=[R_sd])
                for h in range(8):
                    p_, a_ = divmod(h, 2)
                    rows = slice(a_ * 64, a_ * 64 + 64)
                    S.op("pe", lambda e, h=h: e.matmul(ps[:, 3584 + h * 64:3584 + (h + 1) * 64], lhsT=sd[:, h * 128:(h + 1) * 128],
                                                       rhs=vb[:, h * 64:(h + 1) * 64], start=True, stop=False),
                         reads=[R_sd, R_vb], writes=[PB[7]], inc=False)
                    S.op("pe", lambda e, h=h, p_=p_, rows=rows: e.matmul(ps[:, 3584 + h * 64:3584 + (h + 1) * 64],
                                                                         lhsT=qxT[:, p_ * 128:(p_ + 1) * 128],
                                                                         rhs=Rb[:, h * 64:(h + 1) * 64],
                                                                         start=False, stop=True),
                         reads=[R_qxT, R_Rb], writes=[PB[7]], inc=(h == 7))
            for p_ in range(4):
                S.op("pe", lambda e, p_=p_: e.matmul(ps[:, 2560 + p_ * 128:2560 + (p_ + 1) * 128], lhsT=kz[:, p_ * 128:(p_ + 1) * 128],
                                                     rhs=vb[:, p_ * 128:(p_ + 1) * 128], start=True, stop=True),
                     reads=[R_kz, R_vb], writes=[PB[5]], inc=(p_ == 3))
            for h in range(8):
                p_, a_ = divmod(h, 2)
                rows = slice(a_ * 64, a_ * 64 + 64)
                S.op("dve", lambda e, h=h, p_=p_, a_=a_, rows=rows: e.scalar_tensor_tensor(
                    out=R32[rows, h * 64:(h + 1) * 64], in0=R32[rows, h * 64:(h + 1) * 64], scalar=dec_t[rows, p_:p_ + 1],
                    in1=ps[rows, 2560 + p_ * 128 + a_ * 64:2560 + p_ * 128 + a_ * 64 + 64], op0=ALU.mult, op1=ALU.add),
                    reads=[PB[5], R_dec, R_R32], writes=[R_R32])
            S.op("act", lambda e: e.activation(out=Rb, in_=R32, func=AF.Copy), reads=[R_R32], writes=[R_Rb])
            if KSUB == 4 and n >= HALO:
                return


            if full:
                o3 = bank(7).rearrange("p (h d) -> p h d", h=8)
                s1, s2, mean, msq, var = (mt1[:, 0:8], mt1[:, 8:16], mt1[:, 16:24], mt1[:, 24:32], mt1[:, 32:40])
                S.op("dve", lambda e: e.tensor_reduce(out=s1, in_=o3, axis=AX.X, op=ALU.add), reads=[PB[7]], writes=[R_mt])
                S.op("act", lambda e: e.activation(out=gn1, in_=bank(7), func=AF.Square), reads=[PB[7]], writes=[R_gn1])
                S.op("dve", lambda e: e.tensor_reduce(out=s2, in_=gn1.rearrange("p (h d) -> p h d", h=8), axis=AX.X, op=ALU.add),
                     reads=[R_gn1], writes=[R_mt])
                S.op("dve", lambda e: e.tensor_scalar(out=mean, in0=s1, scalar1=1.0 / 64, scalar2=None, op0=ALU.mult), reads=[R_mt], writes=[R_mt])
                S.op("dve", lambda e: e.tensor_tensor(out=msq, in0=mean, in1=mean, op=ALU.mult), reads=[R_mt], writes=[R_mt])
                S.op("dve", lambda e: e.scalar_tensor_tensor(out=var, in0=s2, scalar=1.0 / 64, in1=msq, op0=ALU.mult, op1=ALU.subtract),
                     reads=[R_mt], writes=[R_mt])
                S.op("dve", lambda e: e.tensor_scalar(out=var, in0=var, scalar1=EPS, scalar2=None, op0=ALU.add), reads=[R_mt], writes=[R_mt])
                S.op("act", lambda e: e.activation(out=var, in_=var, func=AF.Sqrt), reads=[R_mt], writes=[R_mt])
                S.op("dve", lambda e: e.reciprocal(out=var, in_=var), reads=[R_mt], writes=[R_mt])
                g13 = gn1.rearrange("p (h d) -> p h d", h=8)
                S.op("dve", lambda e: e.tensor_tensor(out=g13, in0=o3, in1=bc_last(mean, 64), op=ALU.subtract),
                     reads=[PB[7], R_mt], writes=[R_gn1])
                S.op("dve", lambda e: e.tensor_tensor(out=g13, in0=g13, in1=bc_last(var, 64), op=ALU.mult), reads=[R_gn1, R_mt], writes=[R_gn1])
                S.op("pool", lambda e: e.tensor_tensor(out=gn2, in0=sg, in1=gnw_t, op=ALU.mult), reads=[R_sg, R_gnw], writes=[R_gn2])
                S.op("dve", lambda e: e.tensor_tensor(out=yb, in0=gn1, in1=gn2, op=ALU.mult), reads=[R_gn1, R_gn2], writes=[R_yb])
                for c in range(4):
                    S.op("pe", lambda e, c=c: e.transpose(out=tb[:, c * 128:(c + 1) * 128], in_=yb[:, c * 128:(c + 1) * 128], identity=ident),
                         reads=[R_yb, R_ident], writes=[PB[0]], inc=(c == 3))
                m = n - HALO
                S.op("act", lambda e: e.activation(out=yT, in_=tb[:, 0:512], func=AF.Copy), reads=[PB[0]], writes=[R_yT])
                S.dma("sp", "ymr%d" % b, [lambda e: e.dma_start(out=s_mix[m, :, 0:4, :], in_=yT.rearrange("p (c t) -> p c t", c=4))],
                      reads=[R_yT], writes=[R_smix[m]], nbytes=131072)

            lat = lat_sb
            ckv_ap = lat[:, 256 + latoff:384 + latoff]
            kpe_ap = lat[:, 384 + latoff:416 + latoff]
            ssq = st_small[:, 4:6]
            rq = st_small[:, 6:8]
            if full:
                S.op("act", lambda e: e.activation(out=cqn, in_=lat[:, 0:256], func=AF.Square, accum_out=ssq[:, 0:1]),
                     reads=[R_lat], writes=[R_cqn, R_ssm])
            S.op("act", lambda e: e.activation(out=ckvn, in_=ckv_ap, func=AF.Square, accum_out=ssq[:, 1:2]),
                 reads=[R_lat], writes=[R_ckvn, R_ssm])
            if full:
                S.op("dve", lambda e: e.tensor_scalar(out=ssq[:, 0:1], in0=ssq[:, 0:1], scalar1=1.0 / 256, scalar2=EPS, op0=ALU.mult, op1=ALU.add),
                     reads=[R_ssm], writes=[R_ssm])
            S.op("dve", lambda e: e.tensor_scalar(out=ssq[:, 1:2], in0=ssq[:, 1:2], scalar1=1.0 / 128, scalar2=EPS, op0=ALU.mult, op1=ALU.add),
                 reads=[R_ssm], writes=[R_ssm])
            lo = 0 if full else 1
            S.op("act", lambda e, lo=lo: e.activation(out=ssq[:, lo:2], in_=ssq[:, lo:2], func=AF.Sqrt), reads=[R_ssm], writes=[R_ssm])
            S.op("dve", lambda e, lo=lo: e.reciprocal(out=rq[:, lo:2], in_=ssq[:, lo:2]), reads=[R_ssm], writes=[R_ssm])
            if full:
                S.op("dve", lambda e: e.scalar_tensor_tensor(out=cqn, in0=lat[:, 0:256], scalar=rq[:, 0:1], in1=qnw_t, op0=ALU.mult, op1=ALU.mult),
                     reads=[R_lat, R_ssm, R_qnw], writes=[R_cqn])
            S.op("dve", lambda e: e.scalar_tensor_tensor(out=ckvn, in0=ckv_ap, scalar=rq[:, 1:2], in1=kvnw_t, op0=ALU.mult, op1=ALU.mult),
                 reads=[R_lat, R_ssm, R_kvnw], writes=[R_ckvn])
            rope(kpe_ap, 1, 32, cs_m, ss_m, tA2, tB2, kr, [R_lat], R_kr, RA=R_tA2, RB=R_tB2)
            if KSUB == 5 and n >= HALO:
                return

            S.op("pe", lambda e: e.transpose(out=tb[:, 0:128], in_=ckvn, identity=ident), reads=[R_ckvn, R_ident], writes=[PB[0]], inc=False)
            S.op("pe", lambda e: e.transpose(out=tb[0:32, 128:256], in_=kr, identity=ident), reads=[R_kr, R_ident], writes=[PB[0]], inc=not full)
            if full:
                for c in range(2):
                    S.op("pe", lambda e, c=c: e.transpose(out=tb[:, 256 + c * 128:256 + (c + 1) * 128], in_=cqn[:, c * 128:(c + 1) * 128], identity=ident),
                         reads=[R_cqn, R_ident], writes=[PB[0]], inc=(c == 1))
            S.op("act", lambda e: e.activation(out=ckvnT, in_=tb[:, 0:128], func=AF.Copy), reads=[PB[0]], writes=[R_ckvnT])
            S.op("act", lambda e, gb=gb, gi=gi: e.activation(out=kpe_g[gb][0:32, gi * 128:(gi + 1) * 128], in_=tb[0:32, 128:256], func=AF.Copy),
                 reads=[PB[0]], writes=[R_kpg[gb]])
            if KSUB == 6 and n >= HALO:
                return

            if full:
                S.op("act", lambda e: e.activation(out=cqnT, in_=tb[:, 256:512], func=AF.Copy), reads=[PB[0]], writes=[R_cqnT])
            for p_ in range(4):
                S.op("pe", lambda e, p_=p_: e.matmul(ps[:, 3072 + p_ * 128:3072 + (p_ + 1) * 128], lhsT=wk[:, p_ * 128:(p_ + 1) * 128], rhs=ckvnT,
                                                     start=True, stop=True), reads=[R_wk, R_ckvnT], writes=[PB[6]], inc=(p_ == 3))
            kng3 = kn_g[gb].rearrange("p (a k) -> p a k", a=4)
            S.op("act", lambda e, gi=gi, kng3=kng3: e.activation(out=kng3[:, :, gi * 128:(gi + 1) * 128], in_=bank(6).rearrange("p (a k) -> p a k", a=4), func=AF.Copy),
                 reads=[PB[6]], writes=[R_kng[gb]])
            S.op("pe", lambda e: e.matmul(bank(5), lhsT=ckvnT, rhs=wv, start=True, stop=True), reads=[R_wv, R_ckvnT], writes=[PB[5]])
            vg4 = v_g[gb].rearrange("p (h t e) -> p h t e", h=8, t=4)
            S.op("act", lambda e, gi=gi, vg4=vg4: e.activation(out=vg4[:, :, gi, 0:64], in_=bank(5).rearrange("p (h e) -> p h e", h=8), func=AF.Copy),
                 reads=[PB[5]], writes=[R_vg[gb]])
            S.op("dve", lambda e, gi=gi, vg4=vg4, n=n: e.tensor_copy(out=vg4[:, :, gi, 64:65], in_=bc_mid(valid_t[:, n:n + 1], 8)),
                 reads=[R_valid], writes=[R_vg[gb]])
            if KSUB == 7 and n >= HALO:
                return

            if full:
                for hf in range(2):
                    for c in range(2):
                        S.op("pe", lambda e, hf=hf, c=c: e.matmul(ps[:, (5 + hf) * 512:(5 + hf) * 512 + 384], lhsT=cqnT[:, c * 128:(c + 1) * 128],
                                                                  rhs=w_uq3[:, c, hf * 384:(hf + 1) * 384], start=(c == 0), stop=(c == 1)),
                             reads=[R_cqnT, R_wuq], writes=[PB[5 + hf]], inc=(c == 1))
                qb3 = qb.rearrange("p (h d) -> p h d", h=8)
                for hf in range(2):
                    src = ps[:, (5 + hf) * 512:(5 + hf) * 512 + 384]
                    s3 = src.rearrange("p (h d) -> p h d", h=4)
                    S.op("act", lambda e, hf=hf, s3=s3: e.activation(out=qb3[:, hf * 4:(hf + 1) * 4, 0:64], in_=s3[:, :, 0:64], func=AF.Copy),
                         reads=[PB[5 + hf]], writes=[R_qb])
                    x3 = s3[:, :, 64:96]
                    sw = bass.AP(src.tensor, src.offset + 64 + 16, [list(src.ap[0]), [96, 4], [-16, 2], [1, 16]])
                    a3 = tA3[:, hf * 128:(hf + 1) * 128].rearrange("p (h d) -> p h d", h=4)
                    b4 = tB3[:, hf * 128:(hf + 1) * 128].rearrange("p (h a d) -> p h a d", h=4, a=2)
                    S.op("dve", lambda e, x3=x3, a3=a3: e.tensor_tensor(out=a3, in0=x3, in1=bc_mid(cs_m, 4), op=ALU.mult),
                         reads=[PB[5 + hf], R_tab], writes=[R_tA3])
                    S.op("dve", lambda e, sw=sw, b4=b4: e.tensor_tensor(out=b4, in0=sw, in1=bc_mid(ss_m.rearrange("p (a d) -> p a d", a=2), 4), op=ALU.mult),
                         reads=[PB[5 + hf], R_tab], writes=[R_tB3])
                    S.op("dve", lambda e, hf=hf, a3=a3: e.tensor_tensor(out=qb3[:, hf * 4:(hf + 1) * 4, 64:96], in0=a3,
                                                                        in1=tB3[:, hf * 128:(hf + 1) * 128].rearrange("p (h d) -> p h d", h=4), op=ALU.add),
                         reads=[R_tA3, R_tB3], writes=[R_qb])
                for h in range(8):
                    S.op("pe", lambda e, h=h: e.transpose(out=tb[0:96, h * 128:(h + 1) * 128], in_=qb[:, h * 96:(h + 1) * 96], identity=ident),
                         reads=[R_qb, R_ident], writes=[PB[0]], inc=(h == 7))
                mg, mi = divmod(n - HALO, 4)
                qg3 = q_g[mg % 2].rearrange("p (h t) -> p h t", h=8)
                S.op("act", lambda e, mi=mi, qg3=qg3: e.activation(out=qg3[0:96, :, mi * 128:(mi + 1) * 128],
                                                                   in_=tb[0:96, :].rearrange("p (h t) -> p h t", h=8), func=AF.Copy),
                     reads=[PB[0]], writes=[R_qg[mg % 2]])
                if mi == 3 or n == NT - 1:
                    ntl = mi + 1
                    S.dma("sp", "sq%d" % (mg % 2),
                          [lambda e, mg=mg, ntl=ntl, qg3=qg3: e.dma_start(
                              out=s_qt[:, :, mg * 512:mg * 512 + ntl * 128].rearrange("h r t -> r h t"),
                              in_=qg3[0:96, :, 0:ntl * 128])],
                          reads=[R_qg[mg % 2]], writes=[Res("sqt")])
            if gi == 3:
                fns = []
                for a_ in range(2):
                    fns.append(lambda e, a_=a_, g=g, kng3=kng3: e.dma_start(
                        out=s_kt[:, 0:64, g * 512:(g + 1) * 512].rearrange("(p a) r k -> a r p k", a=2)[a_],
                        in_=kng3[a_ * 64:(a_ + 1) * 64, :, :]))
                fns.append(lambda e, g=g, gb=gb: e.dma_start(
                    out=s_kt[:, 64:96, g * 512:(g + 1) * 512].rearrange("h r k -> r h k"),
                    in_=bc_mid(kpe_g[gb][0:32, :], 8)))
                S.dma("sp", "sk%d" % gb, fns, reads=[R_kng[gb], R_kpg[gb]], writes=[Res("skt")])
                S.dma("sp", "sv%d" % gb,
                      [lambda e, g=g, vg4=vg4: e.dma_start(
                          out=s_v[:, :, g * 4 * 65:(g + 1) * 4 * 65].rearrange("h p (t e) -> p h t e", t=4),
                          in_=vg4)],
                      reads=[R_vg[gb]], writes=[Res("sv")])

        for n in range(int(os.environ.get('KNT', NT))):
            _tileA(n)
        while pk[1] < len(pieces):
            cast_step(2)

        S.barrier()
        if KSTOP == 'A':
            S.emit(); return nc
        A.off = mark_persist
        ytmp = [A.bf(512) for _ in range(2)]
        R_ytmp = [Res("ytmp0"), Res("ytmp1")]
        QT = [A.bf(NOWN * 128) for _ in range(2)]
        KT = [A.bf(NT * 128) for _ in range(2)]
        VV = [A.bf(NT * 65) for _ in range(2)]
        R_Q, R_K, R_V = [Res("Q0"), Res("Q1")], [Res("K0"), Res("K1")], [Res("V0"), Res("V1")]
        PT = [A.bf(1024) for _ in range(6)]
        R_PT = [Res("PT%d" % i) for i in range(6)]
        rrow = A.f32(512)
        R_rrow = Res("rrow")
        ones_t = A.f32(64)
        R_ones = Res("ones")
        bcs = A.f32(512)
        R_bcs = Res("bcs")
        S.op("dve", lambda e: e.memset(ones_t, 1.0), writes=[R_ones])
        scale = (64 + 32) ** -0.5

        def load_head(h):
            i = h % 2
            S.dma("sp", "lq%d" % i, [lambda e: e.dma_start(out=QT[i][0:96, :], in_=s_qt[h])], reads=[R_sqt], writes=[R_Q[i]])
            S.dma("sp", "lk%d" % i, [lambda e: e.dma_start(out=KT[i][0:96, :], in_=s_kt[h])], reads=[R_skt], writes=[R_K[i]])
            S.dma("sp", "lv%d" % i, [lambda e: e.dma_start(out=VV[i], in_=s_v[h])], reads=[R_sv], writes=[R_V[i]])

        load_head(0)

        groups = []
        blk_id = 0
        for h in range(8):
            qblocks = [(0, 128, [(kt, 0) for kt in range(HALO)] + [(HALO, 0)], HALO)]
            for j in range(8):
                kts = [(kt, 0) for kt in range(32 + 4 * j)] + [(32 + 4 * j + m, 128 * m) for m in range(4)]
                qblocks.append((128 + 512 * j, 512, kts, 32 + 4 * j))
            for (q0, qw, kts, diag0) in qblocks:
                npairs = (len(kts) + 1) // 2
                for gidx in range(npairs):
                    groups.append(dict(h=h, i=h % 2, q0=q0, qw=qw, pair=kts[2 * gidx:2 * gidx + 2], diag0=diag0,
                                       ob=4 + blk_id % 2, first=(gidx == 0), last=(gidx == npairs - 1),
                                       sb=(len(groups) % 2) * 2, pt=len(groups) % 6, yi=blk_id % 2,
                                       newhead=(gidx == 0 and q0 == 0)))
                blk_id += 1

        def emit_qk(g):
            i, q0, qw, sb_ = g["i"], g["q0"], g["qw"], g["sb"]
            if g["newhead"] and g["h"] + 1 < 8:
                load_head(g["h"] + 1)
            for u, (kt, c0) in enumerate(g["pair"]):
                dst = ps[:, (sb_ + u) * 512 + c0:(sb_ + u) * 512 + qw]
                isdiag = kt >= g["diag0"]
                S.op("pe", lambda e, kt=kt, c0=c0, dst=dst, isdiag=isdiag: e.matmul(
                    dst, lhsT=KT[i][0:96, kt * 128:(kt + 1) * 128], rhs=QT[i][0:96, q0 + c0:q0 + qw], start=True, stop=not isdiag),
                    reads=[R_K[i], R_Q[i]], writes=[PB[sb_ + u]], inc=not isdiag)
                if isdiag:
                    S.op("pe", lambda e, dst=dst: e.matmul(dst[:, 0:128], lhsT=ident, rhs=maskb, start=False, stop=True),
                         reads=[R_ident, R_mask], writes=[PB[sb_ + u]])

        def emit_exp(g):
            qw, sb_, pair = g["qw"], g["sb"], g["pair"]
            pt, Rpt = PT[g["pt"]], R_PT[g["pt"]]
            if len(pair) == 2 and pair[0][1] == 0 and pair[1][1] == 0 and qw == 512:
                S.op("act", lambda e: e.activation(out=pt, in_=ps[:, sb_ * 512:sb_ * 512 + 1024], func=AF.Exp, scale=scale),
                     reads=[PB[sb_], PB[sb_ + 1]], writes=[Rpt])
            else:
                for u, (kt, c0) in enumerate(pair):
                    S.op("act", lambda e, u=u, c0=c0: e.activation(
                        out=pt[:, u * 512 + c0:u * 512 + qw], in_=ps[:, (sb_ + u) * 512 + c0:(sb_ + u) * 512 + qw], func=AF.Exp, scale=scale),
                        reads=[PB[sb_ + u]], writes=[Rpt])

        def emit_pv(g):
            i, qw, ob, pair = g["i"], g["qw"], g["ob"], g["pair"]
            pt, Rpt = PT[g["pt"]], R_PT[g["pt"]]
            V3 = VV[i].rearrange("p (t e) -> p t e", t=NT)
            for u, (kt, c0) in enumerate(pair):
                first = g["first"] and u == 0
                last = g["last"] and u == len(pair) - 1
                S.op("pe", lambda e, kt=kt, c0=c0, u=u, first=first, last=last: e.matmul(
                    ps[0:65, ob * 512 + c0:ob * 512 + qw], lhsT=V3[:, kt, :], rhs=pt[:, u * 512 + c0:u * 512 + qw], start=first, stop=last),
                    reads=[R_V[i], Rpt], writes=[PB[ob]], inc=(u == len(pair) - 1))

        def emit_norm(g):
            h, q0, qw, ob, yi_ = g["h"], g["q0"], g["qw"], g["ob"], g["yi"]
            S.op("dve", lambda e: e.tensor_scalar(out=rrow[64:65, 0:qw], in0=ps[64:65, ob * 512:ob * 512 + qw], scalar1=1e-30, scalar2=None, op0=ALU.max),
                 reads=[PB[ob]], writes=[R_rrow])
            S.op("dve", lambda e: e.reciprocal(out=rrow[64:65, 0:qw], in_=rrow[64:65, 0:qw]), reads=[R_rrow], writes=[R_rrow])
            S.op("pe", lambda e: e.matmul(ps[0:64, 6 * 512:6 * 512 + qw], lhsT=ones_t[64:65, 0:64], rhs=rrow[64:65, 0:qw], start=True, stop=True),
                 reads=[R_ones, R_rrow], writes=[PB[6]])
            S.op("dve", lambda e: e.tensor_copy(out=bcs[0:64, 0:qw], in_=ps[0:64, 6 * 512:6 * 512 + qw]), reads=[PB[6]], writes=[R_bcs])
            S.op("dve", lambda e: e.tensor_tensor(out=ytmp[yi_][0:64, 0:qw], in0=ps[0:64, ob * 512:ob * 512 + qw], in1=bcs[0:64, 0:qw], op=ALU.mult),
                 reads=[PB[ob], R_bcs], writes=[R_ytmp[yi_]])
            m0_, nt_ = q0 // 128, qw // 128
            S.dma("sp", "ym%d" % yi_, [lambda e: e.dma_start(
                out=s_mix[m0_:m0_ + nt_, (h % 2) * 64:(h % 2) * 64 + 64, 4 + h // 2, :].rearrange("m r t -> r m t"),
                in_=ytmp[yi_][0:64, 0:qw].rearrange("r (m t) -> r m t", m=nt_))],
                reads=[R_ytmp[yi_]], writes=R_smix[m0_:m0_ + nt_], nbytes=65536)

        pend_norm = None
        emit_qk(groups[0])
        for gi_, g in enumerate(groups):
            emit_exp(g)
            if gi_ + 1 < len(groups):
                emit_qk(groups[gi_ + 1])
            emit_pv(g)
            if pend_norm is not None:
                emit_norm(pend_norm)
                pend_norm = None
            if g["last"]:
                pend_norm = g
        if pend_norm is not None:
            emit_norm(pend_norm)

        S.barrier()
        if KSTOP == 'AB':
            S.emit(); return nc
        A.off = mark_persist
        wout = A.bf(8 * 1024)
        wout3 = wout.rearrange("p (c f) -> p c f", c=8)
        wdown = A.bf(22 * 1024)
        wdown3 = wdown.rearrange("p (c f) -> p c f", c=22)
        R_wout, R_wdown = Res("wout"), Res("wdown")
        S.dma("sp", "wc1", [lambda e: e.dma_start(out=wout, in_=s_wout)], reads=[R_scr["s_wout"]], writes=[R_wout])
        S.dma("sp", "wc2", [lambda e: e.dma_start(out=wdown, in_=s_wdown)], reads=[R_scr["s_wdown"]], writes=[R_wdown])
        fnw_t, R_fnw = load_const(b_fnw, 1024, "fnw")
        onw_t, R_onw = load_const(b_onw, 1024, "onw")
        cw_t, R_cw = load_const(c_cw, 132, "cw")
        cw3 = cw_t.rearrange("p (c j) -> p c j", c=44)
        cb_t, R_cb = load_const(c_cb, 44, "cb")
        mixb = [A.bf(1024) for _ in range(2)]
        R_mixb = [Res("mixb%d" % i) for i in range(2)]
        NWU = 3
        wupb = [A.bf(2048) for _ in range(NWU)]
        R_wupb = [Res("wup%d" % i) for i in range(NWU)]
        xb2 = [A.f32(1024) for _ in range(2)]
        R_xb2 = [Res("xb2_%d" % i) for i in range(2)]
        jnk = A.bf(1024)
        R_jnk = Res("jnk")
        x1 = [A.f32(4 * 1024) for _ in range(2)]
        R_x1 = [[Res("x1_%d_%d" % (j, i)) for i in range(4)] for j in range(2)]
        h2b = [A.bf(1024) for _ in range(2)]
        R_h2b = [Res("h2b0"), Res("h2b1")]
        h2T = [A.bf(8 * 512) for _ in range(2)]
        R_h2T = [Res("h2T0"), Res("h2T1")]
        gT = [A.bf(22 * 512) for _ in range(2)]
        R_gT = [Res("gT0"), Res("gT1")]
        ubuf = [A.f32(514) for _ in range(2)]
        R_ub = [Res("ub0"), Res("ub1")]
        acc = [[A.f32(512) for _ in range(2)] for _ in range(2)]
        R_acc = [[Res("acc%d%d" % (h_, p_)) for p_ in range(2)] for h_ in range(2)]
        carry = A.f32(44 * 2)
        carry3 = carry.rearrange("p (c j) -> p c j", c=44)
        R_carry = [Res("carry%d" % i) for i in range(44)]
        st2 = [A.f32(8) for _ in range(2)]
        R_st2 = [Res("st2_0"), Res("st2_1")]
        R_st3 = [Res("st3_0"), Res("st3_1")]
        ybuf = xb2
        R_yb2 = R_xb2
        S.op("dve", lambda e: e.memset(carry, 0.0), writes=R_carry)
        wupi = [0]
        xli = [0]
        ybi = [0]
        y_tiles = yout.rearrange("(n p) d -> n p d", p=128)

        blocks = [(0, 1)] + [(1 + 4 * j, 4) for j in range(8)]
        wuc = [0]

        def _blockC(bi, m0, ntl):
            W = ntl * 128
            pb = bi % 2
            x13 = x1[pb].rearrange("p (t d) -> p t d", t=4)
            Rx1 = R_x1[pb]
            h2T3 = h2T[pb].rearrange("p (c t) -> p c t", c=8)
            gT3 = gT[pb].rearrange("p (c t) -> p c t", c=22)
            tb = bank(7).bitcast(BF16)

            def _s1(t):
                m = m0 + t
                xi_ = xli[0] % 2
                xli[0] += 1
                S.dma("sp", "xc%d" % xi_, [lambda e: e.dma_start(out=xb2[xi_], in_=x_tiles[HALO + m])], writes=[R_xb2[xi_]], nbytes=524288)
                mi_ = m % 2
                S.dma("sp", "mx%d" % mi_, [lambda e: e.dma_start(out=mixb[mi_].rearrange("p (c t) -> p c t", c=8), in_=s_mix[m])],
                      reads=[R_smix[m]], writes=[R_mixb[mi_]])
                mix3 = mixb[mi_].rearrange("p (c t) -> p c t", c=8)
                for hf in range(2):
                    for c in range(8):
                        S.op("pe", lambda e, c=c, hf=hf: e.matmul(bank(hf), lhsT=mix3[:, c, :], rhs=wout3[:, c, hf * 512:(hf + 1) * 512],
                                                                  start=(c == 0), stop=(c == 7)),
                             reads=[R_mixb[mi_], R_wout], writes=[PB[hf]], inc=(c == 7))
                S.op("dve", lambda e: e.tensor_tensor(out=x13[:, t, :], in0=ps[:, 0:1024], in1=xb2[xi_], op=ALU.add),
                     reads=[PB[0], PB[1], R_xb2[xi_]], writes=[Rx1[t]])
                tp = t % 2
                hb_, Rhb = h2b[tp], R_h2b[tp]
                ss, rstd, Rst = st2[tp][:, 0:1], st2[tp][:, 1:2], R_st2[tp]
                S.op("act", lambda e: e.activation(out=hb_, in_=x13[:, t, :], func=AF.Square, accum_out=ss), reads=[Rx1[t]], writes=[Rhb, Rst])
                S.op("dve", lambda e: e.tensor_scalar(out=ss, in0=ss, scalar1=1.0 / 1024, scalar2=EPS, op0=ALU.mult, op1=ALU.add), reads=[Rst], writes=[Rst])
                S.op("act", lambda e: e.activation(out=ss, in_=ss, func=AF.Sqrt), reads=[Rst], writes=[Rst])
                S.op("dve", lambda e: e.reciprocal(out=rstd, in_=ss), reads=[Rst], writes=[Rst])
                S.op("dve", lambda e: e.scalar_tensor_tensor(out=hb_, in0=x13[:, t, :], scalar=rstd, in1=fnw_t, op0=ALU.mult, op1=ALU.mult),
                     reads=[Rx1[t], Rst, R_fnw], writes=[Rhb])
                for c in range(8):
                    S.op("pe", lambda e, c=c: e.transpose(out=tb[:, c * 128:(c + 1) * 128], in_=hb_[:, c * 128:(c + 1) * 128], identity=ident),
                         reads=[Rhb, R_ident], writes=[PB[7]], inc=(c == 7))
                S.op("act", lambda e: e.activation(out=h2T3[:, :, t * 128:(t + 1) * 128], in_=tb.rearrange("p (c t) -> p c t", c=8), func=AF.Copy),
                     reads=[PB[7]], writes=[R_h2T[pb]])

            for t in range(ntl):
                _s1(t)

            def _chunk(fc, half):
                cidx = fc + 22 * half
                fp = fc % 2
                if half == 0:
                    wupi[0] += 1
                wi = wupi[0] % NWU
                if half == 0:
                    S.dma("sp", "wu%d" % wi, [lambda e: e.dma_start(out=wupb[wi], in_=s_wup[:, fc * 2048:(fc + 1) * 2048])],
                          writes=[R_wupb[wi]], nbytes=524288)
                bk = 2 + wuc[0] % 3
                wuc[0] += 1
                w3 = wupb[wi][:, half * 1024:(half + 1) * 1024].rearrange("p (c f) -> p c f", c=8)
                for c in range(8):
                    S.op("pe", lambda e, c=c: e.matmul(bank(bk, W), lhsT=w3[:, c, :], rhs=h2T3[:, c, 0:W], start=(c == 0), stop=(c == 7)),
                         reads=[R_wupb[wi], R_h2T[pb]], writes=[PB[bk]], inc=(c == 7))
                Rc = R_carry[cidx]
                if bi == 0:
                    S.op("dve", lambda e: e.tensor_copy(out=carry3[:, cidx, :], in_=bank(bk, W)[:, W - 2:W]), reads=[PB[bk]], writes=[Rc])
                    return
                ac, Rac = acc[half][fp], R_acc[half][fp]
                if half == 0:
                    ub, Rub = ubuf[fp], R_ub[fp]
                    S.op("act", lambda e: e.activation(out=ub[:, 0:2], in_=carry3[:, cidx, :], func=AF.Copy), reads=[Rc], writes=[Rub])
                    S.op("act", lambda e: e.activation(out=ub[:, 2:2 + W], in_=bank(bk, W), func=AF.Copy), reads=[PB[bk]], writes=[Rub])
                    S.op("act", lambda e: e.activation(out=ac[:, 0:W], in_=bank(bk, W), func=AF.Identity,
                                                       scale=cw3[:, cidx, 2:3], bias=cb_t[:, cidx:cidx + 1]),
                         reads=[PB[bk], R_cw, R_cb], writes=[Rac])
                    S.op("dve", lambda e: e.tensor_copy(out=carry3[:, cidx, :], in_=ub[:, W:W + 2]), reads=[Rub], writes=[Rc])
                    S.op("dve", lambda e: e.scalar_tensor_tensor(out=ac[:, 0:W], in0=ub[:, 1:1 + W], scalar=cw3[:, cidx, 1:2], in1=ac[:, 0:W],
                                                                 op0=ALU.mult, op1=ALU.add), reads=[Rub, Rac, R_cw], writes=[Rac])
                    S.op("dve", lambda e: e.scalar_tensor_tensor(out=ac[:, 0:W], in0=ub[:, 0:W], scalar=cw3[:, cidx, 0:1], in1=ac[:, 0:W],
                                                                 op0=ALU.mult, op1=ALU.add), reads=[Rub, Rac, R_cw], writes=[Rac])
                    S.op("act", lambda e: e.activation(out=ac[:, 0:W], in_=ac[:, 0:W], func=AF.Silu), reads=[Rac], writes=[Rac])
                else:
                    pu = bank(bk, W)
                    S.op("act", lambda e: e.activation(out=ac[:, 0:W], in_=pu, func=AF.Identity,
                                                       scale=cw3[:, cidx, 2:3], bias=cb_t[:, cidx:cidx + 1]),
                         reads=[PB[bk], R_cw, R_cb], writes=[Rac])
                    S.op("dve", lambda e: e.scalar_tensor_tensor(out=ac[:, 1:W], in0=pu[:, 0:W - 1], scalar=cw3[:, cidx, 1:2], in1=ac[:, 1:W],
                                                                 op0=ALU.mult, op1=ALU.add), reads=[PB[bk], Rac, R_cw], writes=[Rac])
                    S.op("dve", lambda e: e.scalar_tensor_tensor(out=ac[:, 2:W], in0=pu[:, 0:W - 2], scalar=cw3[:, cidx, 0:1], in1=ac[:, 2:W],
                                                                 op0=ALU.mult, op1=ALU.add), reads=[PB[bk], Rac, R_cw], writes=[Rac])
                    S.op("dve", lambda e: e.scalar_tensor_tensor(out=ac[:, 0:1], in0=carry3[:, cidx, 1:2], scalar=cw3[:, cidx, 1:2], in1=ac[:, 0:1],
                                                                 op0=ALU.mult, op1=ALU.add), reads=[Rc, Rac, R_cw], writes=[Rac])
                    S.op("dve", lambda e: e.scalar_tensor_tensor(out=ac[:, 0:2], in0=carry3[:, cidx, 0:2], scalar=cw3[:, cidx, 0:1], in1=ac[:, 0:2],
                                                                 op0=ALU.mult, op1=ALU.add), reads=[Rc, Rac, R_cw], writes=[Rac])
                    S.op("dve", lambda e: e.tensor_copy(out=carry3[:, cidx, :], in_=pu[:, W - 2:W]), reads=[PB[bk]], writes=[Rc])
                    S.op("pool", lambda e: e.tensor_tensor(out=gT3[:, fc, 0:W], in0=acc[0][fp][:, 0:W], in1=ac[:, 0:W], op=ALU.mult),
                         reads=[R_acc[0][fp], Rac], writes=[R_gT[pb]])

            for fc in range(22):
                for half in range(2):
                    _chunk(fc, half)
            if bi == 0:
                return

            def _s3(t):
                m = m0 + t
                for hf in range(2):
                    for fc in range(22):
                        S.op("pe", lambda e, fc=fc, hf=hf: e.matmul(bank(5 + hf), lhsT=gT3[:, fc, t * 128:(t + 1) * 128],
                                                                    rhs=wdown3[:, fc, hf * 512:(hf + 1) * 512], start=(fc == 0), stop=(fc == 21)),
                             reads=[R_gT[pb], R_wdown], writes=[PB[5 + hf]], inc=(fc == 21))
                S.op("dve", lambda e: e.tensor_tensor(out=x13[:, t, :], in0=ps[:, 2560:3584], in1=x13[:, t, :], op=ALU.add),
                     reads=[PB[5], PB[6], Rx1[t]], writes=[Rx1[t]])
                tp = t % 2
                ss2, rstd2, Rst = st2[tp][:, 2:3], st2[tp][:, 3:4], R_st3[tp]
                S.op("act", lambda e: e.activation(out=jnk, in_=x13[:, t, :], func=AF.Square, accum_out=ss2), reads=[Rx1[t]], writes=[R_jnk, Rst])
                S.op("dve", lambda e: e.tensor_scalar(out=ss2, in0=ss2, scalar1=1.0 / 1024, scalar2=EPS, op0=ALU.mult, op1=ALU.add), reads=[Rst], writes=[Rst])
                S.op("act", lambda e: e.activation(out=ss2, in_=ss2, func=AF.Sqrt), reads=[Rst], writes=[Rst])
                S.op("dve", lambda e: e.reciprocal(out=rstd2, in_=ss2), reads=[Rst], writes=[Rst])
                S.op("dve", lambda e: e.scalar_tensor_tensor(out=x13[:, t, :], in0=x13[:, t, :], scalar=rstd2, in1=onw_t, op0=ALU.mult, op1=ALU.mult),
                     reads=[Rx1[t], Rst, R_onw], writes=[Rx1[t]])
                S.dma("sp", "yo%d_%d" % (pb, t), [lambda e: e.dma_start(out=y_tiles[m - 1], in_=x13[:, t, :])], reads=[Rx1[t]], nbytes=524288)

            for t in range(ntl):
                _s3(t)

        for bi, (m0, ntl) in enumerate(blocks):
            _blockC(bi, m0, ntl)

        S.barrier()
        S.emit()
    return nc


def _consts():
    H, C = 8, 128
    lg = np.log1p(-np.power(2.0, -5.0 - np.arange(H, dtype=np.float64)))
    idx = np.arange(C, dtype=np.float64)
    diff = idx[None, :] - idx[:, None]
    dt = np.where(diff[:, None, :] >= 0, np.exp(lg[None, :, None] * np.maximum(diff[:, None, :], 0.0)), 0.0) / 8.0
    c_dt = dt.reshape(128, 1024).astype(np.float32)
    xi = np.exp(lg[:, None] * (idx[None, :] + 1.0))
    c_xi = np.zeros((128, 4, 128))
    for p in range(4):
        for a in range(2):
            c_xi[a * 64:(a + 1) * 64, p, :] = xi[2 * p + a][None, :]
    c_xi = c_xi.reshape(128, 512).astype(np.float32)
    zeta = np.exp(lg[:, None] * (C - 1.0 - idx[None, :])) / 8.0
    c_zeta = np.repeat(zeta.T[:, :, None], 64, axis=2).reshape(128, 512).astype(np.float32)
    dec = np.exp(lg * C)
    c_dec = np.zeros((128, 4))
    for p in range(4):
        c_dec[0:64, p] = dec[2 * p]
        c_dec[64:128, p] = dec[2 * p + 1]
    c_dec = c_dec.astype(np.float32)
    fr = (10000.0 ** (-np.arange(0, 64, 2, dtype=np.float32) / np.float32(64))).astype(np.float32)
    fm = (10000.0 ** (-np.arange(0, 32, 2, dtype=np.float32) / np.float32(32))).astype(np.float32)
    invf = np.concatenate([fr, fr, fr, fr, fm, fm, fm, fm]).astype(np.float64) / (2 * np.pi)
    off = np.concatenate([np.full(64, 0.25), np.full(32, 0.5), np.zeros(32), np.full(32, 0.25), np.full(16, 0.5), np.zeros(16)])
    c_invf = np.broadcast_to(invf[None, :], (128, 192)).astype(np.float32).copy()
    c_off = np.broadcast_to(off[None, :], (128, 192)).astype(np.float32).copy()
    k = np.arange(128)
    c_mask = np.where(k[None, :] < k[:, None], -30000.0, 0.0).astype(np.float32)
    return dict(c_dt=c_dt, c_xi=c_xi, c_zeta=c_zeta, c_dec=c_dec, c_invf=c_invf, c_off=c_off, c_mask=c_mask)


def _bc(v, n=128):
    return np.ascontiguousarray(np.broadcast_to(np.asarray(v, np.float32)[None, :], (n, v.shape[0])))


_PROG = None


def kernel(x, positions, attn_norm_w, w_in, ret_gn_w, mla_q_norm_w, w_uq, mla_kv_norm_w, w_ukv,
           w_out, ffn_norm_w, w_up, conv_w, conv_b, w_down, final_norm_w):
    global _PROG
    x = np.asarray(x, np.float32)
    positions = np.asarray(positions, np.int32)
    shared = _consts()
    shared["b_anw"] = _bc(np.asarray(attn_norm_w)[0])
    shared["b_fnw"] = _bc(np.asarray(ffn_norm_w)[0])
    shared["b_onw"] = _bc(np.asarray(final_norm_w))
    shared["b_qnw"] = _bc(np.asarray(mla_q_norm_w)[0])
    shared["b_kvnw"] = _bc(np.asarray(mla_kv_norm_w)[0])
    shared["b_gnw"] = _bc(np.asarray(ret_gn_w)[0])
    cw = np.asarray(conv_w, np.float32)[0]
    shared["c_cw"] = np.ascontiguousarray(cw.reshape(3, 44, 128).transpose(2, 1, 0)).reshape(128, 132)
    shared["c_cb"] = np.ascontiguousarray(np.asarray(conv_b, np.float32)[0].reshape(44, 128).T)
    shared["w_in_l"] = np.ascontiguousarray(np.asarray(w_in, np.float32)[0].reshape(8, 128, 2464).transpose(1, 0, 2)).reshape(128, -1)
    shared["w_uq_l"] = np.ascontiguousarray(np.asarray(w_uq, np.float32)[0].reshape(2, 128, 768).transpose(1, 0, 2)).reshape(128, -1)
    wukv = np.asarray(w_ukv, np.float32)[0].reshape(128, 8, 128)
    shared["wk_l"] = np.ascontiguousarray(wukv[:, :, 0:64]).reshape(128, 512)
    shared["wv_l"] = np.ascontiguousarray(wukv[:, :, 64:128]).reshape(128, 512)
    wo = np.asarray(w_out, np.float32)[0]
    shared["w_out_l"] = np.ascontiguousarray(wo.reshape(8, 128, 1024).transpose(1, 0, 2)).reshape(128, -1)
    wu = np.asarray(w_up, np.float32)[0]
    shared["w_up_l"] = np.ascontiguousarray(wu.reshape(8, 128, 2, 22, 128).transpose(1, 3, 2, 0, 4)).reshape(128, -1)
    wd = np.asarray(w_down, np.float32)[0]
    shared["w_down_l"] = np.ascontiguousarray(wd.reshape(22, 128, 1024).transpose(1, 0, 2)).reshape(128, -1)

    in_maps = []
    for c in range(8):
        b, z = divmod(c, 2)
        m = dict(shared)
        if z == 1:
            xcore = x[b]
            pc = positions[b]
            vd = np.ones(8192, np.float32)
        else:
            xcore = np.concatenate([np.zeros((4096, 1024), np.float32), x[b, :4096]], axis=0)
            pc = np.concatenate([np.zeros(4096, np.int32), positions[b, :4096]])
            vd = np.concatenate([np.zeros(4096, np.float32), np.ones(4096, np.float32)])
        m["xc"] = np.ascontiguousarray(xcore)
        m["posc"] = np.ascontiguousarray(pc.reshape(NT, 128).T)
        m["valid"] = np.ascontiguousarray(vd.reshape(NT, 128).T)
        in_maps.append(m)
    if _PROG is None:
        _PROG = build_program()
    res = run_bass_kernel_spmd(_PROG, in_maps, core_ids=list(range(8)))
    out = np.empty((4, 8192, 1024), np.float32)
    for c in range(8):
        b, z = divmod(c, 2)
        out[b, z * 4096:(z + 1) * 4096] = res.results[c]["yout"]
    return out
```

```python
import math
import os
from contextlib import ExitStack
import numpy as np
import concourse.bass as bass
import concourse.mybir as mybir
from concourse.bass_utils import run_bass_kernel_spmd

F32 = mybir.dt.float32
BF16 = mybir.dt.bfloat16
I32 = mybir.dt.int32
ALU = mybir.AluOpType
AF = mybir.ActivationFunctionType
AX = mybir.AxisListType

NT = 64
HALO = 31
NOWN = 33
EPS = 1e-6
TWO_PI = 2.0 * math.pi


class Res:
    __slots__ = ("name", "w", "r", "excl")

    def __init__(self, name, excl=False):
        self.name = name
        self.w = None
        self.r = set()
        self.excl = excl


class _Rec:
    def __init__(self):
        self.call = None

    def __getattr__(self, name):
        def f(*a, **k):
            self.call = (name, a, k)
            return self
        return f


def _fsize(ap):
    n = 1
    for d in list(ap.shape)[1:]:
        n *= int(d)
    return n


_TAGGED = ("Sqrt", "Silu", "Sin", "Exp")


def _act_tag(fn):
    try:
        r = _Rec()
        fn(r)
        name, a, k = r.call
        f = str(k.get("func", ""))
        for t in _TAGGED:
            if f.endswith(t):
                return t
    except Exception:
        pass
    return None


def _est(eng, fn):
    try:
        r = _Rec()
        fn(r)
        name, a, k = r.call
        if eng == "pe":
            rhs = k.get("rhs", k.get("identity"))
            n = _fsize(rhs) if name == "matmul" else 128
            return max(n, 64) / 2.4 + 90.0
        out = k.get("out", a[0] if a else None)
        f = _fsize(out)
        if eng == "act":
            return (f + 224) / 1.2
        if eng == "dve":
            if name == "reciprocal":
                return 165 + (6.2 * f if int(out.shape[0]) < 32 else f)
            return (f + 150) / 0.96
        return 200 + (3.4 if name == 'tensor_copy' else 2.2) * f
    except Exception:
        return 500.0


class _Op:
    __slots__ = ("q", "kind", "fns", "deps", "cost", "lat", "slot", "seq", "tag")


class Sched:
    ENG = ("pe", "act", "dve", "pool", "sp")

    def __init__(self, nc, stack):
        self.nc = nc
        self.stack = stack
        self.sem = {e: stack.enter_context(nc.semaphore("s_" + e)) for e in self.ENG}
        self.cnt = {e: 0 for e in self.ENG}
        self.seen = {e: {} for e in self.ENG}
        self.streams = {e: [] for e in self.ENG}
        self.dsem = {}
        self.dcnt = {}
        self.ops = []
        self.base = 0
        self.open_pe = None
        self.reorder = True
        self.prio = bool(int(os.environ.get('KPRIO', '1')))

    def _record_deps(self, idx, reads, writes):
        deps = self.ops[idx].deps
        for r in reads:
            if r.w is not None and r.w >= self.base and r.w != idx:
                deps.add(r.w)
        for w in writes:
            if w.w is not None and w.w >= self.base and w.w != idx:
                deps.add(w.w)
            for t in w.r:
                if t >= self.base and t != idx:
                    deps.add(t)
        for r in reads:
            if r not in writes:
                r.r.add(idx)
        for w in writes:
            w.w = idx
            w.r = set()

    def op(self, eng, fn, reads=(), writes=(), inc=True, tag=None):
        ex = [r for r in reads if r.excl and r not in writes]
        if ex:
            writes = list(writes) + ex
        if eng == "pe" and self.open_pe is not None:
            idx = self.open_pe
            o = self.ops[idx]
            o.fns.append(fn)
            o.cost += _est(eng, fn)
        else:
            o = _Op()
            o.q, o.kind, o.fns, o.deps, o.cost, o.lat, o.slot, o.seq, o.tag = eng, "op", [fn], set(), _est(eng, fn), 0.0, None, None, (_act_tag(fn) if eng == "act" else None)
            idx = len(self.ops)
            self.ops.append(o)
        self._record_deps(idx, reads, writes)
        if eng == "pe":
            self.open_pe = None if inc else idx
        else:
            assert inc
        return idx

    def dma(self, eng, slot, fns, reads=(), writes=(), nbytes=262144):
        assert self.open_pe is None
        if slot not in self.dsem:
            self.dsem[slot] = self.stack.enter_context(self.nc.semaphore("d_" + slot))
            self.dcnt[slot] = 0
        o = _Op()
        o.q, o.kind, o.fns, o.deps, o.cost, o.slot, o.seq, o.tag = eng, "dma", list(fns), set(), 350.0 * len(fns), slot, None, None
        o.lat = 2000.0 + nbytes / 150.0
        idx = len(self.ops)
        self.ops.append(o)
        self._record_deps(idx, reads, writes)
        return idx

    def _wait(self, eng, toks):
        best = {}
        for t in toks:
            k, s, v = t
            if k not in best or best[k][2] < v:
                best[k] = t
        for k, (kk, s, v) in best.items():
            if self.seen[eng].get(k, 0) >= v:
                continue
            self.seen[eng][k] = v
            self.streams[eng].append(("wait", s, v))

    def _token(self, d):
        o = self.ops[d]
        if o.kind == "dma":
            return ("d_" + o.slot, self.dsem[o.slot], o.seq)
        return (o.q, self.sem[o.q], o.seq)

    def flush(self):
        assert self.open_pe is None
        ops, base = self.ops, self.base
        n = len(ops)
        if n == base:
            return
        if self.reorder:
            order = self._list_schedule(base, n)
        else:
            order = list(range(base, n))
        for i in order:
            o = ops[i]
            toks = []
            for d in o.deps:
                od = ops[d]
                if od.kind == "op" and od.q == "pe" and o.q == "pe" and o.kind == "op":
                    continue
                toks.append(self._token(d))
            self._wait(o.q, toks)
            if o.kind == "dma":
                for fn in o.fns:
                    self.dcnt[o.slot] += 16
                    self.streams[o.q].append(("dma", fn, self.dsem[o.slot]))
                o.seq = self.dcnt[o.slot]
            else:
                self.cnt[o.q] += 1
                o.seq = self.cnt[o.q]
                for j, fn in enumerate(o.fns):
                    self.streams[o.q].append(("op", fn, j == len(o.fns) - 1))
        self.base = n

    def _list_schedule(self, base, n):
        ops = self.ops
        indeg = {}
        succ = {}
        for i in range(base, n):
            dd = [d for d in ops[i].deps if d >= base]
            indeg[i] = len(dd)
            for d in dd:
                succ.setdefault(d, []).append(i)
        blev = {}
        for i in range(n - 1, base - 1, -1):
            o = ops[i]
            m = 0.0
            for j in succ.get(i, ()):
                if blev[j] > m:
                    m = blev[j]
            blev[i] = m + o.cost + (o.lat if o.kind == "dma" else 60.0)
        etime = {e: 0.0 for e in self.ENG}
        lasttag = {e: None for e in self.ENG}
        finish = {}
        rtime = {}
        ready = {e: [] for e in self.ENG}
        for i in range(base, n):
            if indeg[i] == 0:
                rtime[i] = 0.0
                ready[ops[i].q].append(i)
        order = []
        PRIO = self.prio
        while len(order) < n - base:
            bestk, besti = None, None
            for e in self.ENG:
                lst = ready[e]
                if not lst:
                    continue
                t = etime[e]
                cand, ck = None, None
                for i in lst:
                    st = rtime[i] if rtime[i] > t else t
                    if e == "act" and ops[i].tag is not None and lasttag["act"] not in (None, ops[i].tag):
                        st += 1300.0
                    k = (st, -blev[i], i) if PRIO else (st, i)
                    if ck is None or k < ck:
                        cand, ck = i, k
                if bestk is None or ck < bestk:
                    bestk, besti = ck, cand
            i = besti
            o = ops[i]
            st = bestk[0]
            c = o.cost
            if o.tag is not None and o.q == "act":
                lasttag["act"] = o.tag
            etime[o.q] = st + c
            finish[i] = st + c + (o.lat if o.kind == "dma" else 60.0)
            ready[o.q].remove(i)
            order.append(i)
            for j in succ.get(i, ()):
                indeg[j] -= 1
                rt = rtime.get(j, 0.0)
                if finish[i] > rt:
                    rtime[j] = finish[i]
                elif j not in rtime:
                    rtime[j] = rt
                if indeg[j] == 0:
                    ready[ops[j].q].append(j)
        self.est_span = max(etime.values())
        return order

    def barrier(self):
        self.flush()
        toks = [(e, self.sem[e], self.cnt[e]) for e in self.ENG if self.cnt[e] > 0]
        toks += [("d_" + s, self.dsem[s], self.dcnt[s]) for s in self.dsem if self.dcnt[s] > 0]
        for e in self.ENG:
            self._wait(e, toks)

    def emit(self):
        self.flush()
        nc = self.nc
        with nc.Block() as block:
            def run(eng, e):
                sem = self.sem[eng]
                for item in self.streams[eng]:
                    if item[0] == "wait":
                        e.wait_ge(item[1], item[2])
                    elif item[0] == "op":
                        ins = item[1](e)
                        if item[2]:
                            ins.then_inc(sem, 1)
                    else:
                        item[1](e).then_inc(item[2], 16)

            @block.tensor
            def _(e):
                run("pe", e)

            @block.scalar
            def _(e):
                run("act", e)

            @block.vector
            def _(e):
                run("dve", e)

            @block.gpsimd
            def _(e):
                run("pool", e)

            @block.sync
            def _(e):
                run("sp", e)


def bc_mid(a, k):
    return bass.AP(a.tensor, a.offset, [list(a.ap[0]), [0, k]] + [list(x) for x in a.ap[1:]])


def bc_last(a, m):
    return bass.AP(a.tensor, a.offset, [list(x) for x in a.ap] + [[0, m]])


class Arena:
    def __init__(self, t, total):
        self.t = t
        self.total = total
        self.off = 0

    def f32(self, n):
        assert self.off + n <= self.total, ("arena overflow", self.off, n, self.total)
        a = self.t[:, self.off:self.off + n]
        self.off += n
        return a

    def bf(self, n):
        w = (n + 1) // 2
        return self.f32(w).bitcast(BF16)[:, 0:n]


import os
KSTOP = os.environ.get('KSTOP', '')
KSUB = int(os.environ.get('KSUB', 0))


def build_program():
    nc = bass.Bass("TRN2", target_bir_lowering=False)
    din = {}

    def inp(name, shape, dt=F32):
        din[name] = nc.dram_tensor(name, list(shape), dt, kind="ExternalInput").ap()
        return din[name]

    xc = inp("xc", [NT * 128, 1024])
    posc = inp("posc", [128, NT], I32)
    valid = inp("valid", [128, NT])
    c_dt = inp("c_dt", [128, 1024])
    c_xi = inp("c_xi", [128, 512])
    c_zeta = inp("c_zeta", [128, 512])
    c_dec = inp("c_dec", [128, 4])
    c_invf = inp("c_invf", [128, 192])
    c_off = inp("c_off", [128, 192])
    c_mask = inp("c_mask", [128, 128])
    b_anw = inp("b_anw", [128, 1024])
    b_fnw = inp("b_fnw", [128, 1024])
    b_onw = inp("b_onw", [128, 1024])
    b_qnw = inp("b_qnw", [128, 256])
    b_kvnw = inp("b_kvnw", [128, 128])
    b_gnw = inp("b_gnw", [128, 512])
    c_cw = inp("c_cw", [128, 44 * 3])
    c_cb = inp("c_cb", [128, 44])
    w_in_l = inp("w_in_l", [128, 8 * 2464])
    w_uq_l = inp("w_uq_l", [128, 2 * 768])
    wk_l = inp("wk_l", [128, 512])
    wv_l = inp("wv_l", [128, 512])
    w_out_l = inp("w_out_l", [128, 8 * 1024])
    w_up_l = inp("w_up_l", [128, 44 * 1024])
    w_down_l = inp("w_down_l", [128, 22 * 1024])
    yout = nc.dram_tensor("yout", [4096, 1024], F32, kind="ExternalOutput").ap()

    s_wup = nc.dram_tensor("s_wup", [128, 44 * 1024], BF16).ap()
    s_wdown = nc.dram_tensor("s_wdown", [128, 22 * 1024], BF16).ap()
    s_wout = nc.dram_tensor("s_wout", [128, 8 * 1024], BF16).ap()
    s_kt = nc.dram_tensor("s_kt", [8, 96, NT * 128], BF16).ap()
    s_v = nc.dram_tensor("s_v", [8, 128, NT * 65], BF16).ap()
    s_qt = nc.dram_tensor("s_qt", [8, 96, NOWN * 128], BF16).ap()
    s_mix = nc.dram_tensor("s_mix", [NOWN, 128, 8, 128], BF16).ap()

    with ExitStack() as st:
        S = Sched(nc, st)
        TOT = 53000
        arena_t = st.enter_context(nc.sbuf_tensor("arena", [128, TOT], F32))
        ps = st.enter_context(nc.psum_tensor("ps", [128, 4096], F32))
        A = Arena(arena_t, TOT)

        def bank(i, n=512):
            return ps[:, i * 512:i * 512 + n]

        PB = [Res("psb%d" % i, excl=True) for i in range(8)]

        ident = A.bf(128)
        maskb = A.bf(128)
        R_smix = [Res("s_mix%d" % i) for i in range(NOWN)]
        R_ident, R_mask = Res("ident"), Res("mask")
        mark_persist = A.off

        S.op("pool", lambda e: e.memset(ident, 0.0), writes=[R_ident])
        S.op("pool", lambda e: e.affine_select(out=ident, in_=ident, pattern=[[-1, 128]], compare_op=ALU.not_equal,
                                               fill=1.0, base=0, channel_multiplier=1), reads=[R_ident], writes=[R_ident])

        w_in = A.bf(8 * 2464)
        w_in3 = w_in.rearrange("p (c f) -> p c f", c=8)
        w_uq = A.bf(2 * 768)
        w_uq3 = w_uq.rearrange("p (c f) -> p c f", c=2)
        wk = A.bf(512)
        wv = A.bf(512)
        R_win, R_wuq, R_wk, R_wv = Res("w_in"), Res("w_uq"), Res("wk"), Res("wv")
        stage = [A.f32(512) for _ in range(2)]
        stageb = [A.bf(512) for _ in range(2)]
        R_stage = [Res("stage0"), Res("stage1")]
        R_stageb = [Res("stageb0"), Res("stageb1")]
        R_scr = {k: Res(k) for k in ["s_wup", "s_wdown", "s_wout"]}
        pieces = []

        def add_pieces(src, ncols, dst_sb=None, dst_res=None, dst_dram=None, dram_res=None):
            c0 = 0
            while c0 < ncols:
                n = min(512, ncols - c0)
                pieces.append((src, c0, n, dst_sb, dst_res, dst_dram, dram_res))
                c0 += n

        def piece_load_cast(k):
            src, c0, n, dst_sb, dst_res, dst_dram, dram_res = pieces[k]
            i = k % 2
            S.dma("act", "stg%d" % i, [lambda e: e.dma_start(out=stage[i][:, 0:n], in_=src[:, c0:c0 + n])], writes=[R_stage[i]])
            if dst_sb is not None:
                S.op("pool", lambda e: e.tensor_copy(out=dst_sb[:, c0:c0 + n], in_=stage[i][:, 0:n]), reads=[R_stage[i]], writes=[dst_res])
            else:
                S.op("pool", lambda e: e.tensor_copy(out=stageb[i][:, 0:n], in_=stage[i][:, 0:n]), reads=[R_stage[i]], writes=[R_stageb[i]])

        def piece_store(k):
            src, c0, n, dst_sb, dst_res, dst_dram, dram_res = pieces[k]
            i = k % 2
            if dst_dram is not None:
                S.dma("act", "stb%d" % i, [lambda e: e.dma_start(out=dst_dram[:, c0:c0 + n], in_=stageb[i][:, 0:n])],
                      reads=[R_stageb[i]], writes=[Res("wscr")])

        add_pieces(w_in_l, 8 * 2464, dst_sb=w_in, dst_res=R_win)
        add_pieces(w_uq_l, 2 * 768, dst_sb=w_uq, dst_res=R_wuq)
        add_pieces(wk_l, 512, dst_sb=wk, dst_res=R_wk)
        add_pieces(wv_l, 512, dst_sb=wv, dst_res=R_wv)
        add_pieces(c_mask, 128, dst_sb=maskb, dst_res=R_mask)
        n_first = len(pieces)
        add_pieces(w_out_l, 8 * 1024, dst_dram=s_wout, dram_res=R_scr["s_wout"])
        add_pieces(w_down_l, 22 * 1024, dst_dram=s_wdown, dram_res=R_scr["s_wdown"])
        add_pieces(w_up_l, 44 * 1024, dst_dram=s_wup, dram_res=R_scr["s_wup"])
        for k in range(n_first):
            piece_load_cast(k)
        pk = [n_first, n_first]

        def cast_step(nload):
            for _ in range(nload):
                if pk[0] < len(pieces):
                    piece_load_cast(pk[0])
                    piece_store(pk[0])
                    pk[0] += 1
            pk[1] = pk[0]

        def load_const(src, n, name, dt=F32):
            a = A.f32(n)
            if dt is not F32:
                a = a.bitcast(dt)
            r = Res(name)
            S.dma("sp", "c_" + name, [lambda e: e.dma_start(out=a, in_=src)], writes=[r])
            return a, r

        dt_t, R_dt = load_const(c_dt, 1024, "dt")
        xi_t, R_xi = load_const(c_xi, 512, "xi")
        zeta_t, R_zeta = load_const(c_zeta, 512, "zeta")
        dec_t, R_dec = load_const(c_dec, 4, "dec")
        invf_t, R_invf = load_const(c_invf, 192, "invf")
        off_t, R_off = load_const(c_off, 192, "off")
        anw_t, R_anw = load_const(b_anw, 1024, "anw")
        qnw_t, R_qnw = load_const(b_qnw, 256, "qnw")
        kvnw_t, R_kvnw = load_const(b_kvnw, 128, "kvnw")
        gnw_t, R_gnw = load_const(b_gnw, 512, "gnw")
        posi, R_pos = load_const(posc, NT, "posi", I32)
        valid_t, R_valid = load_const(valid, NT, "valid")
        posf = A.f32(NT)
        S.op("dve", lambda e: e.tensor_copy(out=posf, in_=posi), reads=[R_pos], writes=[R_pos])

        TB = 4
        tabs = [A.f32(TB * 192) for _ in range(2)]
        R_tabs = [Res("tab0"), Res("tab1")]
        ttmp = A.f32(192)
        tti = A.f32(192).bitcast(I32)
        R_tt = Res("ttmp")

        def make_tables(n0):
            k = (n0 // TB) % 2
            tab, R_tab_ = tabs[k], R_tabs[k]
            tab3_ = tab.rearrange("p (n f) -> p n f", n=TB)
            for n in range(n0, n0 + TB):
                S.op("dve", lambda e, n=n: e.scalar_tensor_tensor(out=ttmp, in0=invf_t, scalar=posf[:, n:n + 1], in1=off_t,
                                                                op0=ALU.mult, op1=ALU.add),
                     reads=[R_invf, R_off, R_pos], writes=[R_tt])
                S.op("dve", lambda e: e.tensor_copy(out=tti, in_=ttmp), reads=[R_tt], writes=[R_tt])
                S.op("dve", lambda e, n=n: e.tensor_tensor(out=tab3_[:, n % TB, :], in0=ttmp, in1=tti, op=ALU.subtract),
                     reads=[R_tt], writes=[R_tab_])
            S.op("act", lambda e: e.activation(out=tab, in_=tab, func=AF.Sin, scale=TWO_PI * (1.0 - 1e-6)),
                 reads=[R_tab_], writes=[R_tab_])

        xbuf = [A.f32(1024) for _ in range(2)]
        R_x = [Res("x0"), Res("x1")]
        R32 = A.f32(512)
        Rb = A.bf(512)
        R_R32, R_Rb = Res("R32"), Res("Rb")
        DB = {}
        R_ssm2 = [Res("ssm0"), Res("ssm1")]
        for nm, kind, sz in [("st_small", "f", 64), ("hb", "b", 1024), ("hT", "b", 1024), ("qk_sb", "f", 1024), ("tmpA", "f", 1024),
                             ("tmpB", "f", 1024), ("qkr", "b", 1024), ("kz", "b", 512), ("vb", "b", 512), ("sg", "f", 512),
                             ("qT", "b", 1024), ("qxT", "b", 512), ("kT", "b", 512), ("sd", "b", 1024), ("gn1", "f", 512),
                             ("gn2", "f", 512), ("yb", "b", 512), ("yT", "b", 512), ("cqn", "b", 256), ("ckvn", "b", 128), ("kr", "b", 32),
                             ("cqnT", "b", 256), ("ckvnT", "b", 128), ("mt1", "f", 64), ("qb", "b", 768), ("lat_sb", "f", 416),
                             ("tA2", "f", 32), ("tB2", "f", 32), ("tA3", "f", 256), ("tB3", "f", 256)]:
            DB[nm] = [((A.f32(sz) if kind == "f" else A.bf(sz)), Res(nm + "_%d" % i)) for i in range(2)]
        kn_g = [A.bf(4 * 512) for _ in range(2)]
        kpe_g = [A.bf(512) for _ in range(2)]
        v_g = [A.bf(4 * 8 * 65) for _ in range(2)]
        q_g = [A.bf(8 * 512) for _ in range(2)]
        R_kng = [Res("kng0"), Res("kng1")]
        R_kpg = [Res("kpg0"), Res("kpg1")]
        R_vg = [Res("vg0"), Res("vg1")]
        R_qg = [Res("qg0"), Res("qg1")]
        R_skt, R_sv, R_sqt = Res("s_kt"), Res("s_v"), Res("s_qt")

        for i_ in range(2):
            S.op("dve", lambda e, i_=i_: e.memset(DB["qT"][i_][0], 0.0), writes=[DB["qT"][i_][1]])
        S.op("dve", lambda e: e.memset(R32, 0.0), writes=[R_R32])
        S.op("dve", lambda e: e.memset(Rb, 0.0), writes=[R_Rb])

        x_tiles = xc.rearrange("(n p) d -> n p d", p=128)

        def load_x(n):
            i = n % 2
            S.dma("sp", "x%d" % i, [lambda e, n=n, i=i: e.dma_start(out=xbuf[i], in_=x_tiles[n])], writes=[R_x[i]])

        if KSTOP == 'A0':
            npz = int(os.environ.get('KNP', 0))
            while pk[1] < min(len(pieces), n_first + npz):
                cast_step(2)
            S.barrier(); S.emit(); return nc
        load_x(0)

        def _tileA(n):
            cast_step(3)
            b = n % 2
            (st_small, R_ss), (hb, R_hb), (hT, R_hT), (qk_sb, R_qksb), (tmpA, R_tA), (tmpB, R_tB) = [DB[k][b] for k in ("st_small", "hb", "hT", "qk_sb", "tmpA", "tmpB")]
            (qkr, R_qkr), (kz, R_kz), (vb, R_vb), (sg, R_sg), (qT, R_qT), (qxT, R_qxT), (kT, R_kT) = [DB[k][b] for k in ("qkr", "kz", "vb", "sg", "qT", "qxT", "kT")]
            (sd, R_sd), (gn1, R_gn1), (gn2, R_gn2), (yb, R_yb), (yT, R_yT), (cqn, R_cqn), (ckvn, R_ckvn), (kr, R_kr) = [DB[k][b] for k in ("sd", "gn1", "gn2", "yb", "yT", "cqn", "ckvn", "kr")]
            (cqnT, R_cqnT), (ckvnT, R_ckvnT), (mt1, R_mt), (qb, R_qb) = [DB[k][b] for k in ("cqnT", "ckvnT", "mt1", "qb")]
            hT3 = hT.rearrange("p (c t) -> p c t", c=8)
            R_ssm = R_ssm2[b]
            (lat_sb, R_lat), (tA2, R_tA2), (tB2, R_tB2), (tA3, R_tA3), (tB3, R_tB3) = [DB[k][b] for k in ("lat_sb", "tA2", "tB2", "tA3", "tB3")]
            full = n >= HALO
            g, gi = divmod(n, 4)
            gb = g % 2
            if n + 1 < NT:
                load_x(n + 1)
            xb_, Rx = xbuf[n % 2], R_x[n % 2]
            ss = st_small[:, 0:1]
            rstd = st_small[:, 1:2]
            S.op("act", lambda e, xb_=xb_: e.activation(out=hb, in_=xb_, func=AF.Square, accum_out=ss),
                 reads=[Rx], writes=[R_hb, R_ss])
            S.op("dve", lambda e: e.tensor_scalar(out=ss, in0=ss, scalar1=1.0 / 1024, scalar2=EPS, op0=ALU.mult, op1=ALU.add),
                 reads=[R_ss], writes=[R_ss])
            S.op("act", lambda e: e.activation(out=ss, in_=ss, func=AF.Sqrt), reads=[R_ss], writes=[R_ss])
            S.op("dve", lambda e: e.reciprocal(out=rstd, in_=ss), reads=[R_ss], writes=[R_ss])
            S.op("dve", lambda e, xb_=xb_: e.scalar_tensor_tensor(out=hb, in0=xb_, scalar=rstd, in1=anw_t, op0=ALU.mult, op1=ALU.mult),
                 reads=[Rx, R_ss, R_anw], writes=[R_hb])
            tb = bank(0).bitcast(BF16)
            tbh = bank(4).bitcast(BF16)
            for c in range(8):
                S.op("pe", lambda e, c=c: e.transpose(out=tbh[:, c * 128:(c + 1) * 128], in_=hb[:, c * 128:(c + 1) * 128], identity=ident),
                     reads=[R_hb, R_ident], writes=[PB[4]], inc=(c == 7))
            S.op("act", lambda e: e.activation(out=hT, in_=tbh, func=AF.Copy), reads=[PB[4]], writes=[R_hT])
            if KSUB == 1 and n >= HALO:
                return


            def proj(bk, col0, ncol, n_=None):
                for c in range(8):
                    S.op("pe", lambda e, c=c: e.matmul(bank(bk, ncol), lhsT=hT3[:, c, :], rhs=w_in3[:, c, col0:col0 + ncol],
                                                       start=(c == 0), stop=(c == 7)),
                         reads=[R_hT, R_win], writes=[PB[bk]], inc=(c == 7))

            if full:
                proj(1, 0, 512)
            proj(2, 512, 512)
            proj(3, 1024, 512)
            if full:
                proj(4, 1536, 512)
                S.op("act", lambda e: e.activation(out=qk_sb, in_=ps[:, 512:1536], func=AF.Copy), reads=[PB[1], PB[2]], writes=[R_qksb])
                proj(1, 2048, 416)
                S.op("act", lambda e: e.activation(out=lat_sb, in_=bank(1, 416), func=AF.Copy), reads=[PB[1]], writes=[R_lat])
            else:
                S.op("act", lambda e: e.activation(out=qk_sb[:, 512:1024], in_=ps[:, 1024:1536], func=AF.Copy), reads=[PB[2]], writes=[R_qksb])
                proj(1, 2304, 160)
                S.op("act", lambda e: e.activation(out=lat_sb[:, 0:160], in_=bank(1, 160), func=AF.Copy), reads=[PB[1]], writes=[R_lat])
            if KSUB == 2 and n >= HALO:
                return

            latoff = 0 if full else -256
            if n % TB == 0:
                make_tables(n)
            R_tab = R_tabs[(n // TB) % 2]
            tabn = tabs[(n // TB) % 2].rearrange("p (n f) -> p n f", n=TB)[:, n % TB, :]
            cs_r, ss_r = tabn[:, 0:64], tabn[:, 64:128]
            cs_m, ss_m = tabn[:, 128:160], tabn[:, 160:192]

            def rope(src_ap, nh, hd, cs, sn, dstA, dstB, dst, reads, wres, RA=None, RB=None, add_eng="dve"):
                RA = R_tA if RA is None else RA
                RB = R_tB if RB is None else RB
                half = hd // 2
                x3 = src_ap.rearrange("p (h d) -> p h d", h=nh)
                sw = bass.AP(src_ap.tensor, src_ap.offset + half,
                             [list(src_ap.ap[0]), [hd, nh], [-half, 2], [1, half]])
                a3 = dstA.rearrange("p (h d) -> p h d", h=nh)
                b4 = dstB.rearrange("p (h a d) -> p h a d", h=nh, a=2)
                S.op("dve", lambda e: e.tensor_tensor(out=a3, in0=x3, in1=bc_mid(cs, nh), op=ALU.mult),
                     reads=reads + [R_tab], writes=[RA])
                S.op("dve", lambda e: e.tensor_tensor(out=b4, in0=sw, in1=bc_mid(sn.rearrange("p (a d) -> p a d", a=2), nh), op=ALU.mult),
                     reads=reads + [R_tab], writes=[RB])
                S.op(add_eng, lambda e: e.tensor_tensor(out=dst, in0=dstA, in1=dstB, op=ALU.add),
                     reads=[RA, RB], writes=[wres])

            if full:
                rope(qk_sb, 16, 64, cs_r, ss_r, tmpA, tmpB, qkr, [R_qksb], R_qkr, add_eng="dve")
            else:
                rope(qk_sb[:, 512:1024], 8, 64, cs_r, ss_r, tmpA[:, 0:512], tmpB[:, 0:512], qkr[:, 512:1024], [R_qksb], R_qkr, add_eng="dve")
            S.op("dve", lambda e: e.tensor_tensor(out=kz, in0=qkr[:, 512:1024], in1=zeta_t, op=ALU.mult),
                 reads=[R_qkr, R_zeta], writes=[R_kz])
            S.op("act", lambda e: e.activation(out=vb, in_=bank(3), func=AF.Copy), reads=[PB[3]], writes=[R_vb])
            if KSUB == 3 and n >= HALO:
                return


            if full:
                S.op("act", lambda e: e.activation(out=sg, in_=bank(4), func=AF.Silu), reads=[PB[4]], writes=[R_sg])
                for c in range(8):
                    S.op("pe", lambda e, c=c: e.transpose(out=tb[:, c * 128:(c + 1) * 128], in_=qkr[:, c * 128:(c + 1) * 128], identity=ident),
                         reads=[R_qkr, R_ident], writes=[PB[0]], inc=(c == 7))
                S.op("act", lambda e: e.activation(out=qT[0:64, 0:512], in_=tb[0:64, 0:512], func=AF.Copy), reads=[PB[0]], writes=[R_qT])
                S.op("act", lambda e: e.activation(out=qT[64:128, 512:1024], in_=tb[64:128, 0:512], func=AF.Copy), reads=[PB[0]], writes=[R_qT])
                S.op("dve", lambda e: e.tensor_tensor(out=qxT, in0=tb[:, 0:512], in1=xi_t, op=ALU.mult), reads=[PB[0], R_xi], writes=[R_qxT])
                S.op("act", lambda e: e.activation(out=kT, in_=tb[:, 512:1024], func=AF.Copy), reads=[PB[0]], writes=[R_kT])
                for h in range(8):
                    p_, a_ = divmod(h, 2)
                    rows = slice(a_ * 64, a_ * 64 + 64)
                    S.op("pe", lambda e, h=h, p_=p_, rows=rows: e.matmul(ps[:, 2560 + h * 128:2560 + (h + 1) * 128],
                                                                         lhsT=kT[:, p_ * 128:(p_ + 1) * 128],
                                                                         rhs=qT[:, (h % 2) * 512 + p_ * 128:(h % 2) * 512 + (p_ + 1) * 128],
                                                                         start=True, stop=True),
                         reads=[R_kT, R_qT], writes=[PB[5], PB[6]], inc=(h == 7))
                S.op("dve", lambda e: e.tensor_tensor(out=sd, in0=ps[:, 2560:3584], in1=dt_t, op=ALU.mult),
                     reads=[PB[5], PB[6], R_dt], writes=[R_sd])
                for h in range(8):
                    p_, a_ = divmod(h, 2)
                    rows = slice(a_ * 64, a_ * 64 + 64)
                    S.op("pe", lambda e, h=h: e.matmul(ps[:, 3584 + h * 64:3584 + (h + 1) * 64], lhsT=sd[:, h * 128:(h + 1) * 128],
                                                       rhs=vb[:, h * 64:(h + 1) * 64], start=True, stop=False),
                         reads=[R_sd, R_vb], writes=[PB[7]], inc=False)
                    S.op("pe", lambda e, h=h, p_=p_, rows=rows: e.matmul(ps[:, 3584 + h * 64:3584 + (h + 1) * 64],
                                                                         lhsT=qxT[:, p_ * 128:(p_ + 1) * 128],
                                                                         rhs=Rb[:, h * 64:(h + 1) * 64],
                                                                         start=False, stop=True),
                         reads=[R_qxT, R_Rb], writes=[PB[7]], inc=(h == 7))
            for p_ in range(4):
                S.op("pe", lambda e, p_=p_: e.matmul(ps[:, 2560 + p_ * 128:2560 + (p_ + 1) * 128], lhsT=kz[:, p_ * 128:(p_ + 1) * 128],
                                                     rhs=vb[:, p_ * 128:(p_ + 1) * 128], start=True, stop=True),
                     reads=[R_kz, R_vb], writes=[PB[5]], inc=(p_ == 3))
            for h in range(8):
                p_, a_ = divmod(h, 2)
                rows = slice(a_ * 64, a_ * 64 + 64)
                S.op("dve", lambda e, h=h, p_=p_, a_=a_, rows=rows: e.scalar_tensor_tensor(
                    out=R32[rows, h * 64:(h + 1) * 64], in0=R32[rows, h * 64:(h + 1) * 64], scalar=dec_t[rows, p_:p_ + 1],
                    in1=ps[rows, 2560 + p_ * 128 + a_ * 64:2560 + p_ * 128 + a_ * 64 + 64], op0=ALU.mult, op1=ALU.add),
                    reads=[PB[5], R_dec, R_R32], writes=[R_R32])
            S.op("act", lambda e: e.activation(out=Rb, in_=R32, func=AF.Copy), reads=[R_R32], writes=[R_Rb])
            if KSUB == 4 and n >= HALO:
                return


            if full:
                o3 = bank(7).rearrange("p (h d) -> p h d", h=8)
                s1, s2, mean, msq, var = (mt1[:, 0:8], mt1[:, 8:16], mt1[:, 16:24], mt1[:, 24:32], mt1[:, 32:40])
                S.op("dve", lambda e: e.tensor_reduce(out=s1, in_=o3, axis=AX.X, op=ALU.add), reads=[PB[7]], writes=[R_mt])
                S.op("act", lambda e: e.activation(out=gn1, in_=bank(7), func=AF.Square), reads=[PB[7]], writes=[R_gn1])
                S.op("dve", lambda e: e.tensor_reduce(out=s2, in_=gn1.rearrange("p (h d) -> p h d", h=8), axis=AX.X, op=ALU.add),
                     reads=[R_gn1], writes=[R_mt])
                S.op("dve", lambda e: e.tensor_scalar(out=mean, in0=s1, scalar1=1.0 / 64, scalar2=None, op0=ALU.mult), reads=[R_mt], writes=[R_mt])
                S.op("dve", lambda e: e.tensor_tensor(out=msq, in0=mean, in1=mean, op=ALU.mult), reads=[R_mt], writes=[R_mt])
                S.op("dve", lambda e: e.scalar_tensor_tensor(out=var, in0=s2, scalar=1.0 / 64, in1=msq, op0=ALU.mult, op1=ALU.subtract),
                     reads=[R_mt], writes=[R_mt])
                S.op("dve", lambda e: e.tensor_scalar(out=var, in0=var, scalar1=EPS, scalar2=None, op0=ALU.add), reads=[R_mt], writes=[R_mt])
                S.op("act", lambda e: e.activation(out=var, in_=var, func=AF.Sqrt), reads=[R_mt], writes=[R_mt])
                S.op("dve", lambda e: e.reciprocal(out=var, in_=var), reads=[R_mt], writes=[R_mt])
                g13 = gn1.rearrange("p (h d) -> p h d", h=8)
                S.op("dve", lambda e: e.tensor_tensor(out=g13, in0=o3, in1=bc_last(mean, 64), op=ALU.subtract),
                     reads=[PB[7], R_mt], writes=[R_gn1])
                S.op("dve", lambda e: e.tensor_tensor(out=g13, in0=g13, in1=bc_last(var, 64), op=ALU.mult), reads=[R_gn1, R_mt], writes=[R_gn1])
                S.op("pool", lambda e: e.tensor_tensor(out=gn2, in0=sg, in1=gnw_t, op=ALU.mult), reads=[R_sg, R_gnw], writes=[R_gn2])
                S.op("dve", lambda e: e.tensor_tensor(out=yb, in0=gn1, in1=gn2, op=ALU.mult), reads=[R_gn1, R_gn2], writes=[R_yb])
                for c in range(4):
                    S.op("pe", lambda e, c=c: e.transpose(out=tb[:, c * 128:(c + 1) * 128], in_=yb[:, c * 128:(c + 1) * 128], identity=ident),
                         reads=[R_yb, R_ident], writes=[PB[0]], inc=(c == 3))
                m = n - HALO
                S.op("act", lambda e: e.activation(out=yT, in_=tb[:, 0:512], func=AF.Copy), reads=[PB[0]], writes=[R_yT])
                S.dma("sp", "ymr%d" % b, [lambda e: e.dma_start(out=s_mix[m, :, 0:4, :], in_=yT.rearrange("p (c t) -> p c t", c=4))],
                      reads=[R_yT], writes=[R_smix[m]], nbytes=131072)

            lat = lat_sb
            ckv_ap = lat[:, 256 + latoff:384 + latoff]
            kpe_ap = lat[:, 384 + latoff:416 + latoff]
            ssq = st_small[:, 4:6]
            rq = st_small[:, 6:8]
            if full:
                S.op("act", lambda e: e.activation(out=cqn, in_=lat[:, 0:256], func=AF.Square, accum_out=ssq[:, 0:1]),
                     reads=[R_lat], writes=[R_cqn, R_ssm])
            S.op("act", lambda e: e.activation(out=ckvn, in_=ckv_ap, func=AF.Square, accum_out=ssq[:, 1:2]),
                 reads=[R_lat], writes=[R_ckvn, R_ssm])
            if full:
                S.op("dve", lambda e: e.tensor_scalar(out=ssq[:, 0:1], in0=ssq[:, 0:1], scalar1=1.0 / 256, scalar2=EPS, op0=ALU.mult, op1=ALU.add),
                     reads=[R_ssm], writes=[R_ssm])
            S.op("dve", lambda e: e.tensor_scalar(out=ssq[:, 1:2], in0=ssq[:, 1:2], scalar1=1.0 / 128, scalar2=EPS, op0=ALU.mult, op1=ALU.add),
                 reads=[R_ssm], writes=[R_ssm])
            lo = 0 if full else 1
            S.op("act", lambda e, lo=lo: e.activation(out=ssq[:, lo:2], in_=ssq[:, lo:2], func=AF.Sqrt), reads=[R_ssm], writes=[R_ssm])
            S.op("dve", lambda e, lo=lo: e.reciprocal(out=rq[:, lo:2], in_=ssq[:, lo:2]), reads=[R_ssm], writes=[R_ssm])
            if full:
                S.op("dve", lambda e: e.scalar_tensor_tensor(out=cqn, in0=lat[:, 0:256], scalar=rq[:, 0:1], in1=qnw_t, op0=ALU.mult, op1=ALU.mult),
                     reads=[R_lat, R_ssm, R_qnw], writes=[R_cqn])
            S.op("dve", lambda e: e.scalar_tensor_tensor(out=ckvn, in0=ckv_ap, scalar=rq[:, 1:2], in1=kvnw_t, op0=ALU.mult, op1=ALU.mult),
                 reads=[R_lat, R_ssm, R_kvnw], writes=[R_ckvn])
            rope(kpe_ap, 1, 32, cs_m, ss_m, tA2, tB2, kr, [R_lat], R_kr, RA=R_tA2, RB=R_tB2)
            if KSUB == 5 and n >= HALO:
                return

            S.op("pe", lambda e: e.transpose(out=tb[:, 0:128], in_=ckvn, identity=ident), reads=[R_ckvn, R_ident], writes=[PB[0]], inc=False)
            S.op("pe", lambda e: e.transpose(out=tb[0:32, 128:256], in_=kr, identity=ident), reads=[R_kr, R_ident], writes=[PB[0]], inc=not full)
            if full:
                for c in range(2):
                    S.op("pe", lambda e, c=c: e.transpose(out=tb[:, 256 + c * 128:256 + (c + 1) * 128], in_=cqn[:, c * 128:(c + 1) * 128], identity=ident),
                         reads=[R_cqn, R_ident], writes=[PB[0]], inc=(c == 1))
            S.op("act", lambda e: e.activation(out=ckvnT, in_=tb[:, 0:128], func=AF.Copy), reads=[PB[0]], writes=[R_ckvnT])
            S.op("act", lambda e, gb=gb, gi=gi: e.activation(out=kpe_g[gb][0:32, gi * 128:(gi + 1) * 128], in_=tb[0:32, 128:256], func=AF.Copy),
                 reads=[PB[0]], writes=[R_kpg[gb]])
            if KSUB == 6 and n >= HALO:
                return

            if full:
                S.op("act", lambda e: e.activation(out=cqnT, in_=tb[:, 256:512], func=AF.Copy), reads=[PB[0]], writes=[R_cqnT])
            for p_ in range(4):
                S.op("pe", lambda e, p_=p_: e.matmul(ps[:, 3072 + p_ * 128:3072 + (p_ + 1) * 128], lhsT=wk[:, p_ * 128:(p_ + 1) * 128], rhs=ckvnT,
                                                     start=True, stop=True), reads=[R_wk, R_ckvnT], writes=[PB[6]], inc=(p_ == 3))
            kng3 = kn_g[gb].rearrange("p (a k) -> p a k", a=4)
            S.op("act", lambda e, gi=gi, kng3=kng3: e.activation(out=kng3[:, :, gi * 128:(gi + 1) * 128], in_=bank(6).rearrange("p (a k) -> p a k", a=4), func=AF.Copy),
                 reads=[PB[6]], writes=[R_kng[gb]])
            S.op("pe", lambda e: e.matmul(bank(5), lhsT=ckvnT, rhs=wv, start=True, stop=True), reads=[R_wv, R_ckvnT], writes=[PB[5]])
            vg4 = v_g[gb].rearrange("p (h t e) -> p h t e", h=8, t=4)
            S.op("act", lambda e, gi=gi, vg4=vg4: e.activation(out=vg4[:, :, gi, 0:64], in_=bank(5).rearrange("p (h e) -> p h e", h=8), func=AF.Copy),
                 reads=[PB[5]], writes=[R_vg[gb]])
            S.op("dve", lambda e, gi=gi, vg4=vg4, n=n: e.tensor_copy(out=vg4[:, :, gi, 64:65], in_=bc_mid(valid_t[:, n:n + 1], 8)),
                 reads=[R_valid], writes=[R_vg[gb]])
            if KSUB == 7 and n >= HALO:
                return

            if full:
                for hf in range(2):
                    for c in range(2):
                        S.op("pe", lambda e, hf=hf, c=c: e.matmul(ps[:, (5 + hf) * 512:(5 + hf) * 512 + 384], lhsT=cqnT[:, c * 128:(c + 1) * 128],
                                                                  rhs=w_uq3[:, c, hf * 384:(hf + 1) * 384], start=(c == 0), stop=(c == 1)),
                             reads=[R_cqnT, R_wuq], writes=[PB[5 + hf]], inc=(c == 1))
                qb3 = qb.rearrange("p (h d) -> p h d", h=8)
                for hf in range(2):
                    src = ps[:, (5 + hf) * 512:(5 + hf) * 512 + 384]
                    s3 = src.rearrange("p (h d) -> p h d", h=4)
                    S.op("act", lambda e, hf=hf, s3=s3: e.activation(out=qb3[:, hf * 4:(hf + 1) * 4, 0:64], in_=s3[:, :, 0:64], func=AF.Copy),
                         reads=[PB[5 + hf]], writes=[R_qb])
                    x3 = s3[:, :, 64:96]
                    sw = bass.AP(src.tensor, src.offset + 64 + 16, [list(src.ap[0]), [96, 4], [-16, 2], [1, 16]])
                    a3 = tA3[:, hf * 128:(hf + 1) * 128].rearrange("p (h d) -> p h d", h=4)
                    b4 = tB3[:, hf * 128:(hf + 1) * 128].rearrange("p (h a d) -> p h a d", h=4, a=2)
                    S.op("dve", lambda e, x3=x3, a3=a3: e.tensor_tensor(out=a3, in0=x3, in1=bc_mid(cs_m, 4), op=ALU.mult),
                         reads=[PB[5 + hf], R_tab], writes=[R_tA3])
                    S.op("dve", lambda e, sw=sw, b4=b4: e.tensor_tensor(out=b4, in0=sw, in1=bc_mid(ss_m.rearrange("p (a d) -> p a d", a=2), 4), op=ALU.mult),
                         reads=[PB[5 + hf], R_tab], writes=[R_tB3])
                    S.op("dve", lambda e, hf=hf, a3=a3: e.tensor_tensor(out=qb3[:, hf * 4:(hf + 1) * 4, 64:96], in0=a3,
                                                                        in1=tB3[:, hf * 128:(hf + 1) * 128].rearrange("p (h d) -> p h d", h=4), op=ALU.add),
                         reads=[R_tA3, R_tB3], writes=[R_qb])
                for h in range(8):
                    S.op("pe", lambda e, h=h: e.transpose(out=tb[0:96, h * 128:(h + 1) * 128], in_=qb[:, h * 96:(h + 1) * 96], identity=ident),
                         reads=[R_qb, R_ident], writes=[PB[0]], inc=(h == 7))
                mg, mi = divmod(n - HALO, 4)
                qg3 = q_g[mg % 2].rearrange("p (h t) -> p h t", h=8)
                S.op("act", lambda e, mi=mi, qg3=qg3: e.activation(out=qg3[0:96, :, mi * 128:(mi + 1) * 128],
                                                                   in_=tb[0:96, :].rearrange("p (h t) -> p h t", h=8), func=AF.Copy),
                     reads=[PB[0]], writes=[R_qg[mg % 2]])
                if mi == 3 or n == NT - 1:
                    ntl = mi + 1
                    S.dma("sp", "sq%d" % (mg % 2),
                          [lambda e, mg=mg, ntl=ntl, qg3=qg3: e.dma_start(
                              out=s_qt[:, :, mg * 512:mg * 512 + ntl * 128].rearrange("h r t -> r h t"),
                              in_=qg3[0:96, :, 0:ntl * 128])],
                          reads=[R_qg[mg % 2]], writes=[Res("sqt")])
            if gi == 3:
                fns = []
                for a_ in range(2):
                    fns.append(lambda e, a_=a_, g=g, kng3=kng3: e.dma_start(
                        out=s_kt[:, 0:64, g * 512:(g + 1) * 512].rearrange("(p a) r k -> a r p k", a=2)[a_],
                        in_=kng3[a_ * 64:(a_ + 1) * 64, :, :]))
                fns.append(lambda e, g=g, gb=gb: e.dma_start(
                    out=s_kt[:, 64:96, g * 512:(g + 1) * 512].rearrange("h r k -> r h k"),
                    in_=bc_mid(kpe_g[gb][0:32, :], 8)))
                S.dma("sp", "sk%d" % gb, fns, reads=[R_kng[gb], R_kpg[gb]], writes=[Res("skt")])
                S.dma("sp", "sv%d" % gb,
                      [lambda e, g=g, vg4=vg4: e.dma_start(
                          out=s_v[:, :, g * 4 * 65:(g + 1) * 4 * 65].rearrange("h p (t e) -> p h t e", t=4),
                          in_=vg4)],
                      reads=[R_vg[gb]], writes=[Res("sv")])

        for n in range(int(os.environ.get('KNT', NT))):
            _tileA(n)
        while pk[1] < len(pieces):
            cast_step(2)

        S.barrier()
        if KSTOP == 'A':
            S.emit(); return nc
        A.off = mark_persist
        ytmp = [A.bf(512) for _ in range(2)]
        R_ytmp = [Res("ytmp0"), Res("ytmp1")]
        QT = [A.bf(NOWN * 128) for _ in range(2)]
        KT = [A.bf(NT * 128) for _ in range(2)]
        VV = [A.bf(NT * 65) for _ in range(2)]
        R_Q, R_K, R_V = [Res("Q0"), Res("Q1")], [Res("K0"), Res("K1")], [Res("V0"), Res("V1")]
        PT = [A.bf(1024) for _ in range(9)]
        R_PT = [Res("PT%d" % i) for i in range(9)]
        rrow = A.f32(512)
        R_rrow = Res("rrow")
        ones_t = A.f32(64)
        R_ones = Res("ones")
        bcs = A.f32(512)
        R_bcs = Res("bcs")
        S.op("dve", lambda e: e.memset(ones_t, 1.0), writes=[R_ones])
        scale = (64 + 32) ** -0.5

        def load_head(h):
            i = h % 2
            S.dma("sp", "lq%d" % i, [lambda e: e.dma_start(out=QT[i][0:96, :], in_=s_qt[h])], reads=[R_sqt], writes=[R_Q[i]])
            S.dma("sp", "lk%d" % i, [lambda e: e.dma_start(out=KT[i][0:96, :], in_=s_kt[h])], reads=[R_skt], writes=[R_K[i]])
            S.dma("sp", "lv%d" % i, [lambda e: e.dma_start(out=VV[i], in_=s_v[h])], reads=[R_sv], writes=[R_V[i]])

        load_head(0)

        groups = []
        blk_id = 0
        for h in range(8):
            qblocks = [(0, 128, [(kt, 0) for kt in range(HALO)] + [(HALO, 0)], HALO)]
            for j in range(8):
                kts = [(kt, 0) for kt in range(32 + 4 * j)] + [(32 + 4 * j + m, 128 * m) for m in range(4)]
                qblocks.append((128 + 512 * j, 512, kts, 32 + 4 * j))
            for (q0, qw, kts, diag0) in qblocks:
                npairs = (len(kts) + 1) // 2
                for gidx in range(npairs):
                    groups.append(dict(h=h, i=h % 2, q0=q0, qw=qw, pair=kts[2 * gidx:2 * gidx + 2], diag0=diag0,
                                       ob=4 + blk_id % 2, first=(gidx == 0), last=(gidx == npairs - 1),
                                       sb=(len(groups) % 2) * 2, pt=len(groups) % 9, yi=blk_id % 2,
                                       newhead=(gidx == 0 and q0 == 0)))
                blk_id += 1

        def emit_qk(g):
            i, q0, qw, sb_ = g["i"], g["q0"], g["qw"], g["sb"]
            if g["newhead"] and g["h"] + 1 < 8:
                load_head(g["h"] + 1)
            for u, (kt, c0) in enumerate(g["pair"]):
                dst = ps[:, (sb_ + u) * 512 + c0:(sb_ + u) * 512 + qw]
                isdiag = kt >= g["diag0"]
                S.op("pe", lambda e, kt=kt, c0=c0, dst=dst, isdiag=isdiag: e.matmul(
                    dst, lhsT=KT[i][0:96, kt * 128:(kt + 1) * 128], rhs=QT[i][0:96, q0 + c0:q0 + qw], start=True, stop=not isdiag),
                    reads=[R_K[i], R_Q[i]], writes=[PB[sb_ + u]], inc=not isdiag)
                if isdiag:
                    S.op("pe", lambda e, dst=dst: e.matmul(dst[:, 0:128], lhsT=ident, rhs=maskb, start=False, stop=True),
                         reads=[R_ident, R_mask], writes=[PB[sb_ + u]])

        def emit_exp(g):
            qw, sb_, pair = g["qw"], g["sb"], g["pair"]
            pt, Rpt = PT[g["pt"]], R_PT[g["pt"]]
            if len(pair) == 2 and pair[0][1] == 0 and pair[1][1] == 0 and qw == 512:
                S.op("act", lambda e: e.activation(out=pt, in_=ps[:, sb_ * 512:sb_ * 512 + 1024], func=AF.Exp, scale=scale),
                     reads=[PB[sb_], PB[sb_ + 1]], writes=[Rpt])
            else:
                for u, (kt, c0) in enumerate(pair):
                    S.op("act", lambda e, u=u, c0=c0: e.activation(
                        out=pt[:, u * 512 + c0:u * 512 + qw], in_=ps[:, (sb_ + u) * 512 + c0:(sb_ + u) * 512 + qw], func=AF.Exp, scale=scale),
                        reads=[PB[sb_ + u]], writes=[Rpt])

        def emit_pv(g):
            i, qw, ob, pair = g["i"], g["qw"], g["ob"], g["pair"]
            pt, Rpt = PT[g["pt"]], R_PT[g["pt"]]
            V3 = VV[i].rearrange("p (t e) -> p t e", t=NT)
            for u, (kt, c0) in enumerate(pair):
                first = g["first"] and u == 0
                last = g["last"] and u == len(pair) - 1
                S.op("pe", lambda e, kt=kt, c0=c0, u=u, first=first, last=last: e.matmul(
                    ps[0:65, ob * 512 + c0:ob * 512 + qw], lhsT=V3[:, kt, :], rhs=pt[:, u * 512 + c0:u * 512 + qw], start=first, stop=last),
                    reads=[R_V[i], Rpt], writes=[PB[ob]], inc=(u == len(pair) - 1))

        def emit_norm(g):
            h, q0, qw, ob, yi_ = g["h"], g["q0"], g["qw"], g["ob"], g["yi"]
            S.op("dve", lambda e: e.tensor_scalar(out=rrow[64:65, 0:qw], in0=ps[64:65, ob * 512:ob * 512 + qw], scalar1=1e-30, scalar2=None, op0=ALU.max),
                 reads=[PB[ob]], writes=[R_rrow])
            S.op("dve", lambda e: e.reciprocal(out=rrow[64:65, 0:qw], in_=rrow[64:65, 0:qw]), reads=[R_rrow], writes=[R_rrow])
            S.op("pe", lambda e: e.matmul(ps[0:64, 6 * 512:6 * 512 + qw], lhsT=ones_t[64:65, 0:64], rhs=rrow[64:65, 0:qw], start=True, stop=True),
                 reads=[R_ones, R_rrow], writes=[PB[6]])
            S.op("dve", lambda e: e.tensor_copy(out=bcs[0:64, 0:qw], in_=ps[0:64, 6 * 512:6 * 512 + qw]), reads=[PB[6]], writes=[R_bcs])
            S.op("dve", lambda e: e.tensor_tensor(out=ytmp[yi_][0:64, 0:qw], in0=ps[0:64, ob * 512:ob * 512 + qw], in1=bcs[0:64, 0:qw], op=ALU.mult),
                 reads=[PB[ob], R_bcs], writes=[R_ytmp[yi_]])
            m0_, nt_ = q0 // 128, qw // 128
            S.dma("sp", "ym%d" % yi_, [lambda e: e.dma_start(
                out=s_mix[m0_:m0_ + nt_, (h % 2) * 64:(h % 2) * 64 + 64, 4 + h // 2, :].rearrange("m r t -> r m t"),
                in_=ytmp[yi_][0:64, 0:qw].rearrange("r (m t) -> r m t", m=nt_))],
                reads=[R_ytmp[yi_]], writes=R_smix[m0_:m0_ + nt_], nbytes=65536)

        pend_norm = None
        emit_qk(groups[0])
        for gi_, g in enumerate(groups):
            emit_exp(g)
            if gi_ + 1 < len(groups):
                emit_qk(groups[gi_ + 1])
            emit_pv(g)
            if pend_norm is not None:
                emit_norm(pend_norm)
                pend_norm = None
            if g["last"]:
                pend_norm = g
        if pend_norm is not None:
            emit_norm(pend_norm)

        S.barrier()
        if KSTOP == 'AB':
            S.emit(); return nc
        A.off = mark_persist
        wout = A.bf(8 * 1024)
        wout3 = wout.rearrange("p (c f) -> p c f", c=8)
        wdown = A.bf(22 * 1024)
        wdown3 = wdown.rearrange("p (c f) -> p c f", c=22)
        R_wout, R_wdown = Res("wout"), Res("wdown")
        S.dma("sp", "wc1", [lambda e: e.dma_start(out=wout, in_=s_wout)], reads=[R_scr["s_wout"]], writes=[R_wout])
        S.dma("sp", "wc2", [lambda e: e.dma_start(out=wdown, in_=s_wdown)], reads=[R_scr["s_wdown"]], writes=[R_wdown])
        fnw_t, R_fnw = load_const(b_fnw, 1024, "fnw")
        onw_t, R_onw = load_const(b_onw, 1024, "onw")
        cw_t, R_cw = load_const(c_cw, 132, "cw")
        cw3 = cw_t.rearrange("p (c j) -> p c j", c=44)
        cb_t, R_cb = load_const(c_cb, 44, "cb")
        mixb = [A.bf(1024) for _ in range(2)]
        R_mixb = [Res("mixb%d" % i) for i in range(2)]
        NWU = 3
        wupb = [A.bf(2048) for _ in range(NWU)]
        R_wupb = [Res("wup%d" % i) for i in range(NWU)]
        xb2 = [A.f32(1024) for _ in range(2)]
        R_xb2 = [Res("xb2_%d" % i) for i in range(2)]
        jnk = A.bf(1024)
        R_jnk = Res("jnk")
        x1 = [A.f32(4 * 1024) for _ in range(2)]
        R_x1 = [[Res("x1_%d_%d" % (j, i)) for i in range(4)] for j in range(2)]
        h2b = [A.bf(1024) for _ in range(2)]
        R_h2b = [Res("h2b0"), Res("h2b1")]
        h2T = [A.bf(8 * 512) for _ in range(2)]
        R_h2T = [Res("h2T0"), Res("h2T1")]
        gT = [A.bf(22 * 512) for _ in range(2)]
        R_gT = [Res("gT0"), Res("gT1")]
        ubuf = [A.f32(514) for _ in range(2)]
        R_ub = [Res("ub0"), Res("ub1")]
        acc = [[A.f32(512) for _ in range(2)] for _ in range(2)]
        R_acc = [[Res("acc%d%d" % (h_, p_)) for p_ in range(2)] for h_ in range(2)]
        carry = A.f32(44 * 2)
        carry3 = carry.rearrange("p (c j) -> p c j", c=44)
        R_carry = [Res("carry%d" % i) for i in range(44)]
        st2 = [A.f32(8) for _ in range(2)]
        R_st2 = [Res("st2_0"), Res("st2_1")]
        R_st3 = [Res("st3_0"), Res("st3_1")]
        ybuf = xb2
        R_yb2 = R_xb2
        S.op("dve", lambda e: e.memset(carry, 0.0), writes=R_carry)
        wupi = [0]
        xli = [0]
        ybi = [0]
        y_tiles = yout.rearrange("(n p) d -> n p d", p=128)

        blocks = [(0, 1)] + [(1 + 4 * j, 4) for j in range(8)]
        wuc = [0]

        def _blockC(bi, m0, ntl):
            W = ntl * 128
            pb = bi % 2
            x13 = x1[pb].rearrange("p (t d) -> p t d", t=4)
            Rx1 = R_x1[pb]
            h2T3 = h2T[pb].rearrange("p (c t) -> p c t", c=8)
            gT3 = gT[pb].rearrange("p (c t) -> p c t", c=22)
            tb = bank(7).bitcast(BF16)

            def _s1(t):
                m = m0 + t
                xi_ = xli[0] % 2
                xli[0] += 1
                S.dma("sp", "xc%d" % xi_, [lambda e: e.dma_start(out=xb2[xi_], in_=x_tiles[HALO + m])], writes=[R_xb2[xi_]], nbytes=524288)
                mi_ = m % 2
                S.dma("sp", "mx%d" % mi_, [lambda e: e.dma_start(out=mixb[mi_].rearrange("p (c t) -> p c t", c=8), in_=s_mix[m])],
                      reads=[R_smix[m]], writes=[R_mixb[mi_]])
                mix3 = mixb[mi_].rearrange("p (c t) -> p c t", c=8)
                for hf in range(2):
                    for c in range(8):
                        S.op("pe", lambda e, c=c, hf=hf: e.matmul(bank(hf), lhsT=mix3[:, c, :], rhs=wout3[:, c, hf * 512:(hf + 1) * 512],
                                                                  start=(c == 0), stop=(c == 7)),
                             reads=[R_mixb[mi_], R_wout], writes=[PB[hf]], inc=(c == 7))
                S.op("dve", lambda e: e.tensor_tensor(out=x13[:, t, :], in0=ps[:, 0:1024], in1=xb2[xi_], op=ALU.add),
                     reads=[PB[0], PB[1], R_xb2[xi_]], writes=[Rx1[t]])
                tp = t % 2
                hb_, Rhb = h2b[tp], R_h2b[tp]
                ss, rstd, Rst = st2[tp][:, 0:1], st2[tp][:, 1:2], R_st2[tp]
                S.op("act", lambda e: e.activation(out=hb_, in_=x13[:, t, :], func=AF.Square, accum_out=ss), reads=[Rx1[t]], writes=[Rhb, Rst])
                S.op("dve", lambda e: e.tensor_scalar(out=ss, in0=ss, scalar1=1.0 / 1024, scalar2=EPS, op0=ALU.mult, op1=ALU.add), reads=[Rst], writes=[Rst])
                S.op("act", lambda e: e.activation(out=ss, in_=ss, func=AF.Sqrt), reads=[Rst], writes=[Rst])
                S.op("dve", lambda e: e.reciprocal(out=rstd, in_=ss), reads=[Rst], writes=[Rst])
                S.op("dve", lambda e: e.scalar_tensor_tensor(out=hb_, in0=x13[:, t, :], scalar=rstd, in1=fnw_t, op0=ALU.mult, op1=ALU.mult),
                     reads=[Rx1[t], Rst, R_fnw], writes=[Rhb])
                for c in range(8):
                    S.op("pe", lambda e, c=c: e.transpose(out=tb[:, c * 128:(c + 1) * 128], in_=hb_[:, c * 128:(c + 1) * 128], identity=ident),
                         reads=[Rhb, R_ident], writes=[PB[7]], inc=(c == 7))
                S.op("act", lambda e: e.activation(out=h2T3[:, :, t * 128:(t + 1) * 128], in_=tb.rearrange("p (c t) -> p c t", c=8), func=AF.Copy),
                     reads=[PB[7]], writes=[R_h2T[pb]])

            for t in range(ntl):
                _s1(t)

            def _chunk(fc, half):
                cidx = fc + 22 * half
                fp = fc % 2
                if half == 0:
                    wupi[0] += 1
                wi = wupi[0] % NWU
                if half == 0:
                    S.dma("sp", "wu%d" % wi, [lambda e: e.dma_start(out=wupb[wi], in_=s_wup[:, fc * 2048:(fc + 1) * 2048])],
                          writes=[R_wupb[wi]], nbytes=524288)
                bk = 2 + wuc[0] % 3
                wuc[0] += 1
                w3 = wupb[wi][:, half * 1024:(half + 1) * 1024].rearrange("p (c f) -> p c f", c=8)
                for c in range(8):
                    S.op("pe", lambda e, c=c: e.matmul(bank(bk, W), lhsT=w3[:, c, :], rhs=h2T3[:, c, 0:W], start=(c == 0), stop=(c == 7)),
                         reads=[R_wupb[wi], R_h2T[pb]], writes=[PB[bk]], inc=(c == 7))
                Rc = R_carry[cidx]
                if bi == 0:
                    S.op("dve", lambda e: e.tensor_copy(out=carry3[:, cidx, :], in_=bank(bk, W)[:, W - 2:W]), reads=[PB[bk]], writes=[Rc])
                    return
                ac, Rac = acc[half][fp], R_acc[half][fp]
                if half == 0:
                    ub, Rub = ubuf[fp], R_ub[fp]
                    S.op("act", lambda e: e.activation(out=ub[:, 0:2], in_=carry3[:, cidx, :], func=AF.Copy), reads=[Rc], writes=[Rub])
                    S.op("act", lambda e: e.activation(out=ub[:, 2:2 + W], in_=bank(bk, W), func=AF.Copy), reads=[PB[bk]], writes=[Rub])
                    S.op("act", lambda e: e.activation(out=ac[:, 0:W], in_=bank(bk, W), func=AF.Identity,
                                                       scale=cw3[:, cidx, 2:3], bias=cb_t[:, cidx:cidx + 1]),
                         reads=[PB[bk], R_cw, R_cb], writes=[Rac])
                    S.op("dve", lambda e: e.tensor_copy(out=carry3[:, cidx, :], in_=ub[:, W:W + 2]), reads=[Rub], writes=[Rc])
                    S.op("dve", lambda e: e.scalar_tensor_tensor(out=ac[:, 0:W], in0=ub[:, 1:1 + W], scalar=cw3[:, cidx, 1:2], in1=ac[:, 0:W],
                                                                 op0=ALU.mult, op1=ALU.add), reads=[Rub, Rac, R_cw], writes=[Rac])
                    S.op("dve", lambda e: e.scalar_tensor_tensor(out=ac[:, 0:W], in0=ub[:, 0:W], scalar=cw3[:, cidx, 0:1], in1=ac[:, 0:W],
                                                                 op0=ALU.mult, op1=ALU.add), reads=[Rub, Rac, R_cw], writes=[Rac])
                    S.op("act", lambda e: e.activation(out=ac[:, 0:W], in_=ac[:, 0:W], func=AF.Silu), reads=[Rac], writes=[Rac])
                else:
                    pu = bank(bk, W)
                    S.op("act", lambda e: e.activation(out=ac[:, 0:W], in_=pu, func=AF.Identity,
                                                       scale=cw3[:, cidx, 2:3], bias=cb_t[:, cidx:cidx + 1]),
                         reads=[PB[bk], R_cw, R_cb], writes=[Rac])
                    S.op("dve", lambda e: e.scalar_tensor_tensor(out=ac[:, 1:W], in0=pu[:, 0:W - 1], scalar=cw3[:, cidx, 1:2], in1=ac[:, 1:W],
                                                                 op0=ALU.mult, op1=ALU.add), reads=[PB[bk], Rac, R_cw], writes=[Rac])
                    S.op("dve", lambda e: e.scalar_tensor_tensor(out=ac[:, 2:W], in0=pu[:, 0:W - 2], scalar=cw3[:, cidx, 0:1], in1=ac[:, 2:W],
                                                                 op0=ALU.mult, op1=ALU.add), reads=[PB[bk], Rac, R_cw], writes=[Rac])
                    S.op("dve", lambda e: e.scalar_tensor_tensor(out=ac[:, 0:1], in0=carry3[:, cidx, 1:2], scalar=cw3[:, cidx, 1:2], in1=ac[:, 0:1],
                                                                 op0=ALU.mult, op1=ALU.add), reads=[Rc, Rac, R_cw], writes=[Rac])
                    S.op("dve", lambda e: e.scalar_tensor_tensor(out=ac[:, 0:2], in0=carry3[:, cidx, 0:2], scalar=cw3[:, cidx, 0:1], in1=ac[:, 0:2],
                                                                 op0=ALU.mult, op1=ALU.add), reads=[Rc, Rac, R_cw], writes=[Rac])
                    S.op("dve", lambda e: e.tensor_copy(out=carry3[:, cidx, :], in_=pu[:, W - 2:W]), reads=[PB[bk]], writes=[Rc])
                    S.op("pool", lambda e: e.tensor_tensor(out=gT3[:, fc, 0:W], in0=acc[0][fp][:, 0:W], in1=ac[:, 0:W], op=ALU.mult),
                         reads=[R_acc[0][fp], Rac], writes=[R_gT[pb]])

            for fc in range(22):
                for half in range(2):
                    _chunk(fc, half)
            if bi == 0:
                return

            def _s3(t):
                m = m0 + t
                for hf in range(2):
                    for fc in range(22):
                        S.op("pe", lambda e, fc=fc, hf=hf: e.matmul(bank(5 + hf), lhsT=gT3[:, fc, t * 128:(t + 1) * 128],
                                                                    rhs=wdown3[:, fc, hf * 512:(hf + 1) * 512], start=(fc == 0), stop=(fc == 21)),
                             reads=[R_gT[pb], R_wdown], writes=[PB[5 + hf]], inc=(fc == 21))
                S.op("dve", lambda e: e.tensor_tensor(out=x13[:, t, :], in0=ps[:, 2560:3584], in1=x13[:, t, :], op=ALU.add),
                     reads=[PB[5], PB[6], Rx1[t]], writes=[Rx1[t]])
                tp = t % 2
                ss2, rstd2, Rst = st2[tp][:, 2:3], st2[tp][:, 3:4], R_st3[tp]
                S.op("act", lambda e: e.activation(out=jnk, in_=x13[:, t, :], func=AF.Square, accum_out=ss2), reads=[Rx1[t]], writes=[R_jnk, Rst])
                S.op("dve", lambda e: e.tensor_scalar(out=ss2, in0=ss2, scalar1=1.0 / 1024, scalar2=EPS, op0=ALU.mult, op1=ALU.add), reads=[Rst], writes=[Rst])
                S.op("act", lambda e: e.activation(out=ss2, in_=ss2, func=AF.Sqrt), reads=[Rst], writes=[Rst])
                S.op("dve", lambda e: e.reciprocal(out=rstd2, in_=ss2), reads=[Rst], writes=[Rst])
                S.op("dve", lambda e: e.scalar_tensor_tensor(out=x13[:, t, :], in0=x13[:, t, :], scalar=rstd2, in1=onw_t, op0=ALU.mult, op1=ALU.mult),
                     reads=[Rx1[t], Rst, R_onw], writes=[Rx1[t]])
                S.dma("sp", "yo%d_%d" % (pb, t), [lambda e: e.dma_start(out=y_tiles[m - 1], in_=x13[:, t, :])], reads=[Rx1[t]], nbytes=524288)

            for t in range(ntl):
                _s3(t)

        for bi, (m0, ntl) in enumerate(blocks):
            _blockC(bi, m0, ntl)

        S.barrier()
        S.emit()
    return nc


def _consts():
    H, C = 8, 128
    lg = np.log1p(-np.power(2.0, -5.0 - np.arange(H, dtype=np.float64)))
    idx = np.arange(C, dtype=np.float64)
    diff = idx[None, :] - idx[:, None]
    dt = np.where(diff[:, None, :] >= 0, np.exp(lg[None, :, None] * np.maximum(diff[:, None, :], 0.0)), 0.0) / 8.0
    c_dt = dt.reshape(128, 1024).astype(np.float32)
    xi = np.exp(lg[:, None] * (idx[None, :] + 1.0))
    c_xi = np.zeros((128, 4, 128))
    for p in range(4):
        for a in range(2):
            c_xi[a * 64:(a + 1) * 64, p, :] = xi[2 * p + a][None, :]
    c_xi = c_xi.reshape(128, 512).astype(np.float32)
    zeta = np.exp(lg[:, None] * (C - 1.0 - idx[None, :])) / 8.0
    c_zeta = np.repeat(zeta.T[:, :, None], 64, axis=2).reshape(128, 512).astype(np.float32)
    dec = np.exp(lg * C)
    c_dec = np.zeros((128, 4))
    for p in range(4):
        c_dec[0:64, p] = dec[2 * p]
        c_dec[64:128, p] = dec[2 * p + 1]
    c_dec = c_dec.astype(np.float32)
    fr = (10000.0 ** (-np.arange(0, 64, 2, dtype=np.float32) / np.float32(64))).astype(np.float32)
    fm = (10000.0 ** (-np.arange(0, 32, 2, dtype=np.float32) / np.float32(32))).astype(np.float32)
    invf = np.concatenate([fr, fr, fr, fr, fm, fm, fm, fm]).astype(np.float64) / (2 * np.pi)
    off = np.concatenate([np.full(64, 0.25), np.full(32, 0.5), np.zeros(32), np.full(32, 0.25), np.full(16, 0.5), np.zeros(16)])
    c_invf = np.broadcast_to(invf[None, :], (128, 192)).astype(np.float32).copy()
    c_off = np.broadcast_to(off[None, :], (128, 192)).astype(np.float32).copy()
    k = np.arange(128)
    c_mask = np.where(k[None, :] < k[:, None], -30000.0, 0.0).astype(np.float32)
    return dict(c_dt=c_dt, c_xi=c_xi, c_zeta=c_zeta, c_dec=c_dec, c_invf=c_invf, c_off=c_off, c_mask=c_mask)


def _bc(v, n=128):
    return np.ascontiguousarray(np.broadcast_to(np.asarray(v, np.float32)[None, :], (n, v.shape[0])))


_PROG = None


def kernel(x, positions, attn_norm_w, w_in, ret_gn_w, mla_q_norm_w, w_uq, mla_kv_norm_w, w_ukv,
           w_out, ffn_norm_w, w_up, conv_w, conv_b, w_down, final_norm_w):
    global _PROG
    x = np.asarray(x, np.float32)
    positions = np.asarray(positions, np.int32)
    shared = _consts()
    shared["b_anw"] = _bc(np.asarray(attn_norm_w)[0])
    shared["b_fnw"] = _bc(np.asarray(ffn_norm_w)[0])
    shared["b_onw"] = _bc(np.asarray(final_norm_w))
    shared["b_qnw"] = _bc(np.asarray(mla_q_norm_w)[0])
    shared["b_kvnw"] = _bc(np.asarray(mla_kv_norm_w)[0])
    shared["b_gnw"] = _bc(np.asarray(ret_gn_w)[0])
    cw = np.asarray(conv_w, np.float32)[0]
    shared["c_cw"] = np.ascontiguousarray(cw.reshape(3, 44, 128).transpose(2, 1, 0)).reshape(128, 132)
    shared["c_cb"] = np.ascontiguousarray(np.asarray(conv_b, np.float32)[0].reshape(44, 128).T)
    shared["w_in_l"] = np.ascontiguousarray(np.asarray(w_in, np.float32)[0].reshape(8, 128, 2464).transpose(1, 0, 2)).reshape(128, -1)
    shared["w_uq_l"] = np.ascontiguousarray(np.asarray(w_uq, np.float32)[0].reshape(2, 128, 768).transpose(1, 0, 2)).reshape(128, -1)
    wukv = np.asarray(w_ukv, np.float32)[0].reshape(128, 8, 128)
    shared["wk_l"] = np.ascontiguousarray(wukv[:, :, 0:64]).reshape(128, 512)
    shared["wv_l"] = np.ascontiguousarray(wukv[:, :, 64:128]).reshape(128, 512)
    wo = np.asarray(w_out, np.float32)[0]
    shared["w_out_l"] = np.ascontiguousarray(wo.reshape(8, 128, 1024).transpose(1, 0, 2)).reshape(128, -1)
    wu = np.asarray(w_up, np.float32)[0]
    shared["w_up_l"] = np.ascontiguousarray(wu.reshape(8, 128, 2, 22, 128).transpose(1, 3, 2, 0, 4)).reshape(128, -1)
    wd = np.asarray(w_down, np.float32)[0]
    shared["w_down_l"] = np.ascontiguousarray(wd.reshape(22, 128, 1024).transpose(1, 0, 2)).reshape(128, -1)

    in_maps = []
    for c in range(8):
        b, z = divmod(c, 2)
        m = dict(shared)
        if z == 1:
            xcore = x[b]
            pc = positions[b]
            vd = np.ones(8192, np.float32)
        else:
            xcore = np.concatenate([np.zeros((4096, 1024), np.float32), x[b, :4096]], axis=0)
            pc = np.concatenate([np.zeros(4096, np.int32), positions[b, :4096]])
            vd = np.concatenate([np.zeros(4096, np.float32), np.ones(4096, np.float32)])
        m["xc"] = np.ascontiguousarray(xcore)
        m["posc"] = np.ascontiguousarray(pc.reshape(NT, 128).T)
        m["valid"] = np.ascontiguousarray(vd.reshape(NT, 128).T)
        in_maps.append(m)
    if _PROG is None:
        _PROG = build_program()
    res = run_bass_kernel_spmd(_PROG, in_maps, core_ids=list(range(8)))
    out = np.empty((4, 8192, 1024), np.float32)
    for c in range(8):
        b, z = divmod(c, 2)
        out[b, z * 4096:(z + 1) * 4096] = res.results[c]["yout"]
    return out
```
